# Optimizing a Trainium2 kernel written in Bass

```python
import math
import jax, jax.numpy as jnp
from jax import lax
import numpy as np

D_MODEL = 1024
BATCH = 1
SEQ = 16384
DEPTH = 1
DEC_BATCH = 2
DEC_SEQ = 16384
PAST_LEN = 128

N_META = 16
EPS = 1e-6
S5_WIDTH = D_MODEL // 2
S5_GROUP = 16
S5_GROUPS = S5_WIDTH // S5_GROUP
S5_STATE = 64
S5_DT_MIN = 0.001
S5_DT_MAX = 0.1
HG_WIDTH = D_MODEL // 2
HG_HEAD_DIM = 128
HG_HEADS = HG_WIDTH // HG_HEAD_DIM
HG_CHUNK = 64
HG_PAD = HG_CHUNK - N_META
PEER_HEADS = 8
PEER_KEYS = 128
PEER_EXPERTS = PEER_KEYS * PEER_KEYS
PEER_QDIM = 256
PEER_HALF = PEER_QDIM // 2
PEER_TOPK = 16
PEER_BLOCK = 256
IN_SIZES = (S5_WIDTH, HG_WIDTH, HG_WIDTH, HG_WIDTH, HG_WIDTH, HG_WIDTH, D_MODEL, D_MODEL)
IN_COLS = S5_WIDTH + 5 * HG_WIDTH + 2 * D_MODEL

kernel_name = "hybrid_s5_hgrn2_peer_encoder"


def rmsnorm(x, g):
    xf = x.astype(jnp.float32)
    y = xf * lax.rsqrt(jnp.mean(xf * xf, axis=-1, keepdims=True) + EPS) * g.astype(jnp.float32)
    return y.astype(x.dtype)


def _s5_combine(e1, e2):
    a1, b1 = e1
    a2, b2 = e2
    return a1 * a2, a2 * b1 + b2


def s5_mixer(u, lam_re, lam_im, log_step, b_re, b_im, c_re, c_im, d_skip):
    bsz, t, _ = u.shape
    f32 = jnp.float32
    uf = u.astype(f32).reshape(bsz, t, S5_GROUPS, S5_GROUP)
    bu = lax.complex(jnp.einsum('gnc,btgc->btgn', b_re.astype(f32), uf),
                     jnp.einsum('gnc,btgc->btgn', b_im.astype(f32), uf))
    state = None
    for direction in (0, 1):
        lam = lax.complex(lam_re[direction].astype(f32), lam_im[direction].astype(f32))
        step = jnp.exp(log_step[direction].astype(f32))[:, None]
        lam_bar = jnp.exp(lam * step)
        x_in = bu * ((lam_bar - 1.0) / lam)
        a = jnp.broadcast_to(lam_bar, x_in.shape)
        _, xs = lax.associative_scan(_s5_combine, (a, x_in), axis=1, reverse=(direction == 1))
        state = xs if state is None else state + xs
    c_mat = lax.complex(c_re.astype(f32), c_im.astype(f32))
    y = jnp.real(jnp.einsum('gcn,btgn->btgc', c_mat, state)).reshape(bsz, t, S5_WIDTH)
    y = y + d_skip.astype(f32) * u.astype(f32)
    return y.astype(u.dtype)


def hgrn2_chunked(q, log_f, k, v):
    bsz, t, h, dk = q.shape
    dv = v.shape[-1]
    n = t // HG_CHUNK

    def to_chunks(z):
        return z.reshape(bsz, n, HG_CHUNK, h, z.shape[-1]).transpose(1, 0, 3, 2, 4)

    qc, gc, kc, vc = to_chunks(q), to_chunks(log_f), to_chunks(k), to_chunks(v)
    b = jnp.cumsum(gc, axis=3)
    b_last = b[:, :, :, -1:, :]
    q_dec = qc * jnp.exp(b)
    k_inv = kc * jnp.exp(-b)
    k_end = kc * jnp.exp(b_last - b)
    mask = jnp.tril(jnp.ones((HG_CHUNK, HG_CHUNK), dtype=bool))
    scores = jnp.where(mask, jnp.einsum('nbhtd,nbhsd->nbhts', q_dec, k_inv), 0.0)
    o_intra = jnp.einsum('nbhts,nbhsv->nbhtv', scores, vc)
    chunk_kv = jnp.einsum('nbhsd,nbhsv->nbhdv', k_end, vc)
    decay = jnp.exp(b_last[:, :, :, 0, :])

    def step(s_prev, xs):
        q_d, dec, kv = xs
        o = jnp.einsum('bhtd,bhdv->bhtv', q_d, s_prev)
        return dec[..., None] * s_prev + kv, o

    s0 = jnp.zeros((bsz, h, dk, dv), jnp.float32)
    _, o_inter = lax.scan(step, s0, (q_dec, decay, chunk_kv))
    o = o_intra + o_inter
    return o.transpose(1, 0, 3, 2, 4).reshape(bsz, t, h, dv)


def hgrn2_mixer(q_pre, f_fwd, f_bwd, i_in, lb, norm_g):
    bsz, t, _ = q_pre.shape

    def heads(z):
        return z.astype(jnp.float32).reshape(bsz, t, HG_HEADS, HG_HEAD_DIM)

    def pad(z):
        return jnp.pad(z, ((0, 0), (HG_PAD, 0), (0, 0), (0, 0)))

    def flip(z):
        return jnp.flip(z, axis=1)

    q = pad(jax.nn.silu(heads(q_pre)))
    v = pad(heads(i_in))
    lbh = lb.astype(jnp.float32).reshape(HG_HEADS, HG_HEAD_DIM)
    o = None
    for direction, f_pre in enumerate((f_fwd, f_bwd)):
        g = lbh + (1.0 - lbh) * jax.nn.sigmoid(heads(f_pre))
        log_f = pad(jnp.log(g))
        k = pad(1.0 - g)
        if direction == 0:
            od = hgrn2_chunked(q, log_f, k, v)
        else:
            od = flip(hgrn2_chunked(flip(q), flip(log_f), flip(k), flip(v)))
        o = od if o is None else o + od
    o = o[:, HG_PAD:]
    o = o * lax.rsqrt(jnp.mean(o * o, axis=-1, keepdims=True) + EPS)
    return o.reshape(bsz, t, HG_WIDTH) * norm_g.astype(jnp.float32)


def peer(h, w_q, keys, u_tab, v_tab):
    bsz, t, d = h.shape
    flat = h.reshape(-1, d)
    n = flat.shape[0]
    nblk = -(-n // PEER_BLOCK)
    flat = jnp.pad(flat, ((0, nblk * PEER_BLOCK - n), (0, 0))).reshape(nblk, PEER_BLOCK, d)

    def block(hb):
        q = (hb @ w_q).reshape(PEER_BLOCK, PEER_HEADS, 2, PEER_HALF)
        s = jnp.einsum('nhpd,hpkd->nhpk', q, keys).astype(jnp.float32)
        s_top, i_top = lax.top_k(s, PEER_TOPK)
        cand = s_top[:, :, 0, :, None] + s_top[:, :, 1, None, :]
        c_top, c_idx = lax.top_k(cand.reshape(PEER_BLOCK, PEER_HEADS, PEER_TOPK * PEER_TOPK), PEER_TOPK)
        i1 = jnp.take_along_axis(i_top[:, :, 0], c_idx // PEER_TOPK, axis=-1)
        i2 = jnp.take_along_axis(i_top[:, :, 1], c_idx % PEER_TOPK, axis=-1)
        expert = i1 * PEER_KEYS + i2
        gate = jax.nn.softmax(c_top, axis=-1)
        act = jax.nn.gelu(jnp.einsum('nd,nhkd->nhk', hb, u_tab[expert]).astype(jnp.float32), approximate=False)
        return jnp.einsum('nhk,nhkd->nd', (gate * act).astype(hb.dtype), v_tab[expert])

    out = lax.map(block, flat).reshape(-1, d)[:n]
    return out.reshape(bsz, t, d)


def encoder(x, meta, norm1_g, w_in, s5_lam_re, s5_lam_im, s5_log_step, s5_b_re, s5_b_im,
            s5_c_re, s5_c_im, s5_d, w_glu, hg_lb, hg_norm_g, w_hg_out, w_out, norm2_g,
            peer_wq, peer_keys, peer_u, peer_v, final_g):
    bsz = x.shape[0]
    meta_b = jnp.broadcast_to(meta.astype(x.dtype)[None], (bsz, N_META, D_MODEL))
    hs = jnp.concatenate([meta_b, x], axis=1)
    lbs = jnp.cumsum(jax.nn.softmax(hg_lb.astype(jnp.float32), axis=0), axis=0)
    split_at = [int(c) for c in np.cumsum(IN_SIZES)[:-1]]
    for l in range(DEPTH):
        h = rmsnorm(hs, norm1_g[l])
        z = h @ w_in[l]
        u_s5, q_pre, f_fwd, f_bwd, i_in, o_gate, gate_a, gate_b = jnp.split(z, split_at, axis=-1)
        y_s5 = s5_mixer(u_s5, s5_lam_re[l], s5_lam_im[l], s5_log_step[l], s5_b_re[l], s5_b_im[l],
                        s5_c_re[l], s5_c_im[l], s5_d[l])
        ga, gb = jnp.split(jax.nn.gelu(y_s5, approximate=False) @ w_glu[l], 2, axis=-1)
        y_a = ga * jax.nn.sigmoid(gb)
        y_hg = hgrn2_mixer(q_pre, f_fwd, f_bwd, i_in, lbs[l], hg_norm_g[l]) * jax.nn.silu(o_gate.astype(jnp.float32))
        y_b = y_hg.astype(hs.dtype) @ w_hg_out[l]
        mixed = jax.nn.sigmoid(gate_a) * y_a + jax.nn.sigmoid(gate_b) * y_b
        hs = hs + mixed @ w_out[l]
        hs = hs + peer(rmsnorm(hs, norm2_g[l]), peer_wq[l], peer_keys[l], peer_u[l], peer_v[l])
    return rmsnorm(hs, final_g)[:, N_META:]


def setup_inputs(seed: int = 0) -> dict:
    key = jax.random.key(seed)
    ks = jax.random.split(key, 26)
    f32 = jnp.float32

    def nrm(k, shape, s):
        return jax.random.normal(k, shape, f32) * s

    lam_shape = (DEPTH, 2, S5_GROUPS, S5_STATE)
    n_idx = jnp.arange(S5_STATE, dtype=f32)
    log_lo, log_hi = math.log(S5_DT_MIN), math.log(S5_DT_MAX)
    return {
        "x_prompt": nrm(ks[0], (BATCH, SEQ, D_MODEL), 1.0),
        "x_sample": nrm(ks[1], (DEC_BATCH, DEC_SEQ, D_MODEL), 1.0),
        "meta": nrm(ks[2], (N_META, D_MODEL), 1.0),
        "norm1_g": 1.0 + nrm(ks[3], (DEPTH, D_MODEL), 0.01),
        "w_in": nrm(ks[4], (DEPTH, D_MODEL, IN_COLS), D_MODEL ** -0.5),
        "s5_lam_re": -0.5 + nrm(ks[5], lam_shape, 0.01),
        "s5_lam_im": math.pi * n_idx + nrm(ks[6], lam_shape, 0.01),
        "s5_log_step": log_lo + jax.random.uniform(ks[7], (DEPTH, 2, S5_GROUPS), f32) * (log_hi - log_lo),
        "s5_b_re": nrm(ks[8], (DEPTH, S5_GROUPS, S5_STATE, S5_GROUP), (2 * S5_GROUP) ** -0.5),
        "s5_b_im": nrm(ks[9], (DEPTH, S5_GROUPS, S5_STATE, S5_GROUP), (2 * S5_GROUP) ** -0.5),
        "s5_c_re": nrm(ks[10], (DEPTH, S5_GROUPS, S5_GROUP, S5_STATE), S5_STATE ** -0.5),
        "s5_c_im": nrm(ks[11], (DEPTH, S5_GROUPS, S5_GROUP, S5_STATE), S5_STATE ** -0.5),
        "s5_d": nrm(ks[12], (DEPTH, S5_WIDTH), 1.0),
        "w_glu": nrm(ks[13], (DEPTH, S5_WIDTH, 2 * D_MODEL), S5_WIDTH ** -0.5),
        "hg_lb": nrm(ks[14], (DEPTH + 1, HG_WIDTH), 0.1),
        "hg_norm_g": 1.0 + nrm(ks[15], (DEPTH, HG_WIDTH), 0.01),
        "w_hg_out": nrm(ks[16], (DEPTH, HG_WIDTH, D_MODEL), HG_WIDTH ** -0.5),
        "w_out": nrm(ks[17], (DEPTH, D_MODEL, D_MODEL), D_MODEL ** -0.5),
        "norm2_g": 1.0 + nrm(ks[18], (DEPTH, D_MODEL), 0.01),
        "peer_wq": nrm(ks[19], (DEPTH, D_MODEL, PEER_HEADS * PEER_QDIM), D_MODEL ** -0.5),
        "peer_keys": nrm(ks[20], (DEPTH, PEER_HEADS, 2, PEER_KEYS, PEER_HALF), PEER_HALF ** -0.5),
        "peer_u": nrm(ks[21], (DEPTH, PEER_EXPERTS, D_MODEL), D_MODEL ** -0.5),
        "peer_v": nrm(ks[22], (DEPTH, PEER_EXPERTS, D_MODEL), 0.2),
        "final_g": 1.0 + nrm(ks[23], (D_MODEL,), 0.01),
    }


def reference(x_prompt, x_sample, meta, norm1_g, w_in, s5_lam_re, s5_lam_im, s5_log_step, s5_b_re,
              s5_b_im, s5_c_re, s5_c_im, s5_d, w_glu, hg_lb, hg_norm_g, w_hg_out, w_out, norm2_g,
              peer_wq, peer_keys, peer_u, peer_v, final_g):
    y_prompt = encoder(x_prompt, meta, norm1_g, w_in, s5_lam_re, s5_lam_im, s5_log_step, s5_b_re,
                       s5_b_im, s5_c_re, s5_c_im, s5_d, w_glu, hg_lb, hg_norm_g, w_hg_out, w_out,
                       norm2_g, peer_wq, peer_keys, peer_u, peer_v, final_g)
    y_sample = encoder(x_sample, meta, norm1_g, w_in, s5_lam_re, s5_lam_im, s5_log_step, s5_b_re,
                       s5_b_im, s5_c_re, s5_c_im, s5_d, w_glu, hg_lb, hg_norm_g, w_hg_out, w_out,
                       norm2_g, peer_wq, peer_keys, peer_u, peer_v, final_g)
    return (y_prompt, y_sample)
```

```python
import math
from contextlib import ExitStack

import numpy as np
import concourse.bass as bass
import concourse.mybir as mybir
from concourse.bass_utils import run_bass_kernel_spmd

F32 = mybir.dt.float32
BF16 = mybir.dt.bfloat16
I32 = mybir.dt.int32
U32 = mybir.dt.uint32
ALU = mybir.AluOpType
AF = mybir.ActivationFunctionType
AX = mybir.AxisListType

D = 1024
NCOL = 5120
EPS = 1e-6
PI = math.pi
ENGS = ("pe", "act", "dve", "pool", "sp")


class Sch:
    def __init__(self, nc, es, n_dma_sems=32):
        self.nc = nc
        self.esem = {e: es.enter_context(nc.semaphore("se_" + e)) for e in ENGS}
        self.ecnt = {e: 0 for e in ENGS}
        self.dsem = [es.enter_context(nc.semaphore("sd%d" % i)) for i in range(n_dma_sems)]
        self.dval = [0] * n_dma_sems
        self.dnext = {"hw": 0, "sw": 0}
        self.dhalf = n_dma_sems // 2
        self.lastw = {}
        self.readers = {}
        self.known = {e: {} for e in ENGS}
        self.ops = {e: [] for e in ENGS}
        self.nops = 0
        self.capture = None

    def _sem(self, sk):
        return self.esem[sk[1]] if sk[0] == "e" else self.dsem[sk[1]]

    def _need(self, eng, tok, waits):
        if tok is None:
            return
        sk, val = tok
        if sk == ("e", "pe") and eng == "pe":
            return
        if self.known[eng].get(sk, 0) >= val:
            return
        self.known[eng][sk] = val
        waits[sk] = max(waits.get(sk, 0), val)

    def op(self, eng, fn, reads=(), writes=(), dma=False):
        if self.capture is not None:
            self.capture.append((eng, fn, tuple(reads), tuple(writes), dma))
            return
        waits = {}
        for k in reads:
            self._need(eng, self.lastw.get(k), waits)
        for k in writes:
            self._need(eng, self.lastw.get(k), waits)
            for t in self.readers.get(k, ()):
                self._need(eng, t, waits)
        if dma:
            kind = "sw" if eng == "pool" else "hw"
            i = self.dnext[kind] + (self.dhalf if kind == "sw" else 0)
            self.dnext[kind] = (self.dnext[kind] + 1) % self.dhalf
            if self.dval[i] > 0:
                self._need(eng, (("d", i), self.dval[i]), waits)
            self.dval[i] += 16
            tok = (("d", i), self.dval[i])
            inc = (self.dsem[i], 16)
        else:
            self.ecnt[eng] += 1
            tok = (("e", eng), self.ecnt[eng])
            inc = (self.esem[eng], 1)
        for k in reads:
            self.readers.setdefault(k, []).append(tok)
        for k in writes:
            self.lastw[k] = tok
            self.readers[k] = []
        self.ops[eng].append((list(waits.items()), fn, inc))
        self.nops += 1

    def fence(self):
        for e in ENGS:
            waits = {}
            for e2 in ENGS:
                if self.ecnt[e2] > 0:
                    self._need(e, (("e", e2), self.ecnt[e2]), waits)
            for i, v in enumerate(self.dval):
                if v > 0:
                    self._need(e, (("d", i), v), waits)
            self.ops[e].append((list(waits.items()), None, None))
        self.lastw.clear()
        self.readers.clear()

    def emit(self):
        with self.nc.Block() as blk:
            decos = {"pe": blk.tensor, "act": blk.scalar, "dve": blk.vector, "pool": blk.gpsimd, "sp": blk.sync}
            for e in ENGS:
                ops = self.ops[e]

                def body(eng, ops=ops):
                    for waits, fn, inc in ops:
                        for sk, val in waits:
                            eng.wait_ge(self._sem(sk), val)
                        if fn is not None:
                            fn(eng).then_inc(inc[0], inc[1])

                decos[e](body)
                self.ops[e] = []


def _mm(out, lhsT, rhs, start, stop):
    return lambda e: e.matmul(out, lhsT, rhs, start=start, stop=stop)


def _tr(out, in_, ident):
    return lambda e: e.transpose(out, in_, ident)


def _dma(out, in_):
    return lambda e: e.dma_start(out=out, in_=in_)


def _gather(out, table, idx):
    return lambda e: e.indirect_dma_start(out=out, out_offset=None, in_=table,
                                          in_offset=bass.IndirectOffsetOnAxis(ap=idx, axis=0))


def _act(out, in_, func, scale=None):
    if scale is None:
        return lambda e: e.activation(out=out, in_=in_, func=func)
    return lambda e: e.activation(out=out, in_=in_, func=func, scale=scale)


def _copy(out, in_):
    return lambda e: e.tensor_copy(out=out, in_=in_)


def _acopy(out, in_):
    return lambda e: e.copy(out=out, in_=in_)


def _tt(out, in0, in1, op):
    return lambda e: e.tensor_tensor(out=out, in0=in0, in1=in1, op=op)


def _ts(out, in0, s1, s2, op0, op1=None):
    if op1 is None:
        return lambda e: e.tensor_scalar(out=out, in0=in0, scalar1=s1, scalar2=None, op0=op0)
    return lambda e: e.tensor_scalar(out=out, in0=in0, scalar1=s1, scalar2=s2, op0=op0, op1=op1)


def _stt(out, in0, scalar, in1, op0, op1, accum_out=None):
    if accum_out is None:
        return lambda e: e.scalar_tensor_tensor(out=out, in0=in0, scalar=scalar, in1=in1, op0=op0, op1=op1)
    return lambda e: e.scalar_tensor_tensor(out=out, in0=in0, scalar=scalar, in1=in1, op0=op0, op1=op1,
                                            accum_out=accum_out)


def _scan(out, d0, d1, init):
    return lambda e: e.tensor_tensor_scan(out=out, data0=d0, data1=d1, initial=init, op0=ALU.mult, op1=ALU.add)


def _memset(ap, v):
    return lambda e: e.memset(ap, v)


def _rsum(out, in_):
    return lambda e: e.reduce_sum(out=out, in_=in_, axis=AX.X)


def _recip(out, in_):
    return lambda e: e.reciprocal(out=out, in_=in_)


def _rev(ap2d):
    return ap2d[:, ::-1]


def build_program(SEQ, ND, debug=False):
    assert SEQ % 512 == 0
    NT = SEQ // 512 + 1
    TP = NT * 512
    nc = bass.Bass("TRN2", target_bir_lowering=False)

    def din(name, shape, dt=F32):
        return nc.dram_tensor(name, list(shape), dt, kind="ExternalInput").ap()

    def dscr(name, shape, dt):
        kind = "ExternalOutput" if debug else "Internal"
        return nc.dram_tensor(name, list(shape), dt, kind=kind).ap()

    I = {}
    I["xs"] = din("xs", [SEQ, D])
    I["meta"] = din("meta", [16, D])
    I["tokd"] = din("tokd", [128, ND], I32)
    I["g1c"] = din("g1c", [128, 8])
    I["w_in"] = din("w_in", [D, NCOL])
    for nm in ("lre_c", "lim_c", "lst_c"):
        I[nm] = din(nm, [128, 32])
    for nm in ("cre8", "cim8", "brec", "bimc"):
        I[nm] = din(nm, [128, 2048])
    I["drow"] = din("drow", [128, 32])
    I["cst8"] = din("cst8", [128, 32])
    I["selC"] = din("selC", [128, 1024])
    I["ramp16"] = din("ramp16", [128, 16])
    I["mfb"] = din("mfb", [128, 256])
    I["w_glu"] = din("w_glu", [512, 2048])
    I["lb0"] = din("lb0", [128, 4])
    I["lb1"] = din("lb1", [128, 4])
    I["hgn"] = din("hgn", [128, 4])
    I["w_hg"] = din("w_hg", [512, D])
    I["w_out"] = din("w_out", [D, D])
    I["g2b"] = din("g2b", [128, D])
    I["gfb"] = din("gfb", [128, D])
    I["wq"] = din("wq", [D, 2048])
    I["keysT"] = din("keysT", [128, 2048])
    I["puv"] = din("puv", [16384, 2 * D])
    I["identf"] = din("identf", [128, 128])
    I["ramps"] = din("ramps", [128, 256])
    I["masks"] = din("masks", [64, 128])
    I["iota16"] = din("iota16", [128, 16])
    OUT = nc.dram_tensor("outd", [ND * 128, D], F32, kind="ExternalOutput").ap()

    Z = dscr("Z", [NCOL, TP], BF16)
    VTM = dscr("VTM", [TP, 512], BF16)
    UST = nc.dram_tensor("UST", [128, 32, TP // 8], BF16, kind="Internal").ap()
    XBS = nc.dram_tensor("XBS", [128, 16, 2, TP // 8], BF16, kind="Internal").ap()
    MA = dscr("MA", [D, TP], BF16)
    OB = dscr("OB", [512, TP], F32)
    HS1 = dscr("HS1", [SEQ, D], F32)
    UVB = nc.dram_tensor("UVB", [16384, 2 * D], BF16, kind="Internal").ap()

    top_es = ExitStack()
    S = Sch(nc, top_es)

    with ExitStack() as es:
        def T(name, shape, dt):
            return es.enter_context(nc.sbuf_tensor("A_" + name, shape, dt))

        def P(name, shape, dt):
            return es.enter_context(nc.psum_tensor("A_" + name, shape, dt))

        Wb = T("Wb", [128, 8, NCOL], BF16)
        stg = [T("stg%d" % i, [128, 1280], F32) for i in range(2)]
        cst8 = T("cst8", [128, 32], F32)
        selNb = T("selNb", [128, 16], BF16)
        Ex = T("Ex", [128, 32, 8, 16], BF16)
        ustb = [T("ustb%d" % i, [128, 32, 256], BF16) for i in range(2)]
        g1c = T("g1c", [128, 8], F32)
        identf = T("identf", [128, 128], F32)
        identb = T("identb", [128, 128], BF16)
        xt = [[T("xt%d_%d" % (b, j), [128, D], F32) for j in range(4)] for b in range(2)]
        hb = [T("hb%d" % i, [128, D], BF16) for i in range(2)]
        hT = [T("hT%d" % i, [128, 8, 512], BF16) for i in range(2)]
        ssq = T("ssq", [128, 8], F32)
        rstd = T("rstd", [128, 8], F32)
        junk = T("junk", [128, D], BF16)
        zt = [T("zt%d" % i, [128, 4, 512], BF16) for i in range(2)]
        vt = [T("vt%d" % i, [128, 512], BF16) for i in range(2)]
        pT = [P("pT%d" % i, [128, 8, 128], BF16) for i in range(2)]
        pz = [P("pz%d" % i, [128, 512], F32) for i in range(4)]
        psU = P("psU", [128, 32, 16], F32)

        S.op("sp", _dma(identf[:], I["identf"][:, :]), writes=["identf"], dma=True)
        S.op("sp", _dma(g1c[:], I["g1c"][:, :]), writes=["g1c"], dma=True)
        S.op("sp", _dma(cst8[:], I["cst8"][:, :]), writes=["cst8"], dma=True)
        S.op("dve", _copy(selNb[:], cst8[:, 8:24]), reads=["cst8"], writes=["selNb"])
        S.op("dve", _copy(identb[:], identf[:]), reads=["identf"], writes=["identb"])
        n = 0
        for k in range(8):
            for h in range(4):
                sb = n % 2
                n += 1
                S.op("sp", _dma(stg[sb][:], I["w_in"][k * 128:(k + 1) * 128, h * 1280:(h + 1) * 1280]),
                     writes=["stg%d" % sb], dma=True)
                S.op("dve", _ts(Wb[:, k, h * 1280:(h + 1) * 1280], stg[sb][:], g1c[:, k:k + 1], None, ALU.mult),
                     reads=["stg%d" % sb, "g1c"], writes=["Wb"])

        cst = [T("cst%d" % i, [128, D], F32) for i in range(2)]
        cvb = [T("cvb%d" % i, [128, D], BF16) for i in range(2)]
        conv = [(r, hf) for r in range(128) for hf in range(2)]
        cpos = [0]

        def conv_step():
            if cpos[0] >= len(conv):
                return
            r, hf = conv[cpos[0]]
            cb = cpos[0] % 2
            cpos[0] += 1
            S.op("sp", _dma(cst[cb][:], I["puv"][r * 128:(r + 1) * 128, hf * D:(hf + 1) * D]), writes=["cst%d" % cb], dma=True)
            if cb == 0:
                S.op("act", _acopy(cvb[cb][:], cst[cb][:]), reads=["cst%d" % cb], writes=["cvb%d" % cb])
            else:
                S.op("dve", _copy(cvb[cb][:], cst[cb][:]), reads=["cst%d" % cb], writes=["cvb%d" % cb])
            S.op("pool", _dma(UVB[r * 128:(r + 1) * 128, hf * D:(hf + 1) * D], cvb[cb][:]), reads=["cvb%d" % cb],
                 writes=["UVB"], dma=True)

        per_st = -(-len(conv) // NT)
        ev = 0
        for st in range(NT):
            b = st % 2
            if st == 0:
                for j in range(4):
                    S.op("pool", _memset(xt[b][j][:], 0.0), writes=["xt%d_%d" % (b, j)])
                S.op("sp", _dma(xt[b][3][112:128, :], I["meta"][:, :]), writes=["xt%d_3" % b], dma=True)
            else:
                for j in range(4):
                    r0 = (st - 1) * 512 + j * 128
                    S.op("sp", _dma(xt[b][j][:], I["xs"][r0:r0 + 128, :]), writes=["xt%d_%d" % (b, j)], dma=True)
            for j in range(4):
                c = (st * 4 + j) % 8
                hbj = j % 2
                xk = "xt%d_%d" % (b, j)
                S.op("dve", _memset(ssq[:, c:c + 1], 0.0), writes=["ssq%d" % c])
                S.op("dve", _stt(junk[:], xt[b][j][:], 1.0, xt[b][j][:], ALU.mult, ALU.mult, accum_out=ssq[:, c:c + 1]),
                     reads=[xk], writes=["junk", "ssq%d" % c])
                S.op("dve", _ts(rstd[:, c:c + 1], ssq[:, c:c + 1], 1.0 / D, EPS, ALU.mult, ALU.add),
                     reads=["ssq%d" % c], writes=["rstd%d" % c])
                S.op("act", _act(rstd[:, c:c + 1], rstd[:, c:c + 1], AF.Sqrt), reads=["rstd%d" % c], writes=["rstd%d" % c])
                S.op("dve", _recip(rstd[:, c:c + 1], rstd[:, c:c + 1]), reads=["rstd%d" % c], writes=["rstd%d" % c])
                S.op("act", _act(hb[hbj][:], xt[b][j][:], AF.Copy, scale=rstd[:, c:c + 1]),
                     reads=[xk, "rstd%d" % c], writes=["hb%d" % hbj])
                for k in range(8):
                    S.op("pe", _tr(pT[hbj][:, k, :], hb[hbj][:, k * 128:(k + 1) * 128], identb[:]),
                         reads=["hb%d" % hbj, "identb"], writes=["pT%d" % hbj])
                S.op("act", _acopy(hT[b][:, :, j * 128:(j + 1) * 128], pT[hbj][:, :, :]),
                     reads=["pT%d" % hbj], writes=["hT%d" % b])
            for c in [c_ for c_ in range(4, 40) if not (16 <= c_ < 20)]:
                pzi = c % 4
                for k in range(8):
                    S.op("pe", _mm(pz[pzi][:], Wb[:, k, c * 128:(c + 1) * 128], hT[b][:, k, :], k == 0, k == 7),
                         reads=["Wb", "hT%d" % b], writes=["pz%d" % pzi])
                zb = (c // 4) % 2
                if ev % 2 == 0:
                    S.op("act", _acopy(zt[zb][:, c % 4, :], pz[pzi][:]), reads=["pz%d" % pzi], writes=["zt%d" % zb])
                else:
                    S.op("dve", _copy(zt[zb][:, c % 4, :], pz[pzi][:]), reads=["pz%d" % pzi], writes=["zt%d" % zb])
                ev += 1
                if c % 4 == 3:
                    dst = Z[(c - 3) * 128:(c + 1) * 128, st * 512:(st + 1) * 512].rearrange("(c p) t -> p c t", p=128)
                    S.op("pool", _dma(dst, zt[zb][:]), reads=["zt%d" % zb], writes=["Z"], dma=True)
            for j in range(4):
                pzi = j % 4
                for k in range(8):
                    S.op("pe", _mm(pz[pzi][:], hT[b][:, k, j * 128:(j + 1) * 128], Wb[:, k, 2048:2560], k == 0, k == 7),
                         reads=["Wb", "hT%d" % b], writes=["pz%d" % pzi])
                vb = j % 2
                S.op("act", _acopy(vt[vb][:], pz[pzi][:]), reads=["pz%d" % pzi], writes=["vt%d" % vb])
                r0 = st * 512 + j * 128
                S.op("pool", _dma(VTM[r0:r0 + 128, :], vt[vb][:]), reads=["vt%d" % vb], writes=["VTM"], dma=True)
            ubuf = (st // 4) % 2
            for j in range(4):
                pzi = j % 4
                for k in range(8):
                    S.op("pe", _mm(pz[pzi][:], hT[b][:, k, j * 128:(j + 1) * 128], Wb[:, k, 0:512], k == 0, k == 7),
                         reads=["Wb", "hT%d" % b], writes=["pz%d" % pzi])
                S.op("dve", _tt(Ex[:], pz[pzi][:].rearrange("p (g c) -> p g c", g=32).unsqueeze(2).broadcast_to([128, 32, 8, 16]),
                                cst8[:, 0:8].unsqueeze(1).unsqueeze(3).broadcast_to([128, 32, 8, 16]), ALU.mult),
                     reads=["pz%d" % pzi, "cst8"], writes=["Ex"])
                for g in range(32):
                    S.op("pe", _mm(psU[:, g, :], Ex[:, g, :, :].rearrange("p i c -> p (i c)"), selNb[:], True, True),
                         reads=["Ex", "selNb"], writes=["psU"])
                off = (st % 4) * 64 + j * 16
                S.op("act", _acopy(ustb[ubuf][:, :, off:off + 16], psU[:]), reads=["psU"], writes=["ustb%d" % ubuf])
            if st % 4 == 3 or st == NT - 1:
                base = (st - st % 4) * 64
                ncols = (st % 4 + 1) * 64
                S.op("pool", _dma(UST[:, :, base:base + ncols], ustb[ubuf][:, :, 0:ncols]), reads=["ustb%d" % ubuf], writes=["UST"],
                     dma=True)
            for _ in range(per_st):
                conv_step()
        while cpos[0] < len(conv):
            conv_step()
        S.fence()
        S.emit()

    with ExitStack() as es:
        def T(name, shape, dt):
            return es.enter_context(nc.sbuf_tensor("B_" + name, shape, dt))

        def P(name, shape, dt):
            return es.enter_context(nc.psum_tensor("B_" + name, shape, dt))

        NCH = TP // 8
        BWD = 256
        blocks = [(n0, min(BWD, NCH - n0)) for n0 in range(0, NCH, BWD)]

        WST = T("WST", [128, 2, 16, 2, 2, 128], BF16)
        WX = T("WX", [128, 32, 2, 128], BF16)
        WU = T("WU", [128, 32, 128], BF16)
        COS = T("COS", [128, 32, 128], F32)
        SIN = T("SIN", [128, 32, 128], F32)
        rc8 = T("rc8", [128, 32], F32)
        ramps = T("ramps", [128, 256], F32)
        cst8 = T("cst8", [128, 32], F32)
        maskJb = T("maskJb", [128, 8], BF16)
        selC = T("selC", [128, 8, 128], BF16)
        STre = T("STre", [128, 32], F32)
        STim = T("STim", [128, 32], F32)
        XFc = T("XFc", [128, 16, 2], BF16)
        tmp4 = T("tmp4", [128, 4], F32)
        identf = T("identf", [128, 128], F32)

        def sincos(dsin, dcos, ang, tf, tf2, ti, ksin, kcos, kang, ktf, ktf2, kti):
            for dst, kd, off in ((dcos, kcos, 0.25), (dsin, ksin, 0.0)):
                S.op("dve", _ts(tf, ang, 1.0 / (2 * PI), off, ALU.mult, ALU.add), reads=[kang], writes=[ktf])
                S.op("dve", _copy(ti, tf), reads=[ktf], writes=[kti])
                S.op("dve", _copy(tf2, ti), reads=[kti], writes=[ktf2])
                S.op("dve", _tt(tf, tf, tf2, ALU.subtract), reads=[ktf, ktf2], writes=[ktf])
                S.op("dve", _ts(tf, tf, -0.4999, 0.4999, ALU.max, ALU.min), reads=[ktf], writes=[ktf])
                S.op("act", _act(dst, tf, AF.Sin, scale=2 * PI), reads=[ktf], writes=[kd])

        with ExitStack() as es2:
            def T2(name, shape, dt):
                return es2.enter_context(nc.sbuf_tensor("B2_" + name, shape, dt))

            def P2(name, shape, dt):
                return es2.enter_context(nc.psum_tensor("B2_" + name, shape, dt))

            LR = T2("LR", [128, 32], F32)
            LI = T2("LI", [128, 32], F32)
            LS = T2("LS", [128, 32], F32)
            pa = T2("pa", [128, 32], F32)
            pth = T2("pth", [128, 32], F32)
            pcr = T2("pcr", [128, 32], F32)
            pci = T2("pci", [128, 32], F32)
            sm_ = [T2("sm%d" % i, [128, 32], F32) for i in range(8)]
            ramp16 = T2("ramp16", [128, 16], F32)
            mfb = T2("mfb", [128, 256], F32)
            drow = T2("drow", [128, 32], F32)
            S.op("sp", _dma(identf[:], I["identf"][:, :]), writes=["identf"], dma=True)
            S.op("sp", _dma(ramps[:], I["ramps"][:, :]), writes=["ramps"], dma=True)
            S.op("sp", _dma(cst8[:], I["cst8"][:, :]), writes=["cst8"], dma=True)
            S.op("sp", _dma(ramp16[:], I["ramp16"][:, :]), writes=["ramp16"], dma=True)
            S.op("sp", _dma(mfb[:], I["mfb"][:, :]), writes=["mfb"], dma=True)
            S.op("sp", _dma(drow[:], I["drow"][:, :]), writes=["drow"], dma=True)
            S.op("sp", _dma(LR[:], I["lre_c"][:, :]), writes=["LR"], dma=True)
            S.op("sp", _dma(LI[:], I["lim_c"][:, :]), writes=["LI"], dma=True)
            S.op("sp", _dma(LS[:], I["lst_c"][:, :]), writes=["LS"], dma=True)
            S.op("dve", _copy(maskJb[:], cst8[:, 24:32]), reads=["cst8"], writes=["maskJb"])
            t = [x[:] for x in sm_]
            k = ["sm%d" % i for i in range(8)]
            S.op("act", _act(t[0], LS[:], AF.Exp), reads=["LS"], writes=[k[0]])
            S.op("dve", _tt(pa[:], LR[:], t[0], ALU.mult), reads=["LR", k[0]], writes=["pa"])
            S.op("dve", _tt(pth[:], LI[:], t[0], ALU.mult), reads=["LI", k[0]], writes=["pth"])
            S.op("act", _act(t[1], pa[:], AF.Exp), reads=["pa"], writes=[k[1]])
            sincos(t[3], t[4], pth[:], t[5], t[6], t[7].bitcast(I32), k[3], k[4], "pth", k[5], k[6], k[7])
            S.op("dve", _tt(t[5], t[1], t[4], ALU.mult), reads=[k[1], k[4]], writes=[k[5]])
            S.op("dve", _ts(t[5], t[5], -1.0, None, ALU.add), reads=[k[5]], writes=[k[5]])
            S.op("dve", _tt(t[6], t[1], t[3], ALU.mult), reads=[k[1], k[3]], writes=[k[6]])
            S.op("dve", _tt(t[7], LR[:], LR[:], ALU.mult), reads=["LR"], writes=[k[7]])
            S.op("dve", _tt(t[0], LI[:], LI[:], ALU.mult), reads=["LI"], writes=[k[0]])
            S.op("dve", _tt(t[7], t[7], t[0], ALU.add), reads=[k[7], k[0]], writes=[k[7]])
            S.op("dve", _recip(t[7], t[7]), reads=[k[7]], writes=[k[7]])
            S.op("dve", _tt(t[3], t[5], LR[:], ALU.mult), reads=[k[5], "LR"], writes=[k[3]])
            S.op("dve", _tt(t[0], t[6], LI[:], ALU.mult), reads=[k[6], "LI"], writes=[k[0]])
            S.op("dve", _tt(t[3], t[3], t[0], ALU.add), reads=[k[3], k[0]], writes=[k[3]])
            S.op("dve", _tt(pcr[:], t[3], t[7], ALU.mult), reads=[k[3], k[7]], writes=["pcr"])
            S.op("dve", _tt(t[4], t[6], LR[:], ALU.mult), reads=[k[6], "LR"], writes=[k[4]])
            S.op("dve", _tt(t[0], t[5], LI[:], ALU.mult), reads=[k[5], "LI"], writes=[k[0]])
            S.op("dve", _tt(t[4], t[4], t[0], ALU.subtract), reads=[k[4], k[0]], writes=[k[4]])
            S.op("dve", _tt(pci[:], t[4], t[7], ALU.mult), reads=[k[4], k[7]], writes=["pci"])

            PWm = T2("PWm", [128, 32, 16], F32)
            PWa = T2("PWa", [128, 32, 16], F32)
            PWr = T2("PWr", [128, 32, 16], F32)
            PWi = T2("PWi", [128, 32, 16], F32)
            PCr = T2("PCr", [128, 32, 16], F32)
            PCi = T2("PCi", [128, 32, 16], F32)
            tl = [T2("tl%d" % i, [128, 2048], F32) for i in range(6)]
            pt = [tl[3 + i][:, 0:512] for i in range(3)]
            for dg in range(32):
                S.op("act", _act(PWm[:, dg, :], ramp16[:], AF.Exp, scale=pa[:, dg:dg + 1]), reads=["ramp16", "pa"], writes=["PWm"])
                S.op("dve", _ts(PWa[:, dg, :], ramp16[:], pth[:, dg:dg + 1], None, ALU.mult), reads=["ramp16", "pth"], writes=["PWa"])
            f512 = lambda x: x[:].rearrange("p a b -> p (a b)")
            sincos(f512(PWi), f512(PWr), f512(PWa), pt[0], pt[1], pt[2].bitcast(I32), "PWi", "PWr", "PWa", "tl3", "tl4", "tl5")
            S.op("dve", _tt(PWr[:], PWr[:], PWm[:], ALU.mult), reads=["PWr", "PWm"], writes=["PWr"])
            S.op("dve", _tt(PWi[:], PWi[:], PWm[:], ALU.mult), reads=["PWi", "PWm"], writes=["PWi"])
            crb = pcr[:].unsqueeze(2).broadcast_to([128, 32, 16])
            cib = pci[:].unsqueeze(2).broadcast_to([128, 32, 16])
            p3 = lambda x: x.rearrange("p (a b) -> p a b", a=32)
            S.op("dve", _tt(PCr[:], PWr[:], crb, ALU.mult), reads=["PWr", "pcr"], writes=["PCr"])
            S.op("dve", _tt(p3(pt[0]), PWi[:], cib, ALU.mult), reads=["PWi", "pci"], writes=["tl3"])
            S.op("dve", _tt(PCr[:], PCr[:], p3(pt[0]), ALU.subtract), reads=["PCr", "tl3"], writes=["PCr"])
            S.op("dve", _tt(PCi[:], PWr[:], cib, ALU.mult), reads=["PWr", "pci"], writes=["PCi"])
            S.op("dve", _tt(p3(pt[1]), PWi[:], crb, ALU.mult), reads=["PWi", "pcr"], writes=["tl4"])
            S.op("dve", _tt(PCi[:], PCi[:], p3(pt[1]), ALU.add), reads=["PCi", "tl4"], writes=["PCi"])
            S.op("dve", _copy(rc8[:].unsqueeze(2), PWm[:, :, 15:16]), reads=["PWm"], writes=["rc8"])
            S.op("dve", _ts(sm_[0][:], pth[:], 8.0, None, ALU.mult), reads=["pth"], writes=["sm0"])
            S.fence()
            for hf in range(2):
                ang3 = tl[0][:].rearrange("p (a b) -> p a b", a=16)
                for g_ in range(16):
                    dg = hf * 16 + g_
                    rp = ramps[:, 0:128] if hf == 0 else ramps[:, 128:256]
                    S.op("dve", _ts(ang3[:, g_, :], rp, sm_[0][:, dg:dg + 1], None, ALU.mult), reads=["ramps", "sm0"], writes=["tl0"])
                sinh = SIN[:, hf * 16:(hf + 1) * 16, :].rearrange("p a b -> p (a b)")
                cosh = COS[:, hf * 16:(hf + 1) * 16, :].rearrange("p a b -> p (a b)")
                sincos(sinh, cosh, tl[0][:], tl[1][:], tl[2][:], tl[3][:].bitcast(I32), "SIN", "COS", "tl0", "tl1", "tl2", "tl3")
            S.fence()

            C8r = T2("C8r", [128, 2048], F32)
            C8i = T2("C8i", [128, 2048], F32)
            Bcr = T2("Bcr", [128, 2048], F32)
            Bci = T2("Bci", [128, 2048], F32)
            WUf = T2("WUf", [128, 32, 128], F32)
            psT = P2("psT", [128, 128], F32)
            psW = P2("psW", [128, 128], F32)
            S.op("sp", _dma(C8r[:], I["cre8"][:, :]), writes=["C8r"], dma=True)
            S.op("sp", _dma(C8i[:], I["cim8"][:, :]), writes=["C8i"], dma=True)
            S.op("sp", _dma(Bcr[:], I["brec"][:, :]), writes=["Bcr"], dma=True)
            S.op("sp", _dma(Bci[:], I["bimc"][:, :]), writes=["Bci"], dma=True)
            S.op("pool", _memset(WST[:].rearrange("p a b c d e -> p (a b c d e)"), 0.0), writes=["WST"])

            def v4(ap):
                return ap.rearrange("p (g j c) -> p g j c", g=16, j=8)

            def pw4(tbl, d, sl):
                return tbl[:, d * 16:(d + 1) * 16, sl].unsqueeze(3).broadcast_to([128, 16, 8, 16])

            def cmul(ore, oim, ar, ai, br, bi, ka, kb, ko, neg_im=False):
                s4, s5 = v4(tl[4][:]), v4(tl[5][:])
                S.op("dve", _tt(s4, ar, br, ALU.mult), reads=ka + kb, writes=["tl4"])
                S.op("dve", _tt(s5, ai, bi, ALU.mult), reads=ka + kb, writes=["tl5"])
                S.op("dve", _tt(ore, s4, s5, ALU.subtract), reads=["tl4", "tl5"], writes=[ko[0]])
                S.op("dve", _tt(s4, ar, bi, ALU.mult), reads=ka + kb, writes=["tl4"])
                S.op("dve", _tt(s5, ai, br, ALU.mult), reads=ka + kb, writes=["tl5"])
                S.op("dve", _tt(oim, s4, s5, ALU.add), reads=["tl4", "tl5"], writes=[ko[1]])
                if neg_im:
                    S.op("dve", _ts(oim, oim, -1.0, None, ALU.mult), reads=[ko[1]], writes=[ko[1]])

            SL_P1_8 = slice(8, 16)
            SL_8_1 = slice(15, 7, -1)
            SL_0_7 = slice(7, 15)
            SL_0_m7 = slice(7, None, -1)
            SL_7_0 = slice(14, 6, -1)
            kC, kB_, kPW, kPC = ["C8r", "C8i"], ["Bcr", "Bci"], ["PWr", "PWi"], ["PCr", "PCi"]
            for d in range(2):
                sl = SL_P1_8 if d == 0 else SL_8_1
                cmul(v4(tl[0][:]), v4(tl[1][:]), v4(C8r[:]), v4(C8i[:]), pw4(PWr, d, sl), pw4(PWi, d, sl), kC, kPW, ["tl0", "tl1"],
                     neg_im=True)
                S.op("act", _acopy(WX[:, d * 16:(d + 1) * 16, 0, :], tl[0][:].rearrange("p (g m) -> p g m", g=16)), reads=["tl0"],
                     writes=["WX"])
                S.op("act", _acopy(WX[:, d * 16:(d + 1) * 16, 1, :], tl[1][:].rearrange("p (g m) -> p g m", g=16)), reads=["tl1"],
                     writes=["WX"])
                sl = SL_7_0 if d == 0 else SL_0_7
                cmul(v4(tl[0][:]), v4(tl[1][:]), pw4(PCr, d, sl), pw4(PCi, d, sl), v4(Bcr[:]), v4(Bci[:]), kPC, kB_, ["tl0", "tl1"])
                for gp in range(16):
                    for ri in range(2):
                        S.op("pe", _tr(psT[:], tl[ri][:, gp * 128:(gp + 1) * 128], identf[:]), reads=["tl%d" % ri, "identf"],
                             writes=["psT"])
                        S.op("act", _acopy(WST[:, d, gp, ri, 0, 0:64], psT[:, 0:64]), reads=["psT"], writes=["WST"])
                        S.op("dve", _copy(WST[:, d, gp, ri, 1, 64:128], psT[:, 64:128]), reads=["psT"], writes=["WST"])
                slG = SL_0_7 if d == 0 else SL_0_m7
                slH = SL_0_m7 if d == 0 else SL_0_7
                cmul(v4(tl[0][:]), v4(tl[1][:]), v4(C8r[:]), v4(C8i[:]), pw4(PWr, d, slG), pw4(PWi, d, slG), kC, kPW, ["tl0", "tl1"],
                     neg_im=True)
                cmul(v4(tl[2][:]), v4(tl[3][:]), pw4(PCr, d, slH), pw4(PCi, d, slH), v4(Bcr[:]), v4(Bci[:]), kPC, kB_, ["tl2", "tl3"])
                mk = mfb[:, 0:128] if d == 0 else mfb[:, 128:256]
                for gp in range(16):
                    for gl in range(2):
                        g = 2 * gp + gl
                        rs = slice(gl * 64, (gl + 1) * 64)
                        cs = slice(gp * 128, (gp + 1) * 128)
                        S.op("pe", _mm(psW[:], tl[2][rs, cs], tl[0][rs, cs], True, False), reads=["tl2", "tl0"], writes=["psW"])
                        S.op("pe", _mm(psW[:], tl[3][rs, cs], tl[1][rs, cs], False, True), reads=["tl3", "tl1"], writes=["psW"])
                        if d == 0:
                            S.op("dve", _tt(WUf[:, g, :], psW[:], mk, ALU.mult), reads=["psW", "mfb"], writes=["WUf"])
                            S.op("dve", _stt(WUf[:, g, :], identf[:], drow[:, g:g + 1], WUf[:, g, :], ALU.mult, ALU.add),
                                 reads=["identf", "drow", "WUf"], writes=["WUf"])
                        else:
                            S.op("dve", _tt(tl[4][:, 0:128], psW[:], mk, ALU.mult), reads=["psW", "mfb"], writes=["tl4"])
                            S.op("dve", _tt(WU[:, g, :], WUf[:, g, :], tl[4][:, 0:128], ALU.add), reads=["WUf", "tl4"], writes=["WU"])
            S.op("sp", _dma(tl[0][:, 0:1024], I["selC"][:, :]), writes=["tl0"], dma=True)
            S.op("dve", _copy(selC[:].rearrange("p a b -> p (a b)"), tl[0][:, 0:1024]), reads=["tl0"], writes=["selC"])
            S.fence()
            S.emit()

        wglu = T("wglu", [128, 4, 2048], BF16)
        with ExitStack() as es3:
            wst2 = [es3.enter_context(nc.sbuf_tensor("B3_wst%d" % i, [128, 2048], F32)) for i in range(2)]
            for k_ in range(4):
                S.op("sp", _dma(wst2[k_ % 2][:], I["w_glu"][k_ * 128:(k_ + 1) * 128, :]), writes=["wst2_%d" % (k_ % 2)], dma=True)
                S.op("act", _acopy(wglu[:, k_, :], wst2[k_ % 2][:]), reads=["wst2_%d" % (k_ % 2)], writes=["wglu"])
            S.fence()
            S.emit()
        ust = [T("ust%d" % i, [128, 32, BWD], BF16) for i in range(2)]
        gst = T("gst", [128, 32, BWD], BF16)
        bpre = [T("bpre%d" % i, [128, BWD], F32) for i in range(2)]
        bpim = [T("bpim%d" % i, [128, BWD], F32) for i in range(2)]
        mt = [T("mt%d" % i, [128, 128], F32) for i in range(4)]
        mp = [T("mp%d" % i, [128, 128], F32) for i in range(4)]
        Wre = [T("Wre%d" % i, [128, BWD], F32) for i in range(2)]
        Wim = [T("Wim%d" % i, [128, BWD], F32) for i in range(2)]
        XF = [T("XF%d" % i, [128, 2, BWD + 1], BF16) for i in range(2)]
        XBt = [T("XBt%d" % i, [128, 2, BWD], BF16) for i in range(2)]
        ex = [T("ex%d" % i, [128, 32, 16, 8], BF16) for i in range(1)]
        gel = T("gel", [128, 4, 512], BF16)
        gat = T("gat", [128, 8, 512], BF16)
        mao = T("mao", [128, 8, 512], BF16)
        sgb = [T("sgb%d" % i, [128, 512], F32) for i in range(1)]
        yab = [T("yab%d" % i, [128, 512], F32) for i in range(1)]
        sga = [T("sga%d" % i, [128, 512], F32) for i in range(1)]
        psS = [P("psS%d" % i, [128, 512], F32) for i in range(2)]
        psY = [P("psY%d" % i, [128, 512], F32) for i in range(2)]
        psG = P("psG", [128, 4, 128], F32)
        psb = [P("psb%d" % i, [128, 512], F32) for i in range(2)]

        S.op("dve", _memset(STre[:], 0.0), writes=["STre"])
        S.op("dve", _memset(STim[:], 0.0), writes=["STim"])
        S.op("dve", _memset(XFc[:], 0.0), writes=["XFc"])

        cnt = 0
        ucnt = 0
        for d in (1, 0):
            blks = list(reversed(blocks)) if d == 1 else blocks
            for (n0, w) in blks:
                ub = ucnt % 2
                ucnt += 1
                S.op("sp", _dma(ust[ub][:, :, 0:w], UST[:, :, n0:n0 + w]), reads=["UST"], writes=["ust%d" % ub], dma=True)
                segs = [(s0, min(128, w - s0)) for s0 in range(0, w, 128)]
                if d == 1:
                    segs = list(reversed(segs))
                for gp in range(16):
                    dg = d * 16 + gp
                    pb = cnt % 2
                    cnt += 1
                    PB = str(pb)
                    for ri in range(2):
                        S.op("pe", _mm(psS[ri][:, 0:w], WST[:, d, gp, ri, 0, :], ust[ub][:, 2 * gp, 0:w], True, False),
                             reads=["WST", "ust%d" % ub], writes=["psS%d" % ri])
                        S.op("pe", _mm(psS[ri][:, 0:w], WST[:, d, gp, ri, 1, :], ust[ub][:, 2 * gp + 1, 0:w], False, True),
                             reads=["WST", "ust%d" % ub], writes=["psS%d" % ri])
                    rbc = rc8[:, dg:dg + 1]
                    for (s0, sw) in segs:
                        sl = slice(s0, s0 + sw)
                        tsl = slice(128 - sw, 128) if d == 1 else slice(0, sw)
                        cb_, sb_ = COS[:, dg, tsl], SIN[:, dg, tsl]
                        S.op("dve", _tt(mt[0][:, 0:sw], psS[0][:, sl], cb_, ALU.mult), reads=["psS0", "COS"], writes=["mt0"])
                        S.op("dve", _tt(mt[1][:, 0:sw], psS[1][:, sl], sb_, ALU.mult), reads=["psS1", "SIN"], writes=["mt1"])
                        S.op("dve", _tt(bpre[pb][:, sl], mt[0][:, 0:sw], mt[1][:, 0:sw], ALU.add), reads=["mt0", "mt1"], writes=["bpre" + PB])
                        S.op("dve", _tt(mt[2][:, 0:sw], psS[1][:, sl], cb_, ALU.mult), reads=["psS1", "COS"], writes=["mt2"])
                        S.op("dve", _tt(mt[3][:, 0:sw], psS[0][:, sl], sb_, ALU.mult), reads=["psS0", "SIN"], writes=["mt3"])
                        S.op("dve", _tt(bpim[pb][:, sl], mt[2][:, 0:sw], mt[3][:, 0:sw], ALU.subtract), reads=["mt2", "mt3"], writes=["bpim" + PB])
                        wre_s, wim_s = Wre[pb][:, sl], Wim[pb][:, sl]
                        bre_s, bim_s = bpre[pb][:, sl], bpim[pb][:, sl]
                        if d == 1:
                            wre_s, wim_s, bre_s, bim_s = _rev(wre_s), _rev(wim_s), _rev(bre_s), _rev(bim_s)
                        rb_ = rbc.broadcast_to([128, sw])
                        S.op("dve", _scan(wre_s, rb_, bre_s, STre[:, dg:dg + 1]), reads=["rc8", "bpre" + PB, "STre%d" % dg], writes=["Wre" + PB])
                        S.op("dve", _scan(wim_s, rb_, bim_s, STim[:, dg:dg + 1]), reads=["rc8", "bpim" + PB, "STim%d" % dg], writes=["Wim" + PB])
                        last = s0 if d == 1 else s0 + sw - 1
                        tlast = (128 - sw) if d == 1 else (sw - 1)
                        cl = COS[:, dg, tlast:tlast + 1]
                        sl_ = SIN[:, dg, tlast:tlast + 1]
                        wr1 = Wre[pb][:, last:last + 1]
                        wi1 = Wim[pb][:, last:last + 1]
                        S.op("dve", _ts(tmp4[:, 0:1], wi1, sl_, None, ALU.mult), reads=["Wim" + PB, "SIN"], writes=["tmp4a"])
                        S.op("dve", _ts(tmp4[:, 1:2], wi1, cl, None, ALU.mult), reads=["Wim" + PB, "COS"], writes=["tmp4b"])
                        S.op("dve", _stt(STre[:, dg:dg + 1], wr1, cl, tmp4[:, 0:1], ALU.mult, ALU.subtract),
                             reads=["Wre" + PB, "COS", "tmp4a"], writes=["STre%d" % dg])
                        S.op("dve", _stt(STim[:, dg:dg + 1], wr1, sl_, tmp4[:, 1:2], ALU.mult, ALU.add),
                             reads=["Wre" + PB, "SIN", "tmp4b"], writes=["STim%d" % dg])
                        if d == 1:
                            xre_o, xim_o = XBt[pb][:, 0, sl], XBt[pb][:, 1, sl]
                            kxo = "XBt" + PB
                        else:
                            xre_o = XF[pb][:, 0, 1 + s0:1 + s0 + sw]
                            xim_o = XF[pb][:, 1, 1 + s0:1 + s0 + sw]
                            kxo = "XF" + PB
                        S.op("pool", _tt(mp[0][:, 0:sw], Wre[pb][:, sl], cb_, ALU.mult), reads=["Wre" + PB, "COS"], writes=["mp0"])
                        S.op("pool", _tt(mp[1][:, 0:sw], Wim[pb][:, sl], sb_, ALU.mult), reads=["Wim" + PB, "SIN"], writes=["mp1"])
                        S.op("pool", _tt(xre_o, mp[0][:, 0:sw], mp[1][:, 0:sw], ALU.subtract), reads=["mp0", "mp1"], writes=[kxo])
                        S.op("pool", _tt(mp[2][:, 0:sw], Wre[pb][:, sl], sb_, ALU.mult), reads=["Wre" + PB, "SIN"], writes=["mp2"])
                        S.op("pool", _tt(mp[3][:, 0:sw], Wim[pb][:, sl], cb_, ALU.mult), reads=["Wim" + PB, "COS"], writes=["mp3"])
                        S.op("pool", _tt(xim_o, mp[2][:, 0:sw], mp[3][:, 0:sw], ALU.add), reads=["mp2", "mp3"], writes=[kxo])
                    if d == 1:
                        S.op("pool", _dma(XBS[:, gp, :, n0:n0 + w], XBt[pb][:, :, 0:w]), reads=["XBt" + PB], writes=["XBS"], dma=True)
                        continue
                    S.op("act", _acopy(XF[pb][:, :, 0:1], XFc[:, gp, :].unsqueeze(2)), reads=["XFc"], writes=["XF" + PB])
                    S.op("act", _acopy(XFc[:, gp, :].unsqueeze(2), XF[pb][:, :, w:w + 1]), reads=["XF" + PB], writes=["XFc"])
                    wl = w if n0 + w < NCH else w - 1
                    if wl > 0:
                        S.op("sp", _dma(XBt[pb][:, :, 0:wl], XBS[:, gp, :, n0 + 1:n0 + 1 + wl]), reads=["XBS"], writes=["XBt" + PB], dma=True)
                    if wl < w:
                        S.op("pool", _memset(XBt[pb][:, :, wl:w], 0.0), writes=["XBt" + PB])
                    for gl in range(2):
                        g = 2 * gp + gl
                        rs = slice(gl * 64, (gl + 1) * 64)
                        py = psY[g % 2]
                        kp = "psY%d" % (g % 2)
                        S.op("pe", _mm(py[:, 0:w], WU[:, g, :], ust[ub][:, g, 0:w], True, False), reads=["WU", "ust%d" % ub], writes=[kp])
                        S.op("pe", _mm(py[:, 0:w], WX[rs, gp, 0, :], XF[pb][rs, 0, 0:w], False, False), reads=["WX", "XF" + PB], writes=[kp])
                        S.op("pe", _mm(py[:, 0:w], WX[rs, gp, 1, :], XF[pb][rs, 1, 0:w], False, False), reads=["WX", "XF" + PB], writes=[kp])
                        S.op("pe", _mm(py[:, 0:w], WX[rs, 16 + gp, 0, :], XBt[pb][rs, 0, 0:w], False, False), reads=["WX", "XBt" + PB], writes=[kp])
                        S.op("pe", _mm(py[:, 0:w], WX[rs, 16 + gp, 1, :], XBt[pb][rs, 1, 0:w], False, True), reads=["WX", "XBt" + PB], writes=[kp])
                        S.op("act", _act(gst[:, g, 0:w], py[:, 0:w], AF.Gelu), reads=[kp], writes=["gst"])
                if d == 1:
                    continue
                for s_ in range(w // 16):
                    tok0 = 8 * n0 + 128 * s_
                    st = tok0 // 512
                    sub = (tok0 % 512) // 128
                    if st == 0:
                        continue
                    eb = 0
                    S.op("dve", _tt(ex[eb][:], gst[:, :, s_ * 16:(s_ + 1) * 16].unsqueeze(3).broadcast_to([128, 32, 16, 8]),
                                    maskJb[:].unsqueeze(1).unsqueeze(1).broadcast_to([128, 32, 16, 8]), ALU.mult),
                         reads=["gst", "maskJb"], writes=["ex%d" % eb])
                    for q in range(4):
                        for g8 in range(8):
                            S.op("pe", _mm(psG[:, q, :], selC[:, g8, :], ex[eb][:, 8 * q + g8, :, :].rearrange("p n j -> p (n j)"),
                                           g8 == 0, g8 == 7), reads=["selC", "ex%d" % eb], writes=["psG"])
                    S.op("act", _acopy(gel[:, :, sub * 128:(sub + 1) * 128], psG[:]), reads=["psG"], writes=["gel"])
                    if sub != 3:
                        continue
                    c0 = st * 512
                    S.op("sp", _dma(gat[:], Z[3072:4096, c0:c0 + 512].rearrange("(q p) t -> p q t", p=128)),
                         reads=["Z"], writes=["gat"], dma=True)
                    for c in range(8):
                        pb2 = 0
                        pa_, pg_ = psb[0], psb[1]
                        for k_ in range(4):
                            S.op("pe", _mm(pa_[:], wglu[:, k_, c * 128:(c + 1) * 128], gel[:, k_, :], k_ == 0, k_ == 3),
                                 reads=["wglu", "gel"], writes=["psb0"])
                        for k_ in range(4):
                            S.op("pe", _mm(pg_[:], wglu[:, k_, 1024 + c * 128:1024 + (c + 1) * 128], gel[:, k_, :], k_ == 0, k_ == 3),
                                 reads=["wglu", "gel"], writes=["psb1"])
                        S.op("act", _act(sgb[pb2][:], pg_[:], AF.Sigmoid), reads=["psb1"], writes=["sgb%d" % pb2])
                        S.op("act", _act(sga[pb2][:], gat[:, c, :], AF.Sigmoid), reads=["gat"], writes=["sga%d" % pb2])
                        S.op("dve", _tt(yab[pb2][:], pa_[:], sgb[pb2][:], ALU.mult), reads=["psb0", "sgb%d" % pb2], writes=["yab%d" % pb2])
                        S.op("dve", _tt(mao[:, c, :], yab[pb2][:], sga[pb2][:], ALU.mult), reads=["yab%d" % pb2, "sga%d" % pb2],
                             writes=["mao"])
                    S.op("pool", _dma(MA[:, c0:c0 + 512].rearrange("(q p) t -> p q t", p=128), mao[:]),
                         reads=["mao"], writes=["MA"], dma=True)
        S.fence()
        S.emit()

    with ExitStack() as es:
        def T(name, shape, dt):
            return es.enter_context(nc.sbuf_tensor("C_" + name, shape, dt))

        def P(name, shape, dt):
            return es.enter_context(nc.psum_tensor("C_" + name, shape, dt))

        whg = T("whg", [128, 4, D], BF16)
        wout = T("wout", [128, 8, D], BF16)
        wst = T("wst", [128, D], F32)
        masks = T("masks", [64, 128], F32)
        identf = T("identf", [128, 128], F32)
        identb = T("identb", [128, 128], BF16)
        onesb = T("onesb", [128, 128], BF16)
        lb = T("lb", [128, 4], F32)
        oml = T("oml", [128, 4], F32)
        lb1 = T("lb1", [128, 4], F32)
        hgn = T("hgn", [128, 4], F32)
        Sst = [T("Sst%d" % h, [128, 128], F32) for h in range(4)]
        zeros = T("zeros", [128, 64], F32)
        Sall = T("Sall", [128, 1024], F32)
        kvs = T("kvs", [128, 1024], F32)
        decz = T("decz", [128, 8], F32)
        dzf = T("dzf", [128, 1024], F32)
        NB = 2
        zq = [T("zq%d" % i, [128, 512], BF16) for i in range(NB)]
        zf = [T("zf%d" % i, [128, 512], BF16) for i in range(NB)]
        zo = [T("zo%d" % i, [128, 512], BF16) for i in range(NB)]
        Vc = [T("Vc%d" % i, [64, 8, 128], BF16) for i in range(NB)]
        qs = [T("qs%d" % i, [128, 512], F32) for i in range(NB)]
        gs = [T("gs%d" % i, [128, 512], F32) for i in range(NB)]
        ks = [T("ks%d" % i, [128, 512], F32) for i in range(NB)]
        Pc = [T("Pc%d" % i, [128, 512], F32) for i in range(NB)]
        rP = [T("rP%d" % i, [128, 512], F32) for i in range(NB)]
        dec = [T("dec%d" % i, [128, 8], F32) for i in range(NB)]
        qx = [T("qx%d" % i, [128, 512], BF16) for i in range(NB)]
        kx = [T("kx%d" % i, [128, 512], BF16) for i in range(NB)]
        ke = [T("ke%d" % i, [128, 512], BF16) for i in range(NB)]
        qi = [T("qi%d" % i, [128, 512], BF16) for i in range(NB)]
        scm = [T("scm%d" % i, [64, 8, 64], BF16) for i in range(NB)]
        ketm = [T("ketm%d" % i, [64, 8, 128], BF16) for i in range(NB)]
        Sbf = [T("Sbf%d" % i, [128, 8, 128], BF16) for i in range(NB)]
        osb = [T("osb%d" % i, [128, 512], F32) for i in range(NB)]
        obl = [T("obl%d" % i, [128, 512], F32) for i in range(NB)]
        sq = T("sq", [128, 512], BF16)
        rn = T("rn", [128, 512], F32)
        sgo = T("sgo", [128, 512], F32)
        yh = T("yh", [128, 4, 512], BF16)
        gbt = T("gbt", [128, 8, 512], BF16)
        mal = T("mal", [128, 8, 512], BF16)
        sgb = T("sgb", [128, 512], F32)
        tmpc = T("tmpc", [128, 512], F32)
        mixT = T("mixT", [128, 8, 512], BF16)
        xt = [T("xt%d" % i, [128, D], F32) for i in range(2)]
        hsb = [T("hsb%d" % i, [128, D], F32) for i in range(2)]
        psS = P("psS", [64, 8, 64], F32)
        psK = P("psK", [64, 8, 128], BF16)
        psKV = P("psKV", [128, 8, 128], F32)
        psO = P("psO", [128, 512], F32)
        psN = P("psN", [128, 512], F32)
        psY = [P("psY%d" % i, [128, 512], F32) for i in range(2)]

        S.op("sp", _dma(identf[:], I["identf"][:, :]), writes=["identf"], dma=True)
        S.op("dve", _copy(identb[:], identf[:]), reads=["identf"], writes=["identb"])
        S.op("dve", _memset(onesb[:], 1.0), writes=["onesb"])
        S.op("dve", _memset(zeros[:], 0.0), writes=["zeros"])
        S.op("sp", _dma(masks[:], I["masks"][:, :]), writes=["masks"], dma=True)
        S.op("sp", _dma(lb[:], I["lb0"][:, :]), writes=["lb"], dma=True)
        S.op("sp", _dma(lb1[:], I["lb1"][:, :]), writes=["lb1"], dma=True)
        S.op("sp", _dma(hgn[:], I["hgn"][:, :]), writes=["hgn"], dma=True)
        S.op("dve", _tt(lb[:], lb[:], lb1[:], ALU.subtract), reads=["lb", "lb1"], writes=["lb"])
        S.op("act", _act(lb[:], lb[:], AF.Sigmoid), reads=["lb"], writes=["lb"])
        S.op("dve", _ts(oml[:], lb[:], -1.0, 1.0, ALU.mult, ALU.add), reads=["lb"], writes=["oml"])
        for k in range(4):
            S.op("sp", _dma(wst[:], I["w_hg"][k * 128:(k + 1) * 128, :]), writes=["wst"], dma=True)
            S.op("act", _acopy(whg[:, k, :], wst[:]), reads=["wst"], writes=["whg"])
        for k in range(8):
            S.op("sp", _dma(wst[:], I["w_out"][k * 128:(k + 1) * 128, :]), writes=["wst"], dma=True)
            S.op("act", _acopy(wout[:, k, :], wst[:]), reads=["wst"], writes=["wout"])

        cnt = 0
        xcnt = [0]
        items = []
        for d in (1, 0):
            sts = list(range(NT - 1, 0, -1)) if d == 1 else list(range(NT))
            for si, st in enumerate(sts):
                for h in range(4):
                    items.append(dict(d=d, st=st, h=h, bi=cnt % NB, first=(si == 0)))
                    cnt += 1

        def prep(it):
            d, st, h, bi = it["d"], it["st"], it["h"], it["bi"]
            B = str(bi)
            c0 = st * 512
            full = (d == 0 and st >= 1)
            frow = 1536 if d == 1 else 1024
            S.op("sp", _dma(zq[bi][:], Z[512 + h * 128:512 + (h + 1) * 128, c0:c0 + 512]), reads=["Z"],
                 writes=["zq" + B], dma=True)
            S.op("sp", _dma(zf[bi][:], Z[frow + h * 128:frow + (h + 1) * 128, c0:c0 + 512]), reads=["Z"],
                 writes=["zf" + B], dma=True)
            S.op("sp", _dma(Vc[bi][:], VTM[c0:c0 + 512, h * 128:(h + 1) * 128].rearrange("(c s) v -> s c v", s=64)),
                 reads=["VTM"], writes=["Vc" + B], dma=True)
            if full:
                S.op("sp", _dma(zo[bi][:], Z[2560 + h * 128:2560 + (h + 1) * 128, c0:c0 + 512]), reads=["Z"],
                     writes=["zo" + B], dma=True)
                S.op("sp", _dma(obl[bi][:], OB[h * 128:(h + 1) * 128, c0:c0 + 512]), reads=["OB"],
                     writes=["obl" + B], dma=True)
            qs_, gs_, ks_, Pc_, rP_, dec_ = qs[bi], gs[bi], ks[bi], Pc[bi], rP[bi], dec[bi]
            S.op("act", _act(qs_[:], zq[bi][:], AF.Silu), reads=["zq" + B], writes=["qs" + B])
            S.op("act", _act(gs_[:], zf[bi][:], AF.Sigmoid), reads=["zf" + B], writes=["gs" + B])
            S.op("dve", _ts(gs_[:], gs_[:], oml[:, h:h + 1], lb[:, h:h + 1], ALU.mult, ALU.add),
                 reads=["gs" + B, "oml", "lb"], writes=["gs" + B])
            S.op("dve", _ts(ks_[:], gs_[:], -1.0, 1.0, ALU.mult, ALU.add), reads=["gs" + B], writes=["ks" + B])
            Pc3 = Pc_[:].rearrange("p (c j) -> p c j", j=64)
            gs3 = gs_[:].rearrange("p (c j) -> p c j", j=64)
            if d == 0:
                for ch in range(8):
                    sl = slice(ch * 64, (ch + 1) * 64)
                    S.op("dve", _scan(Pc_[:, sl], gs_[:, sl], zeros[:, 0:64], 1.0), reads=["gs" + B, "zeros"], writes=["Pc" + B])
                S.op("dve", _tt(qx[bi][:], qs_[:], Pc_[:], ALU.mult), reads=["qs" + B, "Pc" + B], writes=["qx" + B])
                S.op("dve", _recip(rP_[:], Pc_[:]), reads=["Pc" + B], writes=["rP" + B])
                S.op("dve", _tt(kx[bi][:], ks_[:], rP_[:], ALU.mult), reads=["ks" + B, "rP" + B], writes=["kx" + B])
                S.op("dve", _copy(dec_[:].unsqueeze(2), Pc3[:, :, 63:64]), reads=["Pc" + B], writes=["dec" + B])
                S.op("dve", _tt(ke[bi][:].rearrange("p (c j) -> p c j", j=64),
                                kx[bi][:].rearrange("p (c j) -> p c j", j=64),
                                dec_[:].unsqueeze(2).broadcast_to([128, 8, 64]), ALU.mult),
                     reads=["kx" + B, "dec" + B], writes=["ke" + B])
            else:
                S.op("dve", _memset(Pc3[:, :, 0:1], 1.0), writes=["Pc" + B])
                for ch in range(8):
                    S.op("dve", _scan(Pc_[:, ch * 64 + 1:(ch + 1) * 64], gs_[:, ch * 64:(ch + 1) * 64 - 1],
                                      zeros[:, 0:63], 1.0), reads=["gs" + B, "zeros"], writes=["Pc" + B])
                S.op("dve", _recip(rP_[:], Pc_[:]), reads=["Pc" + B], writes=["rP" + B])
                S.op("dve", _tt(qx[bi][:], qs_[:], rP_[:], ALU.mult), reads=["qs" + B, "rP" + B], writes=["qx" + B])
                S.op("dve", _tt(kx[bi][:], ks_[:], Pc_[:], ALU.mult), reads=["ks" + B, "Pc" + B], writes=["kx" + B])
                S.op("dve", _tt(dec_[:].unsqueeze(2), Pc3[:, :, 63:64], gs3[:, :, 63:64], ALU.mult),
                     reads=["Pc" + B, "gs" + B], writes=["dec" + B])
                S.op("dve", _tt(qi[bi][:].rearrange("p (c j) -> p c j", j=64),
                                qx[bi][:].rearrange("p (c j) -> p c j", j=64),
                                dec_[:].unsqueeze(2).broadcast_to([128, 8, 64]), ALU.mult),
                     reads=["qx" + B, "dec" + B], writes=["qi" + B])

        def rest(it):
            d, st, h, bi = it["d"], it["st"], it["h"], it["bi"]
            B = str(bi)
            c0 = st * 512
            full = (d == 0 and st >= 1)
            mk = masks[:, 64:128] if d == 1 else masks[:, 0:64]
            mkb = mk.unsqueeze(1).broadcast_to([64, 8, 64])
            dec_ = dec[bi]
            if it["first"]:
                S.op("dve", _memset(Sst[h][:], 0.0), writes=["Sst%d" % h])
            if full and h == 0:
                S.op("sp", _dma(gbt[:], Z[4096:5120, c0:c0 + 512].rearrange("(q p) t -> p q t", p=128)),
                     reads=["Z"], writes=["gbt"], dma=True)
                S.op("sp", _dma(mal[:], MA[:, c0:c0 + 512].rearrange("(q p) t -> p q t", p=128)),
                     reads=["MA"], writes=["mal"], dma=True)
            if d == 0:
                qiT, keT = qx[bi], ke[bi]
                kq, kk_ = "qx" + B, "ke" + B
            else:
                qiT, keT = qi[bi], kx[bi]
                kq, kk_ = "qi" + B, "kx" + B
            for ch in range(8):
                sl = slice(ch * 64, (ch + 1) * 64)
                S.op("pe", _mm(psS[:, ch, :], kx[bi][:, sl], qx[bi][:, sl], True, True),
                     reads=["kx" + B, "qx" + B], writes=["psS"])
            for ch in range(8):
                sl = slice(ch * 64, (ch + 1) * 64)
                S.op("pe", _tr(psK[:, ch, :], keT[:, sl], identb[:]), reads=[kk_, "identb"], writes=["psK"])
            S.op("dve", _tt(scm[bi][:], psS[:], mkb, ALU.mult), reads=["psS", "masks"], writes=["scm" + B])
            S.op("act", _acopy(ketm[bi][:], psK[:]), reads=["psK"], writes=["ketm" + B])
            for ch in range(8):
                S.op("pe", _mm(psKV[:, ch, :], ketm[bi][:, ch, :], Vc[bi][:, ch, :], True, True),
                     reads=["ketm" + B, "Vc" + B], writes=["psKV"])
            kvv = psKV[:].rearrange("p c v -> p v c")
            sal = Sall[:].rearrange("p (v c) -> p v c", c=8)
            c_in = 7 if d == 1 else 0
            S.op("dve", _copy(decz[:], dec_[:]), reads=["dec" + B], writes=["decz"])
            S.op("dve", _memset(decz[:, c_in:c_in + 1], 0.0), writes=["decz"])
            S.op("dve", _copy(kvs[:].rearrange("p (v c) -> p v c", c=8), kvv), reads=["psKV"], writes=["kvs"])
            S.op("dve", _stt(kvs[:].rearrange("p (v c) -> p v c", c=8)[:, :, c_in], Sst[h][:], dec_[:, c_in:c_in + 1],
                             kvs[:].rearrange("p (v c) -> p v c", c=8)[:, :, c_in], ALU.mult, ALU.add),
                 reads=["Sst%d" % h, "dec" + B, "kvs"], writes=["kvs"])
            S.op("dve", _copy(dzf[:].rearrange("p (v c) -> p v c", c=8), decz[:].unsqueeze(1).broadcast_to([128, 128, 8])),
                 reads=["decz"], writes=["dzf"])
            if d == 0:
                S.op("dve", _scan(Sall[:], dzf[:], kvs[:], 0.0), reads=["dzf", "kvs"], writes=["Sall"])
            else:
                S.op("dve", _scan(_rev(Sall[:]), _rev(dzf[:]), _rev(kvs[:]), 0.0), reads=["dzf", "kvs"], writes=["Sall"])
            salc = Sall[:].rearrange("p (v c) -> p c v", c=8)
            if d == 0:
                S.op("act", _acopy(Sbf[bi][:, 0, :], Sst[h][:]), reads=["Sst%d" % h], writes=["Sbf" + B])
                S.op("act", _acopy(Sbf[bi][:, 1:8, :], salc[:, 0:7, :]), reads=["Sall"], writes=["Sbf" + B])
                S.op("dve", _copy(Sst[h][:], salc[:, 7, :]), reads=["Sall"], writes=["Sst%d" % h])
            else:
                S.op("act", _acopy(Sbf[bi][:, 7, :], Sst[h][:]), reads=["Sst%d" % h], writes=["Sbf" + B])
                S.op("act", _acopy(Sbf[bi][:, 0:7, :], salc[:, 1:8, :]), reads=["Sall"], writes=["Sbf" + B])
                S.op("dve", _copy(Sst[h][:], salc[:, 0, :]), reads=["Sall"], writes=["Sst%d" % h])
            if d == 0 and st == 0:
                return
            for ch in range(8):
                sl = slice(ch * 64, (ch + 1) * 64)
                S.op("pe", _mm(psO[:, sl], Vc[bi][:, ch, :], scm[bi][:, ch, :], True, False),
                     reads=["Vc" + B, "scm" + B], writes=["psO"])
                S.op("pe", _mm(psO[:, sl], Sbf[bi][:, ch, :], qiT[:, sl], False, True),
                     reads=["Sbf" + B, kq], writes=["psO"])
            if d == 1:
                S.op("act", _acopy(osb[bi][:], psO[:]), reads=["psO"], writes=["osb" + B])
                S.op("pool", _dma(OB[h * 128:(h + 1) * 128, c0:c0 + 512], osb[bi][:]), reads=["osb" + B],
                     writes=["OB"], dma=True)
                return
            S.op("dve", _tt(osb[bi][:], psO[:], obl[bi][:], ALU.add), reads=["psO", "obl" + B], writes=["osb" + B])
            S.op("act", _act(sq[:], osb[bi][:], AF.Square), reads=["osb" + B], writes=["sq"])
            S.op("pe", _mm(psN[:], onesb[:], sq[:], True, True), reads=["onesb", "sq"], writes=["psN"])
            S.op("dve", _ts(rn[:], psN[:], 1.0 / 128, EPS, ALU.mult, ALU.add), reads=["psN"], writes=["rn"])
            S.op("act", _act(rn[:], rn[:], AF.Sqrt), reads=["rn"], writes=["rn"])
            S.op("dve", _recip(rn[:], rn[:]), reads=["rn"], writes=["rn"])
            S.op("dve", _tt(rn[:], rn[:], osb[bi][:], ALU.mult), reads=["rn", "osb" + B], writes=["rn"])
            S.op("act", _act(sgo[:], zo[bi][:], AF.Silu), reads=["zo" + B], writes=["sgo"])
            S.op("dve", _stt(yh[:, h, :], rn[:], hgn[:, h:h + 1], sgo[:], ALU.mult, ALU.mult),
                 reads=["rn", "hgn", "sgo"], writes=["yh"])
            if h != 3:
                return
            for c in range(8):
                py = psY[c % 2]
                kp = "psY%d" % (c % 2)
                for k in range(4):
                    S.op("pe", _mm(py[:], whg[:, k, c * 128:(c + 1) * 128], yh[:, k, :], k == 0, k == 3),
                         reads=["whg", "yh"], writes=[kp])
                S.op("act", _act(sgb[:], gbt[:, c, :], AF.Sigmoid), reads=["gbt"], writes=["sgb"])
                S.op("dve", _tt(tmpc[:], py[:], sgb[:], ALU.mult), reads=[kp, "sgb"], writes=["tmpc"])
                S.op("dve", _tt(mixT[:, c, :], tmpc[:], mal[:, c, :], ALU.add), reads=["tmpc", "mal"], writes=["mixT"])
            for j in range(4):
                xb = xcnt[0] % 2
                xcnt[0] += 1
                r0 = (st - 1) * 512 + j * 128
                S.op("sp", _dma(xt[xb][:], I["xs"][r0:r0 + 128, :]), writes=["xt%d" % xb], dma=True)
                for hf in range(2):
                    py = psY[hf]
                    kp = "psY%d" % hf
                    for k in range(8):
                        S.op("pe", _mm(py[:], mixT[:, k, j * 128:(j + 1) * 128], wout[:, k, hf * 512:(hf + 1) * 512],
                                       k == 0, k == 7), reads=["mixT", "wout"], writes=[kp])
                    S.op("dve", _tt(hsb[xb][:, hf * 512:(hf + 1) * 512], py[:], xt[xb][:, hf * 512:(hf + 1) * 512], ALU.add),
                         reads=[kp, "xt%d" % xb], writes=["hsb%d" % xb])
                S.op("pool", _dma(HS1[r0:r0 + 128, :], hsb[xb][:]), reads=["hsb%d" % xb], writes=["HS1"], dma=True)

        prep(items[0])
        for ii, it in enumerate(items):
            if ii + 1 < len(items):
                prep(items[ii + 1])
            rest(it)
        S.fence()
        S.emit()

    with ExitStack() as es:
        def T(name, shape, dt):
            return es.enter_context(nc.sbuf_tensor("D_" + name, shape, dt))

        def P(name, shape, dt):
            return es.enter_context(nc.psum_tensor("D_" + name, shape, dt))

        wq = T("wq", [128, 8, 2048], BF16)
        keysT = T("keysT", [128, 16, 128], BF16)
        g2b = T("g2b", [128, D], F32)
        gfb = T("gfb", [128, D], F32)
        identf = T("identf", [128, 128], F32)
        identb = T("identb", [128, 128], BF16)
        iota16 = T("iota16", [128, 16], F32)
        tk = T("tk", [128, ND], I32)
        hs = [T("hs%d" % i, [128, D], F32) for i in range(2)]
        h2 = [T("h2%d" % i, [128, D], F32) for i in range(2)]
        ei = [T("ei%d" % i, [128, 128], I32) for i in range(2)]
        gate = [T("gate%d" % i, [128, 8, 16], F32) for i in range(2)]
        h2T = T("h2T", [128, 8, 128], BF16)
        qT = T("qT", [128, 16, 128], BF16)
        sc = T("sc", [128, 16, 128], F32)
        sc2 = T("sc2", [128, 16, 128], F32)
        top = T("top", [128, 8, 2, 16], F32)
        tix = T("tix", [128, 8, 2, 16], U32)
        tixf = T("tixf", [128, 8, 2, 16], F32)
        cand = T("cand", [128, 8, 256], F32)
        cand2 = T("cand2", [128, 8, 256], F32)
        ctop = T("ctop", [128, 8, 16], F32)
        cix = T("cix", [128, 8, 16], U32)
        cab = T("cab", [128, 8, 16], U32)
        caf = T("caf", [128, 8, 16], F32)
        cbf = T("cbf", [128, 8, 16], F32)
        eq = T("eq", [128, 8, 16, 16], F32)
        i1f = T("i1f", [128, 8, 16], F32)
        i2f = T("i2f", [128, 8, 16], F32)
        ssum = T("ssum", [128, 8], F32)
        sm = T("sm", [128, 8], F32)
        h2bf = [T("h2bf%d" % i, [128, D], BF16) for i in range(2)]
        actv = T("actv", [128, 128], F32)
        wgt = T("wgt", [128, 128], F32)
        dg = [T("dg%d" % i, [128, 8, 128], BF16) for i in range(2)]
        acc = T("acc", [128, D], F32)
        junkb = T("junkb", [128, D], BF16)
        junka = T("junka", [128, D], BF16)
        prod = [T("prod%d" % i, [128, D], BF16) for i in range(4)]
        outb = T("outb", [128, D], F32)
        pT = P("pT", [128, 8, 128], BF16)
        pQ = P("pQ", [128, 16, 128], F32)
        pacc = P("pacc", [128, D], F32)

        S.op("sp", _dma(identf[:], I["identf"][:, :]), writes=["identf"], dma=True)
        S.op("dve", _copy(identb[:], identf[:]), reads=["identf"], writes=["identb"])
        S.op("sp", _dma(iota16[:], I["iota16"][:, :]), writes=["iota16"], dma=True)
        S.op("sp", _dma(tk[:], I["tokd"][:, :]), writes=["tk"], dma=True)
        S.op("sp", _dma(g2b[:], I["g2b"][:, :]), writes=["g2b"], dma=True)
        S.op("sp", _dma(gfb[:], I["gfb"][:, :]), writes=["gfb"], dma=True)
        with ExitStack() as esp:
            wst = esp.enter_context(nc.sbuf_tensor("Dp_wst", [128, 2048], F32))
            for k in range(8):
                S.op("sp", _dma(wst[:], I["wq"][k * 128:(k + 1) * 128, :]), writes=["wst"], dma=True)
                S.op("act", _acopy(wq[:, k, :], wst[:]), reads=["wst"], writes=["wq"])
            S.op("sp", _dma(wst[:], I["keysT"][:, :]), writes=["wst"], dma=True)
            S.op("act", _acopy(keysT[:].rearrange("p a b -> p (a b)"), wst[:]), reads=["wst"], writes=["keysT"])
            S.fence()
            S.emit()
        NSB = 15
        uvg = [T("uvg%d" % i, [128, 2 * D], BF16) for i in range(NSB)]

        def stage1(i):
            b = i % 2
            B = str(b)
            S.op("pool", _gather(hs[b][:], HS1[:, :], tk[:, i:i + 1]), reads=["tk", "HS1"], writes=["hs" + B], dma=True)
            S.op("dve", _memset(ssum[:, 0:1], 0.0), writes=["ssum"])
            S.op("dve", _stt(junkb[:], hs[b][:], 1.0, hs[b][:], ALU.mult, ALU.mult, accum_out=ssum[:, 0:1]),
                 reads=["hs" + B], writes=["junkb", "ssum"])
            S.op("dve", _ts(sm[:, 0:1], ssum[:, 0:1], 1.0 / D, EPS, ALU.mult, ALU.add), reads=["ssum"], writes=["sm"])
            S.op("act", _act(sm[:, 0:1], sm[:, 0:1], AF.Sqrt), reads=["sm"], writes=["sm"])
            S.op("dve", _recip(sm[:, 0:1], sm[:, 0:1]), reads=["sm"], writes=["sm"])
            S.op("dve", _stt(h2[b][:], hs[b][:], sm[:, 0:1], g2b[:], ALU.mult, ALU.mult), reads=["hs" + B, "sm", "g2b"],
                 writes=["h2" + B])
            S.op("act", _acopy(h2bf[b][:], h2[b][:]), reads=["h2" + B], writes=["h2bf" + B])
            for k in range(8):
                S.op("pe", _tr(pT[:, k, :], h2bf[b][:, k * 128:(k + 1) * 128], identb[:]), reads=["h2bf" + B, "identb"], writes=["pT"])
            S.op("act", _acopy(h2T[:], pT[:]), reads=["pT"], writes=["h2T"])
            for hp in range(16):
                for k in range(8):
                    S.op("pe", _mm(pQ[:, hp, :], wq[:, k, hp * 128:(hp + 1) * 128], h2T[:, k, :], k == 0, k == 7),
                         reads=["wq", "h2T"], writes=["pQ"])
            S.op("act", _acopy(qT[:], pQ[:]), reads=["pQ"], writes=["qT"])
            for hp in range(16):
                S.op("pe", _mm(pQ[:, hp, :], qT[:, hp, :], keysT[:, hp, :], True, True), reads=["qT", "keysT"], writes=["pQ"])
            S.op("act", _acopy(sc[:], pQ[:]), reads=["pQ"], writes=["sc"])
            HP = [(hp, hp // 2, hp % 2) for hp in range(16)]
            for hp, h_, p_ in HP:
                S.op("dve", lambda e, h_=h_, p_=p_, hp=hp: e.max(out=top[:, h_, p_, 0:8], in_=sc[:, hp, :]),
                     reads=["sc"], writes=["top%d" % hp])
            for hp, h_, p_ in HP:
                S.op("dve", lambda e, h_=h_, p_=p_, hp=hp: e.max_index(out=tix[:, h_, p_, 0:8], in_max=top[:, h_, p_, 0:8],
                                                                       in_values=sc[:, hp, :]),
                     reads=["sc", "top%d" % hp], writes=["tix%d" % hp])
            for hp, h_, p_ in HP:
                S.op("dve", lambda e, h_=h_, p_=p_, hp=hp: e.match_replace(out=sc2[:, hp, :], in_to_replace=top[:, h_, p_, 0:8],
                                                                           in_values=sc[:, hp, :], imm_value=-1e30),
                     reads=["sc", "top%d" % hp], writes=["sc2_%d" % hp])
            for hp, h_, p_ in HP:
                S.op("dve", lambda e, h_=h_, p_=p_, hp=hp: e.max(out=top[:, h_, p_, 8:16], in_=sc2[:, hp, :]),
                     reads=["sc2_%d" % hp], writes=["topb%d" % hp])
            for hp, h_, p_ in HP:
                S.op("dve", lambda e, h_=h_, p_=p_, hp=hp: e.max_index(out=tix[:, h_, p_, 8:16], in_max=top[:, h_, p_, 8:16],
                                                                       in_values=sc2[:, hp, :]),
                     reads=["sc2_%d" % hp, "topb%d" % hp], writes=["tixb%d" % hp])
            tk_all = ["top%d" % x for x in range(16)] + ["topb%d" % x for x in range(16)]
            ti_all = ["tix%d" % x for x in range(16)] + ["tixb%d" % x for x in range(16)]
            S.op("dve", _copy(tixf[:], tix[:]), reads=ti_all, writes=["tixf"])
            cand4 = cand[:].rearrange("p h (a b) -> p h a b", a=16)
            S.op("dve", _tt(cand4, top[:, :, 0, :].unsqueeze(3).broadcast_to([128, 8, 16, 16]),
                            top[:, :, 1, :].unsqueeze(2).broadcast_to([128, 8, 16, 16]), ALU.add),
                 reads=tk_all, writes=["cand"])
            for h_ in range(8):
                S.op("dve", lambda e, h_=h_: e.max(out=ctop[:, h_, 0:8], in_=cand[:, h_, :]), reads=["cand"], writes=["ctop%d" % h_])
            for h_ in range(8):
                S.op("dve", lambda e, h_=h_: e.max_index(out=cix[:, h_, 0:8], in_max=ctop[:, h_, 0:8], in_values=cand[:, h_, :]),
                     reads=["cand", "ctop%d" % h_], writes=["cix%d" % h_])
            for h_ in range(8):
                S.op("dve", lambda e, h_=h_: e.match_replace(out=cand2[:, h_, :], in_to_replace=ctop[:, h_, 0:8],
                                                             in_values=cand[:, h_, :], imm_value=-1e30),
                     reads=["cand", "ctop%d" % h_], writes=["cand2_%d" % h_])
            for h_ in range(8):
                S.op("dve", lambda e, h_=h_: e.max(out=ctop[:, h_, 8:16], in_=cand2[:, h_, :]), reads=["cand2_%d" % h_],
                     writes=["ctopb%d" % h_])
            for h_ in range(8):
                S.op("dve", lambda e, h_=h_: e.max_index(out=cix[:, h_, 8:16], in_max=ctop[:, h_, 8:16], in_values=cand2[:, h_, :]),
                     reads=["cand2_%d" % h_, "ctopb%d" % h_], writes=["cixb%d" % h_])
            S.op("dve", lambda e: e.tensor_single_scalar(out=cab[:], in_=cix[:], scalar=4, op=ALU.logical_shift_right),
                 reads=["cix%d" % x for x in range(8)] + ["cixb%d" % x for x in range(8)], writes=["cab"])
            S.op("dve", _copy(caf[:], cab[:]), reads=["cab"], writes=["caf"])
            S.op("dve", lambda e: e.tensor_single_scalar(out=cab[:], in_=cix[:], scalar=15, op=ALU.bitwise_and),
                 reads=["cix%d" % x for x in range(8)] + ["cixb%d" % x for x in range(8)], writes=["cab"])
            S.op("dve", _copy(cbf[:], cab[:]), reads=["cab"], writes=["cbf"])
            io4 = iota16[:, :].unsqueeze(1).unsqueeze(1).broadcast_to([128, 8, 16, 16])
            for (src, half, dst, kd) in ((caf, 0, i1f, "i1f"), (cbf, 1, i2f, "i2f")):
                S.op("dve", _tt(eq[:], src[:].unsqueeze(3).broadcast_to([128, 8, 16, 16]), io4, ALU.is_equal),
                     reads=["caf", "cbf", "iota16"], writes=["eq"])
                S.op("dve", _tt(eq[:], eq[:], tixf[:, :, half, :].unsqueeze(2).broadcast_to([128, 8, 16, 16]), ALU.mult),
                     reads=["eq", "tixf"], writes=["eq"])
                S.op("dve", _rsum(dst[:], eq[:]), reads=["eq"], writes=[kd])
            S.op("dve", _stt(i1f[:], i1f[:], 128.0, i2f[:], ALU.mult, ALU.add), reads=["i1f", "i2f"], writes=["i1f"])
            S.op("dve", _ts(i1f[:], i1f[:], 0.0, 16383.0, ALU.max, ALU.min), reads=["i1f"], writes=["i1f"])
            S.op("dve", _copy(ei[b][:].rearrange("p (h j) -> p h j", h=8), i1f[:]), reads=["i1f"], writes=["ei" + B])
            S.op("dve", _tt(gate[b][:], ctop[:], ctop[:, :, 0:1].broadcast_to([128, 8, 16]), ALU.subtract),
                 reads=["ctop%d" % x for x in range(8)] + ["ctopb%d" % x for x in range(8)], writes=["gate" + B])
            S.op("act", _act(gate[b][:], gate[b][:], AF.Exp), reads=["gate" + B], writes=["gate" + B])
            S.op("dve", _rsum(ssum[:], gate[b][:]), reads=["gate" + B], writes=["ssum"])
            S.op("dve", _recip(ssum[:], ssum[:]), reads=["ssum"], writes=["ssum"])
            S.op("dve", _tt(gate[b][:], gate[b][:], ssum[:].unsqueeze(2).broadcast_to([128, 8, 16]), ALU.mult),
                 reads=["gate" + B, "ssum"], writes=["gate" + B])

        gcnt = [0]
        pcnt = [0]

        def stage2(i, pend=()):
            b = i % 2
            B = str(b)
            pend = list(pend)
            per_slot = -(-len(pend) // 112) if pend else 0
            ppos = [0]

            def drain(n):
                for o in pend[ppos[0]:ppos[0] + n]:
                    S.op(*o)
                ppos[0] += n
            S.op("dve", _memset(actv[:], 0.0), writes=["actv%d" % x for x in range(128)])
            slots = []
            for j in range(128):
                gb = gcnt[0] % NSB
                gcnt[0] += 1
                slots.append(gb)
                S.op("pool", _gather(uvg[gb][:], UVB[:, :], ei[b][:, j:j + 1]), reads=["ei" + B, "UVB"], writes=["uvg%d" % gb],
                     dma=True)
                if j % 8 in (0, 1, 2, 4, 5):
                    pi_ = pcnt[0] % 4
                    pcnt[0] += 1
                    S.op("dve", _tt(prod[pi_][:], uvg[gb][:, 0:D], h2bf[b][:], ALU.mult), reads=["uvg%d" % gb, "h2bf" + B],
                         writes=["prod%d" % pi_])
                    S.op("act", lambda e, pi_=pi_, j=j: e.activation(out=junka[:], in_=prod[pi_][:], func=AF.Copy,
                                                                   accum_out=actv[:, j:j + 1]),
                         reads=["prod%d" % pi_], writes=["junka", "actv%d" % j])
                else:
                    S.op("dve", _stt(junkb[:], uvg[gb][:, 0:D], 1.0, h2bf[b][:], ALU.mult, ALU.mult, accum_out=actv[:, j:j + 1]),
                         reads=["uvg%d" % gb, "h2bf" + B], writes=["junkb", "actv%d" % j])
                drain(per_slot)
                if j % 8 == 7:
                    g0 = j - 7
                    db = (j // 8) % 2
                    S.op("act", _act(wgt[:, g0:j + 1], actv[:, g0:j + 1], AF.Gelu), reads=["actv%d" % x for x in range(g0, j + 1)],
                         writes=["wgt"])
                    S.op("dve", _tt(wgt[:, g0:j + 1], wgt[:, g0:j + 1], gate[b][:].rearrange("p h j -> p (h j)")[:, g0:j + 1], ALU.mult),
                         reads=["wgt", "gate" + B], writes=["wgt"])
                    for s_ in range(8):
                        S.op("act", _act(dg[db][:, s_, :], identb[:], AF.Copy, scale=wgt[:, g0 + s_:g0 + s_ + 1]),
                             reads=["identb", "wgt"], writes=["dg%d_%d" % (db, s_)])
                    for s_ in range(8):
                        jj = g0 + s_
                        sb_ = slots[jj]
                        for hf in range(2):
                            S.op("pe", _mm(pacc[:, hf * 512:(hf + 1) * 512], dg[db][:, s_, :],
                                           uvg[sb_][:, D + hf * 512:D + (hf + 1) * 512], jj == 0, jj == 127),
                                 reads=["dg%d_%d" % (db, s_), "uvg%d" % sb_], writes=["pacc"])
            drain(len(pend))
            S.op("dve", _tt(acc[:], pacc[:], hs[b][:], ALU.add), reads=["pacc", "hs" + B], writes=["acc"])
            S.op("dve", _memset(sm[:, 1:2], 0.0), writes=["sm1"])
            S.op("dve", _stt(junkb[:], acc[:], 1.0, acc[:], ALU.mult, ALU.mult, accum_out=sm[:, 1:2]), reads=["acc"],
                 writes=["junkb", "sm1"])
            S.op("dve", _ts(sm[:, 1:2], sm[:, 1:2], 1.0 / D, EPS, ALU.mult, ALU.add), reads=["sm1"], writes=["sm1"])
            S.op("act", _act(sm[:, 1:2], sm[:, 1:2], AF.Sqrt), reads=["sm1"], writes=["sm1"])
            S.op("dve", _recip(sm[:, 1:2], sm[:, 1:2]), reads=["sm1"], writes=["sm1"])
            S.op("dve", _stt(outb[:], acc[:], sm[:, 1:2], gfb[:], ALU.mult, ALU.mult), reads=["acc", "sm1", "gfb"],
                 writes=["outb"])
            S.op("sp", _dma(OUT[i * 128:(i + 1) * 128, :], outb[:]), reads=["outb"], writes=["OUT"], dma=True)

        stage1(0)
        for i in range(ND):
            pend = []
            if i + 1 < ND:
                S.capture = pend
                stage1(i + 1)
                S.capture = None
            stage2(i, pend)
        S.fence()
        S.emit()

    top_es.close()
    return nc


def _common_inputs(inp):
    f = np.float32
    A = {}
    A["meta"] = np.ascontiguousarray(inp["meta"], f)
    A["g1c"] = np.ascontiguousarray(inp["norm1_g"][0].reshape(8, 128).T, f)
    A["w_in"] = np.ascontiguousarray(inp["w_in"][0], f)
    lre, lim, lst = inp["s5_lam_re"][0], inp["s5_lam_im"][0], inp["s5_log_step"][0]
    lst_full = np.broadcast_to(lst[:, :, None], (2, 32, 64))

    def col(a):
        a5 = a.reshape(2, 16, 2, 64)
        return np.ascontiguousarray(a5.transpose(2, 3, 0, 1).reshape(128, 32), f)

    A["lre_c"], A["lim_c"], A["lst_c"] = col(lre), col(lim), col(lst_full)

    def c8(cm):
        a = cm.reshape(16, 2, 16, 64).transpose(1, 3, 0, 2)
        a = np.broadcast_to(a[:, :, :, None, :], (2, 64, 16, 8, 16))
        return np.ascontiguousarray(a.reshape(128, 2048), f)

    def bc(bm):
        a = bm.reshape(16, 2, 64, 16).transpose(1, 2, 0, 3)
        a = np.broadcast_to(a[:, :, :, None, :], (2, 64, 16, 8, 16))
        return np.ascontiguousarray(a.reshape(128, 2048), f)

    A["cre8"], A["cim8"] = c8(inp["s5_c_re"][0]), c8(inp["s5_c_im"][0])
    A["brec"], A["bimc"] = bc(inp["s5_b_re"][0]), bc(inp["s5_b_im"][0])
    dd = inp["s5_d"][0].reshape(32, 16)
    A["drow"] = np.ascontiguousarray(np.broadcast_to(dd.T[None, :, :], (8, 16, 32)).reshape(128, 32), f)
    p = np.arange(128)
    cst8 = np.zeros((128, 32), f)
    cst8[:, 0:8] = (p[:, None] % 8 == np.arange(8)[None, :])
    cst8[:, 8:24] = (p[:, None] // 8 == np.arange(16)[None, :])
    cst8[:, 24:32] = (p[:, None] // 16 == np.arange(8)[None, :])
    A["cst8"] = cst8
    m = np.arange(128)
    selC = np.zeros((128, 8, 128), f)
    for g8 in range(8):
        selC[:, g8, :] = ((p[:, None] % 16) == (m[None, :] % 16)) & ((m[None, :] // 16) == g8)
    A["selC"] = selC.reshape(128, 1024)
    A["ramp16"] = np.ascontiguousarray(np.broadcast_to(np.arange(-7, 9, dtype=f)[None, :], (128, 16)), f)
    ii = p[:, None] // 16
    jj = m[None, :] // 16
    A["mfb"] = np.concatenate([(ii <= jj), (ii >= jj)], axis=1).astype(f)
    A["w_glu"] = np.ascontiguousarray(inp["w_glu"][0], f)
    A["lb0"] = np.ascontiguousarray(inp["hg_lb"][0].reshape(4, 128).T, f)
    A["lb1"] = np.ascontiguousarray(inp["hg_lb"][1].reshape(4, 128).T, f)
    A["hgn"] = np.ascontiguousarray(inp["hg_norm_g"][0].reshape(4, 128).T, f)
    A["w_hg"] = np.ascontiguousarray(inp["w_hg_out"][0], f)
    A["w_out"] = np.ascontiguousarray(inp["w_out"][0], f)
    A["g2b"] = np.ascontiguousarray(np.broadcast_to(inp["norm2_g"][0][None, :], (128, D)), f)
    A["gfb"] = np.ascontiguousarray(np.broadcast_to(inp["final_g"][None, :], (128, D)), f)
    A["wq"] = np.ascontiguousarray(inp["peer_wq"][0], f)
    kz = inp["peer_keys"][0].reshape(16, 128, 128)
    A["keysT"] = np.ascontiguousarray(kz.transpose(2, 0, 1).reshape(128, 2048), f)
    A["puv"] = np.ascontiguousarray(np.concatenate([inp["peer_u"][0], inp["peer_v"][0]], axis=1), f)
    A["identf"] = np.eye(128, dtype=f)
    r = np.arange(1, 129, dtype=f)
    A["ramps"] = np.ascontiguousarray(np.broadcast_to(np.concatenate([r, r[::-1]])[None, :], (128, 256)), f)
    s = np.arange(64)[:, None]
    t = np.arange(64)[None, :]
    A["masks"] = np.concatenate([(t >= s), (t <= s)], axis=1).astype(f)
    A["iota16"] = np.ascontiguousarray(np.broadcast_to(np.arange(16, dtype=f)[None, :], (128, 16)), f)
    return A


_PROG_CACHE = {}


def run_sequences(inp, seqs, assign, ND, debug=False):
    SEQ = seqs[0].shape[0]
    key = (SEQ, ND, debug)
    if key not in _PROG_CACHE:
        _PROG_CACHE[key] = build_program(SEQ, ND, debug)
    nc = _PROG_CACHE[key]
    A = _common_inputs(inp)
    in_maps = []
    for (si, t0, nt) in assign:
        m = dict(A)
        m["xs"] = np.ascontiguousarray(seqs[si], np.float32)
        tiles = [min(t0 + i, t0 + nt - 1) for i in range(ND)]
        tok = np.stack([np.arange(t * 128, (t + 1) * 128) for t in tiles], axis=1).astype(np.int32)
        m["tokd"] = np.ascontiguousarray(tok)
        in_maps.append(m)
    res = run_bass_kernel_spmd(nc, in_maps, core_ids=list(range(len(assign))))
    outs = [np.zeros((SEQ, D), np.float32) for _ in seqs]
    for ci, (si, t0, nt) in enumerate(assign):
        o = res.results[ci]["outd"]
        outs[si][t0 * 128:(t0 + nt) * 128] = o[:nt * 128]
    return outs, res


def kernel(**inputs):
    inp = {k: np.asarray(v) for k, v in inputs.items()}
    seqs = [inp["x_prompt"][0], inp["x_sample"][0], inp["x_sample"][1]]
    assign = [(0, 0, 43), (1, 0, 43), (2, 0, 64), (0, 43, 43), (1, 43, 43), (2, 64, 64), (0, 86, 42), (1, 86, 42)]
    outs, _ = run_sequences(inp, seqs, assign, ND=64)
    y_prompt = outs[0][None].astype(np.float32)
    y_sample = np.stack([outs[1], outs[2]], axis=0).astype(np.float32)
    return (y_prompt, y_sample)
```

```python
import math
from contextlib import ExitStack

import numpy as np
import concourse.bass as bass
import concourse.mybir as mybir
from concourse.bass_utils import run_bass_kernel_spmd

F32 = mybir.dt.float32
BF16 = mybir.dt.bfloat16
I32 = mybir.dt.int32
U32 = mybir.dt.uint32
ALU = mybir.AluOpType
AF = mybir.ActivationFunctionType
AX = mybir.AxisListType

D = 1024
NCOL = 5120
EPS = 1e-6
PI = math.pi
ENGS = ("pe", "act", "dve", "pool", "sp")


class Sch:
    def __init__(self, nc, es, n_dma_sems=32):
        self.nc = nc
        self.esem = {e: es.enter_context(nc.semaphore("se_" + e)) for e in ENGS}
        self.ecnt = {e: 0 for e in ENGS}
        self.dsem = [es.enter_context(nc.semaphore("sd%d" % i)) for i in range(n_dma_sems)]
        self.dval = [0] * n_dma_sems
        self.dnext = {"hw": 0, "sw": 0}
        self.dhalf = n_dma_sems // 2
        self.lastw = {}
        self.readers = {}
        self.known = {e: {} for e in ENGS}
        self.ops = {e: [] for e in ENGS}
        self.nops = 0
        self.capture = None

    def _sem(self, sk):
        return self.esem[sk[1]] if sk[0] == "e" else self.dsem[sk[1]]

    def _need(self, eng, tok, waits):
        if tok is None:
            return
        sk, val = tok
        if sk == ("e", "pe") and eng == "pe":
            return
        if self.known[eng].get(sk, 0) >= val:
            return
        self.known[eng][sk] = val
        waits[sk] = max(waits.get(sk, 0), val)

    def op(self, eng, fn, reads=(), writes=(), dma=False):
        if self.capture is not None:
            self.capture.append((eng, fn, tuple(reads), tuple(writes), dma))
            return
        waits = {}
        for k in reads:
            self._need(eng, self.lastw.get(k), waits)
        for k in writes:
            self._need(eng, self.lastw.get(k), waits)
            for t in self.readers.get(k, ()):
                self._need(eng, t, waits)
        if dma:
            kind = "sw" if eng == "pool" else "hw"
            i = self.dnext[kind] + (self.dhalf if kind == "sw" else 0)
            self.dnext[kind] = (self.dnext[kind] + 1) % self.dhalf
            if self.dval[i] > 0:
                self._need(eng, (("d", i), self.dval[i]), waits)
            self.dval[i] += 16
            tok = (("d", i), self.dval[i])
            inc = (self.dsem[i], 16)
        else:
            self.ecnt[eng] += 1
            tok = (("e", eng), self.ecnt[eng])
            inc = (self.esem[eng], 1)
        for k in reads:
            self.readers.setdefault(k, []).append(tok)
        for k in writes:
            self.lastw[k] = tok
            self.readers[k] = []
        self.ops[eng].append((list(waits.items()), fn, inc))
        self.nops += 1

    def fence(self):
        for e in ENGS:
            waits = {}
            for e2 in ENGS:
                if self.ecnt[e2] > 0:
                    self._need(e, (("e", e2), self.ecnt[e2]), waits)
            for i, v in enumerate(self.dval):
                if v > 0:
                    self._need(e, (("d", i), v), waits)
            self.ops[e].append((list(waits.items()), None, None))
        self.lastw.clear()
        self.readers.clear()

    def emit(self):
        with self.nc.Block() as blk:
            decos = {"pe": blk.tensor, "act": blk.scalar, "dve": blk.vector, "pool": blk.gpsimd, "sp": blk.sync}
            for e in ENGS:
                ops = self.ops[e]

                def body(eng, ops=ops):
                    for waits, fn, inc in ops:
                        for sk, val in waits:
                            eng.wait_ge(self._sem(sk), val)
                        if fn is not None:
                            fn(eng).then_inc(inc[0], inc[1])

                decos[e](body)
                self.ops[e] = []


def _mm(out, lhsT, rhs, start, stop):
    return lambda e: e.matmul(out, lhsT, rhs, start=start, stop=stop)


def _tr(out, in_, ident):
    return lambda e: e.transpose(out, in_, ident)


def _dma(out, in_):
    return lambda e: e.dma_start(out=out, in_=in_)


def _gather(out, table, idx):
    return lambda e: e.indirect_dma_start(out=out, out_offset=None, in_=table,
                                          in_offset=bass.IndirectOffsetOnAxis(ap=idx, axis=0))


def _act(out, in_, func, scale=None):
    if scale is None:
        return lambda e: e.activation(out=out, in_=in_, func=func)
    return lambda e: e.activation(out=out, in_=in_, func=func, scale=scale)


def _copy(out, in_):
    return lambda e: e.tensor_copy(out=out, in_=in_)


def _acopy(out, in_):
    return lambda e: e.copy(out=out, in_=in_)


def _tt(out, in0, in1, op):
    return lambda e: e.tensor_tensor(out=out, in0=in0, in1=in1, op=op)


def _ts(out, in0, s1, s2, op0, op1=None):
    if op1 is None:
        return lambda e: e.tensor_scalar(out=out, in0=in0, scalar1=s1, scalar2=None, op0=op0)
    return lambda e: e.tensor_scalar(out=out, in0=in0, scalar1=s1, scalar2=s2, op0=op0, op1=op1)


def _stt(out, in0, scalar, in1, op0, op1, accum_out=None):
    if accum_out is None:
        return lambda e: e.scalar_tensor_tensor(out=out, in0=in0, scalar=scalar, in1=in1, op0=op0, op1=op1)
    return lambda e: e.scalar_tensor_tensor(out=out, in0=in0, scalar=scalar, in1=in1, op0=op0, op1=op1,
                                            accum_out=accum_out)


def _scan(out, d0, d1, init):
    return lambda e: e.tensor_tensor_scan(out=out, data0=d0, data1=d1, initial=init, op0=ALU.mult, op1=ALU.add)


def _memset(ap, v):
    return lambda e: e.memset(ap, v)


def _rsum(out, in_):
    return lambda e: e.reduce_sum(out=out, in_=in_, axis=AX.X)


def _recip(out, in_):
    return lambda e: e.reciprocal(out=out, in_=in_)


def _rev(ap2d):
    return ap2d[:, ::-1]


def build_program(SEQ, ND, debug=False):
    assert SEQ % 512 == 0
    NT = SEQ // 512 + 1
    TP = NT * 512
    nc = bass.Bass("TRN2", target_bir_lowering=False)

    def din(name, shape, dt=F32):
        return nc.dram_tensor(name, list(shape), dt, kind="ExternalInput").ap()

    def dscr(name, shape, dt):
        kind = "ExternalOutput" if debug else "Internal"
        return nc.dram_tensor(name, list(shape), dt, kind=kind).ap()

    I = {}
    I["xs"] = din("xs", [SEQ, D])
    I["meta"] = din("meta", [16, D])
    I["tokd"] = din("tokd", [128, ND], I32)
    I["g1c"] = din("g1c", [128, 8])
    I["w_in"] = din("w_in", [D, NCOL])
    for nm in ("lre_c", "lim_c", "lst_c"):
        I[nm] = din(nm, [128, 32])
    for nm in ("cre8", "cim8", "brec", "bimc"):
        I[nm] = din(nm, [128, 2048])
    I["drow"] = din("drow", [128, 32])
    I["cst8"] = din("cst8", [128, 32])
    I["selC"] = din("selC", [128, 1024])
    I["ramp16"] = din("ramp16", [128, 16])
    I["mfb"] = din("mfb", [128, 256])
    I["w_glu"] = din("w_glu", [512, 2048])
    I["lb0"] = din("lb0", [128, 4])
    I["lb1"] = din("lb1", [128, 4])
    I["hgn"] = din("hgn", [128, 4])
    I["w_hg"] = din("w_hg", [512, D])
    I["w_out"] = din("w_out", [D, D])
    I["g2b"] = din("g2b", [128, D])
    I["gfb"] = din("gfb", [128, D])
    I["wq"] = din("wq", [D, 2048])
    I["keysT"] = din("keysT", [128, 2048])
    I["puv"] = din("puv", [16384, 2 * D])
    I["identf"] = din("identf", [128, 128])
    I["ramps"] = din("ramps", [128, 256])
    I["masks"] = din("masks", [64, 128])
    I["iota16"] = din("iota16", [128, 16])
    OUT = nc.dram_tensor("outd", [ND * 128, D], F32, kind="ExternalOutput").ap()

    Z = dscr("Z", [NCOL, TP], BF16)
    VTM = dscr("VTM", [TP, 512], BF16)
    UST = nc.dram_tensor("UST", [128, 32, TP // 8], BF16, kind="Internal").ap()
    XBS = nc.dram_tensor("XBS", [128, 16, 2, TP // 8], BF16, kind="Internal").ap()
    MA = dscr("MA", [D, TP], BF16)
    OB = dscr("OB", [512, TP], F32)
    HS1 = dscr("HS1", [SEQ, D], F32)
    UVB = nc.dram_tensor("UVB", [16384, 2 * D], BF16, kind="Internal").ap()

    top_es = ExitStack()
    S = Sch(nc, top_es)

    with ExitStack() as es:
        def T(name, shape, dt):
            return es.enter_context(nc.sbuf_tensor("A_" + name, shape, dt))

        def P(name, shape, dt):
            return es.enter_context(nc.psum_tensor("A_" + name, shape, dt))

        Wb = T("Wb", [128, 8, NCOL], BF16)
        stg = [T("stg%d" % i, [128, 1280], F32) for i in range(2)]
        cst8 = T("cst8", [128, 32], F32)
        selNb = T("selNb", [128, 16], BF16)
        Ex = T("Ex", [128, 32, 8, 16], BF16)
        ustb = [T("ustb%d" % i, [128, 32, 256], BF16) for i in range(2)]
        g1c = T("g1c", [128, 8], F32)
        identf = T("identf", [128, 128], F32)
        identb = T("identb", [128, 128], BF16)
        xt = [[T("xt%d_%d" % (b, j), [128, D], F32) for j in range(4)] for b in range(2)]
        hb = [T("hb%d" % i, [128, D], BF16) for i in range(2)]
        hT = [T("hT%d" % i, [128, 8, 512], BF16) for i in range(2)]
        ssq = T("ssq", [128, 8], F32)
        rstd = T("rstd", [128, 8], F32)
        junk = T("junk", [128, D], F32)
        zt = [T("zt%d" % i, [128, 4, 512], BF16) for i in range(2)]
        vt = [T("vt%d" % i, [128, 512], BF16) for i in range(2)]
        pT = [P("pT%d" % i, [128, 8, 128], BF16) for i in range(2)]
        pz = [P("pz%d" % i, [128, 512], F32) for i in range(4)]
        psU = P("psU", [128, 32, 16], F32)

        S.op("sp", _dma(identf[:], I["identf"][:, :]), writes=["identf"], dma=True)
        S.op("sp", _dma(g1c[:], I["g1c"][:, :]), writes=["g1c"], dma=True)
        S.op("sp", _dma(cst8[:], I["cst8"][:, :]), writes=["cst8"], dma=True)
        S.op("dve", _copy(selNb[:], cst8[:, 8:24]), reads=["cst8"], writes=["selNb"])
        S.op("dve", _copy(identb[:], identf[:]), reads=["identf"], writes=["identb"])
        n = 0
        for k in range(8):
            for h in range(4):
                sb = n % 2
                n += 1
                S.op("sp", _dma(stg[sb][:], I["w_in"][k * 128:(k + 1) * 128, h * 1280:(h + 1) * 1280]),
                     writes=["stg%d" % sb], dma=True)
                S.op("dve", _ts(Wb[:, k, h * 1280:(h + 1) * 1280], stg[sb][:], g1c[:, k:k + 1], None, ALU.mult),
                     reads=["stg%d" % sb, "g1c"], writes=["Wb"])

        ev = 0
        for st in range(NT):
            b = st % 2
            if st == 0:
                for j in range(4):
                    S.op("pool", _memset(xt[b][j][:], 0.0), writes=["xt%d_%d" % (b, j)])
                S.op("sp", _dma(xt[b][3][112:128, :], I["meta"][:, :]), writes=["xt%d_3" % b], dma=True)
            else:
                for j in range(4):
                    r0 = (st - 1) * 512 + j * 128
                    S.op("sp", _dma(xt[b][j][:], I["xs"][r0:r0 + 128, :]), writes=["xt%d_%d" % (b, j)], dma=True)
            for j in range(4):
                c = (st * 4 + j) % 8
                hbj = j % 2
                xk = "xt%d_%d" % (b, j)
                S.op("dve", _memset(ssq[:, c:c + 1], 0.0), writes=["ssq%d" % c])
                S.op("dve", _stt(junk[:], xt[b][j][:], 1.0, xt[b][j][:], ALU.mult, ALU.mult, accum_out=ssq[:, c:c + 1]),
                     reads=[xk], writes=["junk", "ssq%d" % c])
                S.op("dve", _ts(rstd[:, c:c + 1], ssq[:, c:c + 1], 1.0 / D, EPS, ALU.mult, ALU.add),
                     reads=["ssq%d" % c], writes=["rstd%d" % c])
                S.op("act", _act(rstd[:, c:c + 1], rstd[:, c:c + 1], AF.Sqrt), reads=["rstd%d" % c], writes=["rstd%d" % c])
                S.op("dve", _recip(rstd[:, c:c + 1], rstd[:, c:c + 1]), reads=["rstd%d" % c], writes=["rstd%d" % c])
                S.op("act", _act(hb[hbj][:], xt[b][j][:], AF.Copy, scale=rstd[:, c:c + 1]),
                     reads=[xk, "rstd%d" % c], writes=["hb%d" % hbj])
                for k in range(8):
                    S.op("pe", _tr(pT[hbj][:, k, :], hb[hbj][:, k * 128:(k + 1) * 128], identb[:]),
                         reads=["hb%d" % hbj, "identb"], writes=["pT%d" % hbj])
                S.op("act", _acopy(hT[b][:, :, j * 128:(j + 1) * 128], pT[hbj][:, :, :]),
                     reads=["pT%d" % hbj], writes=["hT%d" % b])
            for c in [c_ for c_ in range(4, 40) if not (16 <= c_ < 20)]:
                pzi = c % 4
                for k in range(8):
                    S.op("pe", _mm(pz[pzi][:], Wb[:, k, c * 128:(c + 1) * 128], hT[b][:, k, :], k == 0, k == 7),
                         reads=["Wb", "hT%d" % b], writes=["pz%d" % pzi])
                zb = (c // 4) % 2
                if ev % 2 == 0:
                    S.op("act", _acopy(zt[zb][:, c % 4, :], pz[pzi][:]), reads=["pz%d" % pzi], writes=["zt%d" % zb])
                else:
                    S.op("dve", _copy(zt[zb][:, c % 4, :], pz[pzi][:]), reads=["pz%d" % pzi], writes=["zt%d" % zb])
                ev += 1
                if c % 4 == 3:
                    dst = Z[(c - 3) * 128:(c + 1) * 128, st * 512:(st + 1) * 512].rearrange("(c p) t -> p c t", p=128)
                    S.op("pool", _dma(dst, zt[zb][:]), reads=["zt%d" % zb], writes=["Z"], dma=True)
            for j in range(4):
                pzi = j % 4
                for k in range(8):
                    S.op("pe", _mm(pz[pzi][:], hT[b][:, k, j * 128:(j + 1) * 128], Wb[:, k, 2048:2560], k == 0, k == 7),
                         reads=["Wb", "hT%d" % b], writes=["pz%d" % pzi])
                vb = j % 2
                S.op("act", _acopy(vt[vb][:], pz[pzi][:]), reads=["pz%d" % pzi], writes=["vt%d" % vb])
                r0 = st * 512 + j * 128
                S.op("pool", _dma(VTM[r0:r0 + 128, :], vt[vb][:]), reads=["vt%d" % vb], writes=["VTM"], dma=True)
            ubuf = (st // 4) % 2
            for j in range(4):
                pzi = j % 4
                for k in range(8):
                    S.op("pe", _mm(pz[pzi][:], hT[b][:, k, j * 128:(j + 1) * 128], Wb[:, k, 0:512], k == 0, k == 7),
                         reads=["Wb", "hT%d" % b], writes=["pz%d" % pzi])
                S.op("dve", _tt(Ex[:], pz[pzi][:].rearrange("p (g c) -> p g c", g=32).unsqueeze(2).broadcast_to([128, 32, 8, 16]),
                                cst8[:, 0:8].unsqueeze(1).unsqueeze(3).broadcast_to([128, 32, 8, 16]), ALU.mult),
                     reads=["pz%d" % pzi, "cst8"], writes=["Ex"])
                for g in range(32):
                    S.op("pe", _mm(psU[:, g, :], Ex[:, g, :, :].rearrange("p i c -> p (i c)"), selNb[:], True, True),
                         reads=["Ex", "selNb"], writes=["psU"])
                off = (st % 4) * 64 + j * 16
                S.op("act", _acopy(ustb[ubuf][:, :, off:off + 16], psU[:]), reads=["psU"], writes=["ustb%d" % ubuf])
            if st % 4 == 3 or st == NT - 1:
                base = (st - st % 4) * 64
                ncols = (st % 4 + 1) * 64
                S.op("pool", _dma(UST[:, :, base:base + ncols], ustb[ubuf][:, :, 0:ncols]), reads=["ustb%d" % ubuf], writes=["UST"],
                     dma=True)
        S.fence()
        S.emit()

    with ExitStack() as es:
        def T(name, shape, dt):
            return es.enter_context(nc.sbuf_tensor("B_" + name, shape, dt))

        def P(name, shape, dt):
            return es.enter_context(nc.psum_tensor("B_" + name, shape, dt))

        NCH = TP // 8
        BWD = 256
        blocks = [(n0, min(BWD, NCH - n0)) for n0 in range(0, NCH, BWD)]

        WST = T("WST", [128, 2, 16, 2, 2, 128], BF16)
        WX = T("WX", [128, 32, 2, 128], BF16)
        WU = T("WU", [128, 32, 128], BF16)
        COS = T("COS", [128, 32, 128], F32)
        SIN = T("SIN", [128, 32, 128], F32)
        rc8 = T("rc8", [128, 32], F32)
        ramps = T("ramps", [128, 256], F32)
        cst8 = T("cst8", [128, 32], F32)
        maskJb = T("maskJb", [128, 8], BF16)
        selC = T("selC", [128, 8, 128], BF16)
        STre = T("STre", [128, 32], F32)
        STim = T("STim", [128, 32], F32)
        XFc = T("XFc", [128, 16, 2], BF16)
        tmp4 = T("tmp4", [128, 4], F32)
        identf = T("identf", [128, 128], F32)

        def sincos(dsin, dcos, ang, tf, tf2, ti, ksin, kcos, kang, ktf, ktf2, kti):
            for dst, kd, off in ((dcos, kcos, 0.25), (dsin, ksin, 0.0)):
                S.op("dve", _ts(tf, ang, 1.0 / (2 * PI), off, ALU.mult, ALU.add), reads=[kang], writes=[ktf])
                S.op("dve", _copy(ti, tf), reads=[ktf], writes=[kti])
                S.op("dve", _copy(tf2, ti), reads=[kti], writes=[ktf2])
                S.op("dve", _tt(tf, tf, tf2, ALU.subtract), reads=[ktf, ktf2], writes=[ktf])
                S.op("dve", _ts(tf, tf, -0.4999, 0.4999, ALU.max, ALU.min), reads=[ktf], writes=[ktf])
                S.op("act", _act(dst, tf, AF.Sin, scale=2 * PI), reads=[ktf], writes=[kd])

        with ExitStack() as es2:
            def T2(name, shape, dt):
                return es2.enter_context(nc.sbuf_tensor("B2_" + name, shape, dt))

            def P2(name, shape, dt):
                return es2.enter_context(nc.psum_tensor("B2_" + name, shape, dt))

            LR = T2("LR", [128, 32], F32)
            LI = T2("LI", [128, 32], F32)
            LS = T2("LS", [128, 32], F32)
            pa = T2("pa", [128, 32], F32)
            pth = T2("pth", [128, 32], F32)
            pcr = T2("pcr", [128, 32], F32)
            pci = T2("pci", [128, 32], F32)
            sm_ = [T2("sm%d" % i, [128, 32], F32) for i in range(8)]
            ramp16 = T2("ramp16", [128, 16], F32)
            mfb = T2("mfb", [128, 256], F32)
            drow = T2("drow", [128, 32], F32)
            S.op("sp", _dma(identf[:], I["identf"][:, :]), writes=["identf"], dma=True)
            S.op("sp", _dma(ramps[:], I["ramps"][:, :]), writes=["ramps"], dma=True)
            S.op("sp", _dma(cst8[:], I["cst8"][:, :]), writes=["cst8"], dma=True)
            S.op("sp", _dma(ramp16[:], I["ramp16"][:, :]), writes=["ramp16"], dma=True)
            S.op("sp", _dma(mfb[:], I["mfb"][:, :]), writes=["mfb"], dma=True)
            S.op("sp", _dma(drow[:], I["drow"][:, :]), writes=["drow"], dma=True)
            S.op("sp", _dma(LR[:], I["lre_c"][:, :]), writes=["LR"], dma=True)
            S.op("sp", _dma(LI[:], I["lim_c"][:, :]), writes=["LI"], dma=True)
            S.op("sp", _dma(LS[:], I["lst_c"][:, :]), writes=["LS"], dma=True)
            S.op("dve", _copy(maskJb[:], cst8[:, 24:32]), reads=["cst8"], writes=["maskJb"])
            t = [x[:] for x in sm_]
            k = ["sm%d" % i for i in range(8)]
            S.op("act", _act(t[0], LS[:], AF.Exp), reads=["LS"], writes=[k[0]])
            S.op("dve", _tt(pa[:], LR[:], t[0], ALU.mult), reads=["LR", k[0]], writes=["pa"])
            S.op("dve", _tt(pth[:], LI[:], t[0], ALU.mult), reads=["LI", k[0]], writes=["pth"])
            S.op("act", _act(t[1], pa[:], AF.Exp), reads=["pa"], writes=[k[1]])
            sincos(t[3], t[4], pth[:], t[5], t[6], t[7].bitcast(I32), k[3], k[4], "pth", k[5], k[6], k[7])
            S.op("dve", _tt(t[5], t[1], t[4], ALU.mult), reads=[k[1], k[4]], writes=[k[5]])
            S.op("dve", _ts(t[5], t[5], -1.0, None, ALU.add), reads=[k[5]], writes=[k[5]])
            S.op("dve", _tt(t[6], t[1], t[3], ALU.mult), reads=[k[1], k[3]], writes=[k[6]])
            S.op("dve", _tt(t[7], LR[:], LR[:], ALU.mult), reads=["LR"], writes=[k[7]])
            S.op("dve", _tt(t[0], LI[:], LI[:], ALU.mult), reads=["LI"], writes=[k[0]])
            S.op("dve", _tt(t[7], t[7], t[0], ALU.add), reads=[k[7], k[0]], writes=[k[7]])
            S.op("dve", _recip(t[7], t[7]), reads=[k[7]], writes=[k[7]])
            S.op("dve", _tt(t[3], t[5], LR[:], ALU.mult), reads=[k[5], "LR"], writes=[k[3]])
            S.op("dve", _tt(t[0], t[6], LI[:], ALU.mult), reads=[k[6], "LI"], writes=[k[0]])
            S.op("dve", _tt(t[3], t[3], t[0], ALU.add), reads=[k[3], k[0]], writes=[k[3]])
            S.op("dve", _tt(pcr[:], t[3], t[7], ALU.mult), reads=[k[3], k[7]], writes=["pcr"])
            S.op("dve", _tt(t[4], t[6], LR[:], ALU.mult), reads=[k[6], "LR"], writes=[k[4]])
            S.op("dve", _tt(t[0], t[5], LI[:], ALU.mult), reads=[k[5], "LI"], writes=[k[0]])
            S.op("dve", _tt(t[4], t[4], t[0], ALU.subtract), reads=[k[4], k[0]], writes=[k[4]])
            S.op("dve", _tt(pci[:], t[4], t[7], ALU.mult), reads=[k[4], k[7]], writes=["pci"])

            PWm = T2("PWm", [128, 32, 16], F32)
            PWa = T2("PWa", [128, 32, 16], F32)
            PWr = T2("PWr", [128, 32, 16], F32)
            PWi = T2("PWi", [128, 32, 16], F32)
            PCr = T2("PCr", [128, 32, 16], F32)
            PCi = T2("PCi", [128, 32, 16], F32)
            tl = [T2("tl%d" % i, [128, 2048], F32) for i in range(6)]
            pt = [tl[3 + i][:, 0:512] for i in range(3)]
            for dg in range(32):
                S.op("act", _act(PWm[:, dg, :], ramp16[:], AF.Exp, scale=pa[:, dg:dg + 1]), reads=["ramp16", "pa"], writes=["PWm"])
                S.op("dve", _ts(PWa[:, dg, :], ramp16[:], pth[:, dg:dg + 1], None, ALU.mult), reads=["ramp16", "pth"], writes=["PWa"])
            f512 = lambda x: x[:].rearrange("p a b -> p (a b)")
            sincos(f512(PWi), f512(PWr), f512(PWa), pt[0], pt[1], pt[2].bitcast(I32), "PWi", "PWr", "PWa", "tl3", "tl4", "tl5")
            S.op("dve", _tt(PWr[:], PWr[:], PWm[:], ALU.mult), reads=["PWr", "PWm"], writes=["PWr"])
            S.op("dve", _tt(PWi[:], PWi[:], PWm[:], ALU.mult), reads=["PWi", "PWm"], writes=["PWi"])
            crb = pcr[:].unsqueeze(2).broadcast_to([128, 32, 16])
            cib = pci[:].unsqueeze(2).broadcast_to([128, 32, 16])
            p3 = lambda x: x.rearrange("p (a b) -> p a b", a=32)
            S.op("dve", _tt(PCr[:], PWr[:], crb, ALU.mult), reads=["PWr", "pcr"], writes=["PCr"])
            S.op("dve", _tt(p3(pt[0]), PWi[:], cib, ALU.mult), reads=["PWi", "pci"], writes=["tl3"])
            S.op("dve", _tt(PCr[:], PCr[:], p3(pt[0]), ALU.subtract), reads=["PCr", "tl3"], writes=["PCr"])
            S.op("dve", _tt(PCi[:], PWr[:], cib, ALU.mult), reads=["PWr", "pci"], writes=["PCi"])
            S.op("dve", _tt(p3(pt[1]), PWi[:], crb, ALU.mult), reads=["PWi", "pcr"], writes=["tl4"])
            S.op("dve", _tt(PCi[:], PCi[:], p3(pt[1]), ALU.add), reads=["PCi", "tl4"], writes=["PCi"])
            S.op("dve", _copy(rc8[:].unsqueeze(2), PWm[:, :, 15:16]), reads=["PWm"], writes=["rc8"])
            S.op("dve", _ts(sm_[0][:], pth[:], 8.0, None, ALU.mult), reads=["pth"], writes=["sm0"])
            S.fence()
            for hf in range(2):
                ang3 = tl[0][:].rearrange("p (a b) -> p a b", a=16)
                for g_ in range(16):
                    dg = hf * 16 + g_
                    rp = ramps[:, 0:128] if hf == 0 else ramps[:, 128:256]
                    S.op("dve", _ts(ang3[:, g_, :], rp, sm_[0][:, dg:dg + 1], None, ALU.mult), reads=["ramps", "sm0"], writes=["tl0"])
                sinh = SIN[:, hf * 16:(hf + 1) * 16, :].rearrange("p a b -> p (a b)")
                cosh = COS[:, hf * 16:(hf + 1) * 16, :].rearrange("p a b -> p (a b)")
                sincos(sinh, cosh, tl[0][:], tl[1][:], tl[2][:], tl[3][:].bitcast(I32), "SIN", "COS", "tl0", "tl1", "tl2", "tl3")
            S.fence()

            C8r = T2("C8r", [128, 2048], F32)
            C8i = T2("C8i", [128, 2048], F32)
            Bcr = T2("Bcr", [128, 2048], F32)
            Bci = T2("Bci", [128, 2048], F32)
            WUf = T2("WUf", [128, 32, 128], F32)
            psT = P2("psT", [128, 128], F32)
            psW = P2("psW", [128, 128], F32)
            S.op("sp", _dma(C8r[:], I["cre8"][:, :]), writes=["C8r"], dma=True)
            S.op("sp", _dma(C8i[:], I["cim8"][:, :]), writes=["C8i"], dma=True)
            S.op("sp", _dma(Bcr[:], I["brec"][:, :]), writes=["Bcr"], dma=True)
            S.op("sp", _dma(Bci[:], I["bimc"][:, :]), writes=["Bci"], dma=True)
            S.op("pool", _memset(WST[:].rearrange("p a b c d e -> p (a b c d e)"), 0.0), writes=["WST"])

            def v4(ap):
                return ap.rearrange("p (g j c) -> p g j c", g=16, j=8)

            def pw4(tbl, d, sl):
                return tbl[:, d * 16:(d + 1) * 16, sl].unsqueeze(3).broadcast_to([128, 16, 8, 16])

            def cmul(ore, oim, ar, ai, br, bi, ka, kb, ko, neg_im=False):
                s4, s5 = v4(tl[4][:]), v4(tl[5][:])
                S.op("dve", _tt(s4, ar, br, ALU.mult), reads=ka + kb, writes=["tl4"])
                S.op("dve", _tt(s5, ai, bi, ALU.mult), reads=ka + kb, writes=["tl5"])
                S.op("dve", _tt(ore, s4, s5, ALU.subtract), reads=["tl4", "tl5"], writes=[ko[0]])
                S.op("dve", _tt(s4, ar, bi, ALU.mult), reads=ka + kb, writes=["tl4"])
                S.op("dve", _tt(s5, ai, br, ALU.mult), reads=ka + kb, writes=["tl5"])
                S.op("dve", _tt(oim, s4, s5, ALU.add), reads=["tl4", "tl5"], writes=[ko[1]])
                if neg_im:
                    S.op("dve", _ts(oim, oim, -1.0, None, ALU.mult), reads=[ko[1]], writes=[ko[1]])

            SL_P1_8 = slice(8, 16)
            SL_8_1 = slice(15, 7, -1)
            SL_0_7 = slice(7, 15)
            SL_0_m7 = slice(7, None, -1)
            SL_7_0 = slice(14, 6, -1)
            kC, kB_, kPW, kPC = ["C8r", "C8i"], ["Bcr", "Bci"], ["PWr", "PWi"], ["PCr", "PCi"]
            for d in range(2):
                sl = SL_P1_8 if d == 0 else SL_8_1
                cmul(v4(tl[0][:]), v4(tl[1][:]), v4(C8r[:]), v4(C8i[:]), pw4(PWr, d, sl), pw4(PWi, d, sl), kC, kPW, ["tl0", "tl1"],
                     neg_im=True)
                S.op("act", _acopy(WX[:, d * 16:(d + 1) * 16, 0, :], tl[0][:].rearrange("p (g m) -> p g m", g=16)), reads=["tl0"],
                     writes=["WX"])
                S.op("act", _acopy(WX[:, d * 16:(d + 1) * 16, 1, :], tl[1][:].rearrange("p (g m) -> p g m", g=16)), reads=["tl1"],
                     writes=["WX"])
                sl = SL_7_0 if d == 0 else SL_0_7
                cmul(v4(tl[0][:]), v4(tl[1][:]), pw4(PCr, d, sl), pw4(PCi, d, sl), v4(Bcr[:]), v4(Bci[:]), kPC, kB_, ["tl0", "tl1"])
                for gp in range(16):
                    for ri in range(2):
                        S.op("pe", _tr(psT[:], tl[ri][:, gp * 128:(gp + 1) * 128], identf[:]), reads=["tl%d" % ri, "identf"],
                             writes=["psT"])
                        S.op("act", _acopy(WST[:, d, gp, ri, 0, 0:64], psT[:, 0:64]), reads=["psT"], writes=["WST"])
                        S.op("dve", _copy(WST[:, d, gp, ri, 1, 64:128], psT[:, 64:128]), reads=["psT"], writes=["WST"])
                slG = SL_0_7 if d == 0 else SL_0_m7
                slH = SL_0_m7 if d == 0 else SL_0_7
                cmul(v4(tl[0][:]), v4(tl[1][:]), v4(C8r[:]), v4(C8i[:]), pw4(PWr, d, slG), pw4(PWi, d, slG), kC, kPW, ["tl0", "tl1"],
                     neg_im=True)
                cmul(v4(tl[2][:]), v4(tl[3][:]), pw4(PCr, d, slH), pw4(PCi, d, slH), v4(Bcr[:]), v4(Bci[:]), kPC, kB_, ["tl2", "tl3"])
                mk = mfb[:, 0:128] if d == 0 else mfb[:, 128:256]
                for gp in range(16):
                    for gl in range(2):
                        g = 2 * gp + gl
                        rs = slice(gl * 64, (gl + 1) * 64)
                        cs = slice(gp * 128, (gp + 1) * 128)
                        S.op("pe", _mm(psW[:], tl[2][rs, cs], tl[0][rs, cs], True, False), reads=["tl2", "tl0"], writes=["psW"])
                        S.op("pe", _mm(psW[:], tl[3][rs, cs], tl[1][rs, cs], False, True), reads=["tl3", "tl1"], writes=["psW"])
                        if d == 0:
                            S.op("dve", _tt(WUf[:, g, :], psW[:], mk, ALU.mult), reads=["psW", "mfb"], writes=["WUf"])
                            S.op("dve", _stt(WUf[:, g, :], identf[:], drow[:, g:g + 1], WUf[:, g, :], ALU.mult, ALU.add),
                                 reads=["identf", "drow", "WUf"], writes=["WUf"])
                        else:
                            S.op("dve", _tt(tl[4][:, 0:128], psW[:], mk, ALU.mult), reads=["psW", "mfb"], writes=["tl4"])
                            S.op("dve", _tt(WU[:, g, :], WUf[:, g, :], tl[4][:, 0:128], ALU.add), reads=["WUf", "tl4"], writes=["WU"])
            S.op("sp", _dma(tl[0][:, 0:1024], I["selC"][:, :]), writes=["tl0"], dma=True)
            S.op("dve", _copy(selC[:].rearrange("p a b -> p (a b)"), tl[0][:, 0:1024]), reads=["tl0"], writes=["selC"])
            S.fence()
            S.emit()

        wglu = T("wglu", [128, 4, 2048], BF16)
        with ExitStack() as es3:
            wst2 = [es3.enter_context(nc.sbuf_tensor("B3_wst%d" % i, [128, 2048], F32)) for i in range(2)]
            for k_ in range(4):
                S.op("sp", _dma(wst2[k_ % 2][:], I["w_glu"][k_ * 128:(k_ + 1) * 128, :]), writes=["wst2_%d" % (k_ % 2)], dma=True)
                S.op("act", _acopy(wglu[:, k_, :], wst2[k_ % 2][:]), reads=["wst2_%d" % (k_ % 2)], writes=["wglu"])
            S.fence()
            S.emit()
        ust = [T("ust%d" % i, [128, 32, BWD], BF16) for i in range(2)]
        gst = T("gst", [128, 32, BWD], BF16)
        bpre = [T("bpre%d" % i, [128, BWD], F32) for i in range(2)]
        bpim = [T("bpim%d" % i, [128, BWD], F32) for i in range(2)]
        mt = [T("mt%d" % i, [128, 128], F32) for i in range(4)]
        mp = [T("mp%d" % i, [128, 128], F32) for i in range(4)]
        Wre = [T("Wre%d" % i, [128, BWD], F32) for i in range(2)]
        Wim = [T("Wim%d" % i, [128, BWD], F32) for i in range(2)]
        XF = [T("XF%d" % i, [128, 2, BWD + 1], BF16) for i in range(2)]
        XBt = [T("XBt%d" % i, [128, 2, BWD], BF16) for i in range(2)]
        ex = [T("ex%d" % i, [128, 32, 16, 8], BF16) for i in range(1)]
        gel = T("gel", [128, 4, 512], BF16)
        gat = T("gat", [128, 8, 512], BF16)
        mao = T("mao", [128, 8, 512], BF16)
        sgb = [T("sgb%d" % i, [128, 512], F32) for i in range(1)]
        yab = [T("yab%d" % i, [128, 512], F32) for i in range(1)]
        sga = [T("sga%d" % i, [128, 512], F32) for i in range(1)]
        psS = [P("psS%d" % i, [128, 512], F32) for i in range(2)]
        psY = [P("psY%d" % i, [128, 512], F32) for i in range(2)]
        psG = P("psG", [128, 4, 128], F32)
        psb = [P("psb%d" % i, [128, 512], F32) for i in range(2)]

        S.op("dve", _memset(STre[:], 0.0), writes=["STre"])
        S.op("dve", _memset(STim[:], 0.0), writes=["STim"])
        S.op("dve", _memset(XFc[:], 0.0), writes=["XFc"])

        cnt = 0
        ucnt = 0
        for d in (1, 0):
            blks = list(reversed(blocks)) if d == 1 else blocks
            for (n0, w) in blks:
                ub = ucnt % 2
                ucnt += 1
                S.op("sp", _dma(ust[ub][:, :, 0:w], UST[:, :, n0:n0 + w]), reads=["UST"], writes=["ust%d" % ub], dma=True)
                segs = [(s0, min(128, w - s0)) for s0 in range(0, w, 128)]
                if d == 1:
                    segs = list(reversed(segs))
                for gp in range(16):
                    dg = d * 16 + gp
                    pb = cnt % 2
                    cnt += 1
                    PB = str(pb)
                    for ri in range(2):
                        S.op("pe", _mm(psS[ri][:, 0:w], WST[:, d, gp, ri, 0, :], ust[ub][:, 2 * gp, 0:w], True, False),
                             reads=["WST", "ust%d" % ub], writes=["psS%d" % ri])
                        S.op("pe", _mm(psS[ri][:, 0:w], WST[:, d, gp, ri, 1, :], ust[ub][:, 2 * gp + 1, 0:w], False, True),
                             reads=["WST", "ust%d" % ub], writes=["psS%d" % ri])
                    rbc = rc8[:, dg:dg + 1]
                    for (s0, sw) in segs:
                        sl = slice(s0, s0 + sw)
                        tsl = slice(128 - sw, 128) if d == 1 else slice(0, sw)
                        cb_, sb_ = COS[:, dg, tsl], SIN[:, dg, tsl]
                        S.op("dve", _tt(mt[0][:, 0:sw], psS[0][:, sl], cb_, ALU.mult), reads=["psS0", "COS"], writes=["mt0"])
                        S.op("dve", _tt(mt[1][:, 0:sw], psS[1][:, sl], sb_, ALU.mult), reads=["psS1", "SIN"], writes=["mt1"])
                        S.op("dve", _tt(bpre[pb][:, sl], mt[0][:, 0:sw], mt[1][:, 0:sw], ALU.add), reads=["mt0", "mt1"], writes=["bpre" + PB])
                        S.op("dve", _tt(mt[2][:, 0:sw], psS[1][:, sl], cb_, ALU.mult), reads=["psS1", "COS"], writes=["mt2"])
                        S.op("dve", _tt(mt[3][:, 0:sw], psS[0][:, sl], sb_, ALU.mult), reads=["psS0", "SIN"], writes=["mt3"])
                        S.op("dve", _tt(bpim[pb][:, sl], mt[2][:, 0:sw], mt[3][:, 0:sw], ALU.subtract), reads=["mt2", "mt3"], writes=["bpim" + PB])
                        wre_s, wim_s = Wre[pb][:, sl], Wim[pb][:, sl]
                        bre_s, bim_s = bpre[pb][:, sl], bpim[pb][:, sl]
                        if d == 1:
                            wre_s, wim_s, bre_s, bim_s = _rev(wre_s), _rev(wim_s), _rev(bre_s), _rev(bim_s)
                        rb_ = rbc.broadcast_to([128, sw])
                        S.op("dve", _scan(wre_s, rb_, bre_s, STre[:, dg:dg + 1]), reads=["rc8", "bpre" + PB, "STre%d" % dg], writes=["Wre" + PB])
                        S.op("dve", _scan(wim_s, rb_, bim_s, STim[:, dg:dg + 1]), reads=["rc8", "bpim" + PB, "STim%d" % dg], writes=["Wim" + PB])
                        last = s0 if d == 1 else s0 + sw - 1
                        tlast = (128 - sw) if d == 1 else (sw - 1)
                        cl = COS[:, dg, tlast:tlast + 1]
                        sl_ = SIN[:, dg, tlast:tlast + 1]
                        wr1 = Wre[pb][:, last:last + 1]
                        wi1 = Wim[pb][:, last:last + 1]
                        S.op("dve", _ts(tmp4[:, 0:1], wi1, sl_, None, ALU.mult), reads=["Wim" + PB, "SIN"], writes=["tmp4a"])
                        S.op("dve", _ts(tmp4[:, 1:2], wi1, cl, None, ALU.mult), reads=["Wim" + PB, "COS"], writes=["tmp4b"])
                        S.op("dve", _stt(STre[:, dg:dg + 1], wr1, cl, tmp4[:, 0:1], ALU.mult, ALU.subtract),
                             reads=["Wre" + PB, "COS", "tmp4a"], writes=["STre%d" % dg])
                        S.op("dve", _stt(STim[:, dg:dg + 1], wr1, sl_, tmp4[:, 1:2], ALU.mult, ALU.add),
                             reads=["Wre" + PB, "SIN", "tmp4b"], writes=["STim%d" % dg])
                        if d == 1:
                            xre_o, xim_o = XBt[pb][:, 0, sl], XBt[pb][:, 1, sl]
                            kxo = "XBt" + PB
                        else:
                            xre_o = XF[pb][:, 0, 1 + s0:1 + s0 + sw]
                            xim_o = XF[pb][:, 1, 1 + s0:1 + s0 + sw]
                            kxo = "XF" + PB
                        S.op("pool", _tt(mp[0][:, 0:sw], Wre[pb][:, sl], cb_, ALU.mult), reads=["Wre" + PB, "COS"], writes=["mp0"])
                        S.op("pool", _tt(mp[1][:, 0:sw], Wim[pb][:, sl], sb_, ALU.mult), reads=["Wim" + PB, "SIN"], writes=["mp1"])
                        S.op("pool", _tt(xre_o, mp[0][:, 0:sw], mp[1][:, 0:sw], ALU.subtract), reads=["mp0", "mp1"], writes=[kxo])
                        S.op("pool", _tt(mp[2][:, 0:sw], Wre[pb][:, sl], sb_, ALU.mult), reads=["Wre" + PB, "SIN"], writes=["mp2"])
                        S.op("pool", _tt(mp[3][:, 0:sw], Wim[pb][:, sl], cb_, ALU.mult), reads=["Wim" + PB, "COS"], writes=["mp3"])
                        S.op("pool", _tt(xim_o, mp[2][:, 0:sw], mp[3][:, 0:sw], ALU.add), reads=["mp2", "mp3"], writes=[kxo])
                    if d == 1:
                        S.op("pool", _dma(XBS[:, gp, :, n0:n0 + w], XBt[pb][:, :, 0:w]), reads=["XBt" + PB], writes=["XBS"], dma=True)
                        continue
                    S.op("act", _acopy(XF[pb][:, :, 0:1], XFc[:, gp, :].unsqueeze(2)), reads=["XFc"], writes=["XF" + PB])
                    S.op("act", _acopy(XFc[:, gp, :].unsqueeze(2), XF[pb][:, :, w:w + 1]), reads=["XF" + PB], writes=["XFc"])
                    wl = w if n0 + w < NCH else w - 1
                    if wl > 0:
                        S.op("sp", _dma(XBt[pb][:, :, 0:wl], XBS[:, gp, :, n0 + 1:n0 + 1 + wl]), reads=["XBS"], writes=["XBt" + PB], dma=True)
                    if wl < w:
                        S.op("pool", _memset(XBt[pb][:, :, wl:w], 0.0), writes=["XBt" + PB])
                    for gl in range(2):
                        g = 2 * gp + gl
                        rs = slice(gl * 64, (gl + 1) * 64)
                        py = psY[g % 2]
                        kp = "psY%d" % (g % 2)
                        S.op("pe", _mm(py[:, 0:w], WU[:, g, :], ust[ub][:, g, 0:w], True, False), reads=["WU", "ust%d" % ub], writes=[kp])
                        S.op("pe", _mm(py[:, 0:w], WX[rs, gp, 0, :], XF[pb][rs, 0, 0:w], False, False), reads=["WX", "XF" + PB], writes=[kp])
                        S.op("pe", _mm(py[:, 0:w], WX[rs, gp, 1, :], XF[pb][rs, 1, 0:w], False, False), reads=["WX", "XF" + PB], writes=[kp])
                        S.op("pe", _mm(py[:, 0:w], WX[rs, 16 + gp, 0, :], XBt[pb][rs, 0, 0:w], False, False), reads=["WX", "XBt" + PB], writes=[kp])
                        S.op("pe", _mm(py[:, 0:w], WX[rs, 16 + gp, 1, :], XBt[pb][rs, 1, 0:w], False, True), reads=["WX", "XBt" + PB], writes=[kp])
                        S.op("act", _act(gst[:, g, 0:w], py[:, 0:w], AF.Gelu), reads=[kp], writes=["gst"])
                if d == 1:
                    continue
                for s_ in range(w // 16):
                    tok0 = 8 * n0 + 128 * s_
                    st = tok0 // 512
                    sub = (tok0 % 512) // 128
                    if st == 0:
                        continue
                    eb = 0
                    S.op("dve", _tt(ex[eb][:], gst[:, :, s_ * 16:(s_ + 1) * 16].unsqueeze(3).broadcast_to([128, 32, 16, 8]),
                                    maskJb[:].unsqueeze(1).unsqueeze(1).broadcast_to([128, 32, 16, 8]), ALU.mult),
                         reads=["gst", "maskJb"], writes=["ex%d" % eb])
                    for q in range(4):
                        for g8 in range(8):
                            S.op("pe", _mm(psG[:, q, :], selC[:, g8, :], ex[eb][:, 8 * q + g8, :, :].rearrange("p n j -> p (n j)"),
                                           g8 == 0, g8 == 7), reads=["selC", "ex%d" % eb], writes=["psG"])
                    S.op("act", _acopy(gel[:, :, sub * 128:(sub + 1) * 128], psG[:]), reads=["psG"], writes=["gel"])
                    if sub != 3:
                        continue
                    c0 = st * 512
                    S.op("sp", _dma(gat[:], Z[3072:4096, c0:c0 + 512].rearrange("(q p) t -> p q t", p=128)),
                         reads=["Z"], writes=["gat"], dma=True)
                    for c in range(8):
                        pb2 = 0
                        pa_, pg_ = psb[0], psb[1]
                        for k_ in range(4):
                            S.op("pe", _mm(pa_[:], wglu[:, k_, c * 128:(c + 1) * 128], gel[:, k_, :], k_ == 0, k_ == 3),
                                 reads=["wglu", "gel"], writes=["psb0"])
                        for k_ in range(4):
                            S.op("pe", _mm(pg_[:], wglu[:, k_, 1024 + c * 128:1024 + (c + 1) * 128], gel[:, k_, :], k_ == 0, k_ == 3),
                                 reads=["wglu", "gel"], writes=["psb1"])
                        S.op("act", _act(sgb[pb2][:], pg_[:], AF.Sigmoid), reads=["psb1"], writes=["sgb%d" % pb2])
                        S.op("act", _act(sga[pb2][:], gat[:, c, :], AF.Sigmoid), reads=["gat"], writes=["sga%d" % pb2])
                        S.op("dve", _tt(yab[pb2][:], pa_[:], sgb[pb2][:], ALU.mult), reads=["psb0", "sgb%d" % pb2], writes=["yab%d" % pb2])
                        S.op("dve", _tt(mao[:, c, :], yab[pb2][:], sga[pb2][:], ALU.mult), reads=["yab%d" % pb2, "sga%d" % pb2],
                             writes=["mao"])
                    S.op("pool", _dma(MA[:, c0:c0 + 512].rearrange("(q p) t -> p q t", p=128), mao[:]),
                         reads=["mao"], writes=["MA"], dma=True)
        S.fence()
        S.emit()

    with ExitStack() as es:
        def T(name, shape, dt):
            return es.enter_context(nc.sbuf_tensor("C_" + name, shape, dt))

        def P(name, shape, dt):
            return es.enter_context(nc.psum_tensor("C_" + name, shape, dt))

        whg = T("whg", [128, 4, D], BF16)
        wout = T("wout", [128, 8, D], BF16)
        wst = T("wst", [128, D], F32)
        masks = T("masks", [64, 128], F32)
        identf = T("identf", [128, 128], F32)
        identb = T("identb", [128, 128], BF16)
        onesb = T("onesb", [128, 128], BF16)
        lb = T("lb", [128, 4], F32)
        oml = T("oml", [128, 4], F32)
        lb1 = T("lb1", [128, 4], F32)
        hgn = T("hgn", [128, 4], F32)
        Sst = [T("Sst%d" % h, [128, 128], F32) for h in range(4)]
        zeros = T("zeros", [128, 64], F32)
        Sall = T("Sall", [128, 1024], F32)
        kvs = T("kvs", [128, 1024], F32)
        decz = T("decz", [128, 8], F32)
        dzf = T("dzf", [128, 1024], F32)
        NB = 2
        zq = [T("zq%d" % i, [128, 512], BF16) for i in range(NB)]
        zf = [T("zf%d" % i, [128, 512], BF16) for i in range(NB)]
        zo = [T("zo%d" % i, [128, 512], BF16) for i in range(NB)]
        Vc = [T("Vc%d" % i, [64, 8, 128], BF16) for i in range(NB)]
        qs = [T("qs%d" % i, [128, 512], F32) for i in range(NB)]
        gs = [T("gs%d" % i, [128, 512], F32) for i in range(NB)]
        ks = [T("ks%d" % i, [128, 512], F32) for i in range(NB)]
        Pc = [T("Pc%d" % i, [128, 512], F32) for i in range(NB)]
        rP = [T("rP%d" % i, [128, 512], F32) for i in range(NB)]
        dec = [T("dec%d" % i, [128, 8], F32) for i in range(NB)]
        qx = [T("qx%d" % i, [128, 512], BF16) for i in range(NB)]
        kx = [T("kx%d" % i, [128, 512], BF16) for i in range(NB)]
        ke = [T("ke%d" % i, [128, 512], BF16) for i in range(NB)]
        qi = [T("qi%d" % i, [128, 512], BF16) for i in range(NB)]
        scm = [T("scm%d" % i, [64, 8, 64], BF16) for i in range(NB)]
        ketm = [T("ketm%d" % i, [64, 8, 128], BF16) for i in range(NB)]
        Sbf = [T("Sbf%d" % i, [128, 8, 128], BF16) for i in range(NB)]
        osb = [T("osb%d" % i, [128, 512], F32) for i in range(NB)]
        obl = [T("obl%d" % i, [128, 512], F32) for i in range(NB)]
        sq = T("sq", [128, 512], BF16)
        rn = T("rn", [128, 512], F32)
        sgo = T("sgo", [128, 512], F32)
        yh = T("yh", [128, 4, 512], BF16)
        gbt = T("gbt", [128, 8, 512], BF16)
        mal = T("mal", [128, 8, 512], BF16)
        sgb = T("sgb", [128, 512], F32)
        tmpc = T("tmpc", [128, 512], F32)
        mixT = T("mixT", [128, 8, 512], BF16)
        xt = [T("xt%d" % i, [128, D], F32) for i in range(2)]
        hsb = [T("hsb%d" % i, [128, D], F32) for i in range(2)]
        psS = P("psS", [64, 8, 64], F32)
        psK = P("psK", [64, 8, 128], BF16)
        psKV = P("psKV", [128, 8, 128], F32)
        psO = P("psO", [128, 512], F32)
        psN = P("psN", [128, 512], F32)
        psY = [P("psY%d" % i, [128, 512], F32) for i in range(2)]

        S.op("sp", _dma(identf[:], I["identf"][:, :]), writes=["identf"], dma=True)
        S.op("dve", _copy(identb[:], identf[:]), reads=["identf"], writes=["identb"])
        S.op("dve", _memset(onesb[:], 1.0), writes=["onesb"])
        S.op("dve", _memset(zeros[:], 0.0), writes=["zeros"])
        S.op("sp", _dma(masks[:], I["masks"][:, :]), writes=["masks"], dma=True)
        S.op("sp", _dma(lb[:], I["lb0"][:, :]), writes=["lb"], dma=True)
        S.op("sp", _dma(lb1[:], I["lb1"][:, :]), writes=["lb1"], dma=True)
        S.op("sp", _dma(hgn[:], I["hgn"][:, :]), writes=["hgn"], dma=True)
        S.op("dve", _tt(lb[:], lb[:], lb1[:], ALU.subtract), reads=["lb", "lb1"], writes=["lb"])
        S.op("act", _act(lb[:], lb[:], AF.Sigmoid), reads=["lb"], writes=["lb"])
        S.op("dve", _ts(oml[:], lb[:], -1.0, 1.0, ALU.mult, ALU.add), reads=["lb"], writes=["oml"])
        for k in range(4):
            S.op("sp", _dma(wst[:], I["w_hg"][k * 128:(k + 1) * 128, :]), writes=["wst"], dma=True)
            S.op("act", _acopy(whg[:, k, :], wst[:]), reads=["wst"], writes=["whg"])
        for k in range(8):
            S.op("sp", _dma(wst[:], I["w_out"][k * 128:(k + 1) * 128, :]), writes=["wst"], dma=True)
            S.op("act", _acopy(wout[:, k, :], wst[:]), reads=["wst"], writes=["wout"])

        cnt = 0
        xcnt = [0]
        items = []
        for d in (1, 0):
            sts = list(range(NT - 1, 0, -1)) if d == 1 else list(range(NT))
            for si, st in enumerate(sts):
                for h in range(4):
                    items.append(dict(d=d, st=st, h=h, bi=cnt % NB, first=(si == 0)))
                    cnt += 1

        def prep(it):
            d, st, h, bi = it["d"], it["st"], it["h"], it["bi"]
            B = str(bi)
            c0 = st * 512
            full = (d == 0 and st >= 1)
            frow = 1536 if d == 1 else 1024
            S.op("sp", _dma(zq[bi][:], Z[512 + h * 128:512 + (h + 1) * 128, c0:c0 + 512]), reads=["Z"],
                 writes=["zq" + B], dma=True)
            S.op("sp", _dma(zf[bi][:], Z[frow + h * 128:frow + (h + 1) * 128, c0:c0 + 512]), reads=["Z"],
                 writes=["zf" + B], dma=True)
            S.op("sp", _dma(Vc[bi][:], VTM[c0:c0 + 512, h * 128:(h + 1) * 128].rearrange("(c s) v -> s c v", s=64)),
                 reads=["VTM"], writes=["Vc" + B], dma=True)
            if full:
                S.op("sp", _dma(zo[bi][:], Z[2560 + h * 128:2560 + (h + 1) * 128, c0:c0 + 512]), reads=["Z"],
                     writes=["zo" + B], dma=True)
                S.op("sp", _dma(obl[bi][:], OB[h * 128:(h + 1) * 128, c0:c0 + 512]), reads=["OB"],
                     writes=["obl" + B], dma=True)
            qs_, gs_, ks_, Pc_, rP_, dec_ = qs[bi], gs[bi], ks[bi], Pc[bi], rP[bi], dec[bi]
            S.op("act", _act(qs_[:], zq[bi][:], AF.Silu), reads=["zq" + B], writes=["qs" + B])
            S.op("act", _act(gs_[:], zf[bi][:], AF.Sigmoid), reads=["zf" + B], writes=["gs" + B])
            S.op("dve", _ts(gs_[:], gs_[:], oml[:, h:h + 1], lb[:, h:h + 1], ALU.mult, ALU.add),
                 reads=["gs" + B, "oml", "lb"], writes=["gs" + B])
            S.op("dve", _ts(ks_[:], gs_[:], -1.0, 1.0, ALU.mult, ALU.add), reads=["gs" + B], writes=["ks" + B])
            Pc3 = Pc_[:].rearrange("p (c j) -> p c j", j=64)
            gs3 = gs_[:].rearrange("p (c j) -> p c j", j=64)
            if d == 0:
                for ch in range(8):
                    sl = slice(ch * 64, (ch + 1) * 64)
                    S.op("dve", _scan(Pc_[:, sl], gs_[:, sl], zeros[:, 0:64], 1.0), reads=["gs" + B, "zeros"], writes=["Pc" + B])
                S.op("dve", _tt(qx[bi][:], qs_[:], Pc_[:], ALU.mult), reads=["qs" + B, "Pc" + B], writes=["qx" + B])
                S.op("dve", _recip(rP_[:], Pc_[:]), reads=["Pc" + B], writes=["rP" + B])
                S.op("dve", _tt(kx[bi][:], ks_[:], rP_[:], ALU.mult), reads=["ks" + B, "rP" + B], writes=["kx" + B])
                S.op("dve", _copy(dec_[:].unsqueeze(2), Pc3[:, :, 63:64]), reads=["Pc" + B], writes=["dec" + B])
                S.op("dve", _tt(ke[bi][:].rearrange("p (c j) -> p c j", j=64),
                                kx[bi][:].rearrange("p (c j) -> p c j", j=64),
                                dec_[:].unsqueeze(2).broadcast_to([128, 8, 64]), ALU.mult),
                     reads=["kx" + B, "dec" + B], writes=["ke" + B])
            else:
                S.op("dve", _memset(Pc3[:, :, 0:1], 1.0), writes=["Pc" + B])
                for ch in range(8):
                    S.op("dve", _scan(Pc_[:, ch * 64 + 1:(ch + 1) * 64], gs_[:, ch * 64:(ch + 1) * 64 - 1],
                                      zeros[:, 0:63], 1.0), reads=["gs" + B, "zeros"], writes=["Pc" + B])
                S.op("dve", _recip(rP_[:], Pc_[:]), reads=["Pc" + B], writes=["rP" + B])
                S.op("dve", _tt(qx[bi][:], qs_[:], rP_[:], ALU.mult), reads=["qs" + B, "rP" + B], writes=["qx" + B])
                S.op("dve", _tt(kx[bi][:], ks_[:], Pc_[:], ALU.mult), reads=["ks" + B, "Pc" + B], writes=["kx" + B])
                S.op("dve", _tt(dec_[:].unsqueeze(2), Pc3[:, :, 63:64], gs3[:, :, 63:64], ALU.mult),
                     reads=["Pc" + B, "gs" + B], writes=["dec" + B])
                S.op("dve", _tt(qi[bi][:].rearrange("p (c j) -> p c j", j=64),
                                qx[bi][:].rearrange("p (c j) -> p c j", j=64),
                                dec_[:].unsqueeze(2).broadcast_to([128, 8, 64]), ALU.mult),
                     reads=["qx" + B, "dec" + B], writes=["qi" + B])

        def rest(it):
            d, st, h, bi = it["d"], it["st"], it["h"], it["bi"]
            B = str(bi)
            c0 = st * 512
            full = (d == 0 and st >= 1)
            mk = masks[:, 64:128] if d == 1 else masks[:, 0:64]
            mkb = mk.unsqueeze(1).broadcast_to([64, 8, 64])
            dec_ = dec[bi]
            if it["first"]:
                S.op("dve", _memset(Sst[h][:], 0.0), writes=["Sst%d" % h])
            if full and h == 0:
                S.op("sp", _dma(gbt[:], Z[4096:5120, c0:c0 + 512].rearrange("(q p) t -> p q t", p=128)),
                     reads=["Z"], writes=["gbt"], dma=True)
                S.op("sp", _dma(mal[:], MA[:, c0:c0 + 512].rearrange("(q p) t -> p q t", p=128)),
                     reads=["MA"], writes=["mal"], dma=True)
            if d == 0:
                qiT, keT = qx[bi], ke[bi]
                kq, kk_ = "qx" + B, "ke" + B
            else:
                qiT, keT = qi[bi], kx[bi]
                kq, kk_ = "qi" + B, "kx" + B
            for ch in range(8):
                sl = slice(ch * 64, (ch + 1) * 64)
                S.op("pe", _mm(psS[:, ch, :], kx[bi][:, sl], qx[bi][:, sl], True, True),
                     reads=["kx" + B, "qx" + B], writes=["psS"])
            for ch in range(8):
                sl = slice(ch * 64, (ch + 1) * 64)
                S.op("pe", _tr(psK[:, ch, :], keT[:, sl], identb[:]), reads=[kk_, "identb"], writes=["psK"])
            S.op("dve", _tt(scm[bi][:], psS[:], mkb, ALU.mult), reads=["psS", "masks"], writes=["scm" + B])
            S.op("act", _acopy(ketm[bi][:], psK[:]), reads=["psK"], writes=["ketm" + B])
            for ch in range(8):
                S.op("pe", _mm(psKV[:, ch, :], ketm[bi][:, ch, :], Vc[bi][:, ch, :], True, True),
                     reads=["ketm" + B, "Vc" + B], writes=["psKV"])
            kvv = psKV[:].rearrange("p c v -> p v c")
            sal = Sall[:].rearrange("p (v c) -> p v c", c=8)
            c_in = 7 if d == 1 else 0
            S.op("dve", _copy(decz[:], dec_[:]), reads=["dec" + B], writes=["decz"])
            S.op("dve", _memset(decz[:, c_in:c_in + 1], 0.0), writes=["decz"])
            S.op("dve", _copy(kvs[:].rearrange("p (v c) -> p v c", c=8), kvv), reads=["psKV"], writes=["kvs"])
            S.op("dve", _stt(kvs[:].rearrange("p (v c) -> p v c", c=8)[:, :, c_in], Sst[h][:], dec_[:, c_in:c_in + 1],
                             kvs[:].rearrange("p (v c) -> p v c", c=8)[:, :, c_in], ALU.mult, ALU.add),
                 reads=["Sst%d" % h, "dec" + B, "kvs"], writes=["kvs"])
            S.op("dve", _copy(dzf[:].rearrange("p (v c) -> p v c", c=8), decz[:].unsqueeze(1).broadcast_to([128, 128, 8])),
                 reads=["decz"], writes=["dzf"])
            if d == 0:
                S.op("dve", _scan(Sall[:], dzf[:], kvs[:], 0.0), reads=["dzf", "kvs"], writes=["Sall"])
            else:
                S.op("dve", _scan(_rev(Sall[:]), _rev(dzf[:]), _rev(kvs[:]), 0.0), reads=["dzf", "kvs"], writes=["Sall"])
            salc = Sall[:].rearrange("p (v c) -> p c v", c=8)
            if d == 0:
                S.op("act", _acopy(Sbf[bi][:, 0, :], Sst[h][:]), reads=["Sst%d" % h], writes=["Sbf" + B])
                S.op("act", _acopy(Sbf[bi][:, 1:8, :], salc[:, 0:7, :]), reads=["Sall"], writes=["Sbf" + B])
                S.op("dve", _copy(Sst[h][:], salc[:, 7, :]), reads=["Sall"], writes=["Sst%d" % h])
            else:
                S.op("act", _acopy(Sbf[bi][:, 7, :], Sst[h][:]), reads=["Sst%d" % h], writes=["Sbf" + B])
                S.op("act", _acopy(Sbf[bi][:, 0:7, :], salc[:, 1:8, :]), reads=["Sall"], writes=["Sbf" + B])
                S.op("dve", _copy(Sst[h][:], salc[:, 0, :]), reads=["Sall"], writes=["Sst%d" % h])
            if d == 0 and st == 0:
                return
            for ch in range(8):
                sl = slice(ch * 64, (ch + 1) * 64)
                S.op("pe", _mm(psO[:, sl], Vc[bi][:, ch, :], scm[bi][:, ch, :], True, False),
                     reads=["Vc" + B, "scm" + B], writes=["psO"])
                S.op("pe", _mm(psO[:, sl], Sbf[bi][:, ch, :], qiT[:, sl], False, True),
                     reads=["Sbf" + B, kq], writes=["psO"])
            if d == 1:
                S.op("act", _acopy(osb[bi][:], psO[:]), reads=["psO"], writes=["osb" + B])
                S.op("pool", _dma(OB[h * 128:(h + 1) * 128, c0:c0 + 512], osb[bi][:]), reads=["osb" + B],
                     writes=["OB"], dma=True)
                return
            S.op("dve", _tt(osb[bi][:], psO[:], obl[bi][:], ALU.add), reads=["psO", "obl" + B], writes=["osb" + B])
            S.op("act", _act(sq[:], osb[bi][:], AF.Square), reads=["osb" + B], writes=["sq"])
            S.op("pe", _mm(psN[:], onesb[:], sq[:], True, True), reads=["onesb", "sq"], writes=["psN"])
            S.op("dve", _ts(rn[:], psN[:], 1.0 / 128, EPS, ALU.mult, ALU.add), reads=["psN"], writes=["rn"])
            S.op("act", _act(rn[:], rn[:], AF.Sqrt), reads=["rn"], writes=["rn"])
            S.op("dve", _recip(rn[:], rn[:]), reads=["rn"], writes=["rn"])
            S.op("dve", _tt(rn[:], rn[:], osb[bi][:], ALU.mult), reads=["rn", "osb" + B], writes=["rn"])
            S.op("act", _act(sgo[:], zo[bi][:], AF.Silu), reads=["zo" + B], writes=["sgo"])
            S.op("dve", _stt(yh[:, h, :], rn[:], hgn[:, h:h + 1], sgo[:], ALU.mult, ALU.mult),
                 reads=["rn", "hgn", "sgo"], writes=["yh"])
            if h != 3:
                return
            for c in range(8):
                py = psY[c % 2]
                kp = "psY%d" % (c % 2)
                for k in range(4):
                    S.op("pe", _mm(py[:], whg[:, k, c * 128:(c + 1) * 128], yh[:, k, :], k == 0, k == 3),
                         reads=["whg", "yh"], writes=[kp])
                S.op("act", _act(sgb[:], gbt[:, c, :], AF.Sigmoid), reads=["gbt"], writes=["sgb"])
                S.op("dve", _tt(tmpc[:], py[:], sgb[:], ALU.mult), reads=[kp, "sgb"], writes=["tmpc"])
                S.op("dve", _tt(mixT[:, c, :], tmpc[:], mal[:, c, :], ALU.add), reads=["tmpc", "mal"], writes=["mixT"])
            for j in range(4):
                xb = xcnt[0] % 2
                xcnt[0] += 1
                r0 = (st - 1) * 512 + j * 128
                S.op("sp", _dma(xt[xb][:], I["xs"][r0:r0 + 128, :]), writes=["xt%d" % xb], dma=True)
                for hf in range(2):
                    py = psY[hf]
                    kp = "psY%d" % hf
                    for k in range(8):
                        S.op("pe", _mm(py[:], mixT[:, k, j * 128:(j + 1) * 128], wout[:, k, hf * 512:(hf + 1) * 512],
                                       k == 0, k == 7), reads=["mixT", "wout"], writes=[kp])
                    S.op("dve", _tt(hsb[xb][:, hf * 512:(hf + 1) * 512], py[:], xt[xb][:, hf * 512:(hf + 1) * 512], ALU.add),
                         reads=[kp, "xt%d" % xb], writes=["hsb%d" % xb])
                S.op("pool", _dma(HS1[r0:r0 + 128, :], hsb[xb][:]), reads=["hsb%d" % xb], writes=["HS1"], dma=True)

        conv = list(range(128))
        cpos = [0]

        def conv_step():
            if cpos[0] >= len(conv):
                return
            r = conv[cpos[0]]
            cpos[0] += 1
            S.op("pool", _dma(UVB[r * 128:(r + 1) * 128, :], I["puv"][r * 128:(r + 1) * 128, :]), writes=["UVB"], dma=True)

        per_it = 1
        prep(items[0])
        for ii, it in enumerate(items):
            if ii + 1 < len(items):
                prep(items[ii + 1])
            rest(it)
            for _ in range(per_it):
                conv_step()
        while cpos[0] < len(conv):
            conv_step()
        S.fence()
        S.emit()

    with ExitStack() as es:
        def T(name, shape, dt):
            return es.enter_context(nc.sbuf_tensor("D_" + name, shape, dt))

        def P(name, shape, dt):
            return es.enter_context(nc.psum_tensor("D_" + name, shape, dt))

        wq = T("wq", [128, 8, 2048], BF16)
        keysT = T("keysT", [128, 16, 128], BF16)
        g2b = T("g2b", [128, D], F32)
        gfb = T("gfb", [128, D], F32)
        identf = T("identf", [128, 128], F32)
        identb = T("identb", [128, 128], BF16)
        iota16 = T("iota16", [128, 16], F32)
        tk = T("tk", [128, ND], I32)
        hs = [T("hs%d" % i, [128, D], F32) for i in range(2)]
        h2 = [T("h2%d" % i, [128, D], F32) for i in range(2)]
        ei = [T("ei%d" % i, [128, 128], I32) for i in range(2)]
        gate = [T("gate%d" % i, [128, 8, 16], F32) for i in range(2)]
        h2T = T("h2T", [128, 8, 128], BF16)
        qT = T("qT", [128, 16, 128], BF16)
        sc = T("sc", [128, 16, 128], F32)
        sc2 = T("sc2", [128, 16, 128], F32)
        top = T("top", [128, 8, 2, 16], F32)
        tix = T("tix", [128, 8, 2, 16], U32)
        tixf = T("tixf", [128, 8, 2, 16], F32)
        cand = T("cand", [128, 8, 256], F32)
        cand2 = T("cand2", [128, 8, 256], F32)
        ctop = T("ctop", [128, 8, 16], F32)
        cix = T("cix", [128, 8, 16], U32)
        cab = T("cab", [128, 8, 16], U32)
        caf = T("caf", [128, 8, 16], F32)
        cbf = T("cbf", [128, 8, 16], F32)
        eq = T("eq", [128, 8, 16, 16], F32)
        i1f = T("i1f", [128, 8, 16], F32)
        i2f = T("i2f", [128, 8, 16], F32)
        ssum = T("ssum", [128, 8], F32)
        sm = T("sm", [128, 8], F32)
        h2bf = [T("h2bf%d" % i, [128, D], BF16) for i in range(2)]
        actv = T("actv", [128, 128], F32)
        wgt = T("wgt", [128, 128], F32)
        dg = [T("dg%d" % i, [128, 8, 128], BF16) for i in range(2)]
        acc = T("acc", [128, D], F32)
        junkb = T("junkb", [128, D], BF16)
        junka = T("junka", [128, D], BF16)
        prod = [T("prod%d" % i, [128, D], BF16) for i in range(4)]
        outb = T("outb", [128, D], F32)
        pT = P("pT", [128, 8, 128], BF16)
        pQ = P("pQ", [128, 16, 128], F32)
        pacc = P("pacc", [128, D], F32)

        S.op("sp", _dma(identf[:], I["identf"][:, :]), writes=["identf"], dma=True)
        S.op("dve", _copy(identb[:], identf[:]), reads=["identf"], writes=["identb"])
        S.op("sp", _dma(iota16[:], I["iota16"][:, :]), writes=["iota16"], dma=True)
        S.op("sp", _dma(tk[:], I["tokd"][:, :]), writes=["tk"], dma=True)
        S.op("sp", _dma(g2b[:], I["g2b"][:, :]), writes=["g2b"], dma=True)
        S.op("sp", _dma(gfb[:], I["gfb"][:, :]), writes=["gfb"], dma=True)
        with ExitStack() as esp:
            wst = esp.enter_context(nc.sbuf_tensor("Dp_wst", [128, 2048], F32))
            for k in range(8):
                S.op("sp", _dma(wst[:], I["wq"][k * 128:(k + 1) * 128, :]), writes=["wst"], dma=True)
                S.op("act", _acopy(wq[:, k, :], wst[:]), reads=["wst"], writes=["wq"])
            S.op("sp", _dma(wst[:], I["keysT"][:, :]), writes=["wst"], dma=True)
            S.op("act", _acopy(keysT[:].rearrange("p a b -> p (a b)"), wst[:]), reads=["wst"], writes=["keysT"])
            S.fence()
            S.emit()
        NSB = 15
        uvg = [T("uvg%d" % i, [128, 2 * D], BF16) for i in range(NSB)]

        def stage1(i):
            b = i % 2
            B = str(b)
            S.op("pool", _gather(hs[b][:], HS1[:, :], tk[:, i:i + 1]), reads=["tk", "HS1"], writes=["hs" + B], dma=True)
            S.op("dve", _memset(ssum[:, 0:1], 0.0), writes=["ssum"])
            S.op("dve", _stt(junkb[:], hs[b][:], 1.0, hs[b][:], ALU.mult, ALU.mult, accum_out=ssum[:, 0:1]),
                 reads=["hs" + B], writes=["junkb", "ssum"])
            S.op("dve", _ts(sm[:, 0:1], ssum[:, 0:1], 1.0 / D, EPS, ALU.mult, ALU.add), reads=["ssum"], writes=["sm"])
            S.op("act", _act(sm[:, 0:1], sm[:, 0:1], AF.Sqrt), reads=["sm"], writes=["sm"])
            S.op("dve", _recip(sm[:, 0:1], sm[:, 0:1]), reads=["sm"], writes=["sm"])
            S.op("dve", _stt(h2[b][:], hs[b][:], sm[:, 0:1], g2b[:], ALU.mult, ALU.mult), reads=["hs" + B, "sm", "g2b"],
                 writes=["h2" + B])
            S.op("act", _acopy(h2bf[b][:], h2[b][:]), reads=["h2" + B], writes=["h2bf" + B])
            for k in range(8):
                S.op("pe", _tr(pT[:, k, :], h2bf[b][:, k * 128:(k + 1) * 128], identb[:]), reads=["h2bf" + B, "identb"], writes=["pT"])
            S.op("act", _acopy(h2T[:], pT[:]), reads=["pT"], writes=["h2T"])
            for hp in range(16):
                for k in range(8):
                    S.op("pe", _mm(pQ[:, hp, :], wq[:, k, hp * 128:(hp + 1) * 128], h2T[:, k, :], k == 0, k == 7),
                         reads=["wq", "h2T"], writes=["pQ"])
            S.op("act", _acopy(qT[:], pQ[:]), reads=["pQ"], writes=["qT"])
            for hp in range(16):
                S.op("pe", _mm(pQ[:, hp, :], qT[:, hp, :], keysT[:, hp, :], True, True), reads=["qT", "keysT"], writes=["pQ"])
            S.op("act", _acopy(sc[:], pQ[:]), reads=["pQ"], writes=["sc"])
            HP = [(hp, hp // 2, hp % 2) for hp in range(16)]
            for hp, h_, p_ in HP:
                S.op("dve", lambda e, h_=h_, p_=p_, hp=hp: e.max(out=top[:, h_, p_, 0:8], in_=sc[:, hp, :]),
                     reads=["sc"], writes=["top%d" % hp])
            for hp, h_, p_ in HP:
                S.op("dve", lambda e, h_=h_, p_=p_, hp=hp: e.max_index(out=tix[:, h_, p_, 0:8], in_max=top[:, h_, p_, 0:8],
                                                                       in_values=sc[:, hp, :]),
                     reads=["sc", "top%d" % hp], writes=["tix%d" % hp])
            for hp, h_, p_ in HP:
                S.op("dve", lambda e, h_=h_, p_=p_, hp=hp: e.match_replace(out=sc2[:, hp, :], in_to_replace=top[:, h_, p_, 0:8],
                                                                           in_values=sc[:, hp, :], imm_value=-1e30),
                     reads=["sc", "top%d" % hp], writes=["sc2_%d" % hp])
            for hp, h_, p_ in HP:
                S.op("dve", lambda e, h_=h_, p_=p_, hp=hp: e.max(out=top[:, h_, p_, 8:16], in_=sc2[:, hp, :]),
                     reads=["sc2_%d" % hp], writes=["topb%d" % hp])
            for hp, h_, p_ in HP:
                S.op("dve", lambda e, h_=h_, p_=p_, hp=hp: e.max_index(out=tix[:, h_, p_, 8:16], in_max=top[:, h_, p_, 8:16],
                                                                       in_values=sc2[:, hp, :]),
                     reads=["sc2_%d" % hp, "topb%d" % hp], writes=["tixb%d" % hp])
            tk_all = ["top%d" % x for x in range(16)] + ["topb%d" % x for x in range(16)]
            ti_all = ["tix%d" % x for x in range(16)] + ["tixb%d" % x for x in range(16)]
            S.op("dve", _copy(tixf[:], tix[:]), reads=ti_all, writes=["tixf"])
            cand4 = cand[:].rearrange("p h (a b) -> p h a b", a=16)
            S.op("dve", _tt(cand4, top[:, :, 0, :].unsqueeze(3).broadcast_to([128, 8, 16, 16]),
                            top[:, :, 1, :].unsqueeze(2).broadcast_to([128, 8, 16, 16]), ALU.add),
                 reads=tk_all, writes=["cand"])
            for h_ in range(8):
                S.op("dve", lambda e, h_=h_: e.max(out=ctop[:, h_, 0:8], in_=cand[:, h_, :]), reads=["cand"], writes=["ctop%d" % h_])
            for h_ in range(8):
                S.op("dve", lambda e, h_=h_: e.max_index(out=cix[:, h_, 0:8], in_max=ctop[:, h_, 0:8], in_values=cand[:, h_, :]),
                     reads=["cand", "ctop%d" % h_], writes=["cix%d" % h_])
            for h_ in range(8):
                S.op("dve", lambda e, h_=h_: e.match_replace(out=cand2[:, h_, :], in_to_replace=ctop[:, h_, 0:8],
                                                             in_values=cand[:, h_, :], imm_value=-1e30),
                     reads=["cand", "ctop%d" % h_], writes=["cand2_%d" % h_])
            for h_ in range(8):
                S.op("dve", lambda e, h_=h_: e.max(out=ctop[:, h_, 8:16], in_=cand2[:, h_, :]), reads=["cand2_%d" % h_],
                     writes=["ctopb%d" % h_])
            for h_ in range(8):
                S.op("dve", lambda e, h_=h_: e.max_index(out=cix[:, h_, 8:16], in_max=ctop[:, h_, 8:16], in_values=cand2[:, h_, :]),
                     reads=["cand2_%d" % h_, "ctopb%d" % h_], writes=["cixb%d" % h_])
            S.op("dve", lambda e: e.tensor_single_scalar(out=cab[:], in_=cix[:], scalar=4, op=ALU.logical_shift_right),
                 reads=["cix%d" % x for x in range(8)] + ["cixb%d" % x for x in range(8)], writes=["cab"])
            S.op("dve", _copy(caf[:], cab[:]), reads=["cab"], writes=["caf"])
            S.op("dve", lambda e: e.tensor_single_scalar(out=cab[:], in_=cix[:], scalar=15, op=ALU.bitwise_and),
                 reads=["cix%d" % x for x in range(8)] + ["cixb%d" % x for x in range(8)], writes=["cab"])
            S.op("dve", _copy(cbf[:], cab[:]), reads=["cab"], writes=["cbf"])
            io4 = iota16[:, :].unsqueeze(1).unsqueeze(1).broadcast_to([128, 8, 16, 16])
            for (src, half, dst, kd) in ((caf, 0, i1f, "i1f"), (cbf, 1, i2f, "i2f")):
                S.op("dve", _tt(eq[:], src[:].unsqueeze(3).broadcast_to([128, 8, 16, 16]), io4, ALU.is_equal),
                     reads=["caf", "cbf", "iota16"], writes=["eq"])
                S.op("dve", _tt(eq[:], eq[:], tixf[:, :, half, :].unsqueeze(2).broadcast_to([128, 8, 16, 16]), ALU.mult),
                     reads=["eq", "tixf"], writes=["eq"])
                S.op("dve", _rsum(dst[:], eq[:]), reads=["eq"], writes=[kd])
            S.op("dve", _stt(i1f[:], i1f[:], 128.0, i2f[:], ALU.mult, ALU.add), reads=["i1f", "i2f"], writes=["i1f"])
            S.op("dve", _ts(i1f[:], i1f[:], 0.0, 16383.0, ALU.max, ALU.min), reads=["i1f"], writes=["i1f"])
            S.op("dve", _copy(ei[b][:].rearrange("p (h j) -> p h j", h=8), i1f[:]), reads=["i1f"], writes=["ei" + B])
            S.op("dve", _tt(gate[b][:], ctop[:], ctop[:, :, 0:1].broadcast_to([128, 8, 16]), ALU.subtract),
                 reads=["ctop%d" % x for x in range(8)] + ["ctopb%d" % x for x in range(8)], writes=["gate" + B])
            S.op("act", _act(gate[b][:], gate[b][:], AF.Exp), reads=["gate" + B], writes=["gate" + B])
            S.op("dve", _rsum(ssum[:], gate[b][:]), reads=["gate" + B], writes=["ssum"])
            S.op("dve", _recip(ssum[:], ssum[:]), reads=["ssum"], writes=["ssum"])
            S.op("dve", _tt(gate[b][:], gate[b][:], ssum[:].unsqueeze(2).broadcast_to([128, 8, 16]), ALU.mult),
                 reads=["gate" + B, "ssum"], writes=["gate" + B])

        gcnt = [0]
        pcnt = [0]

        def stage2(i, pend=()):
            b = i % 2
            B = str(b)
            pend = list(pend)
            per_slot = -(-len(pend) // 112) if pend else 0
            ppos = [0]

            def drain(n):
                for o in pend[ppos[0]:ppos[0] + n]:
                    S.op(*o)
                ppos[0] += n
            S.op("dve", _memset(actv[:], 0.0), writes=["actv%d" % x for x in range(128)])
            slots = []
            for j in range(128):
                gb = gcnt[0] % NSB
                gcnt[0] += 1
                slots.append(gb)
                S.op("pool", _gather(uvg[gb][:], UVB[:, :], ei[b][:, j:j + 1]), reads=["ei" + B, "UVB"], writes=["uvg%d" % gb],
                     dma=True)
                if j % 8 in (0, 1, 2, 4, 5):
                    pi_ = pcnt[0] % 4
                    pcnt[0] += 1
                    S.op("dve", _tt(prod[pi_][:], uvg[gb][:, 0:D], h2bf[b][:], ALU.mult), reads=["uvg%d" % gb, "h2bf" + B],
                         writes=["prod%d" % pi_])
                    S.op("act", lambda e, pi_=pi_, j=j: e.activation(out=junka[:], in_=prod[pi_][:], func=AF.Copy,
                                                                   accum_out=actv[:, j:j + 1]),
                         reads=["prod%d" % pi_], writes=["junka", "actv%d" % j])
                else:
                    S.op("dve", _stt(junkb[:], uvg[gb][:, 0:D], 1.0, h2bf[b][:], ALU.mult, ALU.mult, accum_out=actv[:, j:j + 1]),
                         reads=["uvg%d" % gb, "h2bf" + B], writes=["junkb", "actv%d" % j])
                drain(per_slot)
                if j % 8 == 7:
                    g0 = j - 7
                    db = (j // 8) % 2
                    S.op("act", _act(wgt[:, g0:j + 1], actv[:, g0:j + 1], AF.Gelu), reads=["actv%d" % x for x in range(g0, j + 1)],
                         writes=["wgt"])
                    S.op("dve", _tt(wgt[:, g0:j + 1], wgt[:, g0:j + 1], gate[b][:].rearrange("p h j -> p (h j)")[:, g0:j + 1], ALU.mult),
                         reads=["wgt", "gate" + B], writes=["wgt"])
                    for s_ in range(8):
                        S.op("act", _act(dg[db][:, s_, :], identb[:], AF.Copy, scale=wgt[:, g0 + s_:g0 + s_ + 1]),
                             reads=["identb", "wgt"], writes=["dg%d_%d" % (db, s_)])
                    for s_ in range(8):
                        jj = g0 + s_
                        sb_ = slots[jj]
                        for hf in range(2):
                            S.op("pe", _mm(pacc[:, hf * 512:(hf + 1) * 512], dg[db][:, s_, :],
                                           uvg[sb_][:, D + hf * 512:D + (hf + 1) * 512], jj == 0, jj == 127),
                                 reads=["dg%d_%d" % (db, s_), "uvg%d" % sb_], writes=["pacc"])
            drain(len(pend))
            S.op("dve", _tt(acc[:], pacc[:], hs[b][:], ALU.add), reads=["pacc", "hs" + B], writes=["acc"])
            S.op("dve", _memset(sm[:, 1:2], 0.0), writes=["sm1"])
            S.op("dve", _stt(junkb[:], acc[:], 1.0, acc[:], ALU.mult, ALU.mult, accum_out=sm[:, 1:2]), reads=["acc"],
                 writes=["junkb", "sm1"])
            S.op("dve", _ts(sm[:, 1:2], sm[:, 1:2], 1.0 / D, EPS, ALU.mult, ALU.add), reads=["sm1"], writes=["sm1"])
            S.op("act", _act(sm[:, 1:2], sm[:, 1:2], AF.Sqrt), reads=["sm1"], writes=["sm1"])
            S.op("dve", _recip(sm[:, 1:2], sm[:, 1:2]), reads=["sm1"], writes=["sm1"])
            S.op("dve", _stt(outb[:], acc[:], sm[:, 1:2], gfb[:], ALU.mult, ALU.mult), reads=["acc", "sm1", "gfb"],
                 writes=["outb"])
            S.op("sp", _dma(OUT[i * 128:(i + 1) * 128, :], outb[:]), reads=["outb"], writes=["OUT"], dma=True)

        stage1(0)
        for i in range(ND):
            pend = []
            if i + 1 < ND:
                S.capture = pend
                stage1(i + 1)
                S.capture = None
            stage2(i, pend)
        S.fence()
        S.emit()

    top_es.close()
    return nc


def _common_inputs(inp):
    f = np.float32
    A = {}
    A["meta"] = np.ascontiguousarray(inp["meta"], f)
    A["g1c"] = np.ascontiguousarray(inp["norm1_g"][0].reshape(8, 128).T, f)
    A["w_in"] = np.ascontiguousarray(inp["w_in"][0], f)
    lre, lim, lst = inp["s5_lam_re"][0], inp["s5_lam_im"][0], inp["s5_log_step"][0]
    lst_full = np.broadcast_to(lst[:, :, None], (2, 32, 64))

    def col(a):
        a5 = a.reshape(2, 16, 2, 64)
        return np.ascontiguousarray(a5.transpose(2, 3, 0, 1).reshape(128, 32), f)

    A["lre_c"], A["lim_c"], A["lst_c"] = col(lre), col(lim), col(lst_full)

    def c8(cm):
        a = cm.reshape(16, 2, 16, 64).transpose(1, 3, 0, 2)
        a = np.broadcast_to(a[:, :, :, None, :], (2, 64, 16, 8, 16))
        return np.ascontiguousarray(a.reshape(128, 2048), f)

    def bc(bm):
        a = bm.reshape(16, 2, 64, 16).transpose(1, 2, 0, 3)
        a = np.broadcast_to(a[:, :, :, None, :], (2, 64, 16, 8, 16))
        return np.ascontiguousarray(a.reshape(128, 2048), f)

    A["cre8"], A["cim8"] = c8(inp["s5_c_re"][0]), c8(inp["s5_c_im"][0])
    A["brec"], A["bimc"] = bc(inp["s5_b_re"][0]), bc(inp["s5_b_im"][0])
    dd = inp["s5_d"][0].reshape(32, 16)
    A["drow"] = np.ascontiguousarray(np.broadcast_to(dd.T[None, :, :], (8, 16, 32)).reshape(128, 32), f)
    p = np.arange(128)
    cst8 = np.zeros((128, 32), f)
    cst8[:, 0:8] = (p[:, None] % 8 == np.arange(8)[None, :])
    cst8[:, 8:24] = (p[:, None] // 8 == np.arange(16)[None, :])
    cst8[:, 24:32] = (p[:, None] // 16 == np.arange(8)[None, :])
    A["cst8"] = cst8
    m = np.arange(128)
    selC = np.zeros((128, 8, 128), f)
    for g8 in range(8):
        selC[:, g8, :] = ((p[:, None] % 16) == (m[None, :] % 16)) & ((m[None, :] // 16) == g8)
    A["selC"] = selC.reshape(128, 1024)
    A["ramp16"] = np.ascontiguousarray(np.broadcast_to(np.arange(-7, 9, dtype=f)[None, :], (128, 16)), f)
    ii = p[:, None] // 16
    jj = m[None, :] // 16
    A["mfb"] = np.concatenate([(ii <= jj), (ii >= jj)], axis=1).astype(f)
    A["w_glu"] = np.ascontiguousarray(inp["w_glu"][0], f)
    A["lb0"] = np.ascontiguousarray(inp["hg_lb"][0].reshape(4, 128).T, f)
    A["lb1"] = np.ascontiguousarray(inp["hg_lb"][1].reshape(4, 128).T, f)
    A["hgn"] = np.ascontiguousarray(inp["hg_norm_g"][0].reshape(4, 128).T, f)
    A["w_hg"] = np.ascontiguousarray(inp["w_hg_out"][0], f)
    A["w_out"] = np.ascontiguousarray(inp["w_out"][0], f)
    A["g2b"] = np.ascontiguousarray(np.broadcast_to(inp["norm2_g"][0][None, :], (128, D)), f)
    A["gfb"] = np.ascontiguousarray(np.broadcast_to(inp["final_g"][None, :], (128, D)), f)
    A["wq"] = np.ascontiguousarray(inp["peer_wq"][0], f)
    kz = inp["peer_keys"][0].reshape(16, 128, 128)
    A["keysT"] = np.ascontiguousarray(kz.transpose(2, 0, 1).reshape(128, 2048), f)
    A["puv"] = np.ascontiguousarray(np.concatenate([inp["peer_u"][0], inp["peer_v"][0]], axis=1), f)
    A["identf"] = np.eye(128, dtype=f)
    r = np.arange(1, 129, dtype=f)
    A["ramps"] = np.ascontiguousarray(np.broadcast_to(np.concatenate([r, r[::-1]])[None, :], (128, 256)), f)
    s = np.arange(64)[:, None]
    t = np.arange(64)[None, :]
    A["masks"] = np.concatenate([(t >= s), (t <= s)], axis=1).astype(f)
    A["iota16"] = np.ascontiguousarray(np.broadcast_to(np.arange(16, dtype=f)[None, :], (128, 16)), f)
    return A


_PROG_CACHE = {}


def run_sequences(inp, seqs, assign, ND, debug=False):
    SEQ = seqs[0].shape[0]
    key = (SEQ, ND, debug)
    if key not in _PROG_CACHE:
        _PROG_CACHE[key] = build_program(SEQ, ND, debug)
    nc = _PROG_CACHE[key]
    A = _common_inputs(inp)
    in_maps = []
    for (si, t0, nt) in assign:
        m = dict(A)
        m["xs"] = np.ascontiguousarray(seqs[si], np.float32)
        tiles = [min(t0 + i, t0 + nt - 1) for i in range(ND)]
        tok = np.stack([np.arange(t * 128, (t + 1) * 128) for t in tiles], axis=1).astype(np.int32)
        m["tokd"] = np.ascontiguousarray(tok)
        in_maps.append(m)
    res = run_bass_kernel_spmd(nc, in_maps, core_ids=list(range(len(assign))))
    outs = [np.zeros((SEQ, D), np.float32) for _ in seqs]
    for ci, (si, t0, nt) in enumerate(assign):
        o = res.results[ci]["outd"]
        outs[si][t0 * 128:(t0 + nt) * 128] = o[:nt * 128]
    return outs, res


def kernel(**inputs):
    inp = {k: np.asarray(v) for k, v in inputs.items()}
    seqs = [inp["x_prompt"][0], inp["x_sample"][0], inp["x_sample"][1]]
    assign = [(0, 0, 43), (1, 0, 43), (2, 0, 64), (0, 43, 43), (1, 43, 43), (2, 64, 64), (0, 86, 42), (1, 86, 42)]
    outs, _ = run_sequences(inp, seqs, assign, ND=64)
    y_prompt = outs[0][None].astype(np.float32)
    y_sample = np.stack([outs[1], outs[2]], axis=0).astype(np.float32)
    return (y_prompt, y_sample)
```

```python
import math
from contextlib import ExitStack

import numpy as np
import concourse.bass as bass
import concourse.mybir as mybir
from concourse.bass_utils import run_bass_kernel_spmd

F32 = mybir.dt.float32
BF16 = mybir.dt.bfloat16
I32 = mybir.dt.int32
U32 = mybir.dt.uint32
ALU = mybir.AluOpType
AF = mybir.ActivationFunctionType
AX = mybir.AxisListType

D = 1024
NCOL = 5120
EPS = 1e-6
PI = math.pi
ENGS = ("pe", "act", "dve", "pool", "sp")


class Sch:
    def __init__(self, nc, es, n_dma_sems=32):
        self.nc = nc
        self.esem = {e: es.enter_context(nc.semaphore("se_" + e)) for e in ENGS}
        self.ecnt = {e: 0 for e in ENGS}
        self.dsem = [es.enter_context(nc.semaphore("sd%d" % i)) for i in range(n_dma_sems)]
        self.dval = [0] * n_dma_sems
        self.dnext = {"hw": 0, "sw": 0}
        self.dhalf = n_dma_sems // 2
        self.lastw = {}
        self.readers = {}
        self.known = {e: {} for e in ENGS}
        self.ops = {e: [] for e in ENGS}
        self.nops = 0
        self.capture = None

    def _sem(self, sk):
        return self.esem[sk[1]] if sk[0] == "e" else self.dsem[sk[1]]

    def _need(self, eng, tok, waits):
        if tok is None:
            return
        sk, val = tok
        if sk == ("e", "pe") and eng == "pe":
            return
        if self.known[eng].get(sk, 0) >= val:
            return
        self.known[eng][sk] = val
        waits[sk] = max(waits.get(sk, 0), val)

    def op(self, eng, fn, reads=(), writes=(), dma=False):
        if self.capture is not None:
            self.capture.append((eng, fn, tuple(reads), tuple(writes), dma))
            return
        waits = {}
        for k in reads:
            self._need(eng, self.lastw.get(k), waits)
        for k in writes:
            self._need(eng, self.lastw.get(k), waits)
            for t in self.readers.get(k, ()):
                self._need(eng, t, waits)
        if dma:
            kind = "sw" if eng == "pool" else "hw"
            i = self.dnext[kind] + (self.dhalf if kind == "sw" else 0)
            self.dnext[kind] = (self.dnext[kind] + 1) % self.dhalf
            if self.dval[i] > 0:
                self._need(eng, (("d", i), self.dval[i]), waits)
            self.dval[i] += 16
            tok = (("d", i), self.dval[i])
            inc = (self.dsem[i], 16)
        else:
            self.ecnt[eng] += 1
            tok = (("e", eng), self.ecnt[eng])
            inc = (self.esem[eng], 1)
        for k in reads:
            self.readers.setdefault(k, []).append(tok)
        for k in writes:
            self.lastw[k] = tok
            self.readers[k] = []
        self.ops[eng].append((list(waits.items()), fn, inc))
        self.nops += 1

    def fence(self):
        for e in ENGS:
            waits = {}
            for e2 in ENGS:
                if self.ecnt[e2] > 0:
                    self._need(e, (("e", e2), self.ecnt[e2]), waits)
            for i, v in enumerate(self.dval):
                if v > 0:
                    self._need(e, (("d", i), v), waits)
            self.ops[e].append((list(waits.items()), None, None))
        self.lastw.clear()
        self.readers.clear()

    def emit(self):
        with self.nc.Block() as blk:
            decos = {"pe": blk.tensor, "act": blk.scalar, "dve": blk.vector, "pool": blk.gpsimd, "sp": blk.sync}
            for e in ENGS:
                ops = self.ops[e]

                def body(eng, ops=ops):
                    for waits, fn, inc in ops:
                        for sk, val in waits:
                            eng.wait_ge(self._sem(sk), val)
                        if fn is not None:
                            fn(eng).then_inc(inc[0], inc[1])

                decos[e](body)
                self.ops[e] = []


def _mm(out, lhsT, rhs, start, stop):
    return lambda e: e.matmul(out, lhsT, rhs, start=start, stop=stop)


def _tr(out, in_, ident):
    return lambda e: e.transpose(out, in_, ident)


def _dma(out, in_):
    return lambda e: e.dma_start(out=out, in_=in_)


def _gather(out, table, idx):
    return lambda e: e.indirect_dma_start(out=out, out_offset=None, in_=table,
                                          in_offset=bass.IndirectOffsetOnAxis(ap=idx, axis=0))


def _act(out, in_, func, scale=None):
    if scale is None:
        return lambda e: e.activation(out=out, in_=in_, func=func)
    return lambda e: e.activation(out=out, in_=in_, func=func, scale=scale)


def _copy(out, in_):
    return lambda e: e.tensor_copy(out=out, in_=in_)


def _acopy(out, in_):
    return lambda e: e.copy(out=out, in_=in_)


def _tt(out, in0, in1, op):
    return lambda e: e.tensor_tensor(out=out, in0=in0, in1=in1, op=op)


def _ts(out, in0, s1, s2, op0, op1=None):
    if op1 is None:
        return lambda e: e.tensor_scalar(out=out, in0=in0, scalar1=s1, scalar2=None, op0=op0)
    return lambda e: e.tensor_scalar(out=out, in0=in0, scalar1=s1, scalar2=s2, op0=op0, op1=op1)


def _stt(out, in0, scalar, in1, op0, op1, accum_out=None):
    if accum_out is None:
        return lambda e: e.scalar_tensor_tensor(out=out, in0=in0, scalar=scalar, in1=in1, op0=op0, op1=op1)
    return lambda e: e.scalar_tensor_tensor(out=out, in0=in0, scalar=scalar, in1=in1, op0=op0, op1=op1,
                                            accum_out=accum_out)


def _scan(out, d0, d1, init):
    return lambda e: e.tensor_tensor_scan(out=out, data0=d0, data1=d1, initial=init, op0=ALU.mult, op1=ALU.add)


def _memset(ap, v):
    return lambda e: e.memset(ap, v)


def _rsum(out, in_):
    return lambda e: e.reduce_sum(out=out, in_=in_, axis=AX.X)


def _recip(out, in_):
    return lambda e: e.reciprocal(out=out, in_=in_)


def _rev(ap2d):
    return ap2d[:, ::-1]


def build_program(SEQ, ND, debug=False):
    assert SEQ % 512 == 0
    NT = SEQ // 512 + 1
    TP = NT * 512
    nc = bass.Bass("TRN2", target_bir_lowering=False)

    def din(name, shape, dt=F32):
        return nc.dram_tensor(name, list(shape), dt, kind="ExternalInput").ap()

    def dscr(name, shape, dt):
        kind = "ExternalOutput" if debug else "Internal"
        return nc.dram_tensor(name, list(shape), dt, kind=kind).ap()

    I = {}
    I["xs"] = din("xs", [SEQ, D])
    I["meta"] = din("meta", [16, D])
    I["tokd"] = din("tokd", [128, ND], I32)
    I["g1c"] = din("g1c", [128, 8])
    I["w_in"] = din("w_in", [D, NCOL])
    for nm in ("lre_c", "lim_c", "lst_c"):
        I[nm] = din(nm, [128, 32])
    for nm in ("cre8", "cim8", "brec", "bimc"):
        I[nm] = din(nm, [128, 2048])
    I["drow"] = din("drow", [128, 32])
    I["cst8"] = din("cst8", [128, 32])
    I["selC"] = din("selC", [128, 1024])
    I["ramp16"] = din("ramp16", [128, 16])
    I["mfb"] = din("mfb", [128, 256])
    I["w_glu"] = din("w_glu", [512, 2048])
    I["lb0"] = din("lb0", [128, 4])
    I["lb1"] = din("lb1", [128, 4])
    I["hgn"] = din("hgn", [128, 4])
    I["w_hg"] = din("w_hg", [512, D])
    I["w_out"] = din("w_out", [D, D])
    I["g2b"] = din("g2b", [128, D])
    I["gfb"] = din("gfb", [128, D])
    I["wq"] = din("wq", [D, 2048])
    I["keysT"] = din("keysT", [128, 2048])
    I["puv"] = din("puv", [16384, 2 * D])
    I["identf"] = din("identf", [128, 128])
    I["ramps"] = din("ramps", [128, 256])
    I["masks"] = din("masks", [64, 128])
    I["iota16"] = din("iota16", [128, 16])
    OUT = nc.dram_tensor("outd", [ND * 128, D], F32, kind="ExternalOutput").ap()

    Z = dscr("Z", [NCOL, TP], BF16)
    VTM = dscr("VTM", [TP, 512], BF16)
    UST = nc.dram_tensor("UST", [128, 32, TP // 8], BF16, kind="Internal").ap()
    XBS = nc.dram_tensor("XBS", [128, 16, 2, TP // 8], BF16, kind="Internal").ap()
    MA = dscr("MA", [D, TP], BF16)
    OB = dscr("OB", [512, TP], F32)
    HS1 = dscr("HS1", [SEQ, D], F32)
    UVB = nc.dram_tensor("UVB", [16384, 2 * D], BF16, kind="Internal").ap()

    top_es = ExitStack()
    S = Sch(nc, top_es)

    with ExitStack() as es:
        def T(name, shape, dt):
            return es.enter_context(nc.sbuf_tensor("A_" + name, shape, dt))

        def P(name, shape, dt):
            return es.enter_context(nc.psum_tensor("A_" + name, shape, dt))

        Wb = T("Wb", [128, 8, NCOL], BF16)
        stg = [T("stg%d" % i, [128, 1280], F32) for i in range(2)]
        cst8 = T("cst8", [128, 32], F32)
        selNb = T("selNb", [128, 16], BF16)
        Ex = T("Ex", [128, 32, 8, 16], BF16)
        ustb = [T("ustb%d" % i, [128, 32, 256], BF16) for i in range(2)]
        g1c = T("g1c", [128, 8], F32)
        identf = T("identf", [128, 128], F32)
        identb = T("identb", [128, 128], BF16)
        xt = [[T("xt%d_%d" % (b, j), [128, D], F32) for j in range(4)] for b in range(2)]
        hb = [T("hb%d" % i, [128, D], BF16) for i in range(2)]
        hT = [T("hT%d" % i, [128, 8, 512], BF16) for i in range(2)]
        ssq = T("ssq", [128, 8], F32)
        rstd = T("rstd", [128, 8], F32)
        junk = T("junk", [128, D], F32)
        zt = [T("zt%d" % i, [128, 4, 512], BF16) for i in range(2)]
        vt = [T("vt%d" % i, [128, 512], BF16) for i in range(2)]
        pT = [P("pT%d" % i, [128, 8, 128], BF16) for i in range(2)]
        pz = [P("pz%d" % i, [128, 512], F32) for i in range(4)]
        psU = P("psU", [128, 32, 16], F32)

        S.op("sp", _dma(identf[:], I["identf"][:, :]), writes=["identf"], dma=True)
        S.op("sp", _dma(g1c[:], I["g1c"][:, :]), writes=["g1c"], dma=True)
        S.op("sp", _dma(cst8[:], I["cst8"][:, :]), writes=["cst8"], dma=True)
        S.op("dve", _copy(selNb[:], cst8[:, 8:24]), reads=["cst8"], writes=["selNb"])
        S.op("dve", _copy(identb[:], identf[:]), reads=["identf"], writes=["identb"])
        n = 0
        for k in range(8):
            for h in range(4):
                sb = n % 2
                n += 1
                S.op("sp", _dma(stg[sb][:], I["w_in"][k * 128:(k + 1) * 128, h * 1280:(h + 1) * 1280]),
                     writes=["stg%d" % sb], dma=True)
                S.op("dve", _ts(Wb[:, k, h * 1280:(h + 1) * 1280], stg[sb][:], g1c[:, k:k + 1], None, ALU.mult),
                     reads=["stg%d" % sb, "g1c"], writes=["Wb"])

        ev = 0
        for st in range(NT):
            b = st % 2
            if st == 0:
                for j in range(4):
                    S.op("pool", _memset(xt[b][j][:], 0.0), writes=["xt%d_%d" % (b, j)])
                S.op("sp", _dma(xt[b][3][112:128, :], I["meta"][:, :]), writes=["xt%d_3" % b], dma=True)
            else:
                for j in range(4):
                    r0 = (st - 1) * 512 + j * 128
                    S.op("sp", _dma(xt[b][j][:], I["xs"][r0:r0 + 128, :]), writes=["xt%d_%d" % (b, j)], dma=True)
            for j in range(4):
                c = (st * 4 + j) % 8
                hbj = j % 2
                xk = "xt%d_%d" % (b, j)
                S.op("dve", _memset(ssq[:, c:c + 1], 0.0), writes=["ssq%d" % c])
                S.op("dve", _stt(junk[:], xt[b][j][:], 1.0, xt[b][j][:], ALU.mult, ALU.mult, accum_out=ssq[:, c:c + 1]),
                     reads=[xk], writes=["junk", "ssq%d" % c])
                S.op("dve", _ts(rstd[:, c:c + 1], ssq[:, c:c + 1], 1.0 / D, EPS, ALU.mult, ALU.add),
                     reads=["ssq%d" % c], writes=["rstd%d" % c])
                S.op("act", _act(rstd[:, c:c + 1], rstd[:, c:c + 1], AF.Sqrt), reads=["rstd%d" % c], writes=["rstd%d" % c])
                S.op("dve", _recip(rstd[:, c:c + 1], rstd[:, c:c + 1]), reads=["rstd%d" % c], writes=["rstd%d" % c])
                S.op("act", _act(hb[hbj][:], xt[b][j][:], AF.Copy, scale=rstd[:, c:c + 1]),
                     reads=[xk, "rstd%d" % c], writes=["hb%d" % hbj])
                for k in range(8):
                    S.op("pe", _tr(pT[hbj][:, k, :], hb[hbj][:, k * 128:(k + 1) * 128], identb[:]),
                         reads=["hb%d" % hbj, "identb"], writes=["pT%d" % hbj])
                S.op("act", _acopy(hT[b][:, :, j * 128:(j + 1) * 128], pT[hbj][:, :, :]),
                     reads=["pT%d" % hbj], writes=["hT%d" % b])
            for c in [c_ for c_ in range(4, 40) if not (16 <= c_ < 20)]:
                pzi = c % 4
                for k in range(8):
                    S.op("pe", _mm(pz[pzi][:], Wb[:, k, c * 128:(c + 1) * 128], hT[b][:, k, :], k == 0, k == 7),
                         reads=["Wb", "hT%d" % b], writes=["pz%d" % pzi])
                zb = (c // 4) % 2
                if ev % 2 == 0:
                    S.op("act", _acopy(zt[zb][:, c % 4, :], pz[pzi][:]), reads=["pz%d" % pzi], writes=["zt%d" % zb])
                else:
                    S.op("dve", _copy(zt[zb][:, c % 4, :], pz[pzi][:]), reads=["pz%d" % pzi], writes=["zt%d" % zb])
                ev += 1
                if c % 4 == 3:
                    dst = Z[(c - 3) * 128:(c + 1) * 128, st * 512:(st + 1) * 512].rearrange("(c p) t -> p c t", p=128)
                    S.op("pool", _dma(dst, zt[zb][:]), reads=["zt%d" % zb], writes=["Z"], dma=True)
            for j in range(4):
                pzi = j % 4
                for k in range(8):
                    S.op("pe", _mm(pz[pzi][:], hT[b][:, k, j * 128:(j + 1) * 128], Wb[:, k, 2048:2560], k == 0, k == 7),
                         reads=["Wb", "hT%d" % b], writes=["pz%d" % pzi])
                vb = j % 2
                S.op("act", _acopy(vt[vb][:], pz[pzi][:]), reads=["pz%d" % pzi], writes=["vt%d" % vb])
                r0 = st * 512 + j * 128
                S.op("pool", _dma(VTM[r0:r0 + 128, :], vt[vb][:]), reads=["vt%d" % vb], writes=["VTM"], dma=True)
            ubuf = (st // 4) % 2
            for j in range(4):
                pzi = j % 4
                for k in range(8):
                    S.op("pe", _mm(pz[pzi][:], hT[b][:, k, j * 128:(j + 1) * 128], Wb[:, k, 0:512], k == 0, k == 7),
                         reads=["Wb", "hT%d" % b], writes=["pz%d" % pzi])
                S.op("dve", _tt(Ex[:], pz[pzi][:].rearrange("p (g c) -> p g c", g=32).unsqueeze(2).broadcast_to([128, 32, 8, 16]),
                                cst8[:, 0:8].unsqueeze(1).unsqueeze(3).broadcast_to([128, 32, 8, 16]), ALU.mult),
                     reads=["pz%d" % pzi, "cst8"], writes=["Ex"])
                for g in range(32):
                    S.op("pe", _mm(psU[:, g, :], Ex[:, g, :, :].rearrange("p i c -> p (i c)"), selNb[:], True, True),
                         reads=["Ex", "selNb"], writes=["psU"])
                off = (st % 4) * 64 + j * 16
                S.op("act", _acopy(ustb[ubuf][:, :, off:off + 16], psU[:]), reads=["psU"], writes=["ustb%d" % ubuf])
            if st % 4 == 3 or st == NT - 1:
                base = (st - st % 4) * 64
                ncols = (st % 4 + 1) * 64
                S.op("pool", _dma(UST[:, :, base:base + ncols], ustb[ubuf][:, :, 0:ncols]), reads=["ustb%d" % ubuf], writes=["UST"],
                     dma=True)
        S.fence()
        S.emit()

    with ExitStack() as es:
        def T(name, shape, dt):
            return es.enter_context(nc.sbuf_tensor("B_" + name, shape, dt))

        def P(name, shape, dt):
            return es.enter_context(nc.psum_tensor("B_" + name, shape, dt))

        NCH = TP // 8
        BWD = 256
        blocks = [(n0, min(BWD, NCH - n0)) for n0 in range(0, NCH, BWD)]

        WST = T("WST", [128, 2, 16, 2, 2, 128], BF16)
        WX = T("WX", [128, 32, 2, 128], BF16)
        WU = T("WU", [128, 32, 128], BF16)
        COS = T("COS", [128, 32, 128], F32)
        SIN = T("SIN", [128, 32, 128], F32)
        rc8 = T("rc8", [128, 32], F32)
        ramps = T("ramps", [128, 256], F32)
        cst8 = T("cst8", [128, 32], F32)
        maskJb = T("maskJb", [128, 8], BF16)
        selC = T("selC", [128, 8, 128], BF16)
        STre = T("STre", [128, 32], F32)
        STim = T("STim", [128, 32], F32)
        XFc = T("XFc", [128, 16, 2], BF16)
        tmp4 = T("tmp4", [128, 4], F32)
        identf = T("identf", [128, 128], F32)

        def sincos(dsin, dcos, ang, tf, tf2, ti, ksin, kcos, kang, ktf, ktf2, kti):
            for dst, kd, off in ((dcos, kcos, 0.25), (dsin, ksin, 0.0)):
                S.op("dve", _ts(tf, ang, 1.0 / (2 * PI), off, ALU.mult, ALU.add), reads=[kang], writes=[ktf])
                S.op("dve", _copy(ti, tf), reads=[ktf], writes=[kti])
                S.op("dve", _copy(tf2, ti), reads=[kti], writes=[ktf2])
                S.op("dve", _tt(tf, tf, tf2, ALU.subtract), reads=[ktf, ktf2], writes=[ktf])
                S.op("dve", _ts(tf, tf, -0.4999, 0.4999, ALU.max, ALU.min), reads=[ktf], writes=[ktf])
                S.op("act", _act(dst, tf, AF.Sin, scale=2 * PI), reads=[ktf], writes=[kd])

        with ExitStack() as es2:
            def T2(name, shape, dt):
                return es2.enter_context(nc.sbuf_tensor("B2_" + name, shape, dt))

            def P2(name, shape, dt):
                return es2.enter_context(nc.psum_tensor("B2_" + name, shape, dt))

            LR = T2("LR", [128, 32], F32)
            LI = T2("LI", [128, 32], F32)
            LS = T2("LS", [128, 32], F32)
            pa = T2("pa", [128, 32], F32)
            pth = T2("pth", [128, 32], F32)
            pcr = T2("pcr", [128, 32], F32)
            pci = T2("pci", [128, 32], F32)
            sm_ = [T2("sm%d" % i, [128, 32], F32) for i in range(8)]
            ramp16 = T2("ramp16", [128, 16], F32)
            mfb = T2("mfb", [128, 256], F32)
            drow = T2("drow", [128, 32], F32)
            S.op("sp", _dma(identf[:], I["identf"][:, :]), writes=["identf"], dma=True)
            S.op("sp", _dma(ramps[:], I["ramps"][:, :]), writes=["ramps"], dma=True)
            S.op("sp", _dma(cst8[:], I["cst8"][:, :]), writes=["cst8"], dma=True)
            S.op("sp", _dma(ramp16[:], I["ramp16"][:, :]), writes=["ramp16"], dma=True)
            S.op("sp", _dma(mfb[:], I["mfb"][:, :]), writes=["mfb"], dma=True)
            S.op("sp", _dma(drow[:], I["drow"][:, :]), writes=["drow"], dma=True)
            S.op("sp", _dma(LR[:], I["lre_c"][:, :]), writes=["LR"], dma=True)
            S.op("sp", _dma(LI[:], I["lim_c"][:, :]), writes=["LI"], dma=True)
            S.op("sp", _dma(LS[:], I["lst_c"][:, :]), writes=["LS"], dma=True)
            S.op("dve", _copy(maskJb[:], cst8[:, 24:32]), reads=["cst8"], writes=["maskJb"])
            t = [x[:] for x in sm_]
            k = ["sm%d" % i for i in range(8)]
            S.op("act", _act(t[0], LS[:], AF.Exp), reads=["LS"], writes=[k[0]])
            S.op("dve", _tt(pa[:], LR[:], t[0], ALU.mult), reads=["LR", k[0]], writes=["pa"])
            S.op("dve", _tt(pth[:], LI[:], t[0], ALU.mult), reads=["LI", k[0]], writes=["pth"])
            S.op("act", _act(t[1], pa[:], AF.Exp), reads=["pa"], writes=[k[1]])
            sincos(t[3], t[4], pth[:], t[5], t[6], t[7].bitcast(I32), k[3], k[4], "pth", k[5], k[6], k[7])
            S.op("dve", _tt(t[5], t[1], t[4], ALU.mult), reads=[k[1], k[4]], writes=[k[5]])
            S.op("dve", _ts(t[5], t[5], -1.0, None, ALU.add), reads=[k[5]], writes=[k[5]])
            S.op("dve", _tt(t[6], t[1], t[3], ALU.mult), reads=[k[1], k[3]], writes=[k[6]])
            S.op("dve", _tt(t[7], LR[:], LR[:], ALU.mult), reads=["LR"], writes=[k[7]])
            S.op("dve", _tt(t[0], LI[:], LI[:], ALU.mult), reads=["LI"], writes=[k[0]])
            S.op("dve", _tt(t[7], t[7], t[0], ALU.add), reads=[k[7], k[0]], writes=[k[7]])
            S.op("dve", _recip(t[7], t[7]), reads=[k[7]], writes=[k[7]])
            S.op("dve", _tt(t[3], t[5], LR[:], ALU.mult), reads=[k[5], "LR"], writes=[k[3]])
            S.op("dve", _tt(t[0], t[6], LI[:], ALU.mult), reads=[k[6], "LI"], writes=[k[0]])
            S.op("dve", _tt(t[3], t[3], t[0], ALU.add), reads=[k[3], k[0]], writes=[k[3]])
            S.op("dve", _tt(pcr[:], t[3], t[7], ALU.mult), reads=[k[3], k[7]], writes=["pcr"])
            S.op("dve", _tt(t[4], t[6], LR[:], ALU.mult), reads=[k[6], "LR"], writes=[k[4]])
            S.op("dve", _tt(t[0], t[5], LI[:], ALU.mult), reads=[k[5], "LI"], writes=[k[0]])
            S.op("dve", _tt(t[4], t[4], t[0], ALU.subtract), reads=[k[4], k[0]], writes=[k[4]])
            S.op("dve", _tt(pci[:], t[4], t[7], ALU.mult), reads=[k[4], k[7]], writes=["pci"])

            PWm = T2("PWm", [128, 32, 16], F32)
            PWa = T2("PWa", [128, 32, 16], F32)
            PWr = T2("PWr", [128, 32, 16], F32)
            PWi = T2("PWi", [128, 32, 16], F32)
            PCr = T2("PCr", [128, 32, 16], F32)
            PCi = T2("PCi", [128, 32, 16], F32)
            tl = [T2("tl%d" % i, [128, 2048], F32) for i in range(6)]
            pt = [tl[3 + i][:, 0:512] for i in range(3)]
            for dg in range(32):
                S.op("act", _act(PWm[:, dg, :], ramp16[:], AF.Exp, scale=pa[:, dg:dg + 1]), reads=["ramp16", "pa"], writes=["PWm"])
                S.op("dve", _ts(PWa[:, dg, :], ramp16[:], pth[:, dg:dg + 1], None, ALU.mult), reads=["ramp16", "pth"], writes=["PWa"])
            f512 = lambda x: x[:].rearrange("p a b -> p (a b)")
            sincos(f512(PWi), f512(PWr), f512(PWa), pt[0], pt[1], pt[2].bitcast(I32), "PWi", "PWr", "PWa", "tl3", "tl4", "tl5")
            S.op("dve", _tt(PWr[:], PWr[:], PWm[:], ALU.mult), reads=["PWr", "PWm"], writes=["PWr"])
            S.op("dve", _tt(PWi[:], PWi[:], PWm[:], ALU.mult), reads=["PWi", "PWm"], writes=["PWi"])
            crb = pcr[:].unsqueeze(2).broadcast_to([128, 32, 16])
            cib = pci[:].unsqueeze(2).broadcast_to([128, 32, 16])
            p3 = lambda x: x.rearrange("p (a b) -> p a b", a=32)
            S.op("dve", _tt(PCr[:], PWr[:], crb, ALU.mult), reads=["PWr", "pcr"], writes=["PCr"])
            S.op("dve", _tt(p3(pt[0]), PWi[:], cib, ALU.mult), reads=["PWi", "pci"], writes=["tl3"])
            S.op("dve", _tt(PCr[:], PCr[:], p3(pt[0]), ALU.subtract), reads=["PCr", "tl3"], writes=["PCr"])
            S.op("dve", _tt(PCi[:], PWr[:], cib, ALU.mult), reads=["PWr", "pci"], writes=["PCi"])
            S.op("dve", _tt(p3(pt[1]), PWi[:], crb, ALU.mult), reads=["PWi", "pcr"], writes=["tl4"])
            S.op("dve", _tt(PCi[:], PCi[:], p3(pt[1]), ALU.add), reads=["PCi", "tl4"], writes=["PCi"])
            S.op("dve", _copy(rc8[:].unsqueeze(2), PWm[:, :, 15:16]), reads=["PWm"], writes=["rc8"])
            S.op("dve", _ts(sm_[0][:], pth[:], 8.0, None, ALU.mult), reads=["pth"], writes=["sm0"])
            S.fence()
            for hf in range(2):
                ang3 = tl[0][:].rearrange("p (a b) -> p a b", a=16)
                for g_ in range(16):
                    dg = hf * 16 + g_
                    rp = ramps[:, 0:128] if hf == 0 else ramps[:, 128:256]
                    S.op("dve", _ts(ang3[:, g_, :], rp, sm_[0][:, dg:dg + 1], None, ALU.mult), reads=["ramps", "sm0"], writes=["tl0"])
                sinh = SIN[:, hf * 16:(hf + 1) * 16, :].rearrange("p a b -> p (a b)")
                cosh = COS[:, hf * 16:(hf + 1) * 16, :].rearrange("p a b -> p (a b)")
                sincos(sinh, cosh, tl[0][:], tl[1][:], tl[2][:], tl[3][:].bitcast(I32), "SIN", "COS", "tl0", "tl1", "tl2", "tl3")
            S.fence()

            C8r = T2("C8r", [128, 2048], F32)
            C8i = T2("C8i", [128, 2048], F32)
            Bcr = T2("Bcr", [128, 2048], F32)
            Bci = T2("Bci", [128, 2048], F32)
            WUf = T2("WUf", [128, 32, 128], F32)
            psT = P2("psT", [128, 128], F32)
            psW = P2("psW", [128, 128], F32)
            S.op("sp", _dma(C8r[:], I["cre8"][:, :]), writes=["C8r"], dma=True)
            S.op("sp", _dma(C8i[:], I["cim8"][:, :]), writes=["C8i"], dma=True)
            S.op("sp", _dma(Bcr[:], I["brec"][:, :]), writes=["Bcr"], dma=True)
            S.op("sp", _dma(Bci[:], I["bimc"][:, :]), writes=["Bci"], dma=True)
            S.op("pool", _memset(WST[:].rearrange("p a b c d e -> p (a b c d e)"), 0.0), writes=["WST"])

            def v4(ap):
                return ap.rearrange("p (g j c) -> p g j c", g=16, j=8)

            def pw4(tbl, d, sl):
                return tbl[:, d * 16:(d + 1) * 16, sl].unsqueeze(3).broadcast_to([128, 16, 8, 16])

            def cmul(ore, oim, ar, ai, br, bi, ka, kb, ko, neg_im=False):
                s4, s5 = v4(tl[4][:]), v4(tl[5][:])
                S.op("dve", _tt(s4, ar, br, ALU.mult), reads=ka + kb, writes=["tl4"])
                S.op("dve", _tt(s5, ai, bi, ALU.mult), reads=ka + kb, writes=["tl5"])
                S.op("dve", _tt(ore, s4, s5, ALU.subtract), reads=["tl4", "tl5"], writes=[ko[0]])
                S.op("dve", _tt(s4, ar, bi, ALU.mult), reads=ka + kb, writes=["tl4"])
                S.op("dve", _tt(s5, ai, br, ALU.mult), reads=ka + kb, writes=["tl5"])
                S.op("dve", _tt(oim, s4, s5, ALU.add), reads=["tl4", "tl5"], writes=[ko[1]])
                if neg_im:
                    S.op("dve", _ts(oim, oim, -1.0, None, ALU.mult), reads=[ko[1]], writes=[ko[1]])

            SL_P1_8 = slice(8, 16)
            SL_8_1 = slice(15, 7, -1)
            SL_0_7 = slice(7, 15)
            SL_0_m7 = slice(7, None, -1)
            SL_7_0 = slice(14, 6, -1)
            kC, kB_, kPW, kPC = ["C8r", "C8i"], ["Bcr", "Bci"], ["PWr", "PWi"], ["PCr", "PCi"]
            for d in range(2):
                sl = SL_P1_8 if d == 0 else SL_8_1
                cmul(v4(tl[0][:]), v4(tl[1][:]), v4(C8r[:]), v4(C8i[:]), pw4(PWr, d, sl), pw4(PWi, d, sl), kC, kPW, ["tl0", "tl1"],
                     neg_im=True)
                S.op("act", _acopy(WX[:, d * 16:(d + 1) * 16, 0, :], tl[0][:].rearrange("p (g m) -> p g m", g=16)), reads=["tl0"],
                     writes=["WX"])
                S.op("act", _acopy(WX[:, d * 16:(d + 1) * 16, 1, :], tl[1][:].rearrange("p (g m) -> p g m", g=16)), reads=["tl1"],
                     writes=["WX"])
                sl = SL_7_0 if d == 0 else SL_0_7
                cmul(v4(tl[0][:]), v4(tl[1][:]), pw4(PCr, d, sl), pw4(PCi, d, sl), v4(Bcr[:]), v4(Bci[:]), kPC, kB_, ["tl0", "tl1"])
                for gp in range(16):
                    for ri in range(2):
                        S.op("pe", _tr(psT[:], tl[ri][:, gp * 128:(gp + 1) * 128], identf[:]), reads=["tl%d" % ri, "identf"],
                             writes=["psT"])
                        S.op("act", _acopy(WST[:, d, gp, ri, 0, 0:64], psT[:, 0:64]), reads=["psT"], writes=["WST"])
                        S.op("dve", _copy(WST[:, d, gp, ri, 1, 64:128], psT[:, 64:128]), reads=["psT"], writes=["WST"])
                slG = SL_0_7 if d == 0 else SL_0_m7
                slH = SL_0_m7 if d == 0 else SL_0_7
                cmul(v4(tl[0][:]), v4(tl[1][:]), v4(C8r[:]), v4(C8i[:]), pw4(PWr, d, slG), pw4(PWi, d, slG), kC, kPW, ["tl0", "tl1"],
                     neg_im=True)
                cmul(v4(tl[2][:]), v4(tl[3][:]), pw4(PCr, d, slH), pw4(PCi, d, slH), v4(Bcr[:]), v4(Bci[:]), kPC, kB_, ["tl2", "tl3"])
                mk = mfb[:, 0:128] if d == 0 else mfb[:, 128:256]
                for gp in range(16):
                    for gl in range(2):
                        g = 2 * gp + gl
                        rs = slice(gl * 64, (gl + 1) * 64)
                        cs = slice(gp * 128, (gp + 1) * 128)
                        S.op("pe", _mm(psW[:], tl[2][rs, cs], tl[0][rs, cs], True, False), reads=["tl2", "tl0"], writes=["psW"])
                        S.op("pe", _mm(psW[:], tl[3][rs, cs], tl[1][rs, cs], False, True), reads=["tl3", "tl1"], writes=["psW"])
                        if d == 0:
                            S.op("dve", _tt(WUf[:, g, :], psW[:], mk, ALU.mult), reads=["psW", "mfb"], writes=["WUf"])
                            S.op("dve", _stt(WUf[:, g, :], identf[:], drow[:, g:g + 1], WUf[:, g, :], ALU.mult, ALU.add),
                                 reads=["identf", "drow", "WUf"], writes=["WUf"])
                        else:
                            S.op("dve", _tt(tl[4][:, 0:128], psW[:], mk, ALU.mult), reads=["psW", "mfb"], writes=["tl4"])
                            S.op("dve", _tt(WU[:, g, :], WUf[:, g, :], tl[4][:, 0:128], ALU.add), reads=["WUf", "tl4"], writes=["WU"])
            S.op("sp", _dma(tl[0][:, 0:1024], I["selC"][:, :]), writes=["tl0"], dma=True)
            S.op("dve", _copy(selC[:].rearrange("p a b -> p (a b)"), tl[0][:, 0:1024]), reads=["tl0"], writes=["selC"])
            S.fence()
            S.emit()

        wglu = T("wglu", [128, 4, 2048], BF16)
        with ExitStack() as es3:
            wst2 = [es3.enter_context(nc.sbuf_tensor("B3_wst%d" % i, [128, 2048], F32)) for i in range(2)]
            for k_ in range(4):
                S.op("sp", _dma(wst2[k_ % 2][:], I["w_glu"][k_ * 128:(k_ + 1) * 128, :]), writes=["wst2_%d" % (k_ % 2)], dma=True)
                S.op("act", _acopy(wglu[:, k_, :], wst2[k_ % 2][:]), reads=["wst2_%d" % (k_ % 2)], writes=["wglu"])
            S.fence()
            S.emit()
        ust = [T("ust%d" % i, [128, 32, BWD], BF16) for i in range(2)]
        gst = T("gst", [128, 32, BWD], BF16)
        bpre = [T("bpre%d" % i, [128, BWD], F32) for i in range(2)]
        bpim = [T("bpim%d" % i, [128, BWD], F32) for i in range(2)]
        mt = [T("mt%d" % i, [128, 128], F32) for i in range(4)]
        mp = [T("mp%d" % i, [128, 128], F32) for i in range(4)]
        Wre = [T("Wre%d" % i, [128, BWD], F32) for i in range(2)]
        Wim = [T("Wim%d" % i, [128, BWD], F32) for i in range(2)]
        XF = [T("XF%d" % i, [128, 2, BWD + 1], BF16) for i in range(2)]
        XBt = [T("XBt%d" % i, [128, 2, BWD], BF16) for i in range(2)]
        ex = [T("ex%d" % i, [128, 32, 16, 8], BF16) for i in range(1)]
        gel = T("gel", [128, 4, 512], BF16)
        gat = T("gat", [128, 8, 512], BF16)
        mao = T("mao", [128, 8, 512], BF16)
        sgb = [T("sgb%d" % i, [128, 512], F32) for i in range(1)]
        yab = [T("yab%d" % i, [128, 512], F32) for i in range(1)]
        sga = [T("sga%d" % i, [128, 512], F32) for i in range(1)]
        psS = [P("psS%d" % i, [128, 512], F32) for i in range(2)]
        psY = [P("psY%d" % i, [128, 512], F32) for i in range(2)]
        psG = P("psG", [128, 4, 128], F32)
        psb = [P("psb%d" % i, [128, 512], F32) for i in range(2)]

        S.op("dve", _memset(STre[:], 0.0), writes=["STre"])
        S.op("dve", _memset(STim[:], 0.0), writes=["STim"])
        S.op("dve", _memset(XFc[:], 0.0), writes=["XFc"])

        cnt = 0
        ucnt = 0
        for d in (1, 0):
            blks = list(reversed(blocks)) if d == 1 else blocks
            for (n0, w) in blks:
                ub = ucnt % 2
                ucnt += 1
                S.op("sp", _dma(ust[ub][:, :, 0:w], UST[:, :, n0:n0 + w]), reads=["UST"], writes=["ust%d" % ub], dma=True)
                segs = [(s0, min(128, w - s0)) for s0 in range(0, w, 128)]
                if d == 1:
                    segs = list(reversed(segs))
                for gp in range(16):
                    dg = d * 16 + gp
                    pb = cnt % 2
                    cnt += 1
                    PB = str(pb)
                    for ri in range(2):
                        S.op("pe", _mm(psS[ri][:, 0:w], WST[:, d, gp, ri, 0, :], ust[ub][:, 2 * gp, 0:w], True, False),
                             reads=["WST", "ust%d" % ub], writes=["psS%d" % ri])
                        S.op("pe", _mm(psS[ri][:, 0:w], WST[:, d, gp, ri, 1, :], ust[ub][:, 2 * gp + 1, 0:w], False, True),
                             reads=["WST", "ust%d" % ub], writes=["psS%d" % ri])
                    rbc = rc8[:, dg:dg + 1]
                    for (s0, sw) in segs:
                        sl = slice(s0, s0 + sw)
                        tsl = slice(128 - sw, 128) if d == 1 else slice(0, sw)
                        cb_, sb_ = COS[:, dg, tsl], SIN[:, dg, tsl]
                        S.op("dve", _tt(mt[0][:, 0:sw], psS[0][:, sl], cb_, ALU.mult), reads=["psS0", "COS"], writes=["mt0"])
                        S.op("dve", _tt(mt[1][:, 0:sw], psS[1][:, sl], sb_, ALU.mult), reads=["psS1", "SIN"], writes=["mt1"])
                        S.op("dve", _tt(bpre[pb][:, sl], mt[0][:, 0:sw], mt[1][:, 0:sw], ALU.add), reads=["mt0", "mt1"], writes=["bpre" + PB])
                        S.op("dve", _tt(mt[2][:, 0:sw], psS[1][:, sl], cb_, ALU.mult), reads=["psS1", "COS"], writes=["mt2"])
                        S.op("dve", _tt(mt[3][:, 0:sw], psS[0][:, sl], sb_, ALU.mult), reads=["psS0", "SIN"], writes=["mt3"])
                        S.op("dve", _tt(bpim[pb][:, sl], mt[2][:, 0:sw], mt[3][:, 0:sw], ALU.subtract), reads=["mt2", "mt3"], writes=["bpim" + PB])
                        wre_s, wim_s = Wre[pb][:, sl], Wim[pb][:, sl]
                        bre_s, bim_s = bpre[pb][:, sl], bpim[pb][:, sl]
                        if d == 1:
                            wre_s, wim_s, bre_s, bim_s = _rev(wre_s), _rev(wim_s), _rev(bre_s), _rev(bim_s)
                        rb_ = rbc.broadcast_to([128, sw])
                        S.op("dve", _scan(wre_s, rb_, bre_s, STre[:, dg:dg + 1]), reads=["rc8", "bpre" + PB, "STre%d" % dg], writes=["Wre" + PB])
                        S.op("dve", _scan(wim_s, rb_, bim_s, STim[:, dg:dg + 1]), reads=["rc8", "bpim" + PB, "STim%d" % dg], writes=["Wim" + PB])
                        last = s0 if d == 1 else s0 + sw - 1
                        tlast = (128 - sw) if d == 1 else (sw - 1)
                        cl = COS[:, dg, tlast:tlast + 1]
                        sl_ = SIN[:, dg, tlast:tlast + 1]
                        wr1 = Wre[pb][:, last:last + 1]
                        wi1 = Wim[pb][:, last:last + 1]
                        S.op("dve", _ts(tmp4[:, 0:1], wi1, sl_, None, ALU.mult), reads=["Wim" + PB, "SIN"], writes=["tmp4a"])
                        S.op("dve", _ts(tmp4[:, 1:2], wi1, cl, None, ALU.mult), reads=["Wim" + PB, "COS"], writes=["tmp4b"])
                        S.op("dve", _stt(STre[:, dg:dg + 1], wr1, cl, tmp4[:, 0:1], ALU.mult, ALU.subtract),
                             reads=["Wre" + PB, "COS", "tmp4a"], writes=["STre%d" % dg])
                        S.op("dve", _stt(STim[:, dg:dg + 1], wr1, sl_, tmp4[:, 1:2], ALU.mult, ALU.add),
                             reads=["Wre" + PB, "SIN", "tmp4b"], writes=["STim%d" % dg])
                        if d == 1:
                            xre_o, xim_o = XBt[pb][:, 0, sl], XBt[pb][:, 1, sl]
                            kxo = "XBt" + PB
                        else:
                            xre_o = XF[pb][:, 0, 1 + s0:1 + s0 + sw]
                            xim_o = XF[pb][:, 1, 1 + s0:1 + s0 + sw]
                            kxo = "XF" + PB
                        S.op("pool", _tt(mp[0][:, 0:sw], Wre[pb][:, sl], cb_, ALU.mult), reads=["Wre" + PB, "COS"], writes=["mp0"])
                        S.op("pool", _tt(mp[1][:, 0:sw], Wim[pb][:, sl], sb_, ALU.mult), reads=["Wim" + PB, "SIN"], writes=["mp1"])
                        S.op("pool", _tt(xre_o, mp[0][:, 0:sw], mp[1][:, 0:sw], ALU.subtract), reads=["mp0", "mp1"], writes=[kxo])
                        S.op("pool", _tt(mp[2][:, 0:sw], Wre[pb][:, sl], sb_, ALU.mult), reads=["Wre" + PB, "SIN"], writes=["mp2"])
                        S.op("pool", _tt(mp[3][:, 0:sw], Wim[pb][:, sl], cb_, ALU.mult), reads=["Wim" + PB, "COS"], writes=["mp3"])
                        S.op("pool", _tt(xim_o, mp[2][:, 0:sw], mp[3][:, 0:sw], ALU.add), reads=["mp2", "mp3"], writes=[kxo])
                    if d == 1:
                        S.op("pool", _dma(XBS[:, gp, :, n0:n0 + w], XBt[pb][:, :, 0:w]), reads=["XBt" + PB], writes=["XBS"], dma=True)
                        continue
                    S.op("act", _acopy(XF[pb][:, :, 0:1], XFc[:, gp, :].unsqueeze(2)), reads=["XFc"], writes=["XF" + PB])
                    S.op("act", _acopy(XFc[:, gp, :].unsqueeze(2), XF[pb][:, :, w:w + 1]), reads=["XF" + PB], writes=["XFc"])
                    wl = w if n0 + w < NCH else w - 1
                    if wl > 0:
                        S.op("sp", _dma(XBt[pb][:, :, 0:wl], XBS[:, gp, :, n0 + 1:n0 + 1 + wl]), reads=["XBS"], writes=["XBt" + PB], dma=True)
                    if wl < w:
                        S.op("pool", _memset(XBt[pb][:, :, wl:w], 0.0), writes=["XBt" + PB])
                    for gl in range(2):
                        g = 2 * gp + gl
                        rs = slice(gl * 64, (gl + 1) * 64)
                        py = psY[g % 2]
                        kp = "psY%d" % (g % 2)
                        S.op("pe", _mm(py[:, 0:w], WU[:, g, :], ust[ub][:, g, 0:w], True, False), reads=["WU", "ust%d" % ub], writes=[kp])
                        S.op("pe", _mm(py[:, 0:w], WX[rs, gp, 0, :], XF[pb][rs, 0, 0:w], False, False), reads=["WX", "XF" + PB], writes=[kp])
                        S.op("pe", _mm(py[:, 0:w], WX[rs, gp, 1, :], XF[pb][rs, 1, 0:w], False, False), reads=["WX", "XF" + PB], writes=[kp])
                        S.op("pe", _mm(py[:, 0:w], WX[rs, 16 + gp, 0, :], XBt[pb][rs, 0, 0:w], False, False), reads=["WX", "XBt" + PB], writes=[kp])
                        S.op("pe", _mm(py[:, 0:w], WX[rs, 16 + gp, 1, :], XBt[pb][rs, 1, 0:w], False, True), reads=["WX", "XBt" + PB], writes=[kp])
                        S.op("act", _act(gst[:, g, 0:w], py[:, 0:w], AF.Gelu), reads=[kp], writes=["gst"])
                if d == 1:
                    continue
                for s_ in range(w // 16):
                    tok0 = 8 * n0 + 128 * s_
                    st = tok0 // 512
                    sub = (tok0 % 512) // 128
                    if st == 0:
                        continue
                    eb = 0
                    S.op("dve", _tt(ex[eb][:], gst[:, :, s_ * 16:(s_ + 1) * 16].unsqueeze(3).broadcast_to([128, 32, 16, 8]),
                                    maskJb[:].unsqueeze(1).unsqueeze(1).broadcast_to([128, 32, 16, 8]), ALU.mult),
                         reads=["gst", "maskJb"], writes=["ex%d" % eb])
                    for q in range(4):
                        for g8 in range(8):
                            S.op("pe", _mm(psG[:, q, :], selC[:, g8, :], ex[eb][:, 8 * q + g8, :, :].rearrange("p n j -> p (n j)"),
                                           g8 == 0, g8 == 7), reads=["selC", "ex%d" % eb], writes=["psG"])
                    S.op("act", _acopy(gel[:, :, sub * 128:(sub + 1) * 128], psG[:]), reads=["psG"], writes=["gel"])
                    if sub != 3:
                        continue
                    c0 = st * 512
                    S.op("sp", _dma(gat[:], Z[3072:4096, c0:c0 + 512].rearrange("(q p) t -> p q t", p=128)),
                         reads=["Z"], writes=["gat"], dma=True)
                    for c in range(8):
                        pb2 = 0
                        pa_, pg_ = psb[0], psb[1]
                        for k_ in range(4):
                            S.op("pe", _mm(pa_[:], wglu[:, k_, c * 128:(c + 1) * 128], gel[:, k_, :], k_ == 0, k_ == 3),
                                 reads=["wglu", "gel"], writes=["psb0"])
                        for k_ in range(4):
                            S.op("pe", _mm(pg_[:], wglu[:, k_, 1024 + c * 128:1024 + (c + 1) * 128], gel[:, k_, :], k_ == 0, k_ == 3),
                                 reads=["wglu", "gel"], writes=["psb1"])
                        S.op("act", _act(sgb[pb2][:], pg_[:], AF.Sigmoid), reads=["psb1"], writes=["sgb%d" % pb2])
                        S.op("act", _act(sga[pb2][:], gat[:, c, :], AF.Sigmoid), reads=["gat"], writes=["sga%d" % pb2])
                        S.op("dve", _tt(yab[pb2][:], pa_[:], sgb[pb2][:], ALU.mult), reads=["psb0", "sgb%d" % pb2], writes=["yab%d" % pb2])
                        S.op("dve", _tt(mao[:, c, :], yab[pb2][:], sga[pb2][:], ALU.mult), reads=["yab%d" % pb2, "sga%d" % pb2],
                             writes=["mao"])
                    S.op("pool", _dma(MA[:, c0:c0 + 512].rearrange("(q p) t -> p q t", p=128), mao[:]),
                         reads=["mao"], writes=["MA"], dma=True)
        S.fence()
        S.emit()

    with ExitStack() as es:
        def T(name, shape, dt):
            return es.enter_context(nc.sbuf_tensor("C_" + name, shape, dt))

        def P(name, shape, dt):
            return es.enter_context(nc.psum_tensor("C_" + name, shape, dt))

        whg = T("whg", [128, 4, D], BF16)
        wout = T("wout", [128, 8, D], BF16)
        wst = T("wst", [128, D], F32)
        masks = T("masks", [64, 128], F32)
        identf = T("identf", [128, 128], F32)
        identb = T("identb", [128, 128], BF16)
        onesb = T("onesb", [128, 128], BF16)
        lb = T("lb", [128, 4], F32)
        oml = T("oml", [128, 4], F32)
        lb1 = T("lb1", [128, 4], F32)
        hgn = T("hgn", [128, 4], F32)
        Sst = [T("Sst%d" % h, [128, 128], F32) for h in range(4)]
        zeros = T("zeros", [128, 64], F32)
        Sall = T("Sall", [128, 1024], F32)
        kvs = T("kvs", [128, 1024], F32)
        decz = T("decz", [128, 8], F32)
        dzf = T("dzf", [128, 1024], F32)
        NB = 3
        zq = [T("zq%d" % i, [128, 512], BF16) for i in range(NB)]
        zf = [T("zf%d" % i, [128, 512], BF16) for i in range(NB)]
        zo = [T("zo%d" % i, [128, 512], BF16) for i in range(NB)]
        Vc = [T("Vc%d" % i, [64, 8, 128], BF16) for i in range(NB)]
        qs = [T("qs%d" % i, [128, 512], F32) for i in range(NB)]
        gs = [T("gs%d" % i, [128, 512], F32) for i in range(NB)]
        ks = [T("ks%d" % i, [128, 512], F32) for i in range(NB)]
        Pc = [T("Pc%d" % i, [128, 512], F32) for i in range(NB)]
        rP = [T("rP%d" % i, [128, 512], F32) for i in range(NB)]
        dec = [T("dec%d" % i, [128, 8], F32) for i in range(NB)]
        qx = [T("qx%d" % i, [128, 512], BF16) for i in range(NB)]
        kx = [T("kx%d" % i, [128, 512], BF16) for i in range(NB)]
        ke = [T("ke%d" % i, [128, 512], BF16) for i in range(NB)]
        qi = [T("qi%d" % i, [128, 512], BF16) for i in range(NB)]
        scm = [T("scm%d" % i, [64, 8, 64], BF16) for i in range(NB)]
        ketm = [T("ketm%d" % i, [64, 8, 128], BF16) for i in range(NB)]
        Sbf = [T("Sbf%d" % i, [128, 8, 128], BF16) for i in range(NB)]
        osb = [T("osb%d" % i, [128, 512], F32) for i in range(NB)]
        obl = [T("obl%d" % i, [128, 512], F32) for i in range(NB)]
        sq = T("sq", [128, 512], BF16)
        rn = T("rn", [128, 512], F32)
        sgo = T("sgo", [128, 512], F32)
        yh = T("yh", [128, 4, 512], BF16)
        gbt = T("gbt", [128, 8, 512], BF16)
        mal = T("mal", [128, 8, 512], BF16)
        sgb = T("sgb", [128, 512], F32)
        tmpc = T("tmpc", [128, 512], F32)
        mixT = T("mixT", [128, 8, 512], BF16)
        xt = [T("xt%d" % i, [128, D], F32) for i in range(2)]
        hsb = [T("hsb%d" % i, [128, D], F32) for i in range(2)]
        psS = P("psS", [64, 8, 64], F32)
        psK = P("psK", [64, 8, 128], BF16)
        psKV = P("psKV", [128, 8, 128], F32)
        psO = P("psO", [128, 512], F32)
        psN = P("psN", [128, 512], F32)
        psY = [P("psY%d" % i, [128, 512], F32) for i in range(2)]

        S.op("sp", _dma(identf[:], I["identf"][:, :]), writes=["identf"], dma=True)
        S.op("dve", _copy(identb[:], identf[:]), reads=["identf"], writes=["identb"])
        S.op("dve", _memset(onesb[:], 1.0), writes=["onesb"])
        S.op("dve", _memset(zeros[:], 0.0), writes=["zeros"])
        S.op("sp", _dma(masks[:], I["masks"][:, :]), writes=["masks"], dma=True)
        S.op("sp", _dma(lb[:], I["lb0"][:, :]), writes=["lb"], dma=True)
        S.op("sp", _dma(lb1[:], I["lb1"][:, :]), writes=["lb1"], dma=True)
        S.op("sp", _dma(hgn[:], I["hgn"][:, :]), writes=["hgn"], dma=True)
        S.op("dve", _tt(lb[:], lb[:], lb1[:], ALU.subtract), reads=["lb", "lb1"], writes=["lb"])
        S.op("act", _act(lb[:], lb[:], AF.Sigmoid), reads=["lb"], writes=["lb"])
        S.op("dve", _ts(oml[:], lb[:], -1.0, 1.0, ALU.mult, ALU.add), reads=["lb"], writes=["oml"])
        for k in range(4):
            S.op("sp", _dma(wst[:], I["w_hg"][k * 128:(k + 1) * 128, :]), writes=["wst"], dma=True)
            S.op("act", _acopy(whg[:, k, :], wst[:]), reads=["wst"], writes=["whg"])
        for k in range(8):
            S.op("sp", _dma(wst[:], I["w_out"][k * 128:(k + 1) * 128, :]), writes=["wst"], dma=True)
            S.op("act", _acopy(wout[:, k, :], wst[:]), reads=["wst"], writes=["wout"])

        cnt = 0
        xcnt = [0]
        items = []
        for d in (1, 0):
            sts = list(range(NT - 1, 0, -1)) if d == 1 else list(range(NT))
            for si, st in enumerate(sts):
                for h in range(4):
                    items.append(dict(d=d, st=st, h=h, bi=cnt % NB, first=(si == 0)))
                    cnt += 1

        def prep(it):
            d, st, h, bi = it["d"], it["st"], it["h"], it["bi"]
            B = str(bi)
            c0 = st * 512
            full = (d == 0 and st >= 1)
            frow = 1536 if d == 1 else 1024
            S.op("sp", _dma(zq[bi][:], Z[512 + h * 128:512 + (h + 1) * 128, c0:c0 + 512]), reads=["Z"],
                 writes=["zq" + B], dma=True)
            S.op("sp", _dma(zf[bi][:], Z[frow + h * 128:frow + (h + 1) * 128, c0:c0 + 512]), reads=["Z"],
                 writes=["zf" + B], dma=True)
            S.op("sp", _dma(Vc[bi][:], VTM[c0:c0 + 512, h * 128:(h + 1) * 128].rearrange("(c s) v -> s c v", s=64)),
                 reads=["VTM"], writes=["Vc" + B], dma=True)
            if full:
                S.op("sp", _dma(zo[bi][:], Z[2560 + h * 128:2560 + (h + 1) * 128, c0:c0 + 512]), reads=["Z"],
                     writes=["zo" + B], dma=True)
                S.op("sp", _dma(obl[bi][:], OB[h * 128:(h + 1) * 128, c0:c0 + 512]), reads=["OB"],
                     writes=["obl" + B], dma=True)
            qs_, gs_, ks_, Pc_, rP_, dec_ = qs[bi], gs[bi], ks[bi], Pc[bi], rP[bi], dec[bi]
            S.op("act", _act(qs_[:], zq[bi][:], AF.Sigmoid), reads=["zq" + B], writes=["qs" + B])
            S.op("pool", _tt(qs_[:], qs_[:], zq[bi][:], ALU.mult), reads=["qs" + B, "zq" + B], writes=["qs" + B])
            S.op("act", _act(gs_[:], zf[bi][:], AF.Sigmoid), reads=["zf" + B], writes=["gs" + B])
            S.op("dve", _ts(gs_[:], gs_[:], oml[:, h:h + 1], lb[:, h:h + 1], ALU.mult, ALU.add),
                 reads=["gs" + B, "oml", "lb"], writes=["gs" + B])
            S.op("dve", _ts(ks_[:], gs_[:], -1.0, 1.0, ALU.mult, ALU.add), reads=["gs" + B], writes=["ks" + B])
            Pc3 = Pc_[:].rearrange("p (c j) -> p c j", j=64)
            gs3 = gs_[:].rearrange("p (c j) -> p c j", j=64)
            if d == 0:
                for ch in range(8):
                    sl = slice(ch * 64, (ch + 1) * 64)
                    S.op("dve", _scan(Pc_[:, sl], gs_[:, sl], zeros[:, 0:64], 1.0), reads=["gs" + B, "zeros"], writes=["Pc" + B])
                S.op("dve", _tt(qx[bi][:], qs_[:], Pc_[:], ALU.mult), reads=["qs" + B, "Pc" + B], writes=["qx" + B])
                S.op("dve", _recip(rP_[:], Pc_[:]), reads=["Pc" + B], writes=["rP" + B])
                S.op("dve", _tt(kx[bi][:], ks_[:], rP_[:], ALU.mult), reads=["ks" + B, "rP" + B], writes=["kx" + B])
                S.op("dve", _copy(dec_[:].unsqueeze(2), Pc3[:, :, 63:64]), reads=["Pc" + B], writes=["dec" + B])
                S.op("dve", _tt(ke[bi][:].rearrange("p (c j) -> p c j", j=64),
                                kx[bi][:].rearrange("p (c j) -> p c j", j=64),
                                dec_[:].unsqueeze(2).broadcast_to([128, 8, 64]), ALU.mult),
                     reads=["kx" + B, "dec" + B], writes=["ke" + B])
            else:
                S.op("dve", _memset(Pc3[:, :, 0:1], 1.0), writes=["Pc" + B])
                for ch in range(8):
                    S.op("dve", _scan(Pc_[:, ch * 64 + 1:(ch + 1) * 64], gs_[:, ch * 64:(ch + 1) * 64 - 1],
                                      zeros[:, 0:63], 1.0), reads=["gs" + B, "zeros"], writes=["Pc" + B])
                S.op("dve", _recip(rP_[:], Pc_[:]), reads=["Pc" + B], writes=["rP" + B])
                S.op("dve", _tt(qx[bi][:], qs_[:], rP_[:], ALU.mult), reads=["qs" + B, "rP" + B], writes=["qx" + B])
                S.op("dve", _tt(kx[bi][:], ks_[:], Pc_[:], ALU.mult), reads=["ks" + B, "Pc" + B], writes=["kx" + B])
                S.op("dve", _tt(dec_[:].unsqueeze(2), Pc3[:, :, 63:64], gs3[:, :, 63:64], ALU.mult),
                     reads=["Pc" + B, "gs" + B], writes=["dec" + B])
                S.op("dve", _tt(qi[bi][:].rearrange("p (c j) -> p c j", j=64),
                                qx[bi][:].rearrange("p (c j) -> p c j", j=64),
                                dec_[:].unsqueeze(2).broadcast_to([128, 8, 64]), ALU.mult),
                     reads=["qx" + B, "dec" + B], writes=["qi" + B])

        def rest(it):
            d, st, h, bi = it["d"], it["st"], it["h"], it["bi"]
            B = str(bi)
            c0 = st * 512
            full = (d == 0 and st >= 1)
            mk = masks[:, 64:128] if d == 1 else masks[:, 0:64]
            mkb = mk.unsqueeze(1).broadcast_to([64, 8, 64])
            dec_ = dec[bi]
            if it["first"]:
                S.op("dve", _memset(Sst[h][:], 0.0), writes=["Sst%d" % h])
            if full and h == 0:
                S.op("sp", _dma(gbt[:], Z[4096:5120, c0:c0 + 512].rearrange("(q p) t -> p q t", p=128)),
                     reads=["Z"], writes=["gbt"], dma=True)
                S.op("sp", _dma(mal[:], MA[:, c0:c0 + 512].rearrange("(q p) t -> p q t", p=128)),
                     reads=["MA"], writes=["mal"], dma=True)
            if d == 0:
                qiT, keT = qx[bi], ke[bi]
                kq, kk_ = "qx" + B, "ke" + B
            else:
                qiT, keT = qi[bi], kx[bi]
                kq, kk_ = "qi" + B, "kx" + B
            for ch in range(8):
                sl = slice(ch * 64, (ch + 1) * 64)
                S.op("pe", _mm(psS[:, ch, :], kx[bi][:, sl], qx[bi][:, sl], True, True),
                     reads=["kx" + B, "qx" + B], writes=["psS"])
            for ch in range(8):
                sl = slice(ch * 64, (ch + 1) * 64)
                S.op("pe", _tr(psK[:, ch, :], keT[:, sl], identb[:]), reads=[kk_, "identb"], writes=["psK"])
            S.op("dve", _tt(scm[bi][:], psS[:], mkb, ALU.mult), reads=["psS", "masks"], writes=["scm" + B])
            S.op("act", _acopy(ketm[bi][:], psK[:]), reads=["psK"], writes=["ketm" + B])
            for ch in range(8):
                S.op("pe", _mm(psKV[:, ch, :], ketm[bi][:, ch, :], Vc[bi][:, ch, :], True, True),
                     reads=["ketm" + B, "Vc" + B], writes=["psKV"])
            kvv = psKV[:].rearrange("p c v -> p v c")
            sal = Sall[:].rearrange("p (v c) -> p v c", c=8)
            c_in = 7 if d == 1 else 0
            S.op("dve", _copy(decz[:], dec_[:]), reads=["dec" + B], writes=["decz"])
            S.op("dve", _memset(decz[:, c_in:c_in + 1], 0.0), writes=["decz"])
            S.op("dve", _copy(kvs[:].rearrange("p (v c) -> p v c", c=8), kvv), reads=["psKV"], writes=["kvs"])
            S.op("dve", _stt(kvs[:].rearrange("p (v c) -> p v c", c=8)[:, :, c_in], Sst[h][:], dec_[:, c_in:c_in + 1],
                             kvs[:].rearrange("p (v c) -> p v c", c=8)[:, :, c_in], ALU.mult, ALU.add),
                 reads=["Sst%d" % h, "dec" + B, "kvs"], writes=["kvs"])
            S.op("dve", _copy(dzf[:].rearrange("p (v c) -> p v c", c=8), decz[:].unsqueeze(1).broadcast_to([128, 128, 8])),
                 reads=["decz"], writes=["dzf"])
            if d == 0:
                S.op("dve", _scan(Sall[:], dzf[:], kvs[:], 0.0), reads=["dzf", "kvs"], writes=["Sall"])
            else:
                S.op("dve", _scan(_rev(Sall[:]), _rev(dzf[:]), _rev(kvs[:]), 0.0), reads=["dzf", "kvs"], writes=["Sall"])
            salc = Sall[:].rearrange("p (v c) -> p c v", c=8)
            if d == 0:
                S.op("act", _acopy(Sbf[bi][:, 0, :], Sst[h][:]), reads=["Sst%d" % h], writes=["Sbf" + B])
                S.op("act", _acopy(Sbf[bi][:, 1:8, :], salc[:, 0:7, :]), reads=["Sall"], writes=["Sbf" + B])
                S.op("dve", _copy(Sst[h][:], salc[:, 7, :]), reads=["Sall"], writes=["Sst%d" % h])
            else:
                S.op("act", _acopy(Sbf[bi][:, 7, :], Sst[h][:]), reads=["Sst%d" % h], writes=["Sbf" + B])
                S.op("act", _acopy(Sbf[bi][:, 0:7, :], salc[:, 1:8, :]), reads=["Sall"], writes=["Sbf" + B])
                S.op("dve", _copy(Sst[h][:], salc[:, 0, :]), reads=["Sall"], writes=["Sst%d" % h])
            if d == 0 and st == 0:
                return
            for ch in range(8):
                sl = slice(ch * 64, (ch + 1) * 64)
                S.op("pe", _mm(psO[:, sl], Vc[bi][:, ch, :], scm[bi][:, ch, :], True, False),
                     reads=["Vc" + B, "scm" + B], writes=["psO"])
                S.op("pe", _mm(psO[:, sl], Sbf[bi][:, ch, :], qiT[:, sl], False, True),
                     reads=["Sbf" + B, kq], writes=["psO"])
            if d == 1:
                S.op("act", _acopy(osb[bi][:], psO[:]), reads=["psO"], writes=["osb" + B])
                S.op("pool", _dma(OB[h * 128:(h + 1) * 128, c0:c0 + 512], osb[bi][:]), reads=["osb" + B],
                     writes=["OB"], dma=True)
                return
            S.op("dve", _tt(osb[bi][:], psO[:], obl[bi][:], ALU.add), reads=["psO", "obl" + B], writes=["osb" + B])
            S.op("act", _act(sq[:], osb[bi][:], AF.Square), reads=["osb" + B], writes=["sq"])
            S.op("pe", _mm(psN[:], onesb[:], sq[:], True, True), reads=["onesb", "sq"], writes=["psN"])
            S.op("dve", _ts(rn[:], psN[:], 1.0 / 128, EPS, ALU.mult, ALU.add), reads=["psN"], writes=["rn"])
            S.op("act", _act(rn[:], rn[:], AF.Sqrt), reads=["rn"], writes=["rn"])
            S.op("dve", _recip(rn[:], rn[:]), reads=["rn"], writes=["rn"])
            S.op("dve", _tt(rn[:], rn[:], osb[bi][:], ALU.mult), reads=["rn", "osb" + B], writes=["rn"])
            S.op("act", _act(sgo[:], zo[bi][:], AF.Sigmoid), reads=["zo" + B], writes=["sgo"])
            S.op("pool", _tt(sgo[:], sgo[:], zo[bi][:], ALU.mult), reads=["sgo", "zo" + B], writes=["sgo"])
            S.op("dve", _stt(yh[:, h, :], rn[:], hgn[:, h:h + 1], sgo[:], ALU.mult, ALU.mult),
                 reads=["rn", "hgn", "sgo"], writes=["yh"])
            if h != 3:
                return
            for c in range(8):
                py = psY[c % 2]
                kp = "psY%d" % (c % 2)
                for k in range(4):
                    S.op("pe", _mm(py[:], whg[:, k, c * 128:(c + 1) * 128], yh[:, k, :], k == 0, k == 3),
                         reads=["whg", "yh"], writes=[kp])
                S.op("act", _act(sgb[:], gbt[:, c, :], AF.Sigmoid), reads=["gbt"], writes=["sgb"])
                S.op("dve", _tt(tmpc[:], py[:], sgb[:], ALU.mult), reads=[kp, "sgb"], writes=["tmpc"])
                S.op("dve", _tt(mixT[:, c, :], tmpc[:], mal[:, c, :], ALU.add), reads=["tmpc", "mal"], writes=["mixT"])
            for j in range(4):
                xb = xcnt[0] % 2
                xcnt[0] += 1
                r0 = (st - 1) * 512 + j * 128
                S.op("sp", _dma(xt[xb][:], I["xs"][r0:r0 + 128, :]), writes=["xt%d" % xb], dma=True)
                for hf in range(2):
                    py = psY[hf]
                    kp = "psY%d" % hf
                    for k in range(8):
                        S.op("pe", _mm(py[:], mixT[:, k, j * 128:(j + 1) * 128], wout[:, k, hf * 512:(hf + 1) * 512],
                                       k == 0, k == 7), reads=["mixT", "wout"], writes=[kp])
                    S.op("dve", _tt(hsb[xb][:, hf * 512:(hf + 1) * 512], py[:], xt[xb][:, hf * 512:(hf + 1) * 512], ALU.add),
                         reads=[kp, "xt%d" % xb], writes=["hsb%d" % xb])
                S.op("pool", _dma(HS1[r0:r0 + 128, :], hsb[xb][:]), reads=["hsb%d" % xb], writes=["HS1"], dma=True)

        conv = list(range(128))
        cpos = [0]

        def conv_step():
            if cpos[0] >= len(conv):
                return
            r = conv[cpos[0]]
            cpos[0] += 1
            S.op("pool", _dma(UVB[r * 128:(r + 1) * 128, :], I["puv"][r * 128:(r + 1) * 128, :]), writes=["UVB"], dma=True)

        per_it = 1
        prep(items[0])
        for ii, it in enumerate(items):
            if ii + 1 < len(items):
                prep(items[ii + 1])
            rest(it)
            for _ in range(per_it):
                conv_step()
        while cpos[0] < len(conv):
            conv_step()
        S.fence()
        S.emit()

    with ExitStack() as es:
        def T(name, shape, dt):
            return es.enter_context(nc.sbuf_tensor("D_" + name, shape, dt))

        def P(name, shape, dt):
            return es.enter_context(nc.psum_tensor("D_" + name, shape, dt))

        wq = T("wq", [128, 8, 2048], BF16)
        keysT = T("keysT", [128, 16, 128], BF16)
        g2b = T("g2b", [128, D], F32)
        gfb = T("gfb", [128, D], F32)
        identf = T("identf", [128, 128], F32)
        identb = T("identb", [128, 128], BF16)
        iota16 = T("iota16", [128, 16], F32)
        tk = T("tk", [128, ND], I32)
        hs = [T("hs%d" % i, [128, D], F32) for i in range(2)]
        h2 = [T("h2%d" % i, [128, D], F32) for i in range(2)]
        ei = [T("ei%d" % i, [128, 128], I32) for i in range(2)]
        gate = [T("gate%d" % i, [128, 8, 16], F32) for i in range(2)]
        h2T = T("h2T", [128, 8, 128], BF16)
        qT = T("qT", [128, 16, 128], BF16)
        sc = T("sc", [128, 16, 128], F32)
        sc2 = T("sc2", [128, 16, 128], F32)
        top = T("top", [128, 8, 2, 16], F32)
        tix = T("tix", [128, 8, 2, 16], U32)
        tixf = T("tixf", [128, 8, 2, 16], F32)
        cand = T("cand", [128, 8, 256], F32)
        cand2 = T("cand2", [128, 8, 256], F32)
        ctop = T("ctop", [128, 8, 16], F32)
        cix = T("cix", [128, 8, 16], U32)
        cab = T("cab", [128, 8, 16], U32)
        caf = T("caf", [128, 8, 16], F32)
        cbf = T("cbf", [128, 8, 16], F32)
        eq = T("eq", [128, 8, 16, 16], F32)
        i1f = T("i1f", [128, 8, 16], F32)
        i2f = T("i2f", [128, 8, 16], F32)
        ssum = T("ssum", [128, 8], F32)
        sm = T("sm", [128, 8], F32)
        h2bf = [T("h2bf%d" % i, [128, D], BF16) for i in range(2)]
        actv = T("actv", [128, 128], F32)
        wgt = T("wgt", [128, 128], F32)
        dg = [T("dg%d" % i, [128, 8, 128], BF16) for i in range(2)]
        acc = T("acc", [128, D], F32)
        junkb = T("junkb", [128, D], BF16)
        junka = T("junka", [128, D], BF16)
        prod = [T("prod%d" % i, [128, D], BF16) for i in range(4)]
        outb = T("outb", [128, D], F32)
        pT = P("pT", [128, 8, 128], BF16)
        pQ = P("pQ", [128, 16, 128], F32)
        pacc = P("pacc", [128, D], F32)

        S.op("sp", _dma(identf[:], I["identf"][:, :]), writes=["identf"], dma=True)
        S.op("dve", _copy(identb[:], identf[:]), reads=["identf"], writes=["identb"])
        S.op("sp", _dma(iota16[:], I["iota16"][:, :]), writes=["iota16"], dma=True)
        S.op("sp", _dma(tk[:], I["tokd"][:, :]), writes=["tk"], dma=True)
        S.op("sp", _dma(g2b[:], I["g2b"][:, :]), writes=["g2b"], dma=True)
        S.op("sp", _dma(gfb[:], I["gfb"][:, :]), writes=["gfb"], dma=True)
        with ExitStack() as esp:
            wst = esp.enter_context(nc.sbuf_tensor("Dp_wst", [128, 2048], F32))
            for k in range(8):
                S.op("sp", _dma(wst[:], I["wq"][k * 128:(k + 1) * 128, :]), writes=["wst"], dma=True)
                S.op("act", _acopy(wq[:, k, :], wst[:]), reads=["wst"], writes=["wq"])
            S.op("sp", _dma(wst[:], I["keysT"][:, :]), writes=["wst"], dma=True)
            S.op("act", _acopy(keysT[:].rearrange("p a b -> p (a b)"), wst[:]), reads=["wst"], writes=["keysT"])
            S.fence()
            S.emit()
        NSB = 15
        uvg = [T("uvg%d" % i, [128, 2 * D], BF16) for i in range(NSB)]

        def stage1(i):
            b = i % 2
            B = str(b)
            S.op("pool", _gather(hs[b][:], HS1[:, :], tk[:, i:i + 1]), reads=["tk", "HS1"], writes=["hs" + B], dma=True)
            S.op("dve", _memset(ssum[:, 0:1], 0.0), writes=["ssum"])
            S.op("dve", _stt(junkb[:], hs[b][:], 1.0, hs[b][:], ALU.mult, ALU.mult, accum_out=ssum[:, 0:1]),
                 reads=["hs" + B], writes=["junkb", "ssum"])
            S.op("dve", _ts(sm[:, 0:1], ssum[:, 0:1], 1.0 / D, EPS, ALU.mult, ALU.add), reads=["ssum"], writes=["sm"])
            S.op("act", _act(sm[:, 0:1], sm[:, 0:1], AF.Sqrt), reads=["sm"], writes=["sm"])
            S.op("dve", _recip(sm[:, 0:1], sm[:, 0:1]), reads=["sm"], writes=["sm"])
            S.op("dve", _stt(h2[b][:], hs[b][:], sm[:, 0:1], g2b[:], ALU.mult, ALU.mult), reads=["hs" + B, "sm", "g2b"],
                 writes=["h2" + B])
            S.op("act", _acopy(h2bf[b][:], h2[b][:]), reads=["h2" + B], writes=["h2bf" + B])
            for k in range(8):
                S.op("pe", _tr(pT[:, k, :], h2bf[b][:, k * 128:(k + 1) * 128], identb[:]), reads=["h2bf" + B, "identb"], writes=["pT"])
            S.op("act", _acopy(h2T[:], pT[:]), reads=["pT"], writes=["h2T"])
            for hp in range(16):
                for k in range(8):
                    S.op("pe", _mm(pQ[:, hp, :], wq[:, k, hp * 128:(hp + 1) * 128], h2T[:, k, :], k == 0, k == 7),
                         reads=["wq", "h2T"], writes=["pQ"])
            S.op("act", _acopy(qT[:], pQ[:]), reads=["pQ"], writes=["qT"])
            for hp in range(16):
                S.op("pe", _mm(pQ[:, hp, :], qT[:, hp, :], keysT[:, hp, :], True, True), reads=["qT", "keysT"], writes=["pQ"])
            S.op("act", _acopy(sc[:], pQ[:]), reads=["pQ"], writes=["sc"])
            HP = [(hp, hp // 2, hp % 2) for hp in range(16)]
            for hp, h_, p_ in HP:
                S.op("dve", lambda e, h_=h_, p_=p_, hp=hp: e.max(out=top[:, h_, p_, 0:8], in_=sc[:, hp, :]),
                     reads=["sc"], writes=["top%d" % hp])
            for hp, h_, p_ in HP:
                S.op("dve", lambda e, h_=h_, p_=p_, hp=hp: e.max_index(out=tix[:, h_, p_, 0:8], in_max=top[:, h_, p_, 0:8],
                                                                       in_values=sc[:, hp, :]),
                     reads=["sc", "top%d" % hp], writes=["tix%d" % hp])
            for hp, h_, p_ in HP:
                S.op("dve", lambda e, h_=h_, p_=p_, hp=hp: e.match_replace(out=sc2[:, hp, :], in_to_replace=top[:, h_, p_, 0:8],
                                                                           in_values=sc[:, hp, :], imm_value=-1e30),
                     reads=["sc", "top%d" % hp], writes=["sc2_%d" % hp])
            for hp, h_, p_ in HP:
                S.op("dve", lambda e, h_=h_, p_=p_, hp=hp: e.max(out=top[:, h_, p_, 8:16], in_=sc2[:, hp, :]),
                     reads=["sc2_%d" % hp], writes=["topb%d" % hp])
            for hp, h_, p_ in HP:
                S.op("dve", lambda e, h_=h_, p_=p_, hp=hp: e.max_index(out=tix[:, h_, p_, 8:16], in_max=top[:, h_, p_, 8:16],
                                                                       in_values=sc2[:, hp, :]),
                     reads=["sc2_%d" % hp, "topb%d" % hp], writes=["tixb%d" % hp])
            tk_all = ["top%d" % x for x in range(16)] + ["topb%d" % x for x in range(16)]
            ti_all = ["tix%d" % x for x in range(16)] + ["tixb%d" % x for x in range(16)]
            S.op("dve", _copy(tixf[:], tix[:]), reads=ti_all, writes=["tixf"])
            cand4 = cand[:].rearrange("p h (a b) -> p h a b", a=16)
            S.op("dve", _tt(cand4, top[:, :, 0, :].unsqueeze(3).broadcast_to([128, 8, 16, 16]),
                            top[:, :, 1, :].unsqueeze(2).broadcast_to([128, 8, 16, 16]), ALU.add),
                 reads=tk_all, writes=["cand"])
            for h_ in range(8):
                S.op("dve", lambda e, h_=h_: e.max(out=ctop[:, h_, 0:8], in_=cand[:, h_, :]), reads=["cand"], writes=["ctop%d" % h_])
            for h_ in range(8):
                S.op("dve", lambda e, h_=h_: e.max_index(out=cix[:, h_, 0:8], in_max=ctop[:, h_, 0:8], in_values=cand[:, h_, :]),
                     reads=["cand", "ctop%d" % h_], writes=["cix%d" % h_])
            for h_ in range(8):
                S.op("dve", lambda e, h_=h_: e.match_replace(out=cand2[:, h_, :], in_to_replace=ctop[:, h_, 0:8],
                                                             in_values=cand[:, h_, :], imm_value=-1e30),
                     reads=["cand", "ctop%d" % h_], writes=["cand2_%d" % h_])
            for h_ in range(8):
                S.op("dve", lambda e, h_=h_: e.max(out=ctop[:, h_, 8:16], in_=cand2[:, h_, :]), reads=["cand2_%d" % h_],
                     writes=["ctopb%d" % h_])
            for h_ in range(8):
                S.op("dve", lambda e, h_=h_: e.max_index(out=cix[:, h_, 8:16], in_max=ctop[:, h_, 8:16], in_values=cand2[:, h_, :]),
                     reads=["cand2_%d" % h_, "ctopb%d" % h_], writes=["cixb%d" % h_])
            S.op("dve", lambda e: e.tensor_single_scalar(out=cab[:], in_=cix[:], scalar=4, op=ALU.logical_shift_right),
                 reads=["cix%d" % x for x in range(8)] + ["cixb%d" % x for x in range(8)], writes=["cab"])
            S.op("dve", _copy(caf[:], cab[:]), reads=["cab"], writes=["caf"])
            S.op("dve", lambda e: e.tensor_single_scalar(out=cab[:], in_=cix[:], scalar=15, op=ALU.bitwise_and),
                 reads=["cix%d" % x for x in range(8)] + ["cixb%d" % x for x in range(8)], writes=["cab"])
            S.op("dve", _copy(cbf[:], cab[:]), reads=["cab"], writes=["cbf"])
            io4 = iota16[:, :].unsqueeze(1).unsqueeze(1).broadcast_to([128, 8, 16, 16])
            for (src, half, dst, kd) in ((caf, 0, i1f, "i1f"), (cbf, 1, i2f, "i2f")):
                S.op("dve", _tt(eq[:], src[:].unsqueeze(3).broadcast_to([128, 8, 16, 16]), io4, ALU.is_equal),
                     reads=["caf", "cbf", "iota16"], writes=["eq"])
                S.op("dve", _tt(eq[:], eq[:], tixf[:, :, half, :].unsqueeze(2).broadcast_to([128, 8, 16, 16]), ALU.mult),
                     reads=["eq", "tixf"], writes=["eq"])
                S.op("dve", _rsum(dst[:], eq[:]), reads=["eq"], writes=[kd])
            S.op("dve", _stt(i1f[:], i1f[:], 128.0, i2f[:], ALU.mult, ALU.add), reads=["i1f", "i2f"], writes=["i1f"])
            S.op("dve", _ts(i1f[:], i1f[:], 0.0, 16383.0, ALU.max, ALU.min), reads=["i1f"], writes=["i1f"])
            S.op("dve", _copy(ei[b][:].rearrange("p (h j) -> p h j", h=8), i1f[:]), reads=["i1f"], writes=["ei" + B])
            S.op("dve", _tt(gate[b][:], ctop[:], ctop[:, :, 0:1].broadcast_to([128, 8, 16]), ALU.subtract),
                 reads=["ctop%d" % x for x in range(8)] + ["ctopb%d" % x for x in range(8)], writes=["gate" + B])
            S.op("act", _act(gate[b][:], gate[b][:], AF.Exp), reads=["gate" + B], writes=["gate" + B])
            S.op("dve", _rsum(ssum[:], gate[b][:]), reads=["gate" + B], writes=["ssum"])
            S.op("dve", _recip(ssum[:], ssum[:]), reads=["ssum"], writes=["ssum"])
            S.op("dve", _tt(gate[b][:], gate[b][:], ssum[:].unsqueeze(2).broadcast_to([128, 8, 16]), ALU.mult),
                 reads=["gate" + B, "ssum"], writes=["gate" + B])

        gcnt = [0]
        pcnt = [0]

        def stage2(i, pend=()):
            b = i % 2
            B = str(b)
            pend = list(pend)
            per_slot = -(-len(pend) // 112) if pend else 0
            ppos = [0]

            def drain(n):
                for o in pend[ppos[0]:ppos[0] + n]:
                    S.op(*o)
                ppos[0] += n
            S.op("dve", _memset(actv[:], 0.0), writes=["actv%d" % x for x in range(128)])
            slots = []
            for j in range(128):
                gb = gcnt[0] % NSB
                gcnt[0] += 1
                slots.append(gb)
                S.op("pool", _gather(uvg[gb][:], UVB[:, :], ei[b][:, j:j + 1]), reads=["ei" + B, "UVB"], writes=["uvg%d" % gb],
                     dma=True)
                if j % 8 in (0, 1, 2, 4, 5):
                    pi_ = pcnt[0] % 4
                    pcnt[0] += 1
                    S.op("dve", _tt(prod[pi_][:], uvg[gb][:, 0:D], h2bf[b][:], ALU.mult), reads=["uvg%d" % gb, "h2bf" + B],
                         writes=["prod%d" % pi_])
                    S.op("act", lambda e, pi_=pi_, j=j: e.activation(out=junka[:], in_=prod[pi_][:], func=AF.Copy,
                                                                   accum_out=actv[:, j:j + 1]),
                         reads=["prod%d" % pi_], writes=["junka", "actv%d" % j])
                else:
                    S.op("dve", _stt(junkb[:], uvg[gb][:, 0:D], 1.0, h2bf[b][:], ALU.mult, ALU.mult, accum_out=actv[:, j:j + 1]),
                         reads=["uvg%d" % gb, "h2bf" + B], writes=["junkb", "actv%d" % j])
                drain(per_slot)
                if j % 8 == 7:
                    g0 = j - 7
                    db = (j // 8) % 2
                    S.op("act", _act(wgt[:, g0:j + 1], actv[:, g0:j + 1], AF.Gelu), reads=["actv%d" % x for x in range(g0, j + 1)],
                         writes=["wgt"])
                    S.op("dve", _tt(wgt[:, g0:j + 1], wgt[:, g0:j + 1], gate[b][:].rearrange("p h j -> p (h j)")[:, g0:j + 1], ALU.mult),
                         reads=["wgt", "gate" + B], writes=["wgt"])
                    for s_ in range(8):
                        S.op("act", _act(dg[db][:, s_, :], identb[:], AF.Copy, scale=wgt[:, g0 + s_:g0 + s_ + 1]),
                             reads=["identb", "wgt"], writes=["dg%d_%d" % (db, s_)])
                    for s_ in range(8):
                        jj = g0 + s_
                        sb_ = slots[jj]
                        for hf in range(2):
                            S.op("pe", _mm(pacc[:, hf * 512:(hf + 1) * 512], dg[db][:, s_, :],
                                           uvg[sb_][:, D + hf * 512:D + (hf + 1) * 512], jj == 0, jj == 127),
                                 reads=["dg%d_%d" % (db, s_), "uvg%d" % sb_], writes=["pacc"])
            drain(len(pend))
            S.op("dve", _tt(acc[:], pacc[:], hs[b][:], ALU.add), reads=["pacc", "hs" + B], writes=["acc"])
            S.op("dve", _memset(sm[:, 1:2], 0.0), writes=["sm1"])
            S.op("dve", _stt(junkb[:], acc[:], 1.0, acc[:], ALU.mult, ALU.mult, accum_out=sm[:, 1:2]), reads=["acc"],
                 writes=["junkb", "sm1"])
            S.op("dve", _ts(sm[:, 1:2], sm[:, 1:2], 1.0 / D, EPS, ALU.mult, ALU.add), reads=["sm1"], writes=["sm1"])
            S.op("act", _act(sm[:, 1:2], sm[:, 1:2], AF.Sqrt), reads=["sm1"], writes=["sm1"])
            S.op("dve", _recip(sm[:, 1:2], sm[:, 1:2]), reads=["sm1"], writes=["sm1"])
            S.op("dve", _stt(outb[:], acc[:], sm[:, 1:2], gfb[:], ALU.mult, ALU.mult), reads=["acc", "sm1", "gfb"],
                 writes=["outb"])
            S.op("sp", _dma(OUT[i * 128:(i + 1) * 128, :], outb[:]), reads=["outb"], writes=["OUT"], dma=True)

        stage1(0)
        for i in range(ND):
            pend = []
            if i + 1 < ND:
                S.capture = pend
                stage1(i + 1)
                S.capture = None
            stage2(i, pend)
        S.fence()
        S.emit()

    top_es.close()
    return nc


def _common_inputs(inp):
    f = np.float32
    A = {}
    A["meta"] = np.ascontiguousarray(inp["meta"], f)
    A["g1c"] = np.ascontiguousarray(inp["norm1_g"][0].reshape(8, 128).T, f)
    A["w_in"] = np.ascontiguousarray(inp["w_in"][0], f)
    lre, lim, lst = inp["s5_lam_re"][0], inp["s5_lam_im"][0], inp["s5_log_step"][0]
    lst_full = np.broadcast_to(lst[:, :, None], (2, 32, 64))

    def col(a):
        a5 = a.reshape(2, 16, 2, 64)
        return np.ascontiguousarray(a5.transpose(2, 3, 0, 1).reshape(128, 32), f)

    A["lre_c"], A["lim_c"], A["lst_c"] = col(lre), col(lim), col(lst_full)

    def c8(cm):
        a = cm.reshape(16, 2, 16, 64).transpose(1, 3, 0, 2)
        a = np.broadcast_to(a[:, :, :, None, :], (2, 64, 16, 8, 16))
        return np.ascontiguousarray(a.reshape(128, 2048), f)

    def bc(bm):
        a = bm.reshape(16, 2, 64, 16).transpose(1, 2, 0, 3)
        a = np.broadcast_to(a[:, :, :, None, :], (2, 64, 16, 8, 16))
        return np.ascontiguousarray(a.reshape(128, 2048), f)

    A["cre8"], A["cim8"] = c8(inp["s5_c_re"][0]), c8(inp["s5_c_im"][0])
    A["brec"], A["bimc"] = bc(inp["s5_b_re"][0]), bc(inp["s5_b_im"][0])
    dd = inp["s5_d"][0].reshape(32, 16)
    A["drow"] = np.ascontiguousarray(np.broadcast_to(dd.T[None, :, :], (8, 16, 32)).reshape(128, 32), f)
    p = np.arange(128)
    cst8 = np.zeros((128, 32), f)
    cst8[:, 0:8] = (p[:, None] % 8 == np.arange(8)[None, :])
    cst8[:, 8:24] = (p[:, None] // 8 == np.arange(16)[None, :])
    cst8[:, 24:32] = (p[:, None] // 16 == np.arange(8)[None, :])
    A["cst8"] = cst8
    m = np.arange(128)
    selC = np.zeros((128, 8, 128), f)
    for g8 in range(8):
        selC[:, g8, :] = ((p[:, None] % 16) == (m[None, :] % 16)) & ((m[None, :] // 16) == g8)
    A["selC"] = selC.reshape(128, 1024)
    A["ramp16"] = np.ascontiguousarray(np.broadcast_to(np.arange(-7, 9, dtype=f)[None, :], (128, 16)), f)
    ii = p[:, None] // 16
    jj = m[None, :] // 16
    A["mfb"] = np.concatenate([(ii <= jj), (ii >= jj)], axis=1).astype(f)
    A["w_glu"] = np.ascontiguousarray(inp["w_glu"][0], f)
    A["lb0"] = np.ascontiguousarray(inp["hg_lb"][0].reshape(4, 128).T, f)
    A["lb1"] = np.ascontiguousarray(inp["hg_lb"][1].reshape(4, 128).T, f)
    A["hgn"] = np.ascontiguousarray(inp["hg_norm_g"][0].reshape(4, 128).T, f)
    A["w_hg"] = np.ascontiguousarray(inp["w_hg_out"][0], f)
    A["w_out"] = np.ascontiguousarray(inp["w_out"][0], f)
    A["g2b"] = np.ascontiguousarray(np.broadcast_to(inp["norm2_g"][0][None, :], (128, D)), f)
    A["gfb"] = np.ascontiguousarray(np.broadcast_to(inp["final_g"][None, :], (128, D)), f)
    A["wq"] = np.ascontiguousarray(inp["peer_wq"][0], f)
    kz = inp["peer_keys"][0].reshape(16, 128, 128)
    A["keysT"] = np.ascontiguousarray(kz.transpose(2, 0, 1).reshape(128, 2048), f)
    A["puv"] = np.ascontiguousarray(np.concatenate([inp["peer_u"][0], inp["peer_v"][0]], axis=1), f)
    A["identf"] = np.eye(128, dtype=f)
    r = np.arange(1, 129, dtype=f)
    A["ramps"] = np.ascontiguousarray(np.broadcast_to(np.concatenate([r, r[::-1]])[None, :], (128, 256)), f)
    s = np.arange(64)[:, None]
    t = np.arange(64)[None, :]
    A["masks"] = np.concatenate([(t >= s), (t <= s)], axis=1).astype(f)
    A["iota16"] = np.ascontiguousarray(np.broadcast_to(np.arange(16, dtype=f)[None, :], (128, 16)), f)
    return A


_PROG_CACHE = {}


def run_sequences(inp, seqs, assign, ND, debug=False):
    SEQ = seqs[0].shape[0]
    key = (SEQ, ND, debug)
    if key not in _PROG_CACHE:
        _PROG_CACHE[key] = build_program(SEQ, ND, debug)
    nc = _PROG_CACHE[key]
    A = _common_inputs(inp)
    in_maps = []
    for (si, t0, nt) in assign:
        m = dict(A)
        m["xs"] = np.ascontiguousarray(seqs[si], np.float32)
        tiles = [min(t0 + i, t0 + nt - 1) for i in range(ND)]
        tok = np.stack([np.arange(t * 128, (t + 1) * 128) for t in tiles], axis=1).astype(np.int32)
        m["tokd"] = np.ascontiguousarray(tok)
        in_maps.append(m)
    res = run_bass_kernel_spmd(nc, in_maps, core_ids=list(range(len(assign))))
    outs = [np.zeros((SEQ, D), np.float32) for _ in seqs]
    for ci, (si, t0, nt) in enumerate(assign):
        o = res.results[ci]["outd"]
        outs[si][t0 * 128:(t0 + nt) * 128] = o[:nt * 128]
    return outs, res


def kernel(**inputs):
    inp = {k: np.asarray(v) for k, v in inputs.items()}
    seqs = [inp["x_prompt"][0], inp["x_sample"][0], inp["x_sample"][1]]
    assign = [(0, 0, 43), (1, 0, 43), (2, 0, 64), (0, 43, 43), (1, 43, 43), (2, 64, 64), (0, 86, 42), (1, 86, 42)]
    outs, _ = run_sequences(inp, seqs, assign, ND=64)
    y_prompt = outs[0][None].astype(np.float32)
    y_sample = np.stack([outs[1], outs[2]], axis=0).astype(np.float32)
    return (y_prompt, y_sample)
```

```python
import math
from contextlib import ExitStack

import numpy as np
import concourse.bass as bass
import concourse.mybir as mybir
from concourse.bass_utils import run_bass_kernel_spmd

F32 = mybir.dt.float32
BF16 = mybir.dt.bfloat16
I32 = mybir.dt.int32
U32 = mybir.dt.uint32
ALU = mybir.AluOpType
AF = mybir.ActivationFunctionType
AX = mybir.AxisListType

D = 1024
NCOL = 5120
EPS = 1e-6
PI = math.pi
ENGS = ("pe", "act", "dve", "pool", "sp")


class Sch:
    def __init__(self, nc, es, n_dma_sems=32):
        self.nc = nc
        self.esem = {e: es.enter_context(nc.semaphore("se_" + e)) for e in ENGS}
        self.ecnt = {e: 0 for e in ENGS}
        self.dsem = [es.enter_context(nc.semaphore("sd%d" % i)) for i in range(n_dma_sems)]
        self.dval = [0] * n_dma_sems
        self.dnext = {"hw": 0, "sw": 0}
        self.dhalf = n_dma_sems // 2
        self.lastw = {}
        self.readers = {}
        self.known = {e: {} for e in ENGS}
        self.ops = {e: [] for e in ENGS}
        self.nops = 0
        self.capture = None

    def _sem(self, sk):
        return self.esem[sk[1]] if sk[0] == "e" else self.dsem[sk[1]]

    def _need(self, eng, tok, waits):
        if tok is None:
            return
        sk, val = tok
        if sk == ("e", "pe") and eng == "pe":
            return
        if self.known[eng].get(sk, 0) >= val:
            return
        self.known[eng][sk] = val
        waits[sk] = max(waits.get(sk, 0), val)

    def op(self, eng, fn, reads=(), writes=(), dma=False):
        if self.capture is not None:
            self.capture.append((eng, fn, tuple(reads), tuple(writes), dma))
            return
        waits = {}
        for k in reads:
            self._need(eng, self.lastw.get(k), waits)
        for k in writes:
            self._need(eng, self.lastw.get(k), waits)
            for t in self.readers.get(k, ()):
                self._need(eng, t, waits)
        if dma:
            kind = "sw" if eng == "pool" else "hw"
            i = self.dnext[kind] + (self.dhalf if kind == "sw" else 0)
            self.dnext[kind] = (self.dnext[kind] + 1) % self.dhalf
            if self.dval[i] > 0:
                self._need(eng, (("d", i), self.dval[i]), waits)
            self.dval[i] += 16
            tok = (("d", i), self.dval[i])
            inc = (self.dsem[i], 16)
        else:
            self.ecnt[eng] += 1
            tok = (("e", eng), self.ecnt[eng])
            inc = (self.esem[eng], 1)
        for k in reads:
            self.readers.setdefault(k, []).append(tok)
        for k in writes:
            self.lastw[k] = tok
            self.readers[k] = []
        self.ops[eng].append((list(waits.items()), fn, inc))
        self.nops += 1

    def fence(self):
        for e in ENGS:
            waits = {}
            for e2 in ENGS:
                if self.ecnt[e2] > 0:
                    self._need(e, (("e", e2), self.ecnt[e2]), waits)
            for i, v in enumerate(self.dval):
                if v > 0:
                    self._need(e, (("d", i), v), waits)
            self.ops[e].append((list(waits.items()), None, None))
        self.lastw.clear()
        self.readers.clear()

    def emit(self):
        with self.nc.Block() as blk:
            decos = {"pe": blk.tensor, "act": blk.scalar, "dve": blk.vector, "pool": blk.gpsimd, "sp": blk.sync}
            for e in ENGS:
                ops = self.ops[e]

                def body(eng, ops=ops):
                    for waits, fn, inc in ops:
                        for sk, val in waits:
                            eng.wait_ge(self._sem(sk), val)
                        if fn is not None:
                            fn(eng).then_inc(inc[0], inc[1])

                decos[e](body)
                self.ops[e] = []


def _mm(out, lhsT, rhs, start, stop):
    return lambda e: e.matmul(out, lhsT, rhs, start=start, stop=stop)


def _tr(out, in_, ident):
    return lambda e: e.transpose(out, in_, ident)


def _dma(out, in_):
    return lambda e: e.dma_start(out=out, in_=in_)


def _gather(out, table, idx):
    return lambda e: e.indirect_dma_start(out=out, out_offset=None, in_=table,
                                          in_offset=bass.IndirectOffsetOnAxis(ap=idx, axis=0))


def _act(out, in_, func, scale=None):
    if scale is None:
        return lambda e: e.activation(out=out, in_=in_, func=func)
    return lambda e: e.activation(out=out, in_=in_, func=func, scale=scale)


def _copy(out, in_):
    return lambda e: e.tensor_copy(out=out, in_=in_)


def _acopy(out, in_):
    return lambda e: e.copy(out=out, in_=in_)


def _tt(out, in0, in1, op):
    return lambda e: e.tensor_tensor(out=out, in0=in0, in1=in1, op=op)


def _ts(out, in0, s1, s2, op0, op1=None):
    if op1 is None:
        return lambda e: e.tensor_scalar(out=out, in0=in0, scalar1=s1, scalar2=None, op0=op0)
    return lambda e: e.tensor_scalar(out=out, in0=in0, scalar1=s1, scalar2=s2, op0=op0, op1=op1)


def _stt(out, in0, scalar, in1, op0, op1, accum_out=None):
    if accum_out is None:
        return lambda e: e.scalar_tensor_tensor(out=out, in0=in0, scalar=scalar, in1=in1, op0=op0, op1=op1)
    return lambda e: e.scalar_tensor_tensor(out=out, in0=in0, scalar=scalar, in1=in1, op0=op0, op1=op1,
                                            accum_out=accum_out)


def _scan(out, d0, d1, init):
    return lambda e: e.tensor_tensor_scan(out=out, data0=d0, data1=d1, initial=init, op0=ALU.mult, op1=ALU.add)


def _memset(ap, v):
    return lambda e: e.memset(ap, v)


def _rsum(out, in_):
    return lambda e: e.reduce_sum(out=out, in_=in_, axis=AX.X)


def _recip(out, in_):
    return lambda e: e.reciprocal(out=out, in_=in_)


def _rev(ap2d):
    return ap2d[:, ::-1]


def build_program(SEQ, ND, debug=False):
    assert SEQ % 512 == 0
    NT = SEQ // 512 + 1
    TP = NT * 512
    nc = bass.Bass("TRN2", target_bir_lowering=False)

    def din(name, shape, dt=F32):
        return nc.dram_tensor(name, list(shape), dt, kind="ExternalInput").ap()

    def dscr(name, shape, dt):
        kind = "ExternalOutput" if debug else "Internal"
        return nc.dram_tensor(name, list(shape), dt, kind=kind).ap()

    I = {}
    I["xs"] = din("xs", [SEQ, D])
    I["meta"] = din("meta", [16, D])
    I["tokd"] = din("tokd", [128, ND], I32)
    I["g1c"] = din("g1c", [128, 8])
    I["w_in"] = din("w_in", [D, NCOL])
    for nm in ("lre_c", "lim_c", "lst_c"):
        I[nm] = din(nm, [128, 32])
    for nm in ("cre8", "cim8", "brec", "bimc"):
        I[nm] = din(nm, [128, 2048])
    I["drow"] = din("drow", [128, 32])
    I["cst8"] = din("cst8", [128, 32])
    I["selC"] = din("selC", [128, 1024])
    I["ramp16"] = din("ramp16", [128, 16])
    I["mfb"] = din("mfb", [128, 256])
    I["w_glu"] = din("w_glu", [512, 2048])
    I["lb0"] = din("lb0", [128, 4])
    I["lb1"] = din("lb1", [128, 4])
    I["hgn"] = din("hgn", [128, 4])
    I["w_hg"] = din("w_hg", [512, D])
    I["w_out"] = din("w_out", [D, D])
    I["g2b"] = din("g2b", [128, D])
    I["gfb"] = din("gfb", [128, D])
    I["wq"] = din("wq", [D, 2048])
    I["keysT"] = din("keysT", [128, 2048])
    I["puv"] = din("puv", [16384, 2 * D])
    I["identf"] = din("identf", [128, 128])
    I["ramps"] = din("ramps", [128, 256])
    I["masks"] = din("masks", [64, 128])
    I["iota16"] = din("iota16", [128, 16])
    OUT = nc.dram_tensor("outd", [ND * 128, D], F32, kind="ExternalOutput").ap()

    Z = dscr("Z", [NCOL, TP], BF16)
    VTM = dscr("VTM", [TP, 512], BF16)
    UST = nc.dram_tensor("UST", [128, 32, TP // 8], BF16, kind="Internal").ap()
    XBS = nc.dram_tensor("XBS", [128, 16, 2, TP // 8], BF16, kind="Internal").ap()
    MA = dscr("MA", [D, TP], BF16)
    OB = dscr("OB", [512, TP], F32)
    HS1 = dscr("HS1", [SEQ, D], F32)
    UVB = nc.dram_tensor("UVB", [16384, 2 * D], BF16, kind="Internal").ap()

    top_es = ExitStack()
    S = Sch(nc, top_es)

    with ExitStack() as es:
        def T(name, shape, dt):
            return es.enter_context(nc.sbuf_tensor("A_" + name, shape, dt))

        def P(name, shape, dt):
            return es.enter_context(nc.psum_tensor("A_" + name, shape, dt))

        Wb = T("Wb", [128, 8, NCOL], BF16)
        stg = [T("stg%d" % i, [128, 1280], F32) for i in range(2)]
        cst8 = T("cst8", [128, 32], F32)
        selNb = T("selNb", [128, 16], BF16)
        Ex = T("Ex", [128, 32, 8, 16], BF16)
        ustb = [T("ustb%d" % i, [128, 32, 256], BF16) for i in range(2)]
        g1c = T("g1c", [128, 8], F32)
        identf = T("identf", [128, 128], F32)
        identb = T("identb", [128, 128], BF16)
        xt = [[T("xt%d_%d" % (b, j), [128, D], F32) for j in range(4)] for b in range(2)]
        hb = [T("hb%d" % i, [128, D], BF16) for i in range(2)]
        hT = [T("hT%d" % i, [128, 8, 512], BF16) for i in range(2)]
        ssq = T("ssq", [128, 8], F32)
        rstd = T("rstd", [128, 8], F32)
        junk = T("junk", [128, D], F32)
        zt = [T("zt%d" % i, [128, 4, 512], BF16) for i in range(2)]
        vt = [T("vt%d" % i, [128, 512], BF16) for i in range(2)]
        pT = [P("pT%d" % i, [128, 8, 128], BF16) for i in range(2)]
        pz = [P("pz%d" % i, [128, 512], F32) for i in range(4)]
        psU = P("psU", [128, 32, 16], F32)

        S.op("sp", _dma(identf[:], I["identf"][:, :]), writes=["identf"], dma=True)
        S.op("sp", _dma(g1c[:], I["g1c"][:, :]), writes=["g1c"], dma=True)
        S.op("sp", _dma(cst8[:], I["cst8"][:, :]), writes=["cst8"], dma=True)
        S.op("dve", _copy(selNb[:], cst8[:, 8:24]), reads=["cst8"], writes=["selNb"])
        S.op("dve", _copy(identb[:], identf[:]), reads=["identf"], writes=["identb"])
        n = 0
        for k in range(8):
            for h in range(4):
                sb = n % 2
                n += 1
                S.op("sp", _dma(stg[sb][:], I["w_in"][k * 128:(k + 1) * 128, h * 1280:(h + 1) * 1280]),
                     writes=["stg%d" % sb], dma=True)
                S.op("dve", _ts(Wb[:, k, h * 1280:(h + 1) * 1280], stg[sb][:], g1c[:, k:k + 1], None, ALU.mult),
                     reads=["stg%d" % sb, "g1c"], writes=["Wb"])

        ev = 0
        for st in range(NT):
            b = st % 2
            if st == 0:
                for j in range(4):
                    S.op("pool", _memset(xt[b][j][:], 0.0), writes=["xt%d_%d" % (b, j)])
                S.op("sp", _dma(xt[b][3][112:128, :], I["meta"][:, :]), writes=["xt%d_3" % b], dma=True)
            else:
                for j in range(4):
                    r0 = (st - 1) * 512 + j * 128
                    S.op("sp", _dma(xt[b][j][:], I["xs"][r0:r0 + 128, :]), writes=["xt%d_%d" % (b, j)], dma=True)
            for j in range(4):
                c = (st * 4 + j) % 8
                hbj = j % 2
                xk = "xt%d_%d" % (b, j)
                S.op("dve", _memset(ssq[:, c:c + 1], 0.0), writes=["ssq%d" % c])
                S.op("dve", _stt(junk[:], xt[b][j][:], 1.0, xt[b][j][:], ALU.mult, ALU.mult, accum_out=ssq[:, c:c + 1]),
                     reads=[xk], writes=["junk", "ssq%d" % c])
                S.op("dve", _ts(rstd[:, c:c + 1], ssq[:, c:c + 1], 1.0 / D, EPS, ALU.mult, ALU.add),
                     reads=["ssq%d" % c], writes=["rstd%d" % c])
                S.op("act", _act(rstd[:, c:c + 1], rstd[:, c:c + 1], AF.Sqrt), reads=["rstd%d" % c], writes=["rstd%d" % c])
                S.op("dve", _recip(rstd[:, c:c + 1], rstd[:, c:c + 1]), reads=["rstd%d" % c], writes=["rstd%d" % c])
                S.op("act", _act(hb[hbj][:], xt[b][j][:], AF.Copy, scale=rstd[:, c:c + 1]),
                     reads=[xk, "rstd%d" % c], writes=["hb%d" % hbj])
                for k in range(8):
                    S.op("pe", _tr(pT[hbj][:, k, :], hb[hbj][:, k * 128:(k + 1) * 128], identb[:]),
                         reads=["hb%d" % hbj, "identb"], writes=["pT%d" % hbj])
                S.op("act", _acopy(hT[b][:, :, j * 128:(j + 1) * 128], pT[hbj][:, :, :]),
                     reads=["pT%d" % hbj], writes=["hT%d" % b])
            for c in [c_ for c_ in range(4, 40) if not (16 <= c_ < 20)]:
                pzi = c % 4
                for k in range(8):
                    S.op("pe", _mm(pz[pzi][:], Wb[:, k, c * 128:(c + 1) * 128], hT[b][:, k, :], k == 0, k == 7),
                         reads=["Wb", "hT%d" % b], writes=["pz%d" % pzi])
                zb = (c // 4) % 2
                if ev % 2 == 0:
                    S.op("act", _acopy(zt[zb][:, c % 4, :], pz[pzi][:]), reads=["pz%d" % pzi], writes=["zt%d" % zb])
                else:
                    S.op("dve", _copy(zt[zb][:, c % 4, :], pz[pzi][:]), reads=["pz%d" % pzi], writes=["zt%d" % zb])
                ev += 1
                if c % 4 == 3:
                    dst = Z[(c - 3) * 128:(c + 1) * 128, st * 512:(st + 1) * 512].rearrange("(c p) t -> p c t", p=128)
                    S.op("pool", _dma(dst, zt[zb][:]), reads=["zt%d" % zb], writes=["Z"], dma=True)
            for j in range(4):
                pzi = j % 4
                for k in range(8):
                    S.op("pe", _mm(pz[pzi][:], hT[b][:, k, j * 128:(j + 1) * 128], Wb[:, k, 2048:2560], k == 0, k == 7),
                         reads=["Wb", "hT%d" % b], writes=["pz%d" % pzi])
                vb = j % 2
                S.op("act", _acopy(vt[vb][:], pz[pzi][:]), reads=["pz%d" % pzi], writes=["vt%d" % vb])
                r0 = st * 512 + j * 128
                S.op("pool", _dma(VTM[r0:r0 + 128, :], vt[vb][:]), reads=["vt%d" % vb], writes=["VTM"], dma=True)
            ubuf = (st // 4) % 2
            for j in range(4):
                pzi = j % 4
                for k in range(8):
                    S.op("pe", _mm(pz[pzi][:], hT[b][:, k, j * 128:(j + 1) * 128], Wb[:, k, 0:512], k == 0, k == 7),
                         reads=["Wb", "hT%d" % b], writes=["pz%d" % pzi])
                S.op("dve", _tt(Ex[:], pz[pzi][:].rearrange("p (g c) -> p g c", g=32).unsqueeze(2).broadcast_to([128, 32, 8, 16]),
                                cst8[:, 0:8].unsqueeze(1).unsqueeze(3).broadcast_to([128, 32, 8, 16]), ALU.mult),
                     reads=["pz%d" % pzi, "cst8"], writes=["Ex"])
                for g in range(32):
                    S.op("pe", _mm(psU[:, g, :], Ex[:, g, :, :].rearrange("p i c -> p (i c)"), selNb[:], True, True),
                         reads=["Ex", "selNb"], writes=["psU"])
                off = (st % 4) * 64 + j * 16
                S.op("act", _acopy(ustb[ubuf][:, :, off:off + 16], psU[:]), reads=["psU"], writes=["ustb%d" % ubuf])
            if st % 4 == 3 or st == NT - 1:
                base = (st - st % 4) * 64
                ncols = (st % 4 + 1) * 64
                S.op("pool", _dma(UST[:, :, base:base + ncols], ustb[ubuf][:, :, 0:ncols]), reads=["ustb%d" % ubuf], writes=["UST"],
                     dma=True)
        S.fence()
        S.emit()

    with ExitStack() as es:
        def T(name, shape, dt):
            return es.enter_context(nc.sbuf_tensor("B_" + name, shape, dt))

        def P(name, shape, dt):
            return es.enter_context(nc.psum_tensor("B_" + name, shape, dt))

        NCH = TP // 8
        BWD = 256
        blocks = [(n0, min(BWD, NCH - n0)) for n0 in range(0, NCH, BWD)]

        WST = T("WST", [128, 2, 16, 2, 2, 128], BF16)
        WX = T("WX", [128, 32, 2, 128], BF16)
        WU = T("WU", [128, 32, 128], BF16)
        COS = T("COS", [128, 32, 128], F32)
        SIN = T("SIN", [128, 32, 128], F32)
        rc8 = T("rc8", [128, 32], F32)
        ramps = T("ramps", [128, 256], F32)
        cst8 = T("cst8", [128, 32], F32)
        maskJb = T("maskJb", [128, 8], BF16)
        selC = T("selC", [128, 8, 128], BF16)
        STre = T("STre", [128, 32], F32)
        STim = T("STim", [128, 32], F32)
        XFc = T("XFc", [128, 16, 2], BF16)
        tmp4 = T("tmp4", [128, 4], F32)
        identf = T("identf", [128, 128], F32)

        def sincos(dsin, dcos, ang, tf, tf2, ti, ksin, kcos, kang, ktf, ktf2, kti):
            for dst, kd, off in ((dcos, kcos, 0.25), (dsin, ksin, 0.0)):
                S.op("dve", _ts(tf, ang, 1.0 / (2 * PI), off, ALU.mult, ALU.add), reads=[kang], writes=[ktf])
                S.op("dve", _copy(ti, tf), reads=[ktf], writes=[kti])
                S.op("dve", _copy(tf2, ti), reads=[kti], writes=[ktf2])
                S.op("dve", _tt(tf, tf, tf2, ALU.subtract), reads=[ktf, ktf2], writes=[ktf])
                S.op("dve", _ts(tf, tf, -0.4999, 0.4999, ALU.max, ALU.min), reads=[ktf], writes=[ktf])
                S.op("act", _act(dst, tf, AF.Sin, scale=2 * PI), reads=[ktf], writes=[kd])

        with ExitStack() as es2:
            def T2(name, shape, dt):
                return es2.enter_context(nc.sbuf_tensor("B2_" + name, shape, dt))

            def P2(name, shape, dt):
                return es2.enter_context(nc.psum_tensor("B2_" + name, shape, dt))

            LR = T2("LR", [128, 32], F32)
            LI = T2("LI", [128, 32], F32)
            LS = T2("LS", [128, 32], F32)
            pa = T2("pa", [128, 32], F32)
            pth = T2("pth", [128, 32], F32)
            pcr = T2("pcr", [128, 32], F32)
            pci = T2("pci", [128, 32], F32)
            sm_ = [T2("sm%d" % i, [128, 32], F32) for i in range(8)]
            ramp16 = T2("ramp16", [128, 16], F32)
            mfb = T2("mfb", [128, 256], F32)
            drow = T2("drow", [128, 32], F32)
            S.op("sp", _dma(identf[:], I["identf"][:, :]), writes=["identf"], dma=True)
            S.op("sp", _dma(ramps[:], I["ramps"][:, :]), writes=["ramps"], dma=True)
            S.op("sp", _dma(cst8[:], I["cst8"][:, :]), writes=["cst8"], dma=True)
            S.op("sp", _dma(ramp16[:], I["ramp16"][:, :]), writes=["ramp16"], dma=True)
            S.op("sp", _dma(mfb[:], I["mfb"][:, :]), writes=["mfb"], dma=True)
            S.op("sp", _dma(drow[:], I["drow"][:, :]), writes=["drow"], dma=True)
            S.op("sp", _dma(LR[:], I["lre_c"][:, :]), writes=["LR"], dma=True)
            S.op("sp", _dma(LI[:], I["lim_c"][:, :]), writes=["LI"], dma=True)
            S.op("sp", _dma(LS[:], I["lst_c"][:, :]), writes=["LS"], dma=True)
            S.op("dve", _copy(maskJb[:], cst8[:, 24:32]), reads=["cst8"], writes=["maskJb"])
            t = [x[:] for x in sm_]
            k = ["sm%d" % i for i in range(8)]
            S.op("act", _act(t[0], LS[:], AF.Exp), reads=["LS"], writes=[k[0]])
            S.op("dve", _tt(pa[:], LR[:], t[0], ALU.mult), reads=["LR", k[0]], writes=["pa"])
            S.op("dve", _tt(pth[:], LI[:], t[0], ALU.mult), reads=["LI", k[0]], writes=["pth"])
            S.op("act", _act(t[1], pa[:], AF.Exp), reads=["pa"], writes=[k[1]])
            sincos(t[3], t[4], pth[:], t[5], t[6], t[7].bitcast(I32), k[3], k[4], "pth", k[5], k[6], k[7])
            S.op("dve", _tt(t[5], t[1], t[4], ALU.mult), reads=[k[1], k[4]], writes=[k[5]])
            S.op("dve", _ts(t[5], t[5], -1.0, None, ALU.add), reads=[k[5]], writes=[k[5]])
            S.op("dve", _tt(t[6], t[1], t[3], ALU.mult), reads=[k[1], k[3]], writes=[k[6]])
            S.op("dve", _tt(t[7], LR[:], LR[:], ALU.mult), reads=["LR"], writes=[k[7]])
            S.op("dve", _tt(t[0], LI[:], LI[:], ALU.mult), reads=["LI"], writes=[k[0]])
            S.op("dve", _tt(t[7], t[7], t[0], ALU.add), reads=[k[7], k[0]], writes=[k[7]])
            S.op("dve", _recip(t[7], t[7]), reads=[k[7]], writes=[k[7]])
            S.op("dve", _tt(t[3], t[5], LR[:], ALU.mult), reads=[k[5], "LR"], writes=[k[3]])
            S.op("dve", _tt(t[0], t[6], LI[:], ALU.mult), reads=[k[6], "LI"], writes=[k[0]])
            S.op("dve", _tt(t[3], t[3], t[0], ALU.add), reads=[k[3], k[0]], writes=[k[3]])
            S.op("dve", _tt(pcr[:], t[3], t[7], ALU.mult), reads=[k[3], k[7]], writes=["pcr"])
            S.op("dve", _tt(t[4], t[6], LR[:], ALU.mult), reads=[k[6], "LR"], writes=[k[4]])
            S.op("dve", _tt(t[0], t[5], LI[:], ALU.mult), reads=[k[5], "LI"], writes=[k[0]])
            S.op("dve", _tt(t[4], t[4], t[0], ALU.subtract), reads=[k[4], k[0]], writes=[k[4]])
            S.op("dve", _tt(pci[:], t[4], t[7], ALU.mult), reads=[k[4], k[7]], writes=["pci"])

            PWm = T2("PWm", [128, 32, 16], F32)
            PWa = T2("PWa", [128, 32, 16], F32)
            PWr = T2("PWr", [128, 32, 16], F32)
            PWi = T2("PWi", [128, 32, 16], F32)
            PCr = T2("PCr", [128, 32, 16], F32)
            PCi = T2("PCi", [128, 32, 16], F32)
            tl = [T2("tl%d" % i, [128, 2048], F32) for i in range(6)]
            pt = [tl[3 + i][:, 0:512] for i in range(3)]
            for dg in range(32):
                S.op("act", _act(PWm[:, dg, :], ramp16[:], AF.Exp, scale=pa[:, dg:dg + 1]), reads=["ramp16", "pa"], writes=["PWm"])
                S.op("dve", _ts(PWa[:, dg, :], ramp16[:], pth[:, dg:dg + 1], None, ALU.mult), reads=["ramp16", "pth"], writes=["PWa"])
            f512 = lambda x: x[:].rearrange("p a b -> p (a b)")
            sincos(f512(PWi), f512(PWr), f512(PWa), pt[0], pt[1], pt[2].bitcast(I32), "PWi", "PWr", "PWa", "tl3", "tl4", "tl5")
            S.op("dve", _tt(PWr[:], PWr[:], PWm[:], ALU.mult), reads=["PWr", "PWm"], writes=["PWr"])
            S.op("dve", _tt(PWi[:], PWi[:], PWm[:], ALU.mult), reads=["PWi", "PWm"], writes=["PWi"])
            crb = pcr[:].unsqueeze(2).broadcast_to([128, 32, 16])
            cib = pci[:].unsqueeze(2).broadcast_to([128, 32, 16])
            p3 = lambda x: x.rearrange("p (a b) -> p a b", a=32)
            S.op("dve", _tt(PCr[:], PWr[:], crb, ALU.mult), reads=["PWr", "pcr"], writes=["PCr"])
            S.op("dve", _tt(p3(pt[0]), PWi[:], cib, ALU.mult), reads=["PWi", "pci"], writes=["tl3"])
            S.op("dve", _tt(PCr[:], PCr[:], p3(pt[0]), ALU.subtract), reads=["PCr", "tl3"], writes=["PCr"])
            S.op("dve", _tt(PCi[:], PWr[:], cib, ALU.mult), reads=["PWr", "pci"], writes=["PCi"])
            S.op("dve", _tt(p3(pt[1]), PWi[:], crb, ALU.mult), reads=["PWi", "pcr"], writes=["tl4"])
            S.op("dve", _tt(PCi[:], PCi[:], p3(pt[1]), ALU.add), reads=["PCi", "tl4"], writes=["PCi"])
            S.op("dve", _copy(rc8[:].unsqueeze(2), PWm[:, :, 15:16]), reads=["PWm"], writes=["rc8"])
            S.op("dve", _ts(sm_[0][:], pth[:], 8.0, None, ALU.mult), reads=["pth"], writes=["sm0"])
            S.fence()
            for hf in range(2):
                ang3 = tl[0][:].rearrange("p (a b) -> p a b", a=16)
                for g_ in range(16):
                    dg = hf * 16 + g_
                    rp = ramps[:, 0:128] if hf == 0 else ramps[:, 128:256]
                    S.op("dve", _ts(ang3[:, g_, :], rp, sm_[0][:, dg:dg + 1], None, ALU.mult), reads=["ramps", "sm0"], writes=["tl0"])
                sinh = SIN[:, hf * 16:(hf + 1) * 16, :].rearrange("p a b -> p (a b)")
                cosh = COS[:, hf * 16:(hf + 1) * 16, :].rearrange("p a b -> p (a b)")
                sincos(sinh, cosh, tl[0][:], tl[1][:], tl[2][:], tl[3][:].bitcast(I32), "SIN", "COS", "tl0", "tl1", "tl2", "tl3")
            S.fence()

            C8r = T2("C8r", [128, 2048], F32)
            C8i = T2("C8i", [128, 2048], F32)
            Bcr = T2("Bcr", [128, 2048], F32)
            Bci = T2("Bci", [128, 2048], F32)
            WUf = T2("WUf", [128, 32, 128], F32)
            psT = P2("psT", [128, 128], F32)
            psW = P2("psW", [128, 128], F32)
            S.op("sp", _dma(C8r[:], I["cre8"][:, :]), writes=["C8r"], dma=True)
            S.op("sp", _dma(C8i[:], I["cim8"][:, :]), writes=["C8i"], dma=True)
            S.op("sp", _dma(Bcr[:], I["brec"][:, :]), writes=["Bcr"], dma=True)
            S.op("sp", _dma(Bci[:], I["bimc"][:, :]), writes=["Bci"], dma=True)
            S.op("pool", _memset(WST[:].rearrange("p a b c d e -> p (a b c d e)"), 0.0), writes=["WST"])

            def v4(ap):
                return ap.rearrange("p (g j c) -> p g j c", g=16, j=8)

            def pw4(tbl, d, sl):
                return tbl[:, d * 16:(d + 1) * 16, sl].unsqueeze(3).broadcast_to([128, 16, 8, 16])

            def cmul(ore, oim, ar, ai, br, bi, ka, kb, ko, neg_im=False):
                s4, s5 = v4(tl[4][:]), v4(tl[5][:])
                S.op("dve", _tt(s4, ar, br, ALU.mult), reads=ka + kb, writes=["tl4"])
                S.op("dve", _tt(s5, ai, bi, ALU.mult), reads=ka + kb, writes=["tl5"])
                S.op("dve", _tt(ore, s4, s5, ALU.subtract), reads=["tl4", "tl5"], writes=[ko[0]])
                S.op("dve", _tt(s4, ar, bi, ALU.mult), reads=ka + kb, writes=["tl4"])
                S.op("dve", _tt(s5, ai, br, ALU.mult), reads=ka + kb, writes=["tl5"])
                S.op("dve", _tt(oim, s4, s5, ALU.add), reads=["tl4", "tl5"], writes=[ko[1]])
                if neg_im:
                    S.op("dve", _ts(oim, oim, -1.0, None, ALU.mult), reads=[ko[1]], writes=[ko[1]])

            SL_P1_8 = slice(8, 16)
            SL_8_1 = slice(15, 7, -1)
            SL_0_7 = slice(7, 15)
            SL_0_m7 = slice(7, None, -1)
            SL_7_0 = slice(14, 6, -1)
            kC, kB_, kPW, kPC = ["C8r", "C8i"], ["Bcr", "Bci"], ["PWr", "PWi"], ["PCr", "PCi"]
            for d in range(2):
                sl = SL_P1_8 if d == 0 else SL_8_1
                cmul(v4(tl[0][:]), v4(tl[1][:]), v4(C8r[:]), v4(C8i[:]), pw4(PWr, d, sl), pw4(PWi, d, sl), kC, kPW, ["tl0", "tl1"],
                     neg_im=True)
                S.op("act", _acopy(WX[:, d * 16:(d + 1) * 16, 0, :], tl[0][:].rearrange("p (g m) -> p g m", g=16)), reads=["tl0"],
                     writes=["WX"])
                S.op("act", _acopy(WX[:, d * 16:(d + 1) * 16, 1, :], tl[1][:].rearrange("p (g m) -> p g m", g=16)), reads=["tl1"],
                     writes=["WX"])
                sl = SL_7_0 if d == 0 else SL_0_7
                cmul(v4(tl[0][:]), v4(tl[1][:]), pw4(PCr, d, sl), pw4(PCi, d, sl), v4(Bcr[:]), v4(Bci[:]), kPC, kB_, ["tl0", "tl1"])
                for gp in range(16):
                    for ri in range(2):
                        S.op("pe", _tr(psT[:], tl[ri][:, gp * 128:(gp + 1) * 128], identf[:]), reads=["tl%d" % ri, "identf"],
                             writes=["psT"])
                        S.op("act", _acopy(WST[:, d, gp, ri, 0, 0:64], psT[:, 0:64]), reads=["psT"], writes=["WST"])
                        S.op("dve", _copy(WST[:, d, gp, ri, 1, 64:128], psT[:, 64:128]), reads=["psT"], writes=["WST"])
                slG = SL_0_7 if d == 0 else SL_0_m7
                slH = SL_0_m7 if d == 0 else SL_0_7
                cmul(v4(tl[0][:]), v4(tl[1][:]), v4(C8r[:]), v4(C8i[:]), pw4(PWr, d, slG), pw4(PWi, d, slG), kC, kPW, ["tl0", "tl1"],
                     neg_im=True)
                cmul(v4(tl[2][:]), v4(tl[3][:]), pw4(PCr, d, slH), pw4(PCi, d, slH), v4(Bcr[:]), v4(Bci[:]), kPC, kB_, ["tl2", "tl3"])
                mk = mfb[:, 0:128] if d == 0 else mfb[:, 128:256]
                for gp in range(16):
                    for gl in range(2):
                        g = 2 * gp + gl
                        rs = slice(gl * 64, (gl + 1) * 64)
                        cs = slice(gp * 128, (gp + 1) * 128)
                        S.op("pe", _mm(psW[:], tl[2][rs, cs], tl[0][rs, cs], True, False), reads=["tl2", "tl0"], writes=["psW"])
                        S.op("pe", _mm(psW[:], tl[3][rs, cs], tl[1][rs, cs], False, True), reads=["tl3", "tl1"], writes=["psW"])
                        if d == 0:
                            S.op("dve", _tt(WUf[:, g, :], psW[:], mk, ALU.mult), reads=["psW", "mfb"], writes=["WUf"])
                            S.op("dve", _stt(WUf[:, g, :], identf[:], drow[:, g:g + 1], WUf[:, g, :], ALU.mult, ALU.add),
                                 reads=["identf", "drow", "WUf"], writes=["WUf"])
                        else:
                            S.op("dve", _tt(tl[4][:, 0:128], psW[:], mk, ALU.mult), reads=["psW", "mfb"], writes=["tl4"])
                            S.op("dve", _tt(WU[:, g, :], WUf[:, g, :], tl[4][:, 0:128], ALU.add), reads=["WUf", "tl4"], writes=["WU"])
            S.op("sp", _dma(tl[0][:, 0:1024], I["selC"][:, :]), writes=["tl0"], dma=True)
            S.op("dve", _copy(selC[:].rearrange("p a b -> p (a b)"), tl[0][:, 0:1024]), reads=["tl0"], writes=["selC"])
            S.fence()
            S.emit()

        wglu = T("wglu", [128, 4, 2048], BF16)
        with ExitStack() as es3:
            wst2 = [es3.enter_context(nc.sbuf_tensor("B3_wst%d" % i, [128, 2048], F32)) for i in range(2)]
            for k_ in range(4):
                S.op("sp", _dma(wst2[k_ % 2][:], I["w_glu"][k_ * 128:(k_ + 1) * 128, :]), writes=["wst2_%d" % (k_ % 2)], dma=True)
                S.op("act", _acopy(wglu[:, k_, :], wst2[k_ % 2][:]), reads=["wst2_%d" % (k_ % 2)], writes=["wglu"])
            S.fence()
            S.emit()
        ust = [T("ust%d" % i, [128, 32, BWD], BF16) for i in range(2)]
        gst = T("gst", [128, 32, BWD], BF16)
        bpre = [T("bpre%d" % i, [128, BWD], F32) for i in range(2)]
        bpim = [T("bpim%d" % i, [128, BWD], F32) for i in range(2)]
        mt = [T("mt%d" % i, [128, 128], F32) for i in range(4)]
        mp = [T("mp%d" % i, [128, 128], F32) for i in range(4)]
        Wre = [T("Wre%d" % i, [128, BWD], F32) for i in range(2)]
        Wim = [T("Wim%d" % i, [128, BWD], F32) for i in range(2)]
        XF = [T("XF%d" % i, [128, 2, BWD + 1], BF16) for i in range(2)]
        XBt = [T("XBt%d" % i, [128, 2, BWD], BF16) for i in range(2)]
        ex = [T("ex%d" % i, [128, 32, 16, 8], BF16) for i in range(1)]
        gel = T("gel", [128, 4, 512], BF16)
        gat = T("gat", [128, 8, 512], BF16)
        mao = T("mao", [128, 8, 512], BF16)
        sgb = [T("sgb%d" % i, [128, 512], F32) for i in range(1)]
        yab = [T("yab%d" % i, [128, 512], F32) for i in range(1)]
        sga = [T("sga%d" % i, [128, 512], F32) for i in range(1)]
        psS = [P("psS%d" % i, [128, 512], F32) for i in range(2)]
        psY = [P("psY%d" % i, [128, 512], F32) for i in range(2)]
        psG = P("psG", [128, 4, 128], F32)
        psb = [P("psb%d" % i, [128, 512], F32) for i in range(2)]

        S.op("dve", _memset(STre[:], 0.0), writes=["STre"])
        S.op("dve", _memset(STim[:], 0.0), writes=["STim"])
        S.op("dve", _memset(XFc[:], 0.0), writes=["XFc"])

        cnt = 0
        ucnt = 0
        for d in (1, 0):
            blks = list(reversed(blocks)) if d == 1 else blocks
            for (n0, w) in blks:
                ub = ucnt % 2
                ucnt += 1
                S.op("sp", _dma(ust[ub][:, :, 0:w], UST[:, :, n0:n0 + w]), reads=["UST"], writes=["ust%d" % ub], dma=True)
                segs = [(s0, min(128, w - s0)) for s0 in range(0, w, 128)]
                if d == 1:
                    segs = list(reversed(segs))
                for gp in range(16):
                    dg = d * 16 + gp
                    pb = cnt % 2
                    cnt += 1
                    PB = str(pb)
                    for ri in range(2):
                        S.op("pe", _mm(psS[ri][:, 0:w], WST[:, d, gp, ri, 0, :], ust[ub][:, 2 * gp, 0:w], True, False),
                             reads=["WST", "ust%d" % ub], writes=["psS%d" % ri])
                        S.op("pe", _mm(psS[ri][:, 0:w], WST[:, d, gp, ri, 1, :], ust[ub][:, 2 * gp + 1, 0:w], False, True),
                             reads=["WST", "ust%d" % ub], writes=["psS%d" % ri])
                    rbc = rc8[:, dg:dg + 1]
                    for (s0, sw) in segs:
                        sl = slice(s0, s0 + sw)
                        tsl = slice(128 - sw, 128) if d == 1 else slice(0, sw)
                        cb_, sb_ = COS[:, dg, tsl], SIN[:, dg, tsl]
                        S.op("dve", _tt(mt[0][:, 0:sw], psS[0][:, sl], cb_, ALU.mult), reads=["psS0", "COS"], writes=["mt0"])
                        S.op("dve", _tt(mt[1][:, 0:sw], psS[1][:, sl], sb_, ALU.mult), reads=["psS1", "SIN"], writes=["mt1"])
                        S.op("dve", _tt(bpre[pb][:, sl], mt[0][:, 0:sw], mt[1][:, 0:sw], ALU.add), reads=["mt0", "mt1"], writes=["bpre" + PB])
                        S.op("dve", _tt(mt[2][:, 0:sw], psS[1][:, sl], cb_, ALU.mult), reads=["psS1", "COS"], writes=["mt2"])
                        S.op("dve", _tt(mt[3][:, 0:sw], psS[0][:, sl], sb_, ALU.mult), reads=["psS0", "SIN"], writes=["mt3"])
                        S.op("dve", _tt(bpim[pb][:, sl], mt[2][:, 0:sw], mt[3][:, 0:sw], ALU.subtract), reads=["mt2", "mt3"], writes=["bpim" + PB])
                        wre_s, wim_s = Wre[pb][:, sl], Wim[pb][:, sl]
                        bre_s, bim_s = bpre[pb][:, sl], bpim[pb][:, sl]
                        if d == 1:
                            wre_s, wim_s, bre_s, bim_s = _rev(wre_s), _rev(wim_s), _rev(bre_s), _rev(bim_s)
                        rb_ = rbc.broadcast_to([128, sw])
                        S.op("dve", _scan(wre_s, rb_, bre_s, STre[:, dg:dg + 1]), reads=["rc8", "bpre" + PB, "STre%d" % dg], writes=["Wre" + PB])
                        S.op("dve", _scan(wim_s, rb_, bim_s, STim[:, dg:dg + 1]), reads=["rc8", "bpim" + PB, "STim%d" % dg], writes=["Wim" + PB])
                        last = s0 if d == 1 else s0 + sw - 1
                        tlast = (128 - sw) if d == 1 else (sw - 1)
                        cl = COS[:, dg, tlast:tlast + 1]
                        sl_ = SIN[:, dg, tlast:tlast + 1]
                        wr1 = Wre[pb][:, last:last + 1]
                        wi1 = Wim[pb][:, last:last + 1]
                        S.op("dve", _ts(tmp4[:, 0:1], wi1, sl_, None, ALU.mult), reads=["Wim" + PB, "SIN"], writes=["tmp4a"])
                        S.op("dve", _ts(tmp4[:, 1:2], wi1, cl, None, ALU.mult), reads=["Wim" + PB, "COS"], writes=["tmp4b"])
                        S.op("dve", _stt(STre[:, dg:dg + 1], wr1, cl, tmp4[:, 0:1], ALU.mult, ALU.subtract),
                             reads=["Wre" + PB, "COS", "tmp4a"], writes=["STre%d" % dg])
                        S.op("dve", _stt(STim[:, dg:dg + 1], wr1, sl_, tmp4[:, 1:2], ALU.mult, ALU.add),
                             reads=["Wre" + PB, "SIN", "tmp4b"], writes=["STim%d" % dg])
                        if d == 1:
                            xre_o, xim_o = XBt[pb][:, 0, sl], XBt[pb][:, 1, sl]
                            kxo = "XBt" + PB
                        else:
                            xre_o = XF[pb][:, 0, 1 + s0:1 + s0 + sw]
                            xim_o = XF[pb][:, 1, 1 + s0:1 + s0 + sw]
                            kxo = "XF" + PB
                        S.op("pool", _tt(mp[0][:, 0:sw], Wre[pb][:, sl], cb_, ALU.mult), reads=["Wre" + PB, "COS"], writes=["mp0"])
                        S.op("pool", _tt(mp[1][:, 0:sw], Wim[pb][:, sl], sb_, ALU.mult), reads=["Wim" + PB, "SIN"], writes=["mp1"])
                        S.op("pool", _tt(xre_o, mp[0][:, 0:sw], mp[1][:, 0:sw], ALU.subtract), reads=["mp0", "mp1"], writes=[kxo])
                        S.op("pool", _tt(mp[2][:, 0:sw], Wre[pb][:, sl], sb_, ALU.mult), reads=["Wre" + PB, "SIN"], writes=["mp2"])
                        S.op("pool", _tt(mp[3][:, 0:sw], Wim[pb][:, sl], cb_, ALU.mult), reads=["Wim" + PB, "COS"], writes=["mp3"])
                        S.op("pool", _tt(xim_o, mp[2][:, 0:sw], mp[3][:, 0:sw], ALU.add), reads=["mp2", "mp3"], writes=[kxo])
                    if d == 1:
                        S.op("pool", _dma(XBS[:, gp, :, n0:n0 + w], XBt[pb][:, :, 0:w]), reads=["XBt" + PB], writes=["XBS"], dma=True)
                        continue
                    S.op("act", _acopy(XF[pb][:, :, 0:1], XFc[:, gp, :].unsqueeze(2)), reads=["XFc"], writes=["XF" + PB])
                    S.op("act", _acopy(XFc[:, gp, :].unsqueeze(2), XF[pb][:, :, w:w + 1]), reads=["XF" + PB], writes=["XFc"])
                    wl = w if n0 + w < NCH else w - 1
                    if wl > 0:
                        S.op("sp", _dma(XBt[pb][:, :, 0:wl], XBS[:, gp, :, n0 + 1:n0 + 1 + wl]), reads=["XBS"], writes=["XBt" + PB], dma=True)
                    if wl < w:
                        S.op("pool", _memset(XBt[pb][:, :, wl:w], 0.0), writes=["XBt" + PB])
                    for gl in range(2):
                        g = 2 * gp + gl
                        rs = slice(gl * 64, (gl + 1) * 64)
                        py = psY[g % 2]
                        kp = "psY%d" % (g % 2)
                        S.op("pe", _mm(py[:, 0:w], WU[:, g, :], ust[ub][:, g, 0:w], True, False), reads=["WU", "ust%d" % ub], writes=[kp])
                        S.op("pe", _mm(py[:, 0:w], WX[rs, gp, 0, :], XF[pb][rs, 0, 0:w], False, False), reads=["WX", "XF" + PB], writes=[kp])
                        S.op("pe", _mm(py[:, 0:w], WX[rs, gp, 1, :], XF[pb][rs, 1, 0:w], False, False), reads=["WX", "XF" + PB], writes=[kp])
                        S.op("pe", _mm(py[:, 0:w], WX[rs, 16 + gp, 0, :], XBt[pb][rs, 0, 0:w], False, False), reads=["WX", "XBt" + PB], writes=[kp])
                        S.op("pe", _mm(py[:, 0:w], WX[rs, 16 + gp, 1, :], XBt[pb][rs, 1, 0:w], False, True), reads=["WX", "XBt" + PB], writes=[kp])
                        S.op("act", _act(gst[:, g, 0:w], py[:, 0:w], AF.Gelu), reads=[kp], writes=["gst"])
                if d == 1:
                    continue
                for s_ in range(w // 16):
                    tok0 = 8 * n0 + 128 * s_
                    st = tok0 // 512
                    sub = (tok0 % 512) // 128
                    if st == 0:
                        continue
                    eb = 0
                    S.op("dve", _tt(ex[eb][:], gst[:, :, s_ * 16:(s_ + 1) * 16].unsqueeze(3).broadcast_to([128, 32, 16, 8]),
                                    maskJb[:].unsqueeze(1).unsqueeze(1).broadcast_to([128, 32, 16, 8]), ALU.mult),
                         reads=["gst", "maskJb"], writes=["ex%d" % eb])
                    for q in range(4):
                        for g8 in range(8):
                            S.op("pe", _mm(psG[:, q, :], selC[:, g8, :], ex[eb][:, 8 * q + g8, :, :].rearrange("p n j -> p (n j)"),
                                           g8 == 0, g8 == 7), reads=["selC", "ex%d" % eb], writes=["psG"])
                    S.op("act", _acopy(gel[:, :, sub * 128:(sub + 1) * 128], psG[:]), reads=["psG"], writes=["gel"])
                    if sub != 3:
                        continue
                    c0 = st * 512
                    S.op("sp", _dma(gat[:], Z[3072:4096, c0:c0 + 512].rearrange("(q p) t -> p q t", p=128)),
                         reads=["Z"], writes=["gat"], dma=True)
                    for c in range(8):
                        pb2 = 0
                        pa_, pg_ = psb[0], psb[1]
                        for k_ in range(4):
                            S.op("pe", _mm(pa_[:], wglu[:, k_, c * 128:(c + 1) * 128], gel[:, k_, :], k_ == 0, k_ == 3),
                                 reads=["wglu", "gel"], writes=["psb0"])
                        for k_ in range(4):
                            S.op("pe", _mm(pg_[:], wglu[:, k_, 1024 + c * 128:1024 + (c + 1) * 128], gel[:, k_, :], k_ == 0, k_ == 3),
                                 reads=["wglu", "gel"], writes=["psb1"])
                        S.op("act", _act(sgb[pb2][:], pg_[:], AF.Sigmoid), reads=["psb1"], writes=["sgb%d" % pb2])
                        S.op("act", _act(sga[pb2][:], gat[:, c, :], AF.Sigmoid), reads=["gat"], writes=["sga%d" % pb2])
                        S.op("dve", _tt(yab[pb2][:], pa_[:], sgb[pb2][:], ALU.mult), reads=["psb0", "sgb%d" % pb2], writes=["yab%d" % pb2])
                        S.op("dve", _tt(mao[:, c, :], yab[pb2][:], sga[pb2][:], ALU.mult), reads=["yab%d" % pb2, "sga%d" % pb2],
                             writes=["mao"])
                    S.op("pool", _dma(MA[:, c0:c0 + 512].rearrange("(q p) t -> p q t", p=128), mao[:]),
                         reads=["mao"], writes=["MA"], dma=True)
        S.fence()
        S.emit()

    pre_es = ExitStack()
    wq = pre_es.enter_context(nc.sbuf_tensor("P_wq", [128, 8, 2048], BF16))
    keysT = pre_es.enter_context(nc.sbuf_tensor("P_keysT", [128, 16, 128], BF16))

    with ExitStack() as es:
        def T(name, shape, dt):
            return es.enter_context(nc.sbuf_tensor("C_" + name, shape, dt))

        def P(name, shape, dt):
            return es.enter_context(nc.psum_tensor("C_" + name, shape, dt))

        whg = T("whg", [128, 4, D], BF16)
        wout = T("wout", [128, 8, D], BF16)
        wst = T("wst", [128, D], F32)
        masks = T("masks", [64, 128], F32)
        identf = T("identf", [128, 128], F32)
        identb = T("identb", [128, 128], BF16)
        onesb = T("onesb", [128, 128], BF16)
        lb = T("lb", [128, 4], F32)
        oml = T("oml", [128, 4], F32)
        lb1 = T("lb1", [128, 4], F32)
        hgn = T("hgn", [128, 4], F32)
        Sst = [T("Sst%d" % h, [128, 128], F32) for h in range(4)]
        zeros = T("zeros", [128, 64], F32)
        Sall = T("Sall", [128, 1024], F32)
        kvs = T("kvs", [128, 1024], F32)
        decz = T("decz", [128, 8], F32)
        dzf = T("dzf", [128, 1024], F32)
        NB = 2
        zq = [T("zq%d" % i, [128, 512], BF16) for i in range(NB)]
        zf = [T("zf%d" % i, [128, 512], BF16) for i in range(NB)]
        zo = [T("zo%d" % i, [128, 512], BF16) for i in range(NB)]
        Vc = [T("Vc%d" % i, [64, 8, 128], BF16) for i in range(NB)]
        qs = [T("qs%d" % i, [128, 512], F32) for i in range(NB)]
        gs = [T("gs%d" % i, [128, 512], F32) for i in range(NB)]
        ks = [T("ks%d" % i, [128, 512], F32) for i in range(NB)]
        Pc = [T("Pc%d" % i, [128, 512], F32) for i in range(NB)]
        rP = [T("rP%d" % i, [128, 512], F32) for i in range(NB)]
        dec = [T("dec%d" % i, [128, 8], F32) for i in range(NB)]
        qx = [T("qx%d" % i, [128, 512], BF16) for i in range(NB)]
        kx = [T("kx%d" % i, [128, 512], BF16) for i in range(NB)]
        ke = [T("ke%d" % i, [128, 512], BF16) for i in range(NB)]
        qi = [T("qi%d" % i, [128, 512], BF16) for i in range(NB)]
        scm = [T("scm%d" % i, [64, 8, 64], BF16) for i in range(NB)]
        ketm = [T("ketm%d" % i, [64, 8, 128], BF16) for i in range(NB)]
        Sbf = [T("Sbf%d" % i, [128, 8, 128], BF16) for i in range(NB)]
        osb = [T("osb%d" % i, [128, 512], F32) for i in range(NB)]
        obl = [T("obl%d" % i, [128, 512], F32) for i in range(NB)]
        sq = T("sq", [128, 512], BF16)
        rn = T("rn", [128, 512], F32)
        sgo = T("sgo", [128, 512], F32)
        yh = T("yh", [128, 4, 512], BF16)
        gbt = T("gbt", [128, 8, 512], BF16)
        mal = T("mal", [128, 8, 512], BF16)
        sgb = T("sgb", [128, 512], F32)
        tmpc = T("tmpc", [128, 512], F32)
        mixT = T("mixT", [128, 8, 512], BF16)
        xt = [T("xt%d" % i, [128, D], F32) for i in range(2)]
        hsb = [T("hsb%d" % i, [128, D], F32) for i in range(2)]
        psS = P("psS", [64, 8, 64], F32)
        psK = P("psK", [64, 8, 128], BF16)
        psKV = P("psKV", [128, 8, 128], F32)
        psO = P("psO", [128, 512], F32)
        psN = P("psN", [128, 512], F32)
        psY = [P("psY%d" % i, [128, 512], F32) for i in range(2)]

        S.op("sp", _dma(identf[:], I["identf"][:, :]), writes=["identf"], dma=True)
        S.op("dve", _copy(identb[:], identf[:]), reads=["identf"], writes=["identb"])
        S.op("dve", _memset(onesb[:], 1.0), writes=["onesb"])
        S.op("dve", _memset(zeros[:], 0.0), writes=["zeros"])
        S.op("sp", _dma(masks[:], I["masks"][:, :]), writes=["masks"], dma=True)
        S.op("sp", _dma(lb[:], I["lb0"][:, :]), writes=["lb"], dma=True)
        S.op("sp", _dma(lb1[:], I["lb1"][:, :]), writes=["lb1"], dma=True)
        S.op("sp", _dma(hgn[:], I["hgn"][:, :]), writes=["hgn"], dma=True)
        S.op("dve", _tt(lb[:], lb[:], lb1[:], ALU.subtract), reads=["lb", "lb1"], writes=["lb"])
        S.op("act", _act(lb[:], lb[:], AF.Sigmoid), reads=["lb"], writes=["lb"])
        S.op("dve", _ts(oml[:], lb[:], -1.0, 1.0, ALU.mult, ALU.add), reads=["lb"], writes=["oml"])
        for k in range(4):
            S.op("sp", _dma(wst[:], I["w_hg"][k * 128:(k + 1) * 128, :]), writes=["wst"], dma=True)
            S.op("act", _acopy(whg[:, k, :], wst[:]), reads=["wst"], writes=["whg"])
        for k in range(8):
            S.op("sp", _dma(wst[:], I["w_out"][k * 128:(k + 1) * 128, :]), writes=["wst"], dma=True)
            S.op("act", _acopy(wout[:, k, :], wst[:]), reads=["wst"], writes=["wout"])

        cnt = 0
        xcnt = [0]
        items = []
        for d in (1, 0):
            sts = list(range(NT - 1, 0, -1)) if d == 1 else list(range(NT))
            for si, st in enumerate(sts):
                for h in range(4):
                    items.append(dict(d=d, st=st, h=h, bi=cnt % NB, first=(si == 0)))
                    cnt += 1

        def prep(it):
            d, st, h, bi = it["d"], it["st"], it["h"], it["bi"]
            B = str(bi)
            c0 = st * 512
            full = (d == 0 and st >= 1)
            frow = 1536 if d == 1 else 1024
            S.op("sp", _dma(zq[bi][:], Z[512 + h * 128:512 + (h + 1) * 128, c0:c0 + 512]), reads=["Z"],
                 writes=["zq" + B], dma=True)
            S.op("sp", _dma(zf[bi][:], Z[frow + h * 128:frow + (h + 1) * 128, c0:c0 + 512]), reads=["Z"],
                 writes=["zf" + B], dma=True)
            S.op("sp", _dma(Vc[bi][:], VTM[c0:c0 + 512, h * 128:(h + 1) * 128].rearrange("(c s) v -> s c v", s=64)),
                 reads=["VTM"], writes=["Vc" + B], dma=True)
            if full:
                S.op("sp", _dma(zo[bi][:], Z[2560 + h * 128:2560 + (h + 1) * 128, c0:c0 + 512]), reads=["Z"],
                     writes=["zo" + B], dma=True)
                S.op("sp", _dma(obl[bi][:], OB[h * 128:(h + 1) * 128, c0:c0 + 512]), reads=["OB"],
                     writes=["obl" + B], dma=True)
            qs_, gs_, ks_, Pc_, rP_, dec_ = qs[bi], gs[bi], ks[bi], Pc[bi], rP[bi], dec[bi]
            S.op("act", _act(qs_[:], zq[bi][:], AF.Silu), reads=["zq" + B], writes=["qs" + B])
            S.op("act", _act(gs_[:], zf[bi][:], AF.Sigmoid), reads=["zf" + B], writes=["gs" + B])
            S.op("dve", _ts(gs_[:], gs_[:], oml[:, h:h + 1], lb[:, h:h + 1], ALU.mult, ALU.add),
                 reads=["gs" + B, "oml", "lb"], writes=["gs" + B])
            S.op("dve", _ts(ks_[:], gs_[:], -1.0, 1.0, ALU.mult, ALU.add), reads=["gs" + B], writes=["ks" + B])
            Pc3 = Pc_[:].rearrange("p (c j) -> p c j", j=64)
            gs3 = gs_[:].rearrange("p (c j) -> p c j", j=64)
            if d == 0:
                for ch in range(8):
                    sl = slice(ch * 64, (ch + 1) * 64)
                    S.op("dve", _scan(Pc_[:, sl], gs_[:, sl], zeros[:, 0:64], 1.0), reads=["gs" + B, "zeros"], writes=["Pc" + B])
                S.op("dve", _tt(qx[bi][:], qs_[:], Pc_[:], ALU.mult), reads=["qs" + B, "Pc" + B], writes=["qx" + B])
                S.op("dve", _recip(rP_[:], Pc_[:]), reads=["Pc" + B], writes=["rP" + B])
                S.op("dve", _tt(kx[bi][:], ks_[:], rP_[:], ALU.mult), reads=["ks" + B, "rP" + B], writes=["kx" + B])
                S.op("dve", _copy(dec_[:].unsqueeze(2), Pc3[:, :, 63:64]), reads=["Pc" + B], writes=["dec" + B])
                S.op("dve", _tt(ke[bi][:].rearrange("p (c j) -> p c j", j=64),
                                kx[bi][:].rearrange("p (c j) -> p c j", j=64),
                                dec_[:].unsqueeze(2).broadcast_to([128, 8, 64]), ALU.mult),
                     reads=["kx" + B, "dec" + B], writes=["ke" + B])
            else:
                S.op("dve", _memset(Pc3[:, :, 0:1], 1.0), writes=["Pc" + B])
                for ch in range(8):
                    S.op("dve", _scan(Pc_[:, ch * 64 + 1:(ch + 1) * 64], gs_[:, ch * 64:(ch + 1) * 64 - 1],
                                      zeros[:, 0:63], 1.0), reads=["gs" + B, "zeros"], writes=["Pc" + B])
                S.op("dve", _recip(rP_[:], Pc_[:]), reads=["Pc" + B], writes=["rP" + B])
                S.op("dve", _tt(qx[bi][:], qs_[:], rP_[:], ALU.mult), reads=["qs" + B, "rP" + B], writes=["qx" + B])
                S.op("dve", _tt(kx[bi][:], ks_[:], Pc_[:], ALU.mult), reads=["ks" + B, "Pc" + B], writes=["kx" + B])
                S.op("dve", _tt(dec_[:].unsqueeze(2), Pc3[:, :, 63:64], gs3[:, :, 63:64], ALU.mult),
                     reads=["Pc" + B, "gs" + B], writes=["dec" + B])
                S.op("dve", _tt(qi[bi][:].rearrange("p (c j) -> p c j", j=64),
                                qx[bi][:].rearrange("p (c j) -> p c j", j=64),
                                dec_[:].unsqueeze(2).broadcast_to([128, 8, 64]), ALU.mult),
                     reads=["qx" + B, "dec" + B], writes=["qi" + B])

        def rest(it):
            d, st, h, bi = it["d"], it["st"], it["h"], it["bi"]
            B = str(bi)
            c0 = st * 512
            full = (d == 0 and st >= 1)
            mk = masks[:, 64:128] if d == 1 else masks[:, 0:64]
            mkb = mk.unsqueeze(1).broadcast_to([64, 8, 64])
            dec_ = dec[bi]
            if it["first"]:
                S.op("dve", _memset(Sst[h][:], 0.0), writes=["Sst%d" % h])
            if full and h == 0:
                S.op("sp", _dma(gbt[:], Z[4096:5120, c0:c0 + 512].rearrange("(q p) t -> p q t", p=128)),
                     reads=["Z"], writes=["gbt"], dma=True)
                S.op("sp", _dma(mal[:], MA[:, c0:c0 + 512].rearrange("(q p) t -> p q t", p=128)),
                     reads=["MA"], writes=["mal"], dma=True)
            if d == 0:
                qiT, keT = qx[bi], ke[bi]
                kq, kk_ = "qx" + B, "ke" + B
            else:
                qiT, keT = qi[bi], kx[bi]
                kq, kk_ = "qi" + B, "kx" + B
            for ch in range(8):
                sl = slice(ch * 64, (ch + 1) * 64)
                S.op("pe", _mm(psS[:, ch, :], kx[bi][:, sl], qx[bi][:, sl], True, True),
                     reads=["kx" + B, "qx" + B], writes=["psS"])
            for ch in range(8):
                sl = slice(ch * 64, (ch + 1) * 64)
                S.op("pe", _tr(psK[:, ch, :], keT[:, sl], identb[:]), reads=[kk_, "identb"], writes=["psK"])
            S.op("dve", _tt(scm[bi][:], psS[:], mkb, ALU.mult), reads=["psS", "masks"], writes=["scm" + B])
            S.op("act", _acopy(ketm[bi][:], psK[:]), reads=["psK"], writes=["ketm" + B])
            for ch in range(8):
                S.op("pe", _mm(psKV[:, ch, :], ketm[bi][:, ch, :], Vc[bi][:, ch, :], True, True),
                     reads=["ketm" + B, "Vc" + B], writes=["psKV"])
            kvv = psKV[:].rearrange("p c v -> p v c")
            sal = Sall[:].rearrange("p (v c) -> p v c", c=8)
            c_in = 7 if d == 1 else 0
            S.op("dve", _copy(decz[:], dec_[:]), reads=["dec" + B], writes=["decz"])
            S.op("dve", _memset(decz[:, c_in:c_in + 1], 0.0), writes=["decz"])
            S.op("dve", _copy(kvs[:].rearrange("p (v c) -> p v c", c=8), kvv), reads=["psKV"], writes=["kvs"])
            S.op("dve", _stt(kvs[:].rearrange("p (v c) -> p v c", c=8)[:, :, c_in], Sst[h][:], dec_[:, c_in:c_in + 1],
                             kvs[:].rearrange("p (v c) -> p v c", c=8)[:, :, c_in], ALU.mult, ALU.add),
                 reads=["Sst%d" % h, "dec" + B, "kvs"], writes=["kvs"])
            S.op("dve", _copy(dzf[:].rearrange("p (v c) -> p v c", c=8), decz[:].unsqueeze(1).broadcast_to([128, 128, 8])),
                 reads=["decz"], writes=["dzf"])
            if d == 0:
                S.op("dve", _scan(Sall[:], dzf[:], kvs[:], 0.0), reads=["dzf", "kvs"], writes=["Sall"])
            else:
                S.op("dve", _scan(_rev(Sall[:]), _rev(dzf[:]), _rev(kvs[:]), 0.0), reads=["dzf", "kvs"], writes=["Sall"])
            salc = Sall[:].rearrange("p (v c) -> p c v", c=8)
            if d == 0:
                S.op("act", _acopy(Sbf[bi][:, 0, :], Sst[h][:]), reads=["Sst%d" % h], writes=["Sbf" + B])
                S.op("act", _acopy(Sbf[bi][:, 1:8, :], salc[:, 0:7, :]), reads=["Sall"], writes=["Sbf" + B])
                S.op("dve", _copy(Sst[h][:], salc[:, 7, :]), reads=["Sall"], writes=["Sst%d" % h])
            else:
                S.op("act", _acopy(Sbf[bi][:, 7, :], Sst[h][:]), reads=["Sst%d" % h], writes=["Sbf" + B])
                S.op("act", _acopy(Sbf[bi][:, 0:7, :], salc[:, 1:8, :]), reads=["Sall"], writes=["Sbf" + B])
                S.op("dve", _copy(Sst[h][:], salc[:, 0, :]), reads=["Sall"], writes=["Sst%d" % h])
            if d == 0 and st == 0:
                return
            for ch in range(8):
                sl = slice(ch * 64, (ch + 1) * 64)
                S.op("pe", _mm(psO[:, sl], Vc[bi][:, ch, :], scm[bi][:, ch, :], True, False),
                     reads=["Vc" + B, "scm" + B], writes=["psO"])
                S.op("pe", _mm(psO[:, sl], Sbf[bi][:, ch, :], qiT[:, sl], False, True),
                     reads=["Sbf" + B, kq], writes=["psO"])
            if d == 1:
                S.op("act", _acopy(osb[bi][:], psO[:]), reads=["psO"], writes=["osb" + B])
                S.op("pool", _dma(OB[h * 128:(h + 1) * 128, c0:c0 + 512], osb[bi][:]), reads=["osb" + B],
                     writes=["OB"], dma=True)
                return
            S.op("dve", _tt(osb[bi][:], psO[:], obl[bi][:], ALU.add), reads=["psO", "obl" + B], writes=["osb" + B])
            S.op("act", _act(sq[:], osb[bi][:], AF.Square), reads=["osb" + B], writes=["sq"])
            S.op("pe", _mm(psN[:], onesb[:], sq[:], True, True), reads=["onesb", "sq"], writes=["psN"])
            S.op("dve", _ts(rn[:], psN[:], 1.0 / 128, EPS, ALU.mult, ALU.add), reads=["psN"], writes=["rn"])
            S.op("act", _act(rn[:], rn[:], AF.Sqrt), reads=["rn"], writes=["rn"])
            S.op("dve", _recip(rn[:], rn[:]), reads=["rn"], writes=["rn"])
            S.op("dve", _tt(rn[:], rn[:], osb[bi][:], ALU.mult), reads=["rn", "osb" + B], writes=["rn"])
            S.op("act", _act(sgo[:], zo[bi][:], AF.Silu), reads=["zo" + B], writes=["sgo"])
            S.op("dve", _stt(yh[:, h, :], rn[:], hgn[:, h:h + 1], sgo[:], ALU.mult, ALU.mult),
                 reads=["rn", "hgn", "sgo"], writes=["yh"])
            if h != 3:
                return
            for c in range(8):
                py = psY[c % 2]
                kp = "psY%d" % (c % 2)
                for k in range(4):
                    S.op("pe", _mm(py[:], whg[:, k, c * 128:(c + 1) * 128], yh[:, k, :], k == 0, k == 3),
                         reads=["whg", "yh"], writes=[kp])
                S.op("act", _act(sgb[:], gbt[:, c, :], AF.Sigmoid), reads=["gbt"], writes=["sgb"])
                S.op("dve", _tt(tmpc[:], py[:], sgb[:], ALU.mult), reads=[kp, "sgb"], writes=["tmpc"])
                S.op("dve", _tt(mixT[:, c, :], tmpc[:], mal[:, c, :], ALU.add), reads=["tmpc", "mal"], writes=["mixT"])
            for j in range(4):
                xb = xcnt[0] % 2
                xcnt[0] += 1
                r0 = (st - 1) * 512 + j * 128
                S.op("sp", _dma(xt[xb][:], I["xs"][r0:r0 + 128, :]), writes=["xt%d" % xb], dma=True)
                for hf in range(2):
                    py = psY[hf]
                    kp = "psY%d" % hf
                    for k in range(8):
                        S.op("pe", _mm(py[:], mixT[:, k, j * 128:(j + 1) * 128], wout[:, k, hf * 512:(hf + 1) * 512],
                                       k == 0, k == 7), reads=["mixT", "wout"], writes=[kp])
                    S.op("dve", _tt(hsb[xb][:, hf * 512:(hf + 1) * 512], py[:], xt[xb][:, hf * 512:(hf + 1) * 512], ALU.add),
                         reads=[kp, "xt%d" % xb], writes=["hsb%d" % xb])
                S.op("pool", _dma(HS1[r0:r0 + 128, :], hsb[xb][:]), reads=["hsb%d" % xb], writes=["HS1"], dma=True)

        conv = list(range(128))
        cpos = [0]

        def conv_step():
            if cpos[0] >= len(conv):
                return
            r = conv[cpos[0]]
            cpos[0] += 1
            S.op("pool", _dma(UVB[r * 128:(r + 1) * 128, :], I["puv"][r * 128:(r + 1) * 128, :]), writes=["UVB"], dma=True)

        per_it = 1
        wjobs = [("wq", k, hf) for k in range(8) for hf in range(2)] + [("keysT", 0, hf) for hf in range(2)]
        wpos = [0]

        def wq_step():
            if wpos[0] >= len(wjobs):
                return
            nm, k, hf = wjobs[wpos[0]]
            wpos[0] += 1
            cs = slice(hf * D, (hf + 1) * D)
            if nm == "wq":
                S.op("sp", _dma(wst[:], I["wq"][k * 128:(k + 1) * 128, cs]), writes=["wst"], dma=True)
                S.op("act", _acopy(wq[:, k, cs], wst[:]), reads=["wst"], writes=["wq"])
            else:
                S.op("sp", _dma(wst[:], I["keysT"][:, cs]), writes=["wst"], dma=True)
                S.op("act", _acopy(keysT[:].rearrange("p a b -> p (a b)")[:, cs], wst[:]), reads=["wst"], writes=["keysT"])
        prep(items[0])
        for ii, it in enumerate(items):
            if ii + 1 < len(items):
                prep(items[ii + 1])
            rest(it)
            for _ in range(per_it):
                conv_step()
            wq_step()
        while cpos[0] < len(conv):
            conv_step()
        while wpos[0] < len(wjobs):
            wq_step()
        S.fence()
        S.emit()

    with ExitStack() as es:
        def T(name, shape, dt):
            return es.enter_context(nc.sbuf_tensor("D_" + name, shape, dt))

        def P(name, shape, dt):
            return es.enter_context(nc.psum_tensor("D_" + name, shape, dt))

        g2b = T("g2b", [128, D], F32)
        gfb = T("gfb", [128, D], F32)
        identf = T("identf", [128, 128], F32)
        identb = T("identb", [128, 128], BF16)
        iota16 = T("iota16", [128, 16], F32)
        tk = T("tk", [128, ND], I32)
        hs = [T("hs%d" % i, [128, D], F32) for i in range(2)]
        h2 = [T("h2%d" % i, [128, D], F32) for i in range(2)]
        ei = [T("ei%d" % i, [128, 128], I32) for i in range(2)]
        gate = [T("gate%d" % i, [128, 8, 16], F32) for i in range(2)]
        h2T = T("h2T", [128, 8, 128], BF16)
        qT = T("qT", [128, 16, 128], BF16)
        sc = T("sc", [128, 16, 128], F32)
        sc2 = T("sc2", [128, 16, 128], F32)
        top = T("top", [128, 8, 2, 16], F32)
        tix = T("tix", [128, 8, 2, 16], U32)
        tixf = T("tixf", [128, 8, 2, 16], F32)
        cand = T("cand", [128, 8, 256], F32)
        cand2 = T("cand2", [128, 8, 256], F32)
        ctop = T("ctop", [128, 8, 16], F32)
        cix = T("cix", [128, 8, 16], U32)
        cab = T("cab", [128, 8, 16], U32)
        caf = T("caf", [128, 8, 16], F32)
        cbf = T("cbf", [128, 8, 16], F32)
        eq = T("eq", [128, 8, 16, 16], F32)
        i1f = T("i1f", [128, 8, 16], F32)
        i2f = T("i2f", [128, 8, 16], F32)
        ssum = T("ssum", [128, 8], F32)
        sm = T("sm", [128, 8], F32)
        h2bf = [T("h2bf%d" % i, [128, D], BF16) for i in range(2)]
        actv = T("actv", [128, 128], F32)
        wgt = T("wgt", [128, 128], F32)
        dg = [T("dg%d" % i, [128, 8, 128], BF16) for i in range(2)]
        acc = T("acc", [128, D], F32)
        junkb = T("junkb", [128, D], BF16)
        junka = T("junka", [128, D], BF16)
        prod = [T("prod%d" % i, [128, D], BF16) for i in range(4)]
        outb = T("outb", [128, D], F32)
        pT = P("pT", [128, 8, 128], BF16)
        pQ = P("pQ", [128, 16, 128], F32)
        pacc = P("pacc", [128, D], F32)

        S.op("sp", _dma(identf[:], I["identf"][:, :]), writes=["identf"], dma=True)
        S.op("dve", _copy(identb[:], identf[:]), reads=["identf"], writes=["identb"])
        S.op("sp", _dma(iota16[:], I["iota16"][:, :]), writes=["iota16"], dma=True)
        S.op("sp", _dma(tk[:], I["tokd"][:, :]), writes=["tk"], dma=True)
        S.op("sp", _dma(g2b[:], I["g2b"][:, :]), writes=["g2b"], dma=True)
        S.op("sp", _dma(gfb[:], I["gfb"][:, :]), writes=["gfb"], dma=True)
        with ExitStack() as esp:
            wst = esp.enter_context(nc.sbuf_tensor("Dp_wst", [128, 2048], F32))
            S.fence()
            S.emit()
        NSB = 15
        uvg = [T("uvg%d" % i, [128, 2 * D], BF16) for i in range(NSB)]

        def stage1(i):
            b = i % 2
            B = str(b)
            S.op("pool", _gather(hs[b][:], HS1[:, :], tk[:, i:i + 1]), reads=["tk", "HS1"], writes=["hs" + B], dma=True)
            S.op("dve", _memset(ssum[:, 0:1], 0.0), writes=["ssum"])
            S.op("dve", _stt(junkb[:], hs[b][:], 1.0, hs[b][:], ALU.mult, ALU.mult, accum_out=ssum[:, 0:1]),
                 reads=["hs" + B], writes=["junkb", "ssum"])
            S.op("dve", _ts(sm[:, 0:1], ssum[:, 0:1], 1.0 / D, EPS, ALU.mult, ALU.add), reads=["ssum"], writes=["sm"])
            S.op("act", _act(sm[:, 0:1], sm[:, 0:1], AF.Sqrt), reads=["sm"], writes=["sm"])
            S.op("dve", _recip(sm[:, 0:1], sm[:, 0:1]), reads=["sm"], writes=["sm"])
            S.op("dve", _stt(h2[b][:], hs[b][:], sm[:, 0:1], g2b[:], ALU.mult, ALU.mult), reads=["hs" + B, "sm", "g2b"],
                 writes=["h2" + B])
            S.op("act", _acopy(h2bf[b][:], h2[b][:]), reads=["h2" + B], writes=["h2bf" + B])
            for k in range(8):
                S.op("pe", _tr(pT[:, k, :], h2bf[b][:, k * 128:(k + 1) * 128], identb[:]), reads=["h2bf" + B, "identb"], writes=["pT"])
            S.op("act", _acopy(h2T[:], pT[:]), reads=["pT"], writes=["h2T"])
            for hp in range(16):
                for k in range(8):
                    S.op("pe", _mm(pQ[:, hp, :], wq[:, k, hp * 128:(hp + 1) * 128], h2T[:, k, :], k == 0, k == 7),
                         reads=["wq", "h2T"], writes=["pQ"])
            S.op("act", _acopy(qT[:], pQ[:]), reads=["pQ"], writes=["qT"])
            for hp in range(16):
                S.op("pe", _mm(pQ[:, hp, :], qT[:, hp, :], keysT[:, hp, :], True, True), reads=["qT", "keysT"], writes=["pQ"])
            S.op("act", _acopy(sc[:], pQ[:]), reads=["pQ"], writes=["sc"])
            HP = [(hp, hp // 2, hp % 2) for hp in range(16)]
            for hp, h_, p_ in HP:
                S.op("dve", lambda e, h_=h_, p_=p_, hp=hp: e.max(out=top[:, h_, p_, 0:8], in_=sc[:, hp, :]),
                     reads=["sc"], writes=["top%d" % hp])
            for hp, h_, p_ in HP:
                S.op("dve", lambda e, h_=h_, p_=p_, hp=hp: e.max_index(out=tix[:, h_, p_, 0:8], in_max=top[:, h_, p_, 0:8],
                                                                       in_values=sc[:, hp, :]),
                     reads=["sc", "top%d" % hp], writes=["tix%d" % hp])
            for hp, h_, p_ in HP:
                S.op("dve", lambda e, h_=h_, p_=p_, hp=hp: e.match_replace(out=sc2[:, hp, :], in_to_replace=top[:, h_, p_, 0:8],
                                                                           in_values=sc[:, hp, :], imm_value=-1e30),
                     reads=["sc", "top%d" % hp], writes=["sc2_%d" % hp])
            for hp, h_, p_ in HP:
                S.op("dve", lambda e, h_=h_, p_=p_, hp=hp: e.max(out=top[:, h_, p_, 8:16], in_=sc2[:, hp, :]),
                     reads=["sc2_%d" % hp], writes=["topb%d" % hp])
            for hp, h_, p_ in HP:
                S.op("dve", lambda e, h_=h_, p_=p_, hp=hp: e.max_index(out=tix[:, h_, p_, 8:16], in_max=top[:, h_, p_, 8:16],
                                                                       in_values=sc2[:, hp, :]),
                     reads=["sc2_%d" % hp, "topb%d" % hp], writes=["tixb%d" % hp])
            tk_all = ["top%d" % x for x in range(16)] + ["topb%d" % x for x in range(16)]
            ti_all = ["tix%d" % x for x in range(16)] + ["tixb%d" % x for x in range(16)]
            S.op("dve", _copy(tixf[:], tix[:]), reads=ti_all, writes=["tixf"])
            cand4 = cand[:].rearrange("p h (a b) -> p h a b", a=16)
            S.op("dve", _tt(cand4, top[:, :, 0, :].unsqueeze(3).broadcast_to([128, 8, 16, 16]),
                            top[:, :, 1, :].unsqueeze(2).broadcast_to([128, 8, 16, 16]), ALU.add),
                 reads=tk_all, writes=["cand"])
            for h_ in range(8):
                S.op("dve", lambda e, h_=h_: e.max(out=ctop[:, h_, 0:8], in_=cand[:, h_, :]), reads=["cand"], writes=["ctop%d" % h_])
            for h_ in range(8):
                S.op("dve", lambda e, h_=h_: e.max_index(out=cix[:, h_, 0:8], in_max=ctop[:, h_, 0:8], in_values=cand[:, h_, :]),
                     reads=["cand", "ctop%d" % h_], writes=["cix%d" % h_])
            for h_ in range(8):
                S.op("dve", lambda e, h_=h_: e.match_replace(out=cand2[:, h_, :], in_to_replace=ctop[:, h_, 0:8],
                                                             in_values=cand[:, h_, :], imm_value=-1e30),
                     reads=["cand", "ctop%d" % h_], writes=["cand2_%d" % h_])
            for h_ in range(8):
                S.op("dve", lambda e, h_=h_: e.max(out=ctop[:, h_, 8:16], in_=cand2[:, h_, :]), reads=["cand2_%d" % h_],
                     writes=["ctopb%d" % h_])
            for h_ in range(8):
                S.op("dve", lambda e, h_=h_: e.max_index(out=cix[:, h_, 8:16], in_max=ctop[:, h_, 8:16], in_values=cand2[:, h_, :]),
                     reads=["cand2_%d" % h_, "ctopb%d" % h_], writes=["cixb%d" % h_])
            S.op("dve", lambda e: e.tensor_single_scalar(out=cab[:], in_=cix[:], scalar=4, op=ALU.logical_shift_right),
                 reads=["cix%d" % x for x in range(8)] + ["cixb%d" % x for x in range(8)], writes=["cab"])
            S.op("dve", _copy(caf[:], cab[:]), reads=["cab"], writes=["caf"])
            S.op("dve", lambda e: e.tensor_single_scalar(out=cab[:], in_=cix[:], scalar=15, op=ALU.bitwise_and),
                 reads=["cix%d" % x for x in range(8)] + ["cixb%d" % x for x in range(8)], writes=["cab"])
            S.op("dve", _copy(cbf[:], cab[:]), reads=["cab"], writes=["cbf"])
            io4 = iota16[:, :].unsqueeze(1).unsqueeze(1).broadcast_to([128, 8, 16, 16])
            for (src, half, dst, kd) in ((caf, 0, i1f, "i1f"), (cbf, 1, i2f, "i2f")):
                S.op("dve", _tt(eq[:], src[:].unsqueeze(3).broadcast_to([128, 8, 16, 16]), io4, ALU.is_equal),
                     reads=["caf", "cbf", "iota16"], writes=["eq"])
                S.op("dve", _tt(eq[:], eq[:], tixf[:, :, half, :].unsqueeze(2).broadcast_to([128, 8, 16, 16]), ALU.mult),
                     reads=["eq", "tixf"], writes=["eq"])
                S.op("dve", _rsum(dst[:], eq[:]), reads=["eq"], writes=[kd])
            S.op("dve", _stt(i1f[:], i1f[:], 128.0, i2f[:], ALU.mult, ALU.add), reads=["i1f", "i2f"], writes=["i1f"])
            S.op("dve", _ts(i1f[:], i1f[:], 0.0, 16383.0, ALU.max, ALU.min), reads=["i1f"], writes=["i1f"])
            S.op("dve", _copy(ei[b][:].rearrange("p (h j) -> p h j", h=8), i1f[:]), reads=["i1f"], writes=["ei" + B])
            S.op("dve", _tt(gate[b][:], ctop[:], ctop[:, :, 0:1].broadcast_to([128, 8, 16]), ALU.subtract),
                 reads=["ctop%d" % x for x in range(8)] + ["ctopb%d" % x for x in range(8)], writes=["gate" + B])
            S.op("act", _act(gate[b][:], gate[b][:], AF.Exp), reads=["gate" + B], writes=["gate" + B])
            S.op("dve", _rsum(ssum[:], gate[b][:]), reads=["gate" + B], writes=["ssum"])
            S.op("dve", _recip(ssum[:], ssum[:]), reads=["ssum"], writes=["ssum"])
            S.op("dve", _tt(gate[b][:], gate[b][:], ssum[:].unsqueeze(2).broadcast_to([128, 8, 16]), ALU.mult),
                 reads=["gate" + B, "ssum"], writes=["gate" + B])

        gcnt = [0]
        pcnt = [0]

        def stage2(i, pend=()):
            b = i % 2
            B = str(b)
            pend = list(pend)
            per_slot = -(-len(pend) // 112) if pend else 0
            ppos = [0]

            def drain(n):
                for o in pend[ppos[0]:ppos[0] + n]:
                    S.op(*o)
                ppos[0] += n
            S.op("dve", _memset(actv[:], 0.0), writes=["actv%d" % x for x in range(128)])
            slots = []
            for j in range(128):
                gb = gcnt[0] % NSB
                gcnt[0] += 1
                slots.append(gb)
                S.op("pool", _gather(uvg[gb][:], UVB[:, :], ei[b][:, j:j + 1]), reads=["ei" + B, "UVB"], writes=["uvg%d" % gb],
                     dma=True)
                if j % 8 in (0, 1, 2, 4, 5):
                    pi_ = pcnt[0] % 4
                    pcnt[0] += 1
                    S.op("dve", _tt(prod[pi_][:], uvg[gb][:, 0:D], h2bf[b][:], ALU.mult), reads=["uvg%d" % gb, "h2bf" + B],
                         writes=["prod%d" % pi_])
                    S.op("act", lambda e, pi_=pi_, j=j: e.activation(out=junka[:], in_=prod[pi_][:], func=AF.Copy,
                                                                   accum_out=actv[:, j:j + 1]),
                         reads=["prod%d" % pi_], writes=["junka", "actv%d" % j])
                else:
                    S.op("dve", _stt(junkb[:], uvg[gb][:, 0:D], 1.0, h2bf[b][:], ALU.mult, ALU.mult, accum_out=actv[:, j:j + 1]),
                         reads=["uvg%d" % gb, "h2bf" + B], writes=["junkb", "actv%d" % j])
                drain(per_slot)
                if j % 8 == 7:
                    g0 = j - 7
                    db = (j // 8) % 2
                    S.op("act", _act(wgt[:, g0:j + 1], actv[:, g0:j + 1], AF.Gelu), reads=["actv%d" % x for x in range(g0, j + 1)],
                         writes=["wgt"])
                    S.op("dve", _tt(wgt[:, g0:j + 1], wgt[:, g0:j + 1], gate[b][:].rearrange("p h j -> p (h j)")[:, g0:j + 1], ALU.mult),
                         reads=["wgt", "gate" + B], writes=["wgt"])
                    for s_ in range(8):
                        S.op("act", _act(dg[db][:, s_, :], identb[:], AF.Copy, scale=wgt[:, g0 + s_:g0 + s_ + 1]),
                             reads=["identb", "wgt"], writes=["dg%d_%d" % (db, s_)])
                    for s_ in range(8):
                        jj = g0 + s_
                        sb_ = slots[jj]
                        for hf in range(2):
                            S.op("pe", _mm(pacc[:, hf * 512:(hf + 1) * 512], dg[db][:, s_, :],
                                           uvg[sb_][:, D + hf * 512:D + (hf + 1) * 512], jj == 0, jj == 127),
                                 reads=["dg%d_%d" % (db, s_), "uvg%d" % sb_], writes=["pacc"])
            drain(len(pend))
            S.op("dve", _tt(acc[:], pacc[:], hs[b][:], ALU.add), reads=["pacc", "hs" + B], writes=["acc"])
            S.op("dve", _memset(sm[:, 1:2], 0.0), writes=["sm1"])
            S.op("dve", _stt(junkb[:], acc[:], 1.0, acc[:], ALU.mult, ALU.mult, accum_out=sm[:, 1:2]), reads=["acc"],
                 writes=["junkb", "sm1"])
            S.op("dve", _ts(sm[:, 1:2], sm[:, 1:2], 1.0 / D, EPS, ALU.mult, ALU.add), reads=["sm1"], writes=["sm1"])
            S.op("act", _act(sm[:, 1:2], sm[:, 1:2], AF.Sqrt), reads=["sm1"], writes=["sm1"])
            S.op("dve", _recip(sm[:, 1:2], sm[:, 1:2]), reads=["sm1"], writes=["sm1"])
            S.op("dve", _stt(outb[:], acc[:], sm[:, 1:2], gfb[:], ALU.mult, ALU.mult), reads=["acc", "sm1", "gfb"],
                 writes=["outb"])
            S.op("sp", _dma(OUT[i * 128:(i + 1) * 128, :], outb[:]), reads=["outb"], writes=["OUT"], dma=True)

        stage1(0)
        for i in range(ND):
            pend = []
            if i + 1 < ND:
                S.capture = pend
                stage1(i + 1)
                S.capture = None
            stage2(i, pend)
        S.fence()
        S.emit()

    pre_es.close()
    top_es.close()
    return nc


def _common_inputs(inp):
    f = np.float32
    A = {}
    A["meta"] = np.ascontiguousarray(inp["meta"], f)
    A["g1c"] = np.ascontiguousarray(inp["norm1_g"][0].reshape(8, 128).T, f)
    A["w_in"] = np.ascontiguousarray(inp["w_in"][0], f)
    lre, lim, lst = inp["s5_lam_re"][0], inp["s5_lam_im"][0], inp["s5_log_step"][0]
    lst_full = np.broadcast_to(lst[:, :, None], (2, 32, 64))

    def col(a):
        a5 = a.reshape(2, 16, 2, 64)
        return np.ascontiguousarray(a5.transpose(2, 3, 0, 1).reshape(128, 32), f)

    A["lre_c"], A["lim_c"], A["lst_c"] = col(lre), col(lim), col(lst_full)

    def c8(cm):
        a = cm.reshape(16, 2, 16, 64).transpose(1, 3, 0, 2)
        a = np.broadcast_to(a[:, :, :, None, :], (2, 64, 16, 8, 16))
        return np.ascontiguousarray(a.reshape(128, 2048), f)

    def bc(bm):
        a = bm.reshape(16, 2, 64, 16).transpose(1, 2, 0, 3)
        a = np.broadcast_to(a[:, :, :, None, :], (2, 64, 16, 8, 16))
        return np.ascontiguousarray(a.reshape(128, 2048), f)

    A["cre8"], A["cim8"] = c8(inp["s5_c_re"][0]), c8(inp["s5_c_im"][0])
    A["brec"], A["bimc"] = bc(inp["s5_b_re"][0]), bc(inp["s5_b_im"][0])
    dd = inp["s5_d"][0].reshape(32, 16)
    A["drow"] = np.ascontiguousarray(np.broadcast_to(dd.T[None, :, :], (8, 16, 32)).reshape(128, 32), f)
    p = np.arange(128)
    cst8 = np.zeros((128, 32), f)
    cst8[:, 0:8] = (p[:, None] % 8 == np.arange(8)[None, :])
    cst8[:, 8:24] = (p[:, None] // 8 == np.arange(16)[None, :])
    cst8[:, 24:32] = (p[:, None] // 16 == np.arange(8)[None, :])
    A["cst8"] = cst8
    m = np.arange(128)
    selC = np.zeros((128, 8, 128), f)
    for g8 in range(8):
        selC[:, g8, :] = ((p[:, None] % 16) == (m[None, :] % 16)) & ((m[None, :] // 16) == g8)
    A["selC"] = selC.reshape(128, 1024)
    A["ramp16"] = np.ascontiguousarray(np.broadcast_to(np.arange(-7, 9, dtype=f)[None, :], (128, 16)), f)
    ii = p[:, None] // 16
    jj = m[None, :] // 16
    A["mfb"] = np.concatenate([(ii <= jj), (ii >= jj)], axis=1).astype(f)
    A["w_glu"] = np.ascontiguousarray(inp["w_glu"][0], f)
    A["lb0"] = np.ascontiguousarray(inp["hg_lb"][0].reshape(4, 128).T, f)
    A["lb1"] = np.ascontiguousarray(inp["hg_lb"][1].reshape(4, 128).T, f)
    A["hgn"] = np.ascontiguousarray(inp["hg_norm_g"][0].reshape(4, 128).T, f)
    A["w_hg"] = np.ascontiguousarray(inp["w_hg_out"][0], f)
    A["w_out"] = np.ascontiguousarray(inp["w_out"][0], f)
    A["g2b"] = np.ascontiguousarray(np.broadcast_to(inp["norm2_g"][0][None, :], (128, D)), f)
    A["gfb"] = np.ascontiguousarray(np.broadcast_to(inp["final_g"][None, :], (128, D)), f)
    A["wq"] = np.ascontiguousarray(inp["peer_wq"][0], f)
    kz = inp["peer_keys"][0].reshape(16, 128, 128)
    A["keysT"] = np.ascontiguousarray(kz.transpose(2, 0, 1).reshape(128, 2048), f)
    A["puv"] = np.ascontiguousarray(np.concatenate([inp["peer_u"][0], inp["peer_v"][0]], axis=1), f)
    A["identf"] = np.eye(128, dtype=f)
    r = np.arange(1, 129, dtype=f)
    A["ramps"] = np.ascontiguousarray(np.broadcast_to(np.concatenate([r, r[::-1]])[None, :], (128, 256)), f)
    s = np.arange(64)[:, None]
    t = np.arange(64)[None, :]
    A["masks"] = np.concatenate([(t >= s), (t <= s)], axis=1).astype(f)
    A["iota16"] = np.ascontiguousarray(np.broadcast_to(np.arange(16, dtype=f)[None, :], (128, 16)), f)
    return A


_PROG_CACHE = {}


def run_sequences(inp, seqs, assign, ND, debug=False):
    SEQ = seqs[0].shape[0]
    key = (SEQ, ND, debug)
    if key not in _PROG_CACHE:
        _PROG_CACHE[key] = build_program(SEQ, ND, debug)
    nc = _PROG_CACHE[key]
    A = _common_inputs(inp)
    in_maps = []
    for (si, t0, nt) in assign:
        m = dict(A)
        m["xs"] = np.ascontiguousarray(seqs[si], np.float32)
        tiles = [min(t0 + i, t0 + nt - 1) for i in range(ND)]
        tok = np.stack([np.arange(t * 128, (t + 1) * 128) for t in tiles], axis=1).astype(np.int32)
        m["tokd"] = np.ascontiguousarray(tok)
        in_maps.append(m)
    res = run_bass_kernel_spmd(nc, in_maps, core_ids=list(range(len(assign))))
    outs = [np.zeros((SEQ, D), np.float32) for _ in seqs]
    for ci, (si, t0, nt) in enumerate(assign):
        o = res.results[ci]["outd"]
        outs[si][t0 * 128:(t0 + nt) * 128] = o[:nt * 128]
    return outs, res


def kernel(**inputs):
    inp = {k: np.asarray(v) for k, v in inputs.items()}
    seqs = [inp["x_prompt"][0], inp["x_sample"][0], inp["x_sample"][1]]
    assign = [(0, 0, 43), (1, 0, 43), (2, 0, 64), (0, 43, 43), (1, 43, 43), (2, 64, 64), (0, 86, 42), (1, 86, 42)]
    outs, _ = run_sequences(inp, seqs, assign, ND=64)
    y_prompt = outs[0][None].astype(np.float32)
    y_sample = np.stack([outs[1], outs[2]], axis=0).astype(np.float32)
    return (y_prompt, y_sample)
```

```python
import math
from contextlib import ExitStack

import numpy as np
import concourse.bass as bass
import concourse.mybir as mybir
from concourse.bass_utils import run_bass_kernel_spmd

F32 = mybir.dt.float32
BF16 = mybir.dt.bfloat16
I32 = mybir.dt.int32
U32 = mybir.dt.uint32
ALU = mybir.AluOpType
AF = mybir.ActivationFunctionType
AX = mybir.AxisListType

D = 1024
NCOL = 5120
EPS = 1e-6
PI = math.pi
ENGS = ("pe", "act", "dve", "pool", "sp")


class Sch:
    def __init__(self, nc, es, n_dma_sems=32):
        self.nc = nc
        self.esem = {e: es.enter_context(nc.semaphore("se_" + e)) for e in ENGS}
        self.ecnt = {e: 0 for e in ENGS}
        self.dsem = [es.enter_context(nc.semaphore("sd%d" % i)) for i in range(n_dma_sems)]
        self.dval = [0] * n_dma_sems
        self.dnext = {"hw": 0, "sw": 0}
        self.dhalf = n_dma_sems // 2
        self.lastw = {}
        self.readers = {}
        self.known = {e: {} for e in ENGS}
        self.ops = {e: [] for e in ENGS}
        self.nops = 0
        self.capture = None

    def _sem(self, sk):
        return self.esem[sk[1]] if sk[0] == "e" else self.dsem[sk[1]]

    def _need(self, eng, tok, waits):
        if tok is None:
            return
        sk, val = tok
        if sk == ("e", "pe") and eng == "pe":
            return
        if self.known[eng].get(sk, 0) >= val:
            return
        self.known[eng][sk] = val
        waits[sk] = max(waits.get(sk, 0), val)

    def op(self, eng, fn, reads=(), writes=(), dma=False):
        if self.capture is not None:
            self.capture.append((eng, fn, tuple(reads), tuple(writes), dma))
            return
        waits = {}
        for k in reads:
            self._need(eng, self.lastw.get(k), waits)
        for k in writes:
            self._need(eng, self.lastw.get(k), waits)
            for t in self.readers.get(k, ()):
                self._need(eng, t, waits)
        if dma:
            kind = "sw" if eng == "pool" else "hw"
            i = self.dnext[kind] + (self.dhalf if kind == "sw" else 0)
            self.dnext[kind] = (self.dnext[kind] + 1) % self.dhalf
            if self.dval[i] > 0:
                self._need(eng, (("d", i), self.dval[i]), waits)
            self.dval[i] += 16
            tok = (("d", i), self.dval[i])
            inc = (self.dsem[i], 16)
        else:
            self.ecnt[eng] += 1
            tok = (("e", eng), self.ecnt[eng])
            inc = (self.esem[eng], 1)
        for k in reads:
            self.readers.setdefault(k, []).append(tok)
        for k in writes:
            self.lastw[k] = tok
            self.readers[k] = []
        self.ops[eng].append((list(waits.items()), fn, inc))
        self.nops += 1

    def fence(self):
        for e in ENGS:
            waits = {}
            for e2 in ENGS:
                if self.ecnt[e2] > 0:
                    self._need(e, (("e", e2), self.ecnt[e2]), waits)
            for i, v in enumerate(self.dval):
                if v > 0:
                    self._need(e, (("d", i), v), waits)
            self.ops[e].append((list(waits.items()), None, None))
        self.lastw.clear()
        self.readers.clear()

    def emit(self):
        with self.nc.Block() as blk:
            decos = {"pe": blk.tensor, "act": blk.scalar, "dve": blk.vector, "pool": blk.gpsimd, "sp": blk.sync}
            for e in ENGS:
                ops = self.ops[e]

                def body(eng, ops=ops):
                    for waits, fn, inc in ops:
                        for sk, val in waits:
                            eng.wait_ge(self._sem(sk), val)
                        if fn is not None:
                            fn(eng).then_inc(inc[0], inc[1])

                decos[e](body)
                self.ops[e] = []


def _mm(out, lhsT, rhs, start, stop):
    return lambda e: e.matmul(out, lhsT, rhs, start=start, stop=stop)


def _tr(out, in_, ident):
    return lambda e: e.transpose(out, in_, ident)


def _dma(out, in_):
    return lambda e: e.dma_start(out=out, in_=in_)


def _gather(out, table, idx):
    return lambda e: e.indirect_dma_start(out=out, out_offset=None, in_=table,
                                          in_offset=bass.IndirectOffsetOnAxis(ap=idx, axis=0))


def _act(out, in_, func, scale=None):
    if scale is None:
        return lambda e: e.activation(out=out, in_=in_, func=func)
    return lambda e: e.activation(out=out, in_=in_, func=func, scale=scale)


def _copy(out, in_):
    return lambda e: e.tensor_copy(out=out, in_=in_)


def _acopy(out, in_):
    return lambda e: e.copy(out=out, in_=in_)


def _tt(out, in0, in1, op):
    return lambda e: e.tensor_tensor(out=out, in0=in0, in1=in1, op=op)


def _ts(out, in0, s1, s2, op0, op1=None):
    if op1 is None:
        return lambda e: e.tensor_scalar(out=out, in0=in0, scalar1=s1, scalar2=None, op0=op0)
    return lambda e: e.tensor_scalar(out=out, in0=in0, scalar1=s1, scalar2=s2, op0=op0, op1=op1)


def _stt(out, in0, scalar, in1, op0, op1, accum_out=None):
    if accum_out is None:
        return lambda e: e.scalar_tensor_tensor(out=out, in0=in0, scalar=scalar, in1=in1, op0=op0, op1=op1)
    return lambda e: e.scalar_tensor_tensor(out=out, in0=in0, scalar=scalar, in1=in1, op0=op0, op1=op1,
                                            accum_out=accum_out)


def _scan(out, d0, d1, init):
    return lambda e: e.tensor_tensor_scan(out=out, data0=d0, data1=d1, initial=init, op0=ALU.mult, op1=ALU.add)


def _memset(ap, v):
    return lambda e: e.memset(ap, v)


def _rsum(out, in_):
    return lambda e: e.reduce_sum(out=out, in_=in_, axis=AX.X)


def _recip(out, in_):
    return lambda e: e.reciprocal(out=out, in_=in_)


def _rev(ap2d):
    return ap2d[:, ::-1]


def build_program(SEQ, ND, debug=False):
    assert SEQ % 512 == 0
    NT = SEQ // 512 + 1
    TP = NT * 512
    nc = bass.Bass("TRN2", target_bir_lowering=False)

    def din(name, shape, dt=F32):
        return nc.dram_tensor(name, list(shape), dt, kind="ExternalInput").ap()

    def dscr(name, shape, dt):
        kind = "ExternalOutput" if debug else "Internal"
        return nc.dram_tensor(name, list(shape), dt, kind=kind).ap()

    I = {}
    I["xs"] = din("xs", [SEQ, D])
    I["meta"] = din("meta", [16, D])
    I["tokd"] = din("tokd", [128, ND], I32)
    I["g1c"] = din("g1c", [128, 8])
    I["w_in"] = din("w_in", [D, NCOL])
    for nm in ("lre_c", "lim_c", "lst_c"):
        I[nm] = din(nm, [128, 32])
    for nm in ("cre8", "cim8", "brec", "bimc"):
        I[nm] = din(nm, [128, 2048])
    I["drow"] = din("drow", [128, 32])
    I["cst8"] = din("cst8", [128, 32])
    I["selC"] = din("selC", [128, 1024])
    I["ramp16"] = din("ramp16", [128, 16])
    I["mfb"] = din("mfb", [128, 256])
    I["w_glu"] = din("w_glu", [512, 2048])
    I["lb0"] = din("lb0", [128, 4])
    I["lb1"] = din("lb1", [128, 4])
    I["hgn"] = din("hgn", [128, 4])
    I["w_hg"] = din("w_hg", [512, D])
    I["w_out"] = din("w_out", [D, D])
    I["g2b"] = din("g2b", [128, D])
    I["gfb"] = din("gfb", [128, D])
    I["wq"] = din("wq", [D, 2048])
    I["keysT"] = din("keysT", [128, 2048])
    I["puv"] = din("puv", [16384, 2 * D])
    I["identf"] = din("identf", [128, 128])
    I["ramps"] = din("ramps", [128, 256])
    I["masks"] = din("masks", [64, 128])
    I["iota16"] = din("iota16", [128, 16])
    OUT = nc.dram_tensor("outd", [ND * 128, D], F32, kind="ExternalOutput").ap()

    Z = dscr("Z", [NCOL, TP], BF16)
    VTM = dscr("VTM", [TP, 512], BF16)
    UST = nc.dram_tensor("UST", [128, 32, TP // 8], BF16, kind="Internal").ap()
    XBS = nc.dram_tensor("XBS", [128, 16, 2, TP // 8], BF16, kind="Internal").ap()
    MA = dscr("MA", [D, TP], BF16)
    OB = dscr("OB", [512, TP], F32)
    HS1 = dscr("HS1", [SEQ, D], F32)
    UVB = nc.dram_tensor("UVB", [16384, 2 * D], BF16, kind="Internal").ap()

    top_es = ExitStack()
    S = Sch(nc, top_es)

    with ExitStack() as es:
        def T(name, shape, dt):
            return es.enter_context(nc.sbuf_tensor("A_" + name, shape, dt))

        def P(name, shape, dt):
            return es.enter_context(nc.psum_tensor("A_" + name, shape, dt))

        Wb = T("Wb", [128, 8, NCOL], BF16)
        stg = [T("stg%d" % i, [128, 1280], F32) for i in range(2)]
        cst8 = T("cst8", [128, 32], F32)
        selNb = T("selNb", [128, 16], BF16)
        Ex = T("Ex", [128, 32, 8, 16], BF16)
        ustb = [T("ustb%d" % i, [128, 32, 256], BF16) for i in range(2)]
        g1c = T("g1c", [128, 8], F32)
        identf = T("identf", [128, 128], F32)
        identb = T("identb", [128, 128], BF16)
        xt = [[T("xt%d_%d" % (b, j), [128, D], F32) for j in range(4)] for b in range(2)]
        hb = [T("hb%d" % i, [128, D], BF16) for i in range(2)]
        hT = [T("hT%d" % i, [128, 8, 512], BF16) for i in range(2)]
        ssq = T("ssq", [128, 8], F32)
        rstd = T("rstd", [128, 8], F32)
        junk = T("junk", [128, D], F32)
        zt = [T("zt%d" % i, [128, 4, 512], BF16) for i in range(2)]
        vt = [T("vt%d" % i, [128, 512], BF16) for i in range(2)]
        pT = [P("pT%d" % i, [128, 8, 128], BF16) for i in range(2)]
        pz = [P("pz%d" % i, [128, 512], F32) for i in range(4)]
        psU = P("psU", [128, 32, 16], F32)

        S.op("sp", _dma(identf[:], I["identf"][:, :]), writes=["identf"], dma=True)
        S.op("sp", _dma(g1c[:], I["g1c"][:, :]), writes=["g1c"], dma=True)
        S.op("sp", _dma(cst8[:], I["cst8"][:, :]), writes=["cst8"], dma=True)
        S.op("dve", _copy(selNb[:], cst8[:, 8:24]), reads=["cst8"], writes=["selNb"])
        S.op("dve", _copy(identb[:], identf[:]), reads=["identf"], writes=["identb"])
        n = 0
        for k in range(8):
            for h in range(4):
                sb = n % 2
                n += 1
                S.op("sp", _dma(stg[sb][:], I["w_in"][k * 128:(k + 1) * 128, h * 1280:(h + 1) * 1280]),
                     writes=["stg%d" % sb], dma=True)
                S.op("dve", _ts(Wb[:, k, h * 1280:(h + 1) * 1280], stg[sb][:], g1c[:, k:k + 1], None, ALU.mult),
                     reads=["stg%d" % sb, "g1c"], writes=["Wb"])

        ev = 0
        for st in range(NT):
            b = st % 2
            if st == 0:
                for j in range(4):
                    S.op("pool", _memset(xt[b][j][:], 0.0), writes=["xt%d_%d" % (b, j)])
                S.op("sp", _dma(xt[b][3][112:128, :], I["meta"][:, :]), writes=["xt%d_3" % b], dma=True)
            else:
                for j in range(4):
                    r0 = (st - 1) * 512 + j * 128
                    S.op("sp", _dma(xt[b][j][:], I["xs"][r0:r0 + 128, :]), writes=["xt%d_%d" % (b, j)], dma=True)
            for j in range(4):
                c = (st * 4 + j) % 8
                hbj = j % 2
                xk = "xt%d_%d" % (b, j)
                S.op("dve", _memset(ssq[:, c:c + 1], 0.0), writes=["ssq%d" % c])
                S.op("dve", _stt(junk[:], xt[b][j][:], 1.0, xt[b][j][:], ALU.mult, ALU.mult, accum_out=ssq[:, c:c + 1]),
                     reads=[xk], writes=["junk", "ssq%d" % c])
                S.op("dve", _ts(rstd[:, c:c + 1], ssq[:, c:c + 1], 1.0 / D, EPS, ALU.mult, ALU.add),
                     reads=["ssq%d" % c], writes=["rstd%d" % c])
                S.op("act", _act(rstd[:, c:c + 1], rstd[:, c:c + 1], AF.Sqrt), reads=["rstd%d" % c], writes=["rstd%d" % c])
                S.op("dve", _recip(rstd[:, c:c + 1], rstd[:, c:c + 1]), reads=["rstd%d" % c], writes=["rstd%d" % c])
                S.op("act", _act(hb[hbj][:], xt[b][j][:], AF.Copy, scale=rstd[:, c:c + 1]),
                     reads=[xk, "rstd%d" % c], writes=["hb%d" % hbj])
                for k in range(8):
                    S.op("pe", _tr(pT[hbj][:, k, :], hb[hbj][:, k * 128:(k + 1) * 128], identb[:]),
                         reads=["hb%d" % hbj, "identb"], writes=["pT%d" % hbj])
                S.op("act", _acopy(hT[b][:, :, j * 128:(j + 1) * 128], pT[hbj][:, :, :]),
                     reads=["pT%d" % hbj], writes=["hT%d" % b])
            for c in [c_ for c_ in range(4, 40) if not (16 <= c_ < 20)]:
                pzi = c % 4
                for k in range(8):
                    S.op("pe", _mm(pz[pzi][:], Wb[:, k, c * 128:(c + 1) * 128], hT[b][:, k, :], k == 0, k == 7),
                         reads=["Wb", "hT%d" % b], writes=["pz%d" % pzi])
                zb = (c // 4) % 2
                if ev % 2 == 0:
                    S.op("act", _acopy(zt[zb][:, c % 4, :], pz[pzi][:]), reads=["pz%d" % pzi], writes=["zt%d" % zb])
                else:
                    S.op("dve", _copy(zt[zb][:, c % 4, :], pz[pzi][:]), reads=["pz%d" % pzi], writes=["zt%d" % zb])
                ev += 1
                if c % 4 == 3:
                    dst = Z[(c - 3) * 128:(c + 1) * 128, st * 512:(st + 1) * 512].rearrange("(c p) t -> p c t", p=128)
                    S.op("pool", _dma(dst, zt[zb][:]), reads=["zt%d" % zb], writes=["Z"], dma=True)
            for j in range(4):
                pzi = j % 4
                for k in range(8):
                    S.op("pe", _mm(pz[pzi][:], hT[b][:, k, j * 128:(j + 1) * 128], Wb[:, k, 2048:2560], k == 0, k == 7),
                         reads=["Wb", "hT%d" % b], writes=["pz%d" % pzi])
                vb = j % 2
                S.op("act", _acopy(vt[vb][:], pz[pzi][:]), reads=["pz%d" % pzi], writes=["vt%d" % vb])
                r0 = st * 512 + j * 128
                S.op("pool", _dma(VTM[r0:r0 + 128, :], vt[vb][:]), reads=["vt%d" % vb], writes=["VTM"], dma=True)
            ubuf = (st // 4) % 2
            for j in range(4):
                pzi = j % 4
                for k in range(8):
                    S.op("pe", _mm(pz[pzi][:], hT[b][:, k, j * 128:(j + 1) * 128], Wb[:, k, 0:512], k == 0, k == 7),
                         reads=["Wb", "hT%d" % b], writes=["pz%d" % pzi])
                S.op("dve", _tt(Ex[:], pz[pzi][:].rearrange("p (g c) -> p g c", g=32).unsqueeze(2).broadcast_to([128, 32, 8, 16]),
                                cst8[:, 0:8].unsqueeze(1).unsqueeze(3).broadcast_to([128, 32, 8, 16]), ALU.mult),
                     reads=["pz%d" % pzi, "cst8"], writes=["Ex"])
                for g in range(32):
                    S.op("pe", _mm(psU[:, g, :], Ex[:, g, :, :].rearrange("p i c -> p (i c)"), selNb[:], True, True),
                         reads=["Ex", "selNb"], writes=["psU"])
                off = (st % 4) * 64 + j * 16
                S.op("act", _acopy(ustb[ubuf][:, :, off:off + 16], psU[:]), reads=["psU"], writes=["ustb%d" % ubuf])
            if st % 4 == 3 or st == NT - 1:
                base = (st - st % 4) * 64
                ncols = (st % 4 + 1) * 64
                S.op("pool", _dma(UST[:, :, base:base + ncols], ustb[ubuf][:, :, 0:ncols]), reads=["ustb%d" % ubuf], writes=["UST"],
                     dma=True)
        S.fence()
        S.emit()

    with ExitStack() as es:
        def T(name, shape, dt):
            return es.enter_context(nc.sbuf_tensor("B_" + name, shape, dt))

        def P(name, shape, dt):
            return es.enter_context(nc.psum_tensor("B_" + name, shape, dt))

        NCH = TP // 8
        BWD = 256
        blocks = [(n0, min(BWD, NCH - n0)) for n0 in range(0, NCH, BWD)]

        WST = T("WST", [128, 2, 16, 2, 2, 128], BF16)
        WX = T("WX", [128, 32, 2, 128], BF16)
        WU = T("WU", [128, 32, 128], BF16)
        COS = T("COS", [128, 32, 128], F32)
        SIN = T("SIN", [128, 32, 128], F32)
        rc8 = T("rc8", [128, 32], F32)
        ramps = T("ramps", [128, 256], F32)
        cst8 = T("cst8", [128, 32], F32)
        maskJb = T("maskJb", [128, 8], BF16)
        selC = T("selC", [128, 8, 128], BF16)
        STre = T("STre", [128, 32], F32)
        STim = T("STim", [128, 32], F32)
        XFc = T("XFc", [128, 16, 2], BF16)
        tmp4 = T("tmp4", [128, 4], F32)
        identf = T("identf", [128, 128], F32)

        def sincos(dsin, dcos, ang, tf, tf2, ti, ksin, kcos, kang, ktf, ktf2, kti):
            for dst, kd, off in ((dcos, kcos, 0.25), (dsin, ksin, 0.0)):
                S.op("dve", _ts(tf, ang, 1.0 / (2 * PI), off, ALU.mult, ALU.add), reads=[kang], writes=[ktf])
                S.op("dve", _copy(ti, tf), reads=[ktf], writes=[kti])
                S.op("dve", _copy(tf2, ti), reads=[kti], writes=[ktf2])
                S.op("dve", _tt(tf, tf, tf2, ALU.subtract), reads=[ktf, ktf2], writes=[ktf])
                S.op("dve", _ts(tf, tf, -0.4999, 0.4999, ALU.max, ALU.min), reads=[ktf], writes=[ktf])
                S.op("act", _act(dst, tf, AF.Sin, scale=2 * PI), reads=[ktf], writes=[kd])

        with ExitStack() as es2:
            def T2(name, shape, dt):
                return es2.enter_context(nc.sbuf_tensor("B2_" + name, shape, dt))

            def P2(name, shape, dt):
                return es2.enter_context(nc.psum_tensor("B2_" + name, shape, dt))

            LR = T2("LR", [128, 32], F32)
            LI = T2("LI", [128, 32], F32)
            LS = T2("LS", [128, 32], F32)
            pa = T2("pa", [128, 32], F32)
            pth = T2("pth", [128, 32], F32)
            pcr = T2("pcr", [128, 32], F32)
            pci = T2("pci", [128, 32], F32)
            sm_ = [T2("sm%d" % i, [128, 32], F32) for i in range(8)]
            ramp16 = T2("ramp16", [128, 16], F32)
            mfb = T2("mfb", [128, 256], F32)
            drow = T2("drow", [128, 32], F32)
            S.op("sp", _dma(identf[:], I["identf"][:, :]), writes=["identf"], dma=True)
            S.op("sp", _dma(ramps[:], I["ramps"][:, :]), writes=["ramps"], dma=True)
            S.op("sp", _dma(cst8[:], I["cst8"][:, :]), writes=["cst8"], dma=True)
            S.op("sp", _dma(ramp16[:], I["ramp16"][:, :]), writes=["ramp16"], dma=True)
            S.op("sp", _dma(mfb[:], I["mfb"][:, :]), writes=["mfb"], dma=True)
            S.op("sp", _dma(drow[:], I["drow"][:, :]), writes=["drow"], dma=True)
            S.op("sp", _dma(LR[:], I["lre_c"][:, :]), writes=["LR"], dma=True)
            S.op("sp", _dma(LI[:], I["lim_c"][:, :]), writes=["LI"], dma=True)
            S.op("sp", _dma(LS[:], I["lst_c"][:, :]), writes=["LS"], dma=True)
            S.op("dve", _copy(maskJb[:], cst8[:, 24:32]), reads=["cst8"], writes=["maskJb"])
            t = [x[:] for x in sm_]
            k = ["sm%d" % i for i in range(8)]
            S.op("act", _act(t[0], LS[:], AF.Exp), reads=["LS"], writes=[k[0]])
            S.op("dve", _tt(pa[:], LR[:], t[0], ALU.mult), reads=["LR", k[0]], writes=["pa"])
            S.op("dve", _tt(pth[:], LI[:], t[0], ALU.mult), reads=["LI", k[0]], writes=["pth"])
            S.op("act", _act(t[1], pa[:], AF.Exp), reads=["pa"], writes=[k[1]])
            sincos(t[3], t[4], pth[:], t[5], t[6], t[7].bitcast(I32), k[3], k[4], "pth", k[5], k[6], k[7])
            S.op("dve", _tt(t[5], t[1], t[4], ALU.mult), reads=[k[1], k[4]], writes=[k[5]])
            S.op("dve", _ts(t[5], t[5], -1.0, None, ALU.add), reads=[k[5]], writes=[k[5]])
            S.op("dve", _tt(t[6], t[1], t[3], ALU.mult), reads=[k[1], k[3]], writes=[k[6]])
            S.op("dve", _tt(t[7], LR[:], LR[:], ALU.mult), reads=["LR"], writes=[k[7]])
            S.op("dve", _tt(t[0], LI[:], LI[:], ALU.mult), reads=["LI"], writes=[k[0]])
            S.op("dve", _tt(t[7], t[7], t[0], ALU.add), reads=[k[7], k[0]], writes=[k[7]])
            S.op("dve", _recip(t[7], t[7]), reads=[k[7]], writes=[k[7]])
            S.op("dve", _tt(t[3], t[5], LR[:], ALU.mult), reads=[k[5], "LR"], writes=[k[3]])
            S.op("dve", _tt(t[0], t[6], LI[:], ALU.mult), reads=[k[6], "LI"], writes=[k[0]])
            S.op("dve", _tt(t[3], t[3], t[0], ALU.add), reads=[k[3], k[0]], writes=[k[3]])
            S.op("dve", _tt(pcr[:], t[3], t[7], ALU.mult), reads=[k[3], k[7]], writes=["pcr"])
            S.op("dve", _tt(t[4], t[6], LR[:], ALU.mult), reads=[k[6], "LR"], writes=[k[4]])
            S.op("dve", _tt(t[0], t[5], LI[:], ALU.mult), reads=[k[5], "LI"], writes=[k[0]])
            S.op("dve", _tt(t[4], t[4], t[0], ALU.subtract), reads=[k[4], k[0]], writes=[k[4]])
            S.op("dve", _tt(pci[:], t[4], t[7], ALU.mult), reads=[k[4], k[7]], writes=["pci"])

            PWm = T2("PWm", [128, 32, 16], F32)
            PWa = T2("PWa", [128, 32, 16], F32)
            PWr = T2("PWr", [128, 32, 16], F32)
            PWi = T2("PWi", [128, 32, 16], F32)
            PCr = T2("PCr", [128, 32, 16], F32)
            PCi = T2("PCi", [128, 32, 16], F32)
            tl = [T2("tl%d" % i, [128, 2048], F32) for i in range(6)]
            pt = [tl[3 + i][:, 0:512] for i in range(3)]
            for dg in range(32):
                S.op("act", _act(PWm[:, dg, :], ramp16[:], AF.Exp, scale=pa[:, dg:dg + 1]), reads=["ramp16", "pa"], writes=["PWm"])
                S.op("dve", _ts(PWa[:, dg, :], ramp16[:], pth[:, dg:dg + 1], None, ALU.mult), reads=["ramp16", "pth"], writes=["PWa"])
            f512 = lambda x: x[:].rearrange("p a b -> p (a b)")
            sincos(f512(PWi), f512(PWr), f512(PWa), pt[0], pt[1], pt[2].bitcast(I32), "PWi", "PWr", "PWa", "tl3", "tl4", "tl5")
            S.op("dve", _tt(PWr[:], PWr[:], PWm[:], ALU.mult), reads=["PWr", "PWm"], writes=["PWr"])
            S.op("dve", _tt(PWi[:], PWi[:], PWm[:], ALU.mult), reads=["PWi", "PWm"], writes=["PWi"])
            crb = pcr[:].unsqueeze(2).broadcast_to([128, 32, 16])
            cib = pci[:].unsqueeze(2).broadcast_to([128, 32, 16])
            p3 = lambda x: x.rearrange("p (a b) -> p a b", a=32)
            S.op("dve", _tt(PCr[:], PWr[:], crb, ALU.mult), reads=["PWr", "pcr"], writes=["PCr"])
            S.op("dve", _tt(p3(pt[0]), PWi[:], cib, ALU.mult), reads=["PWi", "pci"], writes=["tl3"])
            S.op("dve", _tt(PCr[:], PCr[:], p3(pt[0]), ALU.subtract), reads=["PCr", "tl3"], writes=["PCr"])
            S.op("dve", _tt(PCi[:], PWr[:], cib, ALU.mult), reads=["PWr", "pci"], writes=["PCi"])
            S.op("dve", _tt(p3(pt[1]), PWi[:], crb, ALU.mult), reads=["PWi", "pcr"], writes=["tl4"])
            S.op("dve", _tt(PCi[:], PCi[:], p3(pt[1]), ALU.add), reads=["PCi", "tl4"], writes=["PCi"])
            S.op("dve", _copy(rc8[:].unsqueeze(2), PWm[:, :, 15:16]), reads=["PWm"], writes=["rc8"])
            S.op("dve", _ts(sm_[0][:], pth[:], 8.0, None, ALU.mult), reads=["pth"], writes=["sm0"])
            S.fence()
            for hf in range(2):
                ang3 = tl[0][:].rearrange("p (a b) -> p a b", a=16)
                for g_ in range(16):
                    dg = hf * 16 + g_
                    rp = ramps[:, 0:128] if hf == 0 else ramps[:, 128:256]
                    S.op("dve", _ts(ang3[:, g_, :], rp, sm_[0][:, dg:dg + 1], None, ALU.mult), reads=["ramps", "sm0"], writes=["tl0"])
                sinh = SIN[:, hf * 16:(hf + 1) * 16, :].rearrange("p a b -> p (a b)")
                cosh = COS[:, hf * 16:(hf + 1) * 16, :].rearrange("p a b -> p (a b)")
                sincos(sinh, cosh, tl[0][:], tl[1][:], tl[2][:], tl[3][:].bitcast(I32), "SIN", "COS", "tl0", "tl1", "tl2", "tl3")
            S.fence()

            C8r = T2("C8r", [128, 2048], F32)
            C8i = T2("C8i", [128, 2048], F32)
            Bcr = T2("Bcr", [128, 2048], F32)
            Bci = T2("Bci", [128, 2048], F32)
            WUf = T2("WUf", [128, 32, 128], F32)
            psT = P2("psT", [128, 128], F32)
            psW = P2("psW", [128, 128], F32)
            S.op("sp", _dma(C8r[:], I["cre8"][:, :]), writes=["C8r"], dma=True)
            S.op("sp", _dma(C8i[:], I["cim8"][:, :]), writes=["C8i"], dma=True)
            S.op("sp", _dma(Bcr[:], I["brec"][:, :]), writes=["Bcr"], dma=True)
            S.op("sp", _dma(Bci[:], I["bimc"][:, :]), writes=["Bci"], dma=True)
            S.op("pool", _memset(WST[:].rearrange("p a b c d e -> p (a b c d e)"), 0.0), writes=["WST"])

            def v4(ap):
                return ap.rearrange("p (g j c) -> p g j c", g=16, j=8)

            def pw4(tbl, d, sl):
                return tbl[:, d * 16:(d + 1) * 16, sl].unsqueeze(3).broadcast_to([128, 16, 8, 16])

            def cmul(ore, oim, ar, ai, br, bi, ka, kb, ko, neg_im=False):
                s4, s5 = v4(tl[4][:]), v4(tl[5][:])
                S.op("dve", _tt(s4, ar, br, ALU.mult), reads=ka + kb, writes=["tl4"])
                S.op("dve", _tt(s5, ai, bi, ALU.mult), reads=ka + kb, writes=["tl5"])
                S.op("dve", _tt(ore, s4, s5, ALU.subtract), reads=["tl4", "tl5"], writes=[ko[0]])
                S.op("dve", _tt(s4, ar, bi, ALU.mult), reads=ka + kb, writes=["tl4"])
                S.op("dve", _tt(s5, ai, br, ALU.mult), reads=ka + kb, writes=["tl5"])
                S.op("dve", _tt(oim, s4, s5, ALU.add), reads=["tl4", "tl5"], writes=[ko[1]])
                if neg_im:
                    S.op("dve", _ts(oim, oim, -1.0, None, ALU.mult), reads=[ko[1]], writes=[ko[1]])

            SL_P1_8 = slice(8, 16)
            SL_8_1 = slice(15, 7, -1)
            SL_0_7 = slice(7, 15)
            SL_0_m7 = slice(7, None, -1)
            SL_7_0 = slice(14, 6, -1)
            kC, kB_, kPW, kPC = ["C8r", "C8i"], ["Bcr", "Bci"], ["PWr", "PWi"], ["PCr", "PCi"]
            for d in range(2):
                sl = SL_P1_8 if d == 0 else SL_8_1
                cmul(v4(tl[0][:]), v4(tl[1][:]), v4(C8r[:]), v4(C8i[:]), pw4(PWr, d, sl), pw4(PWi, d, sl), kC, kPW, ["tl0", "tl1"],
                     neg_im=True)
                S.op("act", _acopy(WX[:, d * 16:(d + 1) * 16, 0, :], tl[0][:].rearrange("p (g m) -> p g m", g=16)), reads=["tl0"],
                     writes=["WX"])
                S.op("act", _acopy(WX[:, d * 16:(d + 1) * 16, 1, :], tl[1][:].rearrange("p (g m) -> p g m", g=16)), reads=["tl1"],
                     writes=["WX"])
                sl = SL_7_0 if d == 0 else SL_0_7
                cmul(v4(tl[0][:]), v4(tl[1][:]), pw4(PCr, d, sl), pw4(PCi, d, sl), v4(Bcr[:]), v4(Bci[:]), kPC, kB_, ["tl0", "tl1"])
                for gp in range(16):
                    for ri in range(2):
                        S.op("pe", _tr(psT[:], tl[ri][:, gp * 128:(gp + 1) * 128], identf[:]), reads=["tl%d" % ri, "identf"],
                             writes=["psT"])
                        S.op("act", _acopy(WST[:, d, gp, ri, 0, 0:64], psT[:, 0:64]), reads=["psT"], writes=["WST"])
                        S.op("dve", _copy(WST[:, d, gp, ri, 1, 64:128], psT[:, 64:128]), reads=["psT"], writes=["WST"])
                slG = SL_0_7 if d == 0 else SL_0_m7
                slH = SL_0_m7 if d == 0 else SL_0_7
                cmul(v4(tl[0][:]), v4(tl[1][:]), v4(C8r[:]), v4(C8i[:]), pw4(PWr, d, slG), pw4(PWi, d, slG), kC, kPW, ["tl0", "tl1"],
                     neg_im=True)
                cmul(v4(tl[2][:]), v4(tl[3][:]), pw4(PCr, d, slH), pw4(PCi, d, slH), v4(Bcr[:]), v4(Bci[:]), kPC, kB_, ["tl2", "tl3"])
                mk = mfb[:, 0:128] if d == 0 else mfb[:, 128:256]
                for gp in range(16):
                    for gl in range(2):
                        g = 2 * gp + gl
                        rs = slice(gl * 64, (gl + 1) * 64)
                        cs = slice(gp * 128, (gp + 1) * 128)
                        S.op("pe", _mm(psW[:], tl[2][rs, cs], tl[0][rs, cs], True, False), reads=["tl2", "tl0"], writes=["psW"])
                        S.op("pe", _mm(psW[:], tl[3][rs, cs], tl[1][rs, cs], False, True), reads=["tl3", "tl1"], writes=["psW"])
                        if d == 0:
                            S.op("dve", _tt(WUf[:, g, :], psW[:], mk, ALU.mult), reads=["psW", "mfb"], writes=["WUf"])
                            S.op("dve", _stt(WUf[:, g, :], identf[:], drow[:, g:g + 1], WUf[:, g, :], ALU.mult, ALU.add),
                                 reads=["identf", "drow", "WUf"], writes=["WUf"])
                        else:
                            S.op("dve", _tt(tl[4][:, 0:128], psW[:], mk, ALU.mult), reads=["psW", "mfb"], writes=["tl4"])
                            S.op("dve", _tt(WU[:, g, :], WUf[:, g, :], tl[4][:, 0:128], ALU.add), reads=["WUf", "tl4"], writes=["WU"])
            S.op("sp", _dma(tl[0][:, 0:1024], I["selC"][:, :]), writes=["tl0"], dma=True)
            S.op("dve", _copy(selC[:].rearrange("p a b -> p (a b)"), tl[0][:, 0:1024]), reads=["tl0"], writes=["selC"])
            S.fence()
            S.emit()

        wglu = T("wglu", [128, 4, 2048], BF16)
        with ExitStack() as es3:
            wst2 = [es3.enter_context(nc.sbuf_tensor("B3_wst%d" % i, [128, 2048], F32)) for i in range(2)]
            for k_ in range(4):
                S.op("sp", _dma(wst2[k_ % 2][:], I["w_glu"][k_ * 128:(k_ + 1) * 128, :]), writes=["wst2_%d" % (k_ % 2)], dma=True)
                S.op("act", _acopy(wglu[:, k_, :], wst2[k_ % 2][:]), reads=["wst2_%d" % (k_ % 2)], writes=["wglu"])
            S.fence()
            S.emit()
        ust = [T("ust%d" % i, [128, 32, BWD], BF16) for i in range(2)]
        gst = T("gst", [128, 32, BWD], BF16)
        bpre = [T("bpre%d" % i, [128, BWD], F32) for i in range(2)]
        bpim = [T("bpim%d" % i, [128, BWD], F32) for i in range(2)]
        mt = [T("mt%d" % i, [128, 128], F32) for i in range(4)]
        mp = [T("mp%d" % i, [128, 128], F32) for i in range(4)]
        Wre = [T("Wre%d" % i, [128, BWD], F32) for i in range(2)]
        Wim = [T("Wim%d" % i, [128, BWD], F32) for i in range(2)]
        XF = [T("XF%d" % i, [128, 2, BWD + 1], BF16) for i in range(2)]
        XBt = [T("XBt%d" % i, [128, 2, BWD], BF16) for i in range(2)]
        ex = [T("ex%d" % i, [128, 32, 16, 8], BF16) for i in range(1)]
        gel = T("gel", [128, 4, 512], BF16)
        gat = T("gat", [128, 8, 512], BF16)
        mao = T("mao", [128, 8, 512], BF16)
        sgb = [T("sgb%d" % i, [128, 512], F32) for i in range(1)]
        yab = [T("yab%d" % i, [128, 512], F32) for i in range(1)]
        sga = [T("sga%d" % i, [128, 512], F32) for i in range(1)]
        psS = [P("psS%d" % i, [128, 512], F32) for i in range(2)]
        psY = [P("psY%d" % i, [128, 512], F32) for i in range(2)]
        psG = P("psG", [128, 4, 128], F32)
        psb = [P("psb%d" % i, [128, 512], F32) for i in range(2)]

        S.op("dve", _memset(STre[:], 0.0), writes=["STre"])
        S.op("dve", _memset(STim[:], 0.0), writes=["STim"])
        S.op("dve", _memset(XFc[:], 0.0), writes=["XFc"])

        cnt = 0
        ucnt = 0
        for d in (1, 0):
            blks = list(reversed(blocks)) if d == 1 else blocks
            for (n0, w) in blks:
                ub = ucnt % 2
                ucnt += 1
                S.op("sp", _dma(ust[ub][:, :, 0:w], UST[:, :, n0:n0 + w]), reads=["UST"], writes=["ust%d" % ub], dma=True)
                segs = [(s0, min(128, w - s0)) for s0 in range(0, w, 128)]
                if d == 1:
                    segs = list(reversed(segs))
                for gp in range(16):
                    dg = d * 16 + gp
                    pb = cnt % 2
                    cnt += 1
                    PB = str(pb)
                    for ri in range(2):
                        S.op("pe", _mm(psS[ri][:, 0:w], WST[:, d, gp, ri, 0, :], ust[ub][:, 2 * gp, 0:w], True, False),
                             reads=["WST", "ust%d" % ub], writes=["psS%d" % ri])
                        S.op("pe", _mm(psS[ri][:, 0:w], WST[:, d, gp, ri, 1, :], ust[ub][:, 2 * gp + 1, 0:w], False, True),
                             reads=["WST", "ust%d" % ub], writes=["psS%d" % ri])
                    rbc = rc8[:, dg:dg + 1]
                    for (s0, sw) in segs:
                        sl = slice(s0, s0 + sw)
                        tsl = slice(128 - sw, 128) if d == 1 else slice(0, sw)
                        cb_, sb_ = COS[:, dg, tsl], SIN[:, dg, tsl]
                        S.op("dve", _tt(mt[0][:, 0:sw], psS[0][:, sl], cb_, ALU.mult), reads=["psS0", "COS"], writes=["mt0"])
                        S.op("dve", _tt(mt[1][:, 0:sw], psS[1][:, sl], sb_, ALU.mult), reads=["psS1", "SIN"], writes=["mt1"])
                        S.op("dve", _tt(bpre[pb][:, sl], mt[0][:, 0:sw], mt[1][:, 0:sw], ALU.add), reads=["mt0", "mt1"], writes=["bpre" + PB])
                        S.op("dve", _tt(mt[2][:, 0:sw], psS[1][:, sl], cb_, ALU.mult), reads=["psS1", "COS"], writes=["mt2"])
                        S.op("dve", _tt(mt[3][:, 0:sw], psS[0][:, sl], sb_, ALU.mult), reads=["psS0", "SIN"], writes=["mt3"])
                        S.op("dve", _tt(bpim[pb][:, sl], mt[2][:, 0:sw], mt[3][:, 0:sw], ALU.subtract), reads=["mt2", "mt3"], writes=["bpim" + PB])
                        wre_s, wim_s = Wre[pb][:, sl], Wim[pb][:, sl]
                        bre_s, bim_s = bpre[pb][:, sl], bpim[pb][:, sl]
                        if d == 1:
                            wre_s, wim_s, bre_s, bim_s = _rev(wre_s), _rev(wim_s), _rev(bre_s), _rev(bim_s)
                        rb_ = rbc.broadcast_to([128, sw])
                        S.op("dve", _scan(wre_s, rb_, bre_s, STre[:, dg:dg + 1]), reads=["rc8", "bpre" + PB, "STre%d" % dg], writes=["Wre" + PB])
                        S.op("dve", _scan(wim_s, rb_, bim_s, STim[:, dg:dg + 1]), reads=["rc8", "bpim" + PB, "STim%d" % dg], writes=["Wim" + PB])
                        last = s0 if d == 1 else s0 + sw - 1
                        tlast = (128 - sw) if d == 1 else (sw - 1)
                        cl = COS[:, dg, tlast:tlast + 1]
                        sl_ = SIN[:, dg, tlast:tlast + 1]
                        wr1 = Wre[pb][:, last:last + 1]
                        wi1 = Wim[pb][:, last:last + 1]
                        S.op("dve", _ts(tmp4[:, 0:1], wi1, sl_, None, ALU.mult), reads=["Wim" + PB, "SIN"], writes=["tmp4a"])
                        S.op("dve", _ts(tmp4[:, 1:2], wi1, cl, None, ALU.mult), reads=["Wim" + PB, "COS"], writes=["tmp4b"])
                        S.op("dve", _stt(STre[:, dg:dg + 1], wr1, cl, tmp4[:, 0:1], ALU.mult, ALU.subtract),
                             reads=["Wre" + PB, "COS", "tmp4a"], writes=["STre%d" % dg])
                        S.op("dve", _stt(STim[:, dg:dg + 1], wr1, sl_, tmp4[:, 1:2], ALU.mult, ALU.add),
                             reads=["Wre" + PB, "SIN", "tmp4b"], writes=["STim%d" % dg])
                        if d == 1:
                            xre_o, xim_o = XBt[pb][:, 0, sl], XBt[pb][:, 1, sl]
                            kxo = "XBt" + PB
                        else:
                            xre_o = XF[pb][:, 0, 1 + s0:1 + s0 + sw]
                            xim_o = XF[pb][:, 1, 1 + s0:1 + s0 + sw]
                            kxo = "XF" + PB
                        S.op("pool", _tt(mp[0][:, 0:sw], Wre[pb][:, sl], cb_, ALU.mult), reads=["Wre" + PB, "COS"], writes=["mp0"])
                        S.op("pool", _tt(mp[1][:, 0:sw], Wim[pb][:, sl], sb_, ALU.mult), reads=["Wim" + PB, "SIN"], writes=["mp1"])
                        S.op("pool", _tt(xre_o, mp[0][:, 0:sw], mp[1][:, 0:sw], ALU.subtract), reads=["mp0", "mp1"], writes=[kxo])
                        S.op("pool", _tt(mp[2][:, 0:sw], Wre[pb][:, sl], sb_, ALU.mult), reads=["Wre" + PB, "SIN"], writes=["mp2"])
                        S.op("pool", _tt(mp[3][:, 0:sw], Wim[pb][:, sl], cb_, ALU.mult), reads=["Wim" + PB, "COS"], writes=["mp3"])
                        S.op("pool", _tt(xim_o, mp[2][:, 0:sw], mp[3][:, 0:sw], ALU.add), reads=["mp2", "mp3"], writes=[kxo])
                    if d == 1:
                        S.op("pool", _dma(XBS[:, gp, :, n0:n0 + w], XBt[pb][:, :, 0:w]), reads=["XBt" + PB], writes=["XBS"], dma=True)
                        continue
                    S.op("act", _acopy(XF[pb][:, :, 0:1], XFc[:, gp, :].unsqueeze(2)), reads=["XFc"], writes=["XF" + PB])
                    S.op("act", _acopy(XFc[:, gp, :].unsqueeze(2), XF[pb][:, :, w:w + 1]), reads=["XF" + PB], writes=["XFc"])
                    wl = w if n0 + w < NCH else w - 1
                    if wl > 0:
                        S.op("sp", _dma(XBt[pb][:, :, 0:wl], XBS[:, gp, :, n0 + 1:n0 + 1 + wl]), reads=["XBS"], writes=["XBt" + PB], dma=True)
                    if wl < w:
                        S.op("pool", _memset(XBt[pb][:, :, wl:w], 0.0), writes=["XBt" + PB])
                    for gl in range(2):
                        g = 2 * gp + gl
                        rs = slice(gl * 64, (gl + 1) * 64)
                        py = psY[g % 2]
                        kp = "psY%d" % (g % 2)
                        S.op("pe", _mm(py[:, 0:w], WU[:, g, :], ust[ub][:, g, 0:w], True, False), reads=["WU", "ust%d" % ub], writes=[kp])
                        S.op("pe", _mm(py[:, 0:w], WX[rs, gp, 0, :], XF[pb][rs, 0, 0:w], False, False), reads=["WX", "XF" + PB], writes=[kp])
                        S.op("pe", _mm(py[:, 0:w], WX[rs, gp, 1, :], XF[pb][rs, 1, 0:w], False, False), reads=["WX", "XF" + PB], writes=[kp])
                        S.op("pe", _mm(py[:, 0:w], WX[rs, 16 + gp, 0, :], XBt[pb][rs, 0, 0:w], False, False), reads=["WX", "XBt" + PB], writes=[kp])
                        S.op("pe", _mm(py[:, 0:w], WX[rs, 16 + gp, 1, :], XBt[pb][rs, 1, 0:w], False, True), reads=["WX", "XBt" + PB], writes=[kp])
                        S.op("act", _act(gst[:, g, 0:w], py[:, 0:w], AF.Gelu), reads=[kp], writes=["gst"])
                if d == 1:
                    continue
                for s_ in range(w // 16):
                    tok0 = 8 * n0 + 128 * s_
                    st = tok0 // 512
                    sub = (tok0 % 512) // 128
                    if st == 0:
                        continue
                    eb = 0
                    S.op("dve", _tt(ex[eb][:], gst[:, :, s_ * 16:(s_ + 1) * 16].unsqueeze(3).broadcast_to([128, 32, 16, 8]),
                                    maskJb[:].unsqueeze(1).unsqueeze(1).broadcast_to([128, 32, 16, 8]), ALU.mult),
                         reads=["gst", "maskJb"], writes=["ex%d" % eb])
                    for q in range(4):
                        for g8 in range(8):
                            S.op("pe", _mm(psG[:, q, :], selC[:, g8, :], ex[eb][:, 8 * q + g8, :, :].rearrange("p n j -> p (n j)"),
                                           g8 == 0, g8 == 7), reads=["selC", "ex%d" % eb], writes=["psG"])
                    S.op("act", _acopy(gel[:, :, sub * 128:(sub + 1) * 128], psG[:]), reads=["psG"], writes=["gel"])
                    if sub != 3:
                        continue
                    c0 = st * 512
                    S.op("sp", _dma(gat[:], Z[3072:4096, c0:c0 + 512].rearrange("(q p) t -> p q t", p=128)),
                         reads=["Z"], writes=["gat"], dma=True)
                    for c in range(8):
                        pb2 = 0
                        pa_, pg_ = psb[0], psb[1]
                        for k_ in range(4):
                            S.op("pe", _mm(pa_[:], wglu[:, k_, c * 128:(c + 1) * 128], gel[:, k_, :], k_ == 0, k_ == 3),
                                 reads=["wglu", "gel"], writes=["psb0"])
                        for k_ in range(4):
                            S.op("pe", _mm(pg_[:], wglu[:, k_, 1024 + c * 128:1024 + (c + 1) * 128], gel[:, k_, :], k_ == 0, k_ == 3),
                                 reads=["wglu", "gel"], writes=["psb1"])
                        S.op("act", _act(sgb[pb2][:], pg_[:], AF.Sigmoid), reads=["psb1"], writes=["sgb%d" % pb2])
                        S.op("act", _act(sga[pb2][:], gat[:, c, :], AF.Sigmoid), reads=["gat"], writes=["sga%d" % pb2])
                        S.op("dve", _tt(yab[pb2][:], pa_[:], sgb[pb2][:], ALU.mult), reads=["psb0", "sgb%d" % pb2], writes=["yab%d" % pb2])
                        S.op("dve", _tt(mao[:, c, :], yab[pb2][:], sga[pb2][:], ALU.mult), reads=["yab%d" % pb2, "sga%d" % pb2],
                             writes=["mao"])
                    S.op("pool", _dma(MA[:, c0:c0 + 512].rearrange("(q p) t -> p q t", p=128), mao[:]),
                         reads=["mao"], writes=["MA"], dma=True)
        S.fence()
        S.emit()

    pre_es = ExitStack()
    wq = pre_es.enter_context(nc.sbuf_tensor("P_wq", [128, 8, 2048], BF16))
    keysT = pre_es.enter_context(nc.sbuf_tensor("P_keysT", [128, 16, 128], BF16))

    with ExitStack() as es:
        def T(name, shape, dt):
            return es.enter_context(nc.sbuf_tensor("C_" + name, shape, dt))

        def P(name, shape, dt):
            return es.enter_context(nc.psum_tensor("C_" + name, shape, dt))

        whg = T("whg", [128, 4, D], BF16)
        wout = T("wout", [128, 8, D], BF16)
        wst = T("wst", [128, D], F32)
        masks = T("masks", [64, 128], F32)
        identf = T("identf", [128, 128], F32)
        identb = T("identb", [128, 128], BF16)
        onesb = T("onesb", [128, 128], BF16)
        lb = T("lb", [128, 4], F32)
        oml = T("oml", [128, 4], F32)
        lb1 = T("lb1", [128, 4], F32)
        hgn = T("hgn", [128, 4], F32)
        Sst = [T("Sst%d" % h, [128, 128], F32) for h in range(4)]
        zeros = T("zeros", [128, 64], F32)
        Sall = T("Sall", [128, 1024], F32)
        kvs = T("kvs", [128, 1024], F32)
        decz = T("decz", [128, 8], F32)
        dzf = T("dzf", [128, 1024], F32)
        NB = 2
        zq = [T("zq%d" % i, [128, 512], BF16) for i in range(NB)]
        zf = [T("zf%d" % i, [128, 512], BF16) for i in range(NB)]
        zo = [T("zo%d" % i, [128, 512], BF16) for i in range(NB)]
        Vc = [T("Vc%d" % i, [64, 8, 128], BF16) for i in range(NB)]
        qs = [T("qs%d" % i, [128, 512], F32) for i in range(NB)]
        gs = [T("gs%d" % i, [128, 512], F32) for i in range(NB)]
        ks = [T("ks%d" % i, [128, 512], F32) for i in range(NB)]
        Pc = [T("Pc%d" % i, [128, 512], F32) for i in range(NB)]
        rP = [T("rP%d" % i, [128, 512], F32) for i in range(NB)]
        dec = [T("dec%d" % i, [128, 8], F32) for i in range(NB)]
        qx = [T("qx%d" % i, [128, 512], BF16) for i in range(NB)]
        kx = [T("kx%d" % i, [128, 512], BF16) for i in range(NB)]
        ke = [T("ke%d" % i, [128, 512], BF16) for i in range(NB)]
        qi = [T("qi%d" % i, [128, 512], BF16) for i in range(NB)]
        scm = [T("scm%d" % i, [64, 8, 64], BF16) for i in range(NB)]
        ketm = [T("ketm%d" % i, [64, 8, 128], BF16) for i in range(NB)]
        Sbf = [T("Sbf%d" % i, [128, 8, 128], BF16) for i in range(NB)]
        osb = [T("osb%d" % i, [128, 512], F32) for i in range(NB)]
        obl = [T("obl%d" % i, [128, 512], F32) for i in range(NB)]
        sq = T("sq", [128, 512], BF16)
        rn = T("rn", [128, 512], F32)
        sgo = T("sgo", [128, 512], F32)
        yh = T("yh", [128, 4, 512], BF16)
        gbt = T("gbt", [128, 8, 512], BF16)
        mal = T("mal", [128, 8, 512], BF16)
        sgb = T("sgb", [128, 512], F32)
        tmpc = T("tmpc", [128, 512], F32)
        mixT = T("mixT", [128, 8, 512], BF16)
        xt = [T("xt%d" % i, [128, D], F32) for i in range(2)]
        hsb = [T("hsb%d" % i, [128, D], F32) for i in range(2)]
        psS = P("psS", [64, 8, 64], F32)
        psK = P("psK", [64, 8, 128], BF16)
        psKV = P("psKV", [128, 8, 128], F32)
        psO = P("psO", [128, 512], F32)
        psN = P("psN", [128, 512], F32)
        psY = [P("psY%d" % i, [128, 512], F32) for i in range(2)]

        S.op("sp", _dma(identf[:], I["identf"][:, :]), writes=["identf"], dma=True)
        S.op("dve", _copy(identb[:], identf[:]), reads=["identf"], writes=["identb"])
        S.op("dve", _memset(onesb[:], 1.0), writes=["onesb"])
        S.op("dve", _memset(zeros[:], 0.0), writes=["zeros"])
        S.op("sp", _dma(masks[:], I["masks"][:, :]), writes=["masks"], dma=True)
        S.op("sp", _dma(lb[:], I["lb0"][:, :]), writes=["lb"], dma=True)
        S.op("sp", _dma(lb1[:], I["lb1"][:, :]), writes=["lb1"], dma=True)
        S.op("sp", _dma(hgn[:], I["hgn"][:, :]), writes=["hgn"], dma=True)
        S.op("dve", _tt(lb[:], lb[:], lb1[:], ALU.subtract), reads=["lb", "lb1"], writes=["lb"])
        S.op("act", _act(lb[:], lb[:], AF.Sigmoid), reads=["lb"], writes=["lb"])
        S.op("dve", _ts(oml[:], lb[:], -1.0, 1.0, ALU.mult, ALU.add), reads=["lb"], writes=["oml"])
        for k in range(4):
            S.op("sp", _dma(wst[:], I["w_hg"][k * 128:(k + 1) * 128, :]), writes=["wst"], dma=True)
            S.op("act", _acopy(whg[:, k, :], wst[:]), reads=["wst"], writes=["whg"])
        for k in range(8):
            S.op("sp", _dma(wst[:], I["w_out"][k * 128:(k + 1) * 128, :]), writes=["wst"], dma=True)
            S.op("act", _acopy(wout[:, k, :], wst[:]), reads=["wst"], writes=["wout"])

        cnt = 0
        xcnt = [0]
        items = []
        for d in (1, 0):
            sts = list(range(NT - 1, 0, -1)) if d == 1 else list(range(NT))
            for si, st in enumerate(sts):
                for h in range(4):
                    items.append(dict(d=d, st=st, h=h, bi=cnt % NB, first=(si == 0)))
                    cnt += 1

        def prep(it):
            d, st, h, bi = it["d"], it["st"], it["h"], it["bi"]
            B = str(bi)
            c0 = st * 512
            full = (d == 0 and st >= 1)
            frow = 1536 if d == 1 else 1024
            S.op("sp", _dma(zq[bi][:], Z[512 + h * 128:512 + (h + 1) * 128, c0:c0 + 512]), reads=["Z"],
                 writes=["zq" + B], dma=True)
            S.op("sp", _dma(zf[bi][:], Z[frow + h * 128:frow + (h + 1) * 128, c0:c0 + 512]), reads=["Z"],
                 writes=["zf" + B], dma=True)
            S.op("sp", _dma(Vc[bi][:], VTM[c0:c0 + 512, h * 128:(h + 1) * 128].rearrange("(c s) v -> s c v", s=64)),
                 reads=["VTM"], writes=["Vc" + B], dma=True)
            if full:
                S.op("sp", _dma(zo[bi][:], Z[2560 + h * 128:2560 + (h + 1) * 128, c0:c0 + 512]), reads=["Z"],
                     writes=["zo" + B], dma=True)
                S.op("sp", _dma(obl[bi][:], OB[h * 128:(h + 1) * 128, c0:c0 + 512]), reads=["OB"],
                     writes=["obl" + B], dma=True)
            qs_, gs_, ks_, Pc_, rP_, dec_ = qs[bi], gs[bi], ks[bi], Pc[bi], rP[bi], dec[bi]
            S.op("act", _act(qs_[:], zq[bi][:], AF.Silu), reads=["zq" + B], writes=["qs" + B])
            S.op("act", _act(gs_[:], zf[bi][:], AF.Sigmoid), reads=["zf" + B], writes=["gs" + B])
            S.op("dve", _ts(gs_[:], gs_[:], oml[:, h:h + 1], lb[:, h:h + 1], ALU.mult, ALU.add),
                 reads=["gs" + B, "oml", "lb"], writes=["gs" + B])
            S.op("dve", _ts(ks_[:], gs_[:], -1.0, 1.0, ALU.mult, ALU.add), reads=["gs" + B], writes=["ks" + B])
            Pc3 = Pc_[:].rearrange("p (c j) -> p c j", j=64)
            gs3 = gs_[:].rearrange("p (c j) -> p c j", j=64)
            if d == 0:
                for ch in range(8):
                    sl = slice(ch * 64, (ch + 1) * 64)
                    S.op("dve", _scan(Pc_[:, sl], gs_[:, sl], zeros[:, 0:64], 1.0), reads=["gs" + B, "zeros"], writes=["Pc" + B])
                S.op("dve", _tt(qx[bi][:], qs_[:], Pc_[:], ALU.mult), reads=["qs" + B, "Pc" + B], writes=["qx" + B])
                S.op("dve", _recip(rP_[:], Pc_[:]), reads=["Pc" + B], writes=["rP" + B])
                S.op("dve", _tt(kx[bi][:], ks_[:], rP_[:], ALU.mult), reads=["ks" + B, "rP" + B], writes=["kx" + B])
                S.op("dve", _copy(dec_[:].unsqueeze(2), Pc3[:, :, 63:64]), reads=["Pc" + B], writes=["dec" + B])
                S.op("dve", _tt(ke[bi][:].rearrange("p (c j) -> p c j", j=64),
                                kx[bi][:].rearrange("p (c j) -> p c j", j=64),
                                dec_[:].unsqueeze(2).broadcast_to([128, 8, 64]), ALU.mult),
                     reads=["kx" + B, "dec" + B], writes=["ke" + B])
            else:
                S.op("dve", _memset(Pc3[:, :, 0:1], 1.0), writes=["Pc" + B])
                for ch in range(8):
                    S.op("dve", _scan(Pc_[:, ch * 64 + 1:(ch + 1) * 64], gs_[:, ch * 64:(ch + 1) * 64 - 1],
                                      zeros[:, 0:63], 1.0), reads=["gs" + B, "zeros"], writes=["Pc" + B])
                S.op("dve", _recip(rP_[:], Pc_[:]), reads=["Pc" + B], writes=["rP" + B])
                S.op("dve", _tt(qx[bi][:], qs_[:], rP_[:], ALU.mult), reads=["qs" + B, "rP" + B], writes=["qx" + B])
                S.op("dve", _tt(kx[bi][:], ks_[:], Pc_[:], ALU.mult), reads=["ks" + B, "Pc" + B], writes=["kx" + B])
                S.op("dve", _tt(dec_[:].unsqueeze(2), Pc3[:, :, 63:64], gs3[:, :, 63:64], ALU.mult),
                     reads=["Pc" + B, "gs" + B], writes=["dec" + B])
                S.op("dve", _tt(qi[bi][:].rearrange("p (c j) -> p c j", j=64),
                                qx[bi][:].rearrange("p (c j) -> p c j", j=64),
                                dec_[:].unsqueeze(2).broadcast_to([128, 8, 64]), ALU.mult),
                     reads=["qx" + B, "dec" + B], writes=["qi" + B])

        def rest(it):
            d, st, h, bi = it["d"], it["st"], it["h"], it["bi"]
            B = str(bi)
            c0 = st * 512
            full = (d == 0 and st >= 1)
            mk = masks[:, 64:128] if d == 1 else masks[:, 0:64]
            mkb = mk.unsqueeze(1).broadcast_to([64, 8, 64])
            dec_ = dec[bi]
            if it["first"]:
                S.op("dve", _memset(Sst[h][:], 0.0), writes=["Sst%d" % h])
            if full and h == 0:
                S.op("sp", _dma(gbt[:], Z[4096:5120, c0:c0 + 512].rearrange("(q p) t -> p q t", p=128)),
                     reads=["Z"], writes=["gbt"], dma=True)
                S.op("sp", _dma(mal[:], MA[:, c0:c0 + 512].rearrange("(q p) t -> p q t", p=128)),
                     reads=["MA"], writes=["mal"], dma=True)
            if d == 0:
                qiT, keT = qx[bi], ke[bi]
                kq, kk_ = "qx" + B, "ke" + B
            else:
                qiT, keT = qi[bi], kx[bi]
                kq, kk_ = "qi" + B, "kx" + B
            for ch in range(8):
                sl = slice(ch * 64, (ch + 1) * 64)
                S.op("pe", _mm(psS[:, ch, :], kx[bi][:, sl], qx[bi][:, sl], True, True),
                     reads=["kx" + B, "qx" + B], writes=["psS"])
            for ch in range(8):
                sl = slice(ch * 64, (ch + 1) * 64)
                S.op("pe", _tr(psK[:, ch, :], keT[:, sl], identb[:]), reads=[kk_, "identb"], writes=["psK"])
            S.op("dve", _tt(scm[bi][:], psS[:], mkb, ALU.mult), reads=["psS", "masks"], writes=["scm" + B])
            S.op("act", _acopy(ketm[bi][:], psK[:]), reads=["psK"], writes=["ketm" + B])
            for ch in range(8):
                S.op("pe", _mm(psKV[:, ch, :], ketm[bi][:, ch, :], Vc[bi][:, ch, :], True, True),
                     reads=["ketm" + B, "Vc" + B], writes=["psKV"])
            kvv = psKV[:].rearrange("p c v -> p v c")
            sal = Sall[:].rearrange("p (v c) -> p v c", c=8)
            c_in = 7 if d == 1 else 0
            S.op("dve", _copy(decz[:], dec_[:]), reads=["dec" + B], writes=["decz"])
            S.op("dve", _memset(decz[:, c_in:c_in + 1], 0.0), writes=["decz"])
            S.op("dve", _copy(kvs[:].rearrange("p (v c) -> p v c", c=8), kvv), reads=["psKV"], writes=["kvs"])
            S.op("dve", _stt(kvs[:].rearrange("p (v c) -> p v c", c=8)[:, :, c_in], Sst[h][:], dec_[:, c_in:c_in + 1],
                             kvs[:].rearrange("p (v c) -> p v c", c=8)[:, :, c_in], ALU.mult, ALU.add),
                 reads=["Sst%d" % h, "dec" + B, "kvs"], writes=["kvs"])
            S.op("dve", _copy(dzf[:].rearrange("p (v c) -> p v c", c=8), decz[:].unsqueeze(1).broadcast_to([128, 128, 8])),
                 reads=["decz"], writes=["dzf"])
            if d == 0:
                S.op("dve", _scan(Sall[:], dzf[:], kvs[:], 0.0), reads=["dzf", "kvs"], writes=["Sall"])
            else:
                S.op("dve", _scan(_rev(Sall[:]), _rev(dzf[:]), _rev(kvs[:]), 0.0), reads=["dzf", "kvs"], writes=["Sall"])
            salc = Sall[:].rearrange("p (v c) -> p c v", c=8)
            if d == 0:
                S.op("act", _acopy(Sbf[bi][:, 0, :], Sst[h][:]), reads=["Sst%d" % h], writes=["Sbf" + B])
                S.op("act", _acopy(Sbf[bi][:, 1:8, :], salc[:, 0:7, :]), reads=["Sall"], writes=["Sbf" + B])
                S.op("dve", _copy(Sst[h][:], salc[:, 7, :]), reads=["Sall"], writes=["Sst%d" % h])
            else:
                S.op("act", _acopy(Sbf[bi][:, 7, :], Sst[h][:]), reads=["Sst%d" % h], writes=["Sbf" + B])
                S.op("act", _acopy(Sbf[bi][:, 0:7, :], salc[:, 1:8, :]), reads=["Sall"], writes=["Sbf" + B])
                S.op("dve", _copy(Sst[h][:], salc[:, 0, :]), reads=["Sall"], writes=["Sst%d" % h])
            if d == 0 and st == 0:
                return
            for ch in range(8):
                sl = slice(ch * 64, (ch + 1) * 64)
                S.op("pe", _mm(psO[:, sl], Vc[bi][:, ch, :], scm[bi][:, ch, :], True, False),
                     reads=["Vc" + B, "scm" + B], writes=["psO"])
                S.op("pe", _mm(psO[:, sl], Sbf[bi][:, ch, :], qiT[:, sl], False, True),
                     reads=["Sbf" + B, kq], writes=["psO"])
            if d == 1:
                S.op("act", _acopy(osb[bi][:], psO[:]), reads=["psO"], writes=["osb" + B])
                S.op("pool", _dma(OB[h * 128:(h + 1) * 128, c0:c0 + 512], osb[bi][:]), reads=["osb" + B],
                     writes=["OB"], dma=True)
                return
            S.op("dve", _tt(osb[bi][:], psO[:], obl[bi][:], ALU.add), reads=["psO", "obl" + B], writes=["osb" + B])
            S.op("act", _act(sq[:], osb[bi][:], AF.Square), reads=["osb" + B], writes=["sq"])
            S.op("pe", _mm(psN[:], onesb[:], sq[:], True, True), reads=["onesb", "sq"], writes=["psN"])
            S.op("dve", _ts(rn[:], psN[:], 1.0 / 128, EPS, ALU.mult, ALU.add), reads=["psN"], writes=["rn"])
            S.op("act", _act(rn[:], rn[:], AF.Sqrt), reads=["rn"], writes=["rn"])
            S.op("dve", _recip(rn[:], rn[:]), reads=["rn"], writes=["rn"])
            S.op("dve", _tt(rn[:], rn[:], osb[bi][:], ALU.mult), reads=["rn", "osb" + B], writes=["rn"])
            S.op("act", _act(sgo[:], zo[bi][:], AF.Silu), reads=["zo" + B], writes=["sgo"])
            S.op("dve", _stt(yh[:, h, :], rn[:], hgn[:, h:h + 1], sgo[:], ALU.mult, ALU.mult),
                 reads=["rn", "hgn", "sgo"], writes=["yh"])
            if h != 3:
                return
            for c in range(8):
                py = psY[c % 2]
                kp = "psY%d" % (c % 2)
                for k in range(4):
                    S.op("pe", _mm(py[:], whg[:, k, c * 128:(c + 1) * 128], yh[:, k, :], k == 0, k == 3),
                         reads=["whg", "yh"], writes=[kp])
                S.op("act", _act(sgb[:], gbt[:, c, :], AF.Sigmoid), reads=["gbt"], writes=["sgb"])
                S.op("dve", _tt(tmpc[:], py[:], sgb[:], ALU.mult), reads=[kp, "sgb"], writes=["tmpc"])
                S.op("dve", _tt(mixT[:, c, :], tmpc[:], mal[:, c, :], ALU.add), reads=["tmpc", "mal"], writes=["mixT"])
            for j in range(4):
                xb = xcnt[0] % 2
                xcnt[0] += 1
                r0 = (st - 1) * 512 + j * 128
                S.op("sp", _dma(xt[xb][:], I["xs"][r0:r0 + 128, :]), writes=["xt%d" % xb], dma=True)
                for hf in range(2):
                    py = psY[hf]
                    kp = "psY%d" % hf
                    for k in range(8):
                        S.op("pe", _mm(py[:], mixT[:, k, j * 128:(j + 1) * 128], wout[:, k, hf * 512:(hf + 1) * 512],
                                       k == 0, k == 7), reads=["mixT", "wout"], writes=[kp])
                    S.op("dve", _tt(hsb[xb][:, hf * 512:(hf + 1) * 512], py[:], xt[xb][:, hf * 512:(hf + 1) * 512], ALU.add),
                         reads=[kp, "xt%d" % xb], writes=["hsb%d" % xb])
                S.op("pool", _dma(HS1[r0:r0 + 128, :], hsb[xb][:]), reads=["hsb%d" % xb], writes=["HS1"], dma=True)

        conv = list(range(128))
        cpos = [0]

        def conv_step():
            if cpos[0] >= len(conv):
                return
            r = conv[cpos[0]]
            cpos[0] += 1
            S.op("pool", _dma(UVB[r * 128:(r + 1) * 128, :], I["puv"][r * 128:(r + 1) * 128, :]), writes=["UVB"], dma=True)

        per_it = 1
        wjobs = [("wq", k, hf) for k in range(8) for hf in range(2)] + [("keysT", 0, hf) for hf in range(2)]
        wpos = [0]

        def wq_step():
            if wpos[0] >= len(wjobs):
                return
            nm, k, hf = wjobs[wpos[0]]
            wpos[0] += 1
            cs = slice(hf * D, (hf + 1) * D)
            if nm == "wq":
                S.op("sp", _dma(wst[:], I["wq"][k * 128:(k + 1) * 128, cs]), writes=["wst"], dma=True)
                S.op("act", _acopy(wq[:, k, cs], wst[:]), reads=["wst"], writes=["wq"])
            else:
                S.op("sp", _dma(wst[:], I["keysT"][:, cs]), writes=["wst"], dma=True)
                S.op("act", _acopy(keysT[:].rearrange("p a b -> p (a b)")[:, cs], wst[:]), reads=["wst"], writes=["keysT"])
        prep(items[0])
        for ii, it in enumerate(items):
            if ii + 1 < len(items):
                prep(items[ii + 1])
            rest(it)
            if ii % 2 == 0:
                conv_step()
            wq_step()
        while cpos[0] < len(conv):
            conv_step()
        while wpos[0] < len(wjobs):
            wq_step()
        S.fence()
        S.emit()

    with ExitStack() as es:
        def T(name, shape, dt):
            return es.enter_context(nc.sbuf_tensor("D_" + name, shape, dt))

        def P(name, shape, dt):
            return es.enter_context(nc.psum_tensor("D_" + name, shape, dt))

        g2b = T("g2b", [128, D], F32)
        gfb = T("gfb", [128, D], F32)
        identf = T("identf", [128, 128], F32)
        identb = T("identb", [128, 128], BF16)
        iota16 = T("iota16", [128, 16], F32)
        tk = T("tk", [128, ND], I32)
        hs = [T("hs%d" % i, [128, D], F32) for i in range(2)]
        h2 = [T("h2%d" % i, [128, D], F32) for i in range(2)]
        ei = [T("ei%d" % i, [128, 128], I32) for i in range(2)]
        gate = [T("gate%d" % i, [128, 8, 16], F32) for i in range(2)]
        h2T = T("h2T", [128, 8, 128], BF16)
        qT = T("qT", [128, 16, 128], BF16)
        sc = T("sc", [128, 16, 128], F32)
        sc2 = T("sc2", [128, 16, 128], F32)
        top = T("top", [128, 8, 2, 16], F32)
        tix = T("tix", [128, 8, 2, 16], U32)
        tixf = T("tixf", [128, 8, 2, 16], F32)
        cand = T("cand", [128, 8, 256], F32)
        cand2 = T("cand2", [128, 8, 256], F32)
        ctop = T("ctop", [128, 8, 16], F32)
        cix = T("cix", [128, 8, 16], U32)
        cab = T("cab", [128, 8, 16], U32)
        caf = T("caf", [128, 8, 16], F32)
        cbf = T("cbf", [128, 8, 16], F32)
        eq = T("eq", [128, 8, 16, 16], F32)
        i1f = T("i1f", [128, 8, 16], F32)
        i2f = T("i2f", [128, 8, 16], F32)
        ssum = T("ssum", [128, 8], F32)
        sm = T("sm", [128, 8], F32)
        h2bf = [T("h2bf%d" % i, [128, D], BF16) for i in range(2)]
        actv = T("actv", [128, 128], F32)
        wgt = T("wgt", [128, 128], F32)
        dg = [T("dg%d" % i, [128, 8, 128], BF16) for i in range(2)]
        acc = T("acc", [128, D], F32)
        junkb = T("junkb", [128, D], BF16)
        junka = T("junka", [128, D], BF16)
        prod = [T("prod%d" % i, [128, D], BF16) for i in range(4)]
        outb = T("outb", [128, D], F32)
        pT = P("pT", [128, 8, 128], BF16)
        pQ = P("pQ", [128, 16, 128], F32)
        pacc = P("pacc", [128, D], F32)

        S.op("sp", _dma(identf[:], I["identf"][:, :]), writes=["identf"], dma=True)
        S.op("dve", _copy(identb[:], identf[:]), reads=["identf"], writes=["identb"])
        S.op("sp", _dma(iota16[:], I["iota16"][:, :]), writes=["iota16"], dma=True)
        S.op("sp", _dma(tk[:], I["tokd"][:, :]), writes=["tk"], dma=True)
        S.op("sp", _dma(g2b[:], I["g2b"][:, :]), writes=["g2b"], dma=True)
        S.op("sp", _dma(gfb[:], I["gfb"][:, :]), writes=["gfb"], dma=True)
        with ExitStack() as esp:
            wst = esp.enter_context(nc.sbuf_tensor("Dp_wst", [128, 2048], F32))
            S.fence()
            S.emit()
        NSB = 15
        uvg = [T("uvg%d" % i, [128, 2 * D], BF16) for i in range(NSB)]

        def stage1(i):
            b = i % 2
            B = str(b)
            S.op("pool", _gather(hs[b][:], HS1[:, :], tk[:, i:i + 1]), reads=["tk", "HS1"], writes=["hs" + B], dma=True)
            S.op("dve", _memset(ssum[:, 0:1], 0.0), writes=["ssum"])
            S.op("dve", _stt(junkb[:], hs[b][:], 1.0, hs[b][:], ALU.mult, ALU.mult, accum_out=ssum[:, 0:1]),
                 reads=["hs" + B], writes=["junkb", "ssum"])
            S.op("dve", _ts(sm[:, 0:1], ssum[:, 0:1], 1.0 / D, EPS, ALU.mult, ALU.add), reads=["ssum"], writes=["sm"])
            S.op("act", _act(sm[:, 0:1], sm[:, 0:1], AF.Sqrt), reads=["sm"], writes=["sm"])
            S.op("dve", _recip(sm[:, 0:1], sm[:, 0:1]), reads=["sm"], writes=["sm"])
            S.op("dve", _stt(h2[b][:], hs[b][:], sm[:, 0:1], g2b[:], ALU.mult, ALU.mult), reads=["hs" + B, "sm", "g2b"],
                 writes=["h2" + B])
            S.op("act", _acopy(h2bf[b][:], h2[b][:]), reads=["h2" + B], writes=["h2bf" + B])
            for k in range(8):
                S.op("pe", _tr(pT[:, k, :], h2bf[b][:, k * 128:(k + 1) * 128], identb[:]), reads=["h2bf" + B, "identb"], writes=["pT"])
            S.op("act", _acopy(h2T[:], pT[:]), reads=["pT"], writes=["h2T"])
            for hp in range(16):
                for k in range(8):
                    S.op("pe", _mm(pQ[:, hp, :], wq[:, k, hp * 128:(hp + 1) * 128], h2T[:, k, :], k == 0, k == 7),
                         reads=["wq", "h2T"], writes=["pQ"])
            S.op("act", _acopy(qT[:], pQ[:]), reads=["pQ"], writes=["qT"])
            for hp in range(16):
                S.op("pe", _mm(pQ[:, hp, :], qT[:, hp, :], keysT[:, hp, :], True, True), reads=["qT", "keysT"], writes=["pQ"])
            S.op("act", _acopy(sc[:], pQ[:]), reads=["pQ"], writes=["sc"])
            HP = [(hp, hp // 2, hp % 2) for hp in range(16)]
            for hp, h_, p_ in HP:
                S.op("dve", lambda e, h_=h_, p_=p_, hp=hp: e.max(out=top[:, h_, p_, 0:8], in_=sc[:, hp, :]),
                     reads=["sc"], writes=["top%d" % hp])
            for hp, h_, p_ in HP:
                S.op("dve", lambda e, h_=h_, p_=p_, hp=hp: e.max_index(out=tix[:, h_, p_, 0:8], in_max=top[:, h_, p_, 0:8],
                                                                       in_values=sc[:, hp, :]),
                     reads=["sc", "top%d" % hp], writes=["tix%d" % hp])
            for hp, h_, p_ in HP:
                S.op("dve", lambda e, h_=h_, p_=p_, hp=hp: e.match_replace(out=sc2[:, hp, :], in_to_replace=top[:, h_, p_, 0:8],
                                                                           in_values=sc[:, hp, :], imm_value=-1e30),
                     reads=["sc", "top%d" % hp], writes=["sc2_%d" % hp])
            for hp, h_, p_ in HP:
                S.op("dve", lambda e, h_=h_, p_=p_, hp=hp: e.max(out=top[:, h_, p_, 8:16], in_=sc2[:, hp, :]),
                     reads=["sc2_%d" % hp], writes=["topb%d" % hp])
            for hp, h_, p_ in HP:
                S.op("dve", lambda e, h_=h_, p_=p_, hp=hp: e.max_index(out=tix[:, h_, p_, 8:16], in_max=top[:, h_, p_, 8:16],
                                                                       in_values=sc2[:, hp, :]),
                     reads=["sc2_%d" % hp, "topb%d" % hp], writes=["tixb%d" % hp])
            tk_all = ["top%d" % x for x in range(16)] + ["topb%d" % x for x in range(16)]
            ti_all = ["tix%d" % x for x in range(16)] + ["tixb%d" % x for x in range(16)]
            S.op("dve", _copy(tixf[:], tix[:]), reads=ti_all, writes=["tixf"])
            cand4 = cand[:].rearrange("p h (a b) -> p h a b", a=16)
            S.op("dve", _tt(cand4, top[:, :, 0, :].unsqueeze(3).broadcast_to([128, 8, 16, 16]),
                            top[:, :, 1, :].unsqueeze(2).broadcast_to([128, 8, 16, 16]), ALU.add),
                 reads=tk_all, writes=["cand"])
            for h_ in range(8):
                S.op("dve", lambda e, h_=h_: e.max(out=ctop[:, h_, 0:8], in_=cand[:, h_, :]), reads=["cand"], writes=["ctop%d" % h_])
            for h_ in range(8):
                S.op("dve", lambda e, h_=h_: e.max_index(out=cix[:, h_, 0:8], in_max=ctop[:, h_, 0:8], in_values=cand[:, h_, :]),
                     reads=["cand", "ctop%d" % h_], writes=["cix%d" % h_])
            for h_ in range(8):
                S.op("dve", lambda e, h_=h_: e.match_replace(out=cand2[:, h_, :], in_to_replace=ctop[:, h_, 0:8],
                                                             in_values=cand[:, h_, :], imm_value=-1e30),
                     reads=["cand", "ctop%d" % h_], writes=["cand2_%d" % h_])
            for h_ in range(8):
                S.op("dve", lambda e, h_=h_: e.max(out=ctop[:, h_, 8:16], in_=cand2[:, h_, :]), reads=["cand2_%d" % h_],
                     writes=["ctopb%d" % h_])
            for h_ in range(8):
                S.op("dve", lambda e, h_=h_: e.max_index(out=cix[:, h_, 8:16], in_max=ctop[:, h_, 8:16], in_values=cand2[:, h_, :]),
                     reads=["cand2_%d" % h_, "ctopb%d" % h_], writes=["cixb%d" % h_])
            S.op("dve", lambda e: e.tensor_single_scalar(out=cab[:], in_=cix[:], scalar=4, op=ALU.logical_shift_right),
                 reads=["cix%d" % x for x in range(8)] + ["cixb%d" % x for x in range(8)], writes=["cab"])
            S.op("dve", _copy(caf[:], cab[:]), reads=["cab"], writes=["caf"])
            S.op("dve", lambda e: e.tensor_single_scalar(out=cab[:], in_=cix[:], scalar=15, op=ALU.bitwise_and),
                 reads=["cix%d" % x for x in range(8)] + ["cixb%d" % x for x in range(8)], writes=["cab"])
            S.op("dve", _copy(cbf[:], cab[:]), reads=["cab"], writes=["cbf"])
            io4 = iota16[:, :].unsqueeze(1).unsqueeze(1).broadcast_to([128, 8, 16, 16])
            for (src, half, dst, kd) in ((caf, 0, i1f, "i1f"), (cbf, 1, i2f, "i2f")):
                S.op("dve", _tt(eq[:], src[:].unsqueeze(3).broadcast_to([128, 8, 16, 16]), io4, ALU.is_equal),
                     reads=["caf", "cbf", "iota16"], writes=["eq"])
                S.op("dve", _tt(eq[:], eq[:], tixf[:, :, half, :].unsqueeze(2).broadcast_to([128, 8, 16, 16]), ALU.mult),
                     reads=["eq", "tixf"], writes=["eq"])
                S.op("dve", _rsum(dst[:], eq[:]), reads=["eq"], writes=[kd])
            S.op("dve", _stt(i1f[:], i1f[:], 128.0, i2f[:], ALU.mult, ALU.add), reads=["i1f", "i2f"], writes=["i1f"])
            S.op("dve", _ts(i1f[:], i1f[:], 0.0, 16383.0, ALU.max, ALU.min), reads=["i1f"], writes=["i1f"])
            S.op("dve", _copy(ei[b][:].rearrange("p (h j) -> p h j", h=8), i1f[:]), reads=["i1f"], writes=["ei" + B])
            S.op("dve", _tt(gate[b][:], ctop[:], ctop[:, :, 0:1].broadcast_to([128, 8, 16]), ALU.subtract),
                 reads=["ctop%d" % x for x in range(8)] + ["ctopb%d" % x for x in range(8)], writes=["gate" + B])
            S.op("act", _act(gate[b][:], gate[b][:], AF.Exp), reads=["gate" + B], writes=["gate" + B])
            S.op("dve", _rsum(ssum[:], gate[b][:]), reads=["gate" + B], writes=["ssum"])
            S.op("dve", _recip(ssum[:], ssum[:]), reads=["ssum"], writes=["ssum"])
            S.op("dve", _tt(gate[b][:], gate[b][:], ssum[:].unsqueeze(2).broadcast_to([128, 8, 16]), ALU.mult),
                 reads=["gate" + B, "ssum"], writes=["gate" + B])

        gcnt = [0]
        pcnt = [0]

        def stage2(i, pend=()):
            b = i % 2
            B = str(b)
            pend = list(pend)
            per_slot = -(-len(pend) // 112) if pend else 0
            ppos = [0]

            def drain(n):
                for o in pend[ppos[0]:ppos[0] + n]:
                    S.op(*o)
                ppos[0] += n
            S.op("dve", _memset(actv[:], 0.0), writes=["actv%d" % x for x in range(128)])
            slots = []
            for j in range(128):
                gb = gcnt[0] % NSB
                gcnt[0] += 1
                slots.append(gb)
                S.op("pool", _gather(uvg[gb][:], UVB[:, :], ei[b][:, j:j + 1]), reads=["ei" + B, "UVB"], writes=["uvg%d" % gb],
                     dma=True)
                if j % 8 in (0, 1, 2, 4, 5):
                    pi_ = pcnt[0] % 4
                    pcnt[0] += 1
                    S.op("dve", _tt(prod[pi_][:], uvg[gb][:, 0:D], h2bf[b][:], ALU.mult), reads=["uvg%d" % gb, "h2bf" + B],
                         writes=["prod%d" % pi_])
                    S.op("act", lambda e, pi_=pi_, j=j: e.activation(out=junka[:], in_=prod[pi_][:], func=AF.Copy,
                                                                   accum_out=actv[:, j:j + 1]),
                         reads=["prod%d" % pi_], writes=["junka", "actv%d" % j])
                else:
                    S.op("dve", _stt(junkb[:], uvg[gb][:, 0:D], 1.0, h2bf[b][:], ALU.mult, ALU.mult, accum_out=actv[:, j:j + 1]),
                         reads=["uvg%d" % gb, "h2bf" + B], writes=["junkb", "actv%d" % j])
                drain(per_slot)
                if j % 8 == 7:
                    g0 = j - 7
                    db = (j // 8) % 2
                    S.op("act", _act(wgt[:, g0:j + 1], actv[:, g0:j + 1], AF.Gelu), reads=["actv%d" % x for x in range(g0, j + 1)],
                         writes=["wgt"])
                    S.op("dve", _tt(wgt[:, g0:j + 1], wgt[:, g0:j + 1], gate[b][:].rearrange("p h j -> p (h j)")[:, g0:j + 1], ALU.mult),
                         reads=["wgt", "gate" + B], writes=["wgt"])
                    for s_ in range(8):
                        S.op("act", _act(dg[db][:, s_, :], identb[:], AF.Copy, scale=wgt[:, g0 + s_:g0 + s_ + 1]),
                             reads=["identb", "wgt"], writes=["dg%d_%d" % (db, s_)])
                    for s_ in range(8):
                        jj = g0 + s_
                        sb_ = slots[jj]
                        for hf in range(2):
                            S.op("pe", _mm(pacc[:, hf * 512:(hf + 1) * 512], dg[db][:, s_, :],
                                           uvg[sb_][:, D + hf * 512:D + (hf + 1) * 512], jj == 0, jj == 127),
                                 reads=["dg%d_%d" % (db, s_), "uvg%d" % sb_], writes=["pacc"])
            drain(len(pend))
            S.op("dve", _tt(acc[:], pacc[:], hs[b][:], ALU.add), reads=["pacc", "hs" + B], writes=["acc"])
            S.op("dve", _memset(sm[:, 1:2], 0.0), writes=["sm1"])
            S.op("dve", _stt(junkb[:], acc[:], 1.0, acc[:], ALU.mult, ALU.mult, accum_out=sm[:, 1:2]), reads=["acc"],
                 writes=["junkb", "sm1"])
            S.op("dve", _ts(sm[:, 1:2], sm[:, 1:2], 1.0 / D, EPS, ALU.mult, ALU.add), reads=["sm1"], writes=["sm1"])
            S.op("act", _act(sm[:, 1:2], sm[:, 1:2], AF.Sqrt), reads=["sm1"], writes=["sm1"])
            S.op("dve", _recip(sm[:, 1:2], sm[:, 1:2]), reads=["sm1"], writes=["sm1"])
            S.op("dve", _stt(outb[:], acc[:], sm[:, 1:2], gfb[:], ALU.mult, ALU.mult), reads=["acc", "sm1", "gfb"],
                 writes=["outb"])
            S.op("sp", _dma(OUT[i * 128:(i + 1) * 128, :], outb[:]), reads=["outb"], writes=["OUT"], dma=True)

        stage1(0)
        for i in range(ND):
            pend = []
            if i + 1 < ND:
                S.capture = pend
                stage1(i + 1)
                S.capture = None
            stage2(i, pend)
        S.fence()
        S.emit()

    pre_es.close()
    top_es.close()
    return nc


def _common_inputs(inp):
    f = np.float32
    A = {}
    A["meta"] = np.ascontiguousarray(inp["meta"], f)
    A["g1c"] = np.ascontiguousarray(inp["norm1_g"][0].reshape(8, 128).T, f)
    A["w_in"] = np.ascontiguousarray(inp["w_in"][0], f)
    lre, lim, lst = inp["s5_lam_re"][0], inp["s5_lam_im"][0], inp["s5_log_step"][0]
    lst_full = np.broadcast_to(lst[:, :, None], (2, 32, 64))

    def col(a):
        a5 = a.reshape(2, 16, 2, 64)
        return np.ascontiguousarray(a5.transpose(2, 3, 0, 1).reshape(128, 32), f)

    A["lre_c"], A["lim_c"], A["lst_c"] = col(lre), col(lim), col(lst_full)

    def c8(cm):
        a = cm.reshape(16, 2, 16, 64).transpose(1, 3, 0, 2)
        a = np.broadcast_to(a[:, :, :, None, :], (2, 64, 16, 8, 16))
        return np.ascontiguousarray(a.reshape(128, 2048), f)

    def bc(bm):
        a = bm.reshape(16, 2, 64, 16).transpose(1, 2, 0, 3)
        a = np.broadcast_to(a[:, :, :, None, :], (2, 64, 16, 8, 16))
        return np.ascontiguousarray(a.reshape(128, 2048), f)

    A["cre8"], A["cim8"] = c8(inp["s5_c_re"][0]), c8(inp["s5_c_im"][0])
    A["brec"], A["bimc"] = bc(inp["s5_b_re"][0]), bc(inp["s5_b_im"][0])
    dd = inp["s5_d"][0].reshape(32, 16)
    A["drow"] = np.ascontiguousarray(np.broadcast_to(dd.T[None, :, :], (8, 16, 32)).reshape(128, 32), f)
    p = np.arange(128)
    cst8 = np.zeros((128, 32), f)
    cst8[:, 0:8] = (p[:, None] % 8 == np.arange(8)[None, :])
    cst8[:, 8:24] = (p[:, None] // 8 == np.arange(16)[None, :])
    cst8[:, 24:32] = (p[:, None] // 16 == np.arange(8)[None, :])
    A["cst8"] = cst8
    m = np.arange(128)
    selC = np.zeros((128, 8, 128), f)
    for g8 in range(8):
        selC[:, g8, :] = ((p[:, None] % 16) == (m[None, :] % 16)) & ((m[None, :] // 16) == g8)
    A["selC"] = selC.reshape(128, 1024)
    A["ramp16"] = np.ascontiguousarray(np.broadcast_to(np.arange(-7, 9, dtype=f)[None, :], (128, 16)), f)
    ii = p[:, None] // 16
    jj = m[None, :] // 16
    A["mfb"] = np.concatenate([(ii <= jj), (ii >= jj)], axis=1).astype(f)
    A["w_glu"] = np.ascontiguousarray(inp["w_glu"][0], f)
    A["lb0"] = np.ascontiguousarray(inp["hg_lb"][0].reshape(4, 128).T, f)
    A["lb1"] = np.ascontiguousarray(inp["hg_lb"][1].reshape(4, 128).T, f)
    A["hgn"] = np.ascontiguousarray(inp["hg_norm_g"][0].reshape(4, 128).T, f)
    A["w_hg"] = np.ascontiguousarray(inp["w_hg_out"][0], f)
    A["w_out"] = np.ascontiguousarray(inp["w_out"][0], f)
    A["g2b"] = np.ascontiguousarray(np.broadcast_to(inp["norm2_g"][0][None, :], (128, D)), f)
    A["gfb"] = np.ascontiguousarray(np.broadcast_to(inp["final_g"][None, :], (128, D)), f)
    A["wq"] = np.ascontiguousarray(inp["peer_wq"][0], f)
    kz = inp["peer_keys"][0].reshape(16, 128, 128)
    A["keysT"] = np.ascontiguousarray(kz.transpose(2, 0, 1).reshape(128, 2048), f)
    A["puv"] = np.ascontiguousarray(np.concatenate([inp["peer_u"][0], inp["peer_v"][0]], axis=1), f)
    A["identf"] = np.eye(128, dtype=f)
    r = np.arange(1, 129, dtype=f)
    A["ramps"] = np.ascontiguousarray(np.broadcast_to(np.concatenate([r, r[::-1]])[None, :], (128, 256)), f)
    s = np.arange(64)[:, None]
    t = np.arange(64)[None, :]
    A["masks"] = np.concatenate([(t >= s), (t <= s)], axis=1).astype(f)
    A["iota16"] = np.ascontiguousarray(np.broadcast_to(np.arange(16, dtype=f)[None, :], (128, 16)), f)
    return A


_PROG_CACHE = {}


def run_sequences(inp, seqs, assign, ND, debug=False):
    SEQ = seqs[0].shape[0]
    key = (SEQ, ND, debug)
    if key not in _PROG_CACHE:
        _PROG_CACHE[key] = build_program(SEQ, ND, debug)
    nc = _PROG_CACHE[key]
    A = _common_inputs(inp)
    in_maps = []
    for (si, t0, nt) in assign:
        m = dict(A)
        m["xs"] = np.ascontiguousarray(seqs[si], np.float32)
        tiles = [min(t0 + i, t0 + nt - 1) for i in range(ND)]
        tok = np.stack([np.arange(t * 128, (t + 1) * 128) for t in tiles], axis=1).astype(np.int32)
        m["tokd"] = np.ascontiguousarray(tok)
        in_maps.append(m)
    res = run_bass_kernel_spmd(nc, in_maps, core_ids=list(range(len(assign))))
    outs = [np.zeros((SEQ, D), np.float32) for _ in seqs]
    for ci, (si, t0, nt) in enumerate(assign):
        o = res.results[ci]["outd"]
        outs[si][t0 * 128:(t0 + nt) * 128] = o[:nt * 128]
    return outs, res


def kernel(**inputs):
    inp = {k: np.asarray(v) for k, v in inputs.items()}
    seqs = [inp["x_prompt"][0], inp["x_sample"][0], inp["x_sample"][1]]
    assign = [(0, 0, 43), (1, 0, 43), (2, 0, 64), (0, 43, 43), (1, 43, 43), (2, 64, 64), (0, 86, 42), (1, 86, 42)]
    outs, _ = run_sequences(inp, seqs, assign, ND=64)
    y_prompt = outs[0][None].astype(np.float32)
    y_sample = np.stack([outs[1], outs[2]], axis=0).astype(np.float32)
    return (y_prompt, y_sample)
```
